# Optimizing a Trainium2 kernel written in Bass

```python
import math
import jax, jax.numpy as jnp
from jax import lax
import numpy as np

D_MODEL = 2048
BATCH = 2
SEQ = 16384
DEPTH = 2
DEC_BATCH = 2
DEC_SEQ = 8192
PAST_LEN = 128

GDN_HEADS = 8
GDN_DK = 128
GDN_DV = 128
SHORT_CONV = 3
MLSTM_HEADS = 4
MLSTM_DK = 128
MLSTM_DV = 256
CHUNK = 64
ATT_SLOTS = 16
ATT_HEAD_DIM = 128
DILATED_PATTERNS = ((128, 1), (512, 4), (2048, 16))
N_GROUPS = 3
ROT_DIM = ATT_HEAD_DIM // 4
ROPE_THETA = 500000.0
D_FF = 5632
FFN_CONV = 3
EPS = 1e-6

N_EVEN = (DEPTH + 1) // 2
N_ODD = DEPTH // 2

A_QKV = GDN_HEADS * (2 * GDN_DK + GDN_DV)
A_Z = GDN_HEADS * GDN_DV
A_GATES = 4 * GDN_HEADS
B_QKV = MLSTM_HEADS * (2 * MLSTM_DK + MLSTM_DV)
B_O = MLSTM_HEADS * MLSTM_DV
B_GATES = 4 * MLSTM_HEADS
EVEN_IN = A_QKV + A_Z + A_GATES + B_QKV + B_O + B_GATES
EVEN_MIX = GDN_HEADS * GDN_DV + MLSTM_HEADS * MLSTM_DV
ODD_IN = 3 * N_GROUPS * ATT_SLOTS * ATT_HEAD_DIM
ODD_MIX = ATT_SLOTS * ATT_HEAD_DIM

kernel_name = 'hybrid_bidir_gdn_mlstm_dilated_encoder'


def rmsnorm(x, g):
    xf = x.astype(jnp.float32)
    y = xf * lax.rsqrt(jnp.mean(xf * xf, axis=-1, keepdims=True) + EPS)
    return (y * g.astype(jnp.float32)).astype(x.dtype)


def l2norm(x):
    xf = x.astype(jnp.float32)
    return xf * lax.rsqrt(jnp.sum(xf * xf, axis=-1, keepdims=True) + EPS)


def flip_seq(t):
    return jnp.flip(t, axis=1)


def dwconv_centred(x, w):
    width = w.shape[0]
    pad = width // 2
    s = x.shape[1]
    xp = jnp.pad(x, ((0, 0), (pad, pad), (0, 0)))
    return sum(xp[:, j:j + s] * w[j] for j in range(width))


def to_chunks(x):
    b, s, h = x.shape[:3]
    x = x.reshape((b, s // CHUNK, CHUNK, h) + x.shape[3:])
    perm = (1, 0, 3, 2) + tuple(range(4, x.ndim))
    return x.transpose(perm)


def from_chunks(x):
    nc, b, h, l, d = x.shape
    return x.transpose(1, 0, 3, 2, 4).reshape(b, nc * l, h, d)


def gated_delta_scan(q, k, v, g, beta):
    q, k, v, g, beta = (to_chunks(t) for t in (q, k, v, g, beta))
    incl = jnp.tril(jnp.ones((CHUNK, CHUNK), bool))
    strict = jnp.tril(jnp.ones((CHUNK, CHUNK), bool), -1)
    gc = jnp.cumsum(g, axis=-1)
    decay = jnp.exp(jnp.where(incl, gc[..., :, None] - gc[..., None, :], -jnp.inf))
    kb = k * beta[..., None]
    lmat = jnp.where(strict, jnp.einsum('nbhik,nbhjk->nbhij', kb, k) * decay, 0.0)
    eye = jnp.eye(CHUNK, dtype=q.dtype)
    tmat = lax.linalg.triangular_solve(eye + lmat, jnp.broadcast_to(eye, lmat.shape),
                                       left_side=True, lower=True, unit_diagonal=True)
    u = tmat @ (v * beta[..., None])
    w = tmat @ (kb * jnp.exp(gc)[..., None])
    a_qk = jnp.einsum('nbhik,nbhjk->nbhij', q, k) * decay
    q_dec = q * jnp.exp(gc)[..., None]
    g_last = gc[..., -1]
    k_dec = k * jnp.exp(g_last[..., None] - gc)[..., None]

    def step(state, xs):
        u_c, w_c, a_c, qd_c, kd_c, gl_c = xs
        v_new = u_c - w_c @ state
        o = qd_c @ state + a_c @ v_new
        state = state * jnp.exp(gl_c)[..., None, None] + jnp.einsum('bhlk,bhlv->bhkv', kd_c, v_new)
        return state, o

    bsz, h = q.shape[1], q.shape[2]
    s0 = jnp.zeros((bsz, h, q.shape[-1], v.shape[-1]), jnp.float32)
    _, o = lax.scan(step, s0, (u, w, a_qk, q_dec, k_dec, g_last))
    return from_chunks(o)


def mlstm_scan(q, k, v, ig, logf):
    q, k, v, ig, logf = (to_chunks(t) for t in (q, k, v, ig, logf))
    incl = jnp.tril(jnp.ones((CHUNK, CHUNK), bool))
    bcum = jnp.cumsum(logf, axis=-1)
    dmat = jnp.where(incl, bcum[..., :, None] - bcum[..., None, :] + ig[..., None, :], -jnp.inf)
    dmax = jnp.max(dmat, axis=-1)
    qk = jnp.einsum('nbhlk,nbhsk->nbhls', q, k)
    b_last = bcum[..., -1]
    g_end = b_last[..., None] - bcum + ig
    g_end_max = jnp.max(g_end, axis=-1)

    def step(carry, xs):
        c_st, n_st, m_st = carry
        q_c, k_c, v_c, b_c, d_c, dm_c, qk_c, bl_c, ge_c, gm_c = xs
        inter = b_c + m_st[..., None]
        m_t = jnp.maximum(inter, dm_c)
        w_intra = jnp.exp(d_c - m_t[..., None]) * qk_c
        w_inter = jnp.exp(inter - m_t)
        num = (w_inter[..., None] * jnp.einsum('bhlk,bhkv->bhlv', q_c, c_st)
               + jnp.einsum('bhls,bhsv->bhlv', w_intra, v_c))
        den = w_inter * jnp.einsum('bhlk,bhk->bhl', q_c, n_st) + jnp.sum(w_intra, axis=-1)
        h = num / jnp.maximum(jnp.abs(den), jnp.exp(-m_t))[..., None]
        m_new = jnp.maximum(bl_c + m_st, gm_c)
        dec = jnp.exp(bl_c + m_st - m_new)
        wk = jnp.exp(ge_c - m_new[..., None])[..., None] * k_c
        c_st = dec[..., None, None] * c_st + jnp.einsum('bhlk,bhlv->bhkv', wk, v_c)
        n_st = dec[..., None] * n_st + jnp.sum(wk, axis=-2)
        return (c_st, n_st, m_new), h

    bsz, h = q.shape[1], q.shape[2]
    init = (jnp.zeros((bsz, h, q.shape[-1], v.shape[-1]), jnp.float32),
            jnp.zeros((bsz, h, q.shape[-1]), jnp.float32),
            jnp.zeros((bsz, h), jnp.float32))
    _, out = lax.scan(step, init, (q, k, v, bcum, dmat, dmax, qk, b_last, g_end, g_end_max))
    return from_chunks(out)


def even_mixer(h, w_in, conv_w, a_log, dt_bias, gdn_g, ml_bias, ml_g, w_out):
    f32 = jnp.float32
    bsz, s, _ = h.shape
    p = h @ w_in
    c1 = A_QKV
    c2 = c1 + A_Z
    c3 = c2 + A_GATES
    c4 = c3 + B_QKV
    c5 = c4 + B_O
    qkv_a, z_a, gt_a, qkv_b, o_b, gt_b = jnp.split(p, [c1, c2, c3, c4, c5], axis=-1)

    qkv_a = jax.nn.silu(dwconv_centred(qkv_a, conv_w))
    qa, ka, va = jnp.split(qkv_a, [GDN_HEADS * GDN_DK, 2 * GDN_HEADS * GDN_DK], axis=-1)
    qa = l2norm(qa.reshape(bsz, s, GDN_HEADS, GDN_DK)) * GDN_DK ** -0.5
    ka = l2norm(ka.reshape(bsz, s, GDN_HEADS, GDN_DK))
    va = va.reshape(bsz, s, GDN_HEADS, GDN_DV).astype(f32)
    gt_a = gt_a.astype(f32).reshape(bsz, s, 2, 2, GDN_HEADS)
    g = -jnp.exp(a_log) * jax.nn.softplus(gt_a[:, :, 0] + dt_bias)
    beta = jax.nn.sigmoid(gt_a[:, :, 1])
    o_fwd = gated_delta_scan(qa, ka, va, g[:, :, 0], beta[:, :, 0])
    o_bwd = flip_seq(gated_delta_scan(flip_seq(qa), flip_seq(ka), flip_seq(va),
                                      flip_seq(g[:, :, 1]), flip_seq(beta[:, :, 1])))
    za = z_a.reshape(bsz, s, GDN_HEADS, GDN_DV).astype(f32)
    out_a = (rmsnorm(o_fwd + o_bwd, gdn_g) * jax.nn.silu(za)).reshape(bsz, s, GDN_HEADS * GDN_DV)

    qb, kb, vb = jnp.split(qkv_b, [MLSTM_HEADS * MLSTM_DK, 2 * MLSTM_HEADS * MLSTM_DK], axis=-1)
    qb = qb.reshape(bsz, s, MLSTM_HEADS, MLSTM_DK).astype(f32) * MLSTM_DK ** -0.5
    kb = kb.reshape(bsz, s, MLSTM_HEADS, MLSTM_DK).astype(f32)
    vb = vb.reshape(bsz, s, MLSTM_HEADS, MLSTM_DV).astype(f32)
    gt_b = gt_b.astype(f32).reshape(bsz, s, 2, 2, MLSTM_HEADS) + ml_bias
    ig = gt_b[:, :, 0]
    logf = jax.nn.log_sigmoid(gt_b[:, :, 1])
    h_fwd = mlstm_scan(qb, kb, vb, ig[:, :, 0], logf[:, :, 0])
    h_bwd = flip_seq(mlstm_scan(flip_seq(qb), flip_seq(kb), flip_seq(vb),
                                flip_seq(ig[:, :, 1]), flip_seq(logf[:, :, 1])))
    ob = jax.nn.sigmoid(o_b.astype(f32))
    out_b = rmsnorm(h_fwd + h_bwd, ml_g.reshape(MLSTM_HEADS, MLSTM_DV)).reshape(bsz, s, MLSTM_HEADS * MLSTM_DV) * ob

    mixed = jnp.concatenate([out_a, out_b], axis=-1).astype(h.dtype)
    return mixed @ w_out


def rope_tables(s):
    inv = ROPE_THETA ** (-jnp.arange(0, ROT_DIM, 2, dtype=jnp.float32) / ROT_DIM)
    ang = jnp.arange(s, dtype=jnp.float32)[:, None] * inv[None, :]
    return jnp.cos(ang), jnp.sin(ang)


def partial_rope(x, cos, sin):
    xf = x.astype(jnp.float32)
    half = ROT_DIM // 2
    x1, x2, rest = xf[..., :half], xf[..., half:ROT_DIM], xf[..., ROT_DIM:]
    c = cos[None, :, None, :]
    sn = sin[None, :, None, :]
    return jnp.concatenate([x1 * c - x2 * sn, x2 * c + x1 * sn, rest], axis=-1).astype(x.dtype)


def banded_attention(q, k, v, radius):
    n, l, h, dh = q.shape
    blk = radius
    nb = -(-l // blk)
    lp = nb * blk
    qp = jnp.pad(q, ((0, 0), (0, lp - l), (0, 0), (0, 0))).reshape(n, nb, blk, h, dh)

    def windows(t):
        tp = jnp.pad(t, ((0, 0), (blk, lp - l + blk), (0, 0), (0, 0))).reshape(n, nb + 2, blk, h, dh)
        return jnp.concatenate([tp[:, :-2], tp[:, 1:-1], tp[:, 2:]], axis=2)

    kw, vw = windows(k), windows(v)
    qpos = jnp.arange(nb)[:, None] * blk + jnp.arange(blk)[None, :]
    kpos = (jnp.arange(nb)[:, None] - 1) * blk + jnp.arange(3 * blk)[None, :]
    valid = ((jnp.abs(qpos[:, :, None] - kpos[:, None, :]) <= radius)
             & (kpos[:, None, :] >= 0) & (kpos[:, None, :] < l))
    scores = jnp.einsum('nbqhd,nbkhd->nbhqk', qp, kw, preferred_element_type=jnp.float32) * dh ** -0.5
    scores = jnp.where(valid[None, :, None], scores, -jnp.inf)
    lse = jax.nn.logsumexp(scores, axis=-1)
    p = jnp.exp(scores - lse[..., None])
    o = jnp.einsum('nbhqk,nbkhd->nbqhd', p.astype(v.dtype), vw, preferred_element_type=jnp.float32)
    o = o.reshape(n, lp, h, dh)[:, :l]
    lse = lse.transpose(0, 1, 3, 2).reshape(n, lp, h)[:, :l]
    return o, lse


def dilated_attention(q, k, v, dil, radius):
    bsz, s, h, dh = q.shape
    l = s // dil

    def fold(t):
        return t.reshape(bsz, l, dil, h, dh).transpose(0, 2, 1, 3, 4).reshape(bsz * dil, l, h, dh)

    o, lse = banded_attention(fold(q), fold(k), fold(v), radius)
    o = o.reshape(bsz, dil, l, h, dh).transpose(0, 2, 1, 3, 4).reshape(bsz, s, h, dh)
    lse = lse.reshape(bsz, dil, l, h).transpose(0, 2, 1, 3).reshape(bsz, s, h)
    return o, lse


def odd_mixer(h, w_in, w_out):
    bsz, s, _ = h.shape
    p = (h @ w_in).reshape(bsz, s, 3, N_GROUPS, ATT_SLOTS, ATT_HEAD_DIM)
    cos, sin = rope_tables(s)
    outs, lses = [], []
    for gi, (win, dil) in enumerate(DILATED_PATTERNS):
        q = partial_rope(p[:, :, 0, gi], cos, sin)
        k = partial_rope(p[:, :, 1, gi], cos, sin)
        v = p[:, :, 2, gi]
        o, lse = dilated_attention(q, k, v, dil, win // (2 * dil))
        outs.append(o)
        lses.append(lse)
    wts = jax.nn.softmax(jnp.stack(lses, axis=0), axis=0)
    o = jnp.sum(wts[..., None] * jnp.stack(outs, axis=0), axis=0)
    return o.reshape(bsz, s, ODD_MIX).astype(h.dtype) @ w_out


def conv_ffn(h, w_up, conv_w, conv_b, w_down):
    gate, val = jnp.split(h @ w_up, 2, axis=-1)
    gate = dwconv_centred(gate, conv_w) + conv_b
    return (jax.nn.silu(gate) * val) @ w_down


def trunk(x, norm_mix, ev_w_in, gdn_conv, gdn_a_log, gdn_dt_bias, gdn_norm, ml_gate_bias, ml_norm,
          ev_w_out, od_w_in, od_w_out, norm_ffn, ffn_w_up, ffn_conv, ffn_conv_b, ffn_w_down, norm_final):
    for layer in range(DEPTH):
        li = layer // 2
        hn = rmsnorm(x, norm_mix[layer])
        if layer % 2 == 0:
            mix = even_mixer(hn, ev_w_in[li], gdn_conv[li], gdn_a_log[li], gdn_dt_bias[li], gdn_norm[li],
                             ml_gate_bias[li], ml_norm[li], ev_w_out[li])
        else:
            mix = odd_mixer(hn, od_w_in[li], od_w_out[li])
        x = x + mix
        x = x + conv_ffn(rmsnorm(x, norm_ffn[layer]), ffn_w_up[layer], ffn_conv[layer],
                         ffn_conv_b[layer], ffn_w_down[layer])
    return rmsnorm(x, norm_final)


def setup_inputs(seed: int = 0) -> dict:
    key = jax.random.key(seed)
    ks = jax.random.split(key, 20)
    f32 = jnp.float32

    def nrm(k, shape, scale):
        return scale * jax.random.normal(k, shape, f32)

    x_prompt = nrm(ks[0], (BATCH, SEQ, D_MODEL), 1.0)
    x_sample = nrm(ks[1], (DEC_BATCH, DEC_SEQ, D_MODEL), 1.0)
    norm_mix = 1.0 + nrm(ks[2], (DEPTH, D_MODEL), 0.02)
    ev_w_in = nrm(ks[3], (N_EVEN, D_MODEL, EVEN_IN), D_MODEL ** -0.5)
    gdn_conv = nrm(ks[4], (N_EVEN, SHORT_CONV, A_QKV), SHORT_CONV ** -0.5)
    gdn_a_log = jnp.log(jax.random.uniform(ks[5], (N_EVEN, 2, GDN_HEADS), f32, 1.0, 16.0))
    dt = jnp.exp(jax.random.uniform(ks[6], (N_EVEN, 2, GDN_HEADS), f32, math.log(1e-3), math.log(1e-1)))
    gdn_dt_bias = dt + jnp.log(-jnp.expm1(-dt))
    gdn_norm = 1.0 + nrm(ks[7], (N_EVEN, GDN_DV), 0.02)
    i_bias = nrm(ks[8], (N_EVEN, 1, 2, MLSTM_HEADS), 0.1)
    f_bias = jnp.linspace(3.0, 6.0, MLSTM_HEADS, dtype=f32) + nrm(ks[9], (N_EVEN, 1, 2, MLSTM_HEADS), 0.1)
    ml_gate_bias = jnp.concatenate([i_bias, f_bias], axis=1)
    ml_norm = 1.0 + nrm(ks[10], (N_EVEN, MLSTM_HEADS * MLSTM_DV), 0.02)
    ev_w_out = nrm(ks[11], (N_EVEN, EVEN_MIX, D_MODEL), EVEN_MIX ** -0.5)
    od_w_in = nrm(ks[12], (N_ODD, D_MODEL, ODD_IN), D_MODEL ** -0.5)
    od_w_out = nrm(ks[13], (N_ODD, ODD_MIX, D_MODEL), ODD_MIX ** -0.5)
    norm_ffn = 1.0 + nrm(ks[14], (DEPTH, D_MODEL), 0.02)
    ffn_w_up = nrm(ks[15], (DEPTH, D_MODEL, 2 * D_FF), D_MODEL ** -0.5)
    ffn_conv = nrm(ks[16], (DEPTH, FFN_CONV, D_FF), FFN_CONV ** -0.5)
    ffn_conv_b = nrm(ks[17], (DEPTH, D_FF), 0.02)
    ffn_w_down = nrm(ks[18], (DEPTH, D_FF, D_MODEL), D_FF ** -0.5)
    norm_final = 1.0 + nrm(ks[19], (D_MODEL,), 0.02)
    return {'x_prompt': x_prompt, 'x_sample': x_sample, 'norm_mix': norm_mix, 'ev_w_in': ev_w_in,
            'gdn_conv': gdn_conv, 'gdn_a_log': gdn_a_log, 'gdn_dt_bias': gdn_dt_bias, 'gdn_norm': gdn_norm,
            'ml_gate_bias': ml_gate_bias, 'ml_norm': ml_norm, 'ev_w_out': ev_w_out, 'od_w_in': od_w_in,
            'od_w_out': od_w_out, 'norm_ffn': norm_ffn, 'ffn_w_up': ffn_w_up, 'ffn_conv': ffn_conv,
            'ffn_conv_b': ffn_conv_b, 'ffn_w_down': ffn_w_down, 'norm_final': norm_final}


def reference(x_prompt, x_sample, norm_mix, ev_w_in, gdn_conv, gdn_a_log, gdn_dt_bias, gdn_norm,
              ml_gate_bias, ml_norm, ev_w_out, od_w_in, od_w_out, norm_ffn, ffn_w_up, ffn_conv,
              ffn_conv_b, ffn_w_down, norm_final):
    y_prompt = trunk(x_prompt, norm_mix, ev_w_in, gdn_conv, gdn_a_log, gdn_dt_bias, gdn_norm, ml_gate_bias,
                     ml_norm, ev_w_out, od_w_in, od_w_out, norm_ffn, ffn_w_up, ffn_conv, ffn_conv_b,
                     ffn_w_down, norm_final)
    y_sample = trunk(x_sample, norm_mix, ev_w_in, gdn_conv, gdn_a_log, gdn_dt_bias, gdn_norm, ml_gate_bias,
                     ml_norm, ev_w_out, od_w_in, od_w_out, norm_ffn, ffn_w_up, ffn_conv, ffn_conv_b,
                     ffn_w_down, norm_final)
    return (y_prompt, y_sample)
```

```python
import numpy as np
import concourse.bass as bass
import concourse.mybir as mybir
from concourse.bass_utils import run_bass_kernel_spmd

F32 = mybir.dt.float32
BF16 = mybir.dt.bfloat16
AF = mybir.ActivationFunctionType
ALU = mybir.AluOpType
AX = mybir.AxisListType

EPS = 1e-6
BIG = 30000.0


class Cfg:
    def __init__(self, D=2048, GH=8, MH=4, SLOTS=16, DFF=5632, T=16384, TBLK=1024):
        self.D, self.GH, self.MH, self.SLOTS, self.DFF, self.T, self.TBLK = D, GH, MH, SLOTS, DFF, T, TBLK
        self.DC = D // 128
        self.A_QKV = GH * 384
        self.A_Z = GH * 128
        self.A_G = 4 * GH
        self.B_QKV = MH * 512
        self.B_O = MH * 256
        self.B_G = 4 * MH
        self.c1 = self.A_QKV
        self.c2 = self.c1 + self.A_Z
        self.c3 = self.c2 + self.A_G
        self.c4 = self.c3 + self.B_QKV
        self.c5 = self.c4 + self.B_O
        self.EVEN_IN = self.c5 + self.B_G
        self.EVEN_MIX = GH * 128 + MH * 256
        self.ODD_IN = 9 * SLOTS * 128
        self.ODD_MIX = SLOTS * 128
        assert self.EVEN_MIX == D and self.ODD_MIX == D


class Sched:
    ENG = ("pe", "act", "dve", "pool", "sp")

    def __init__(self, nc):
        self.nc = nc
        self.ops = {e: [] for e in self.ENG}
        self.last_w = {}
        self.reads = {}
        self.waited = {}
        self.dma_val = {}
        self.dma_last = {}
        self.final_events = []
        self.dwl = {}
        self.excl = set()

    def _deps(self, eng, reads, writes):
        deps = []
        for k in reads:
            if k in self.last_w:
                deps.append(self.last_w[k])
        for k in writes:
            if k in self.last_w:
                deps.append(self.last_w[k])
            deps.extend(self.reads.get(k, ()))
        return deps

    def _add_waits(self, eng, deps):
        waits = []
        for ev in deps:
            kind, key, val = ev
            if kind == "eng" and key == eng and eng == "pe":
                continue
            wk = (eng, kind, key)
            if self.waited.get(wk, -1) >= val:
                continue
            self.waited[wk] = val
            waits.append(ev)
            if kind == "eng":
                self.ops[key][val]["inc"] = True
        return waits

    def _commit(self, ev, reads, writes):
        for k in writes:
            self.last_w[k] = ev
            self.reads[k] = []
        for k in reads:
            self.reads.setdefault(k, []).append(ev)

    def op(self, eng, fn, reads=(), writes=()):
        ex = [k for k in reads if k in self.excl]
        if ex:
            reads = [k for k in reads if k not in self.excl]
            writes = list(writes) + [k for k in ex if k not in writes]
        deps = self._deps(eng, reads, writes)
        waits = self._add_waits(eng, deps)
        idx = len(self.ops[eng])
        self.ops[eng].append(dict(waits=waits, fn=fn, inc=False, dma=None))
        ev = ("eng", eng, idx)
        self._commit(ev, reads, writes)
        return ev

    def dma(self, q, fn, semkey, reads=(), writes=(), final=False, dr=(), dw=()):
        deps = self._deps(q, reads, writes)
        if semkey in self.dma_last:
            deps.append(self.dma_last[semkey])
        for k in dr:
            deps.extend(self.dwl.get(k, {}).values())
        for k in dw:
            deps.extend(self.reads.get(k, ()))
        waits = self._add_waits(q, deps)
        v = self.dma_val.get(semkey, 0) + 16
        self.dma_val[semkey] = v
        self.ops[q].append(dict(waits=waits, fn=fn, inc=False, dma=(semkey, v)))
        ev = ("dma", semkey, v)
        self.dma_last[semkey] = ev
        self._commit(ev, list(reads) + list(dr), writes)
        for k in dw:
            self.dwl.setdefault(k, {})[semkey] = ev
            self.reads[k] = []
        if final:
            self.final_events.append(ev)
        return ev

    def barrier(self):
        evs = []
        for e in self.ENG:
            if self.ops[e]:
                for idx in range(len(self.ops[e]) - 1, -1, -1):
                    o = self.ops[e][idx]
                    if o["fn"] is not None and o["dma"] is None:
                        evs.append(("eng", e, idx))
                        break
        evs.extend(self.dma_last.values())
        for f in self.ENG:
            waits = self._add_waits(f, evs)
            self.ops[f].append(dict(waits=waits, fn=None, inc=False, dma=None))

    def finish(self):
        waits = self._add_waits("sp", self.final_events)
        self.ops["sp"].append(dict(waits=waits, fn=None, inc=False, dma=None))

    def emit(self, stack):
        nc = self.nc
        esem = {e: stack.enter_context(nc.semaphore("s_" + e)) for e in self.ENG}
        dsem = {}
        for k in self.dma_val:
            dsem[k] = stack.enter_context(nc.semaphore("d_%d" % len(dsem)))
        cnt = {}
        for e in self.ENG:
            c = 0
            arr = []
            for o in self.ops[e]:
                if o["inc"]:
                    c += 1
                arr.append(c)
            cnt[e] = arr
        block = stack.enter_context(nc.Block())

        def run(e, engobj):
            for o in self.ops[e]:
                for (kind, key, val) in o["waits"]:
                    if kind == "eng":
                        engobj.wait_ge(esem[key], cnt[key][val])
                    else:
                        engobj.wait_ge(dsem[key], val)
                if o["fn"] is None:
                    continue
                ins = o["fn"](engobj)
                if o["dma"] is not None:
                    ins.then_inc(dsem[o["dma"][0]], 16)
                elif o["inc"]:
                    ins.then_inc(esem[e], 1)

        block.sync(lambda g: run("sp", g))
        block.scalar(lambda g: run("act", g))
        block.vector(lambda g: run("dve", g))
        block.gpsimd(lambda g: run("pool", g))
        block.tensor(lambda g: run("pe", g))
        n = {e: len(self.ops[e]) for e in self.ENG}
        return n, len(dsem)


class Buf:
    def __init__(self, t, key):
        self.t, self.key = t, key

    def __getitem__(self, k):
        return self.t[k]


class B:
    def __init__(self, cfg, debug_outs=()):
        self.cfg = cfg
        self.nc = bass.Bass("TRN2", target_bir_lowering=False)
        self.s = Sched(self.nc)
        self.debug_outs = set(debug_outs)
        self.rr = {}
        self.dram = {}

    def din(self, name, shape, dt=F32):
        self.dram[name] = self.nc.dram_tensor(name, list(shape), dt, kind="ExternalInput").ap()
        return self.dram[name]

    def dout(self, name, shape, dt=F32):
        self.dram[name] = self.nc.dram_tensor(name, list(shape), dt, kind="ExternalOutput").ap()
        return self.dram[name]

    def dscr(self, name, shape, dt):
        kind = "ExternalOutput" if name in self.debug_outs else "Internal"
        self.dram[name] = self.nc.dram_tensor(name, list(shape), dt, kind=kind).ap()
        return self.dram[name]

    def sb(self, stack, name, shape, dt):
        t = stack.enter_context(self.nc.sbuf_tensor(name, list(shape), dt))
        return Buf(t, name)

    def ps(self, stack, name, shape, dt):
        t = stack.enter_context(self.nc.psum_tensor(name, list(shape), dt))
        self.s.excl.add(name)
        return Buf(t, name)

    def rot(self, name, lst):
        i = self.rr.get(name, 0)
        self.rr[name] = i + 1
        return lst[i % len(lst)]

    def evac_eng(self):
        return self.rot("evac", ["act", "dve"])

    def copy(self, eng, out, in_, reads, writes, scale=None):
        if eng == "act":
            if scale is None:
                fn = lambda e: e.activation(out=out, in_=in_, func=AF.Copy)
            else:
                fn = lambda e: e.activation(out=out, in_=in_, func=AF.Identity, scale=scale)
        else:
            if scale is None:
                fn = lambda e: e.tensor_copy(out=out, in_=in_)
            else:
                fn = lambda e: e.tensor_scalar(out=out, in0=in_, scalar1=scale, scalar2=None, op0=ALU.mult)
        return self.s.op(eng, fn, reads=reads, writes=writes)

    def load(self, q, out, in_, semkey, reads, writes, dr=()):
        return self.s.dma(q, lambda e: e.dma_start(out=out, in_=in_, allow_slow_non_contiguous=True), semkey, reads=reads, writes=writes, dr=dr)

    def store(self, q, out, in_, semkey, reads, dw, final=False):
        return self.s.dma(q, lambda e: e.dma_start(out=out, in_=in_, allow_slow_non_contiguous=True), semkey, reads=reads, writes=(), dw=dw,
                          final=final)


def build_common(b, stack):
    cfg = b.cfg
    s = b.s
    b.identb = b.sb(stack, "identb", [128, 128], BF16)
    b.identf = b.sb(stack, "identf", [128, 128], F32)
    b.U = b.sb(stack, "U", [128, 128], F32)
    b.L = b.sb(stack, "L", [128, 128], F32)
    b.onesf = b.sb(stack, "onesf", [128, 128], F32)
    b.onesb = b.sb(stack, "onesb", [128, 128], BF16)
    b.zerob = b.sb(stack, "zerob", [128, 512], BF16)

    def mk_tri(buf, cmp_, dt_fill=1.0):
        pass

    def init_ident(buf):
        s.op("pool", lambda e: e.memset(buf[:], 0.0), writes=[buf.key])
        s.op("pool", lambda e: e.affine_select(out=buf[:], in_=buf[:], pattern=[[-1, 128]], compare_op=ALU.not_equal,
                                              fill=1.0, base=0, channel_multiplier=1), reads=[buf.key], writes=[buf.key])
    init_ident(b.identb)
    init_ident(b.identf)
    s.op("pool", lambda e: e.memset(b.onesf[:], 1.0), writes=[b.onesf.key])
    s.op("pool", lambda e: e.memset(b.onesb[:], 1.0), writes=[b.onesb.key])
    s.op("pool", lambda e: e.memset(b.zerob[:], 0.0), writes=[b.zerob.key])
    s.op("pool", lambda e: e.memset(b.U[:], 1.0), writes=[b.U.key])
    s.op("pool", lambda e: e.affine_select(out=b.U[:], in_=b.U[:], pattern=[[1, 128]], compare_op=ALU.is_ge,
                                          fill=0.0, base=0, channel_multiplier=-1), reads=[b.U.key], writes=[b.U.key])
    s.op("pool", lambda e: e.memset(b.L[:], 1.0), writes=[b.L.key])
    s.op("pool", lambda e: e.affine_select(out=b.L[:], in_=b.L[:], pattern=[[-1, 128]], compare_op=ALU.is_ge,
                                          fill=0.0, base=0, channel_multiplier=1), reads=[b.L.key], writes=[b.L.key])
    b.gtmp = [b.sb(stack, "g%d" % i, [128, 128], F32) for i in range(36)]
    b.BMf = b.sb(stack, "BMf", [128, 128], F32)
    b.BMb = b.sb(stack, "BMb", [128, 128], F32)
    b.SMf = b.sb(stack, "SMf", [128, 128], F32)
    b.SMb = b.sb(stack, "SMb", [128, 128], F32)
    SMf, SMb, BMf, BMb = b.SMf, b.SMb, b.BMf, b.BMb
    s.op("dve", lambda e: e.tensor_scalar(out=SMf[:], in0=b.U[:], scalar1=-1.0, scalar2=1.0, op0=ALU.mult, op1=ALU.add),
         reads=[b.U.key], writes=[SMf.key])
    s.op("dve", lambda e: e.tensor_scalar(out=SMb[:], in0=b.L[:], scalar1=-1.0, scalar2=1.0, op0=ALU.mult, op1=ALU.add),
         reads=[b.L.key], writes=[SMb.key])
    s.op("dve", lambda e: e.tensor_scalar(out=BMf[:], in0=SMb[:], scalar1=BIG, scalar2=None, op0=ALU.mult),
         reads=[SMb.key], writes=[BMf.key])
    s.op("dve", lambda e: e.tensor_scalar(out=BMb[:], in0=SMf[:], scalar1=BIG, scalar2=None, op0=ALU.mult),
         reads=[SMf.key], writes=[BMb.key])
    b.H = [b.sb(stack, "H%d" % i, [128, 24576], BF16) for i in range(1)]
    b.W = [b.sb(stack, "W%d" % i, [128, 8192], BF16) for i in range(2)]
    b.X = [b.sb(stack, "X%d" % i, [128, 2048], F32) for i in range(2)]
    b.XB = [b.sb(stack, "XB%d" % i, [128, 2048], BF16) for i in range(2)]
    b.SF = [b.sb(stack, "SF%d" % i, [128, 512], F32) for i in range(4)]
    b.SBF = [b.sb(stack, "SBF%d" % i, [128, 512], BF16) for i in range(4)]
    b.SC = [b.sb(stack, "SC%d" % i, [128, 8], F32) for i in range(8)]
    b.P = [b.ps(stack, "P%d" % i, [128, 512], F32) for i in range(6)]
    b.PT = [b.ps(stack, "PT%d" % i, [128, 8, 128], BF16) for i in range(2)]


def rmsnorm_T(b, src, gT, gkey, dst, dstkey, src_reads=(), dr=(), preloaded=None, halo_dst=None):
    cfg, s = b.cfg, b.s
    D, DC = cfg.D, cfg.DC
    XB = b.rot("XB", b.XB)
    SC = b.rot("SC", b.SC)
    if preloaded is not None:
        X = preloaded
    else:
        X = b.rot("X", b.X)
        b.load("sp", X[:, 0:D], src, X.key, reads=src_reads, writes=[X.key], dr=dr)
    s.op("act", lambda e: e.activation(out=XB[:, 0:D], in_=X[:, 0:D], func=AF.Square, accum_out=SC[:, 0:1]),
         reads=[X.key], writes=[XB.key, SC.key])
    s.op("act", lambda e: e.activation(out=SC[:, 1:2], in_=SC[:, 0:1], func=AF.Sqrt, scale=1.0 / D, bias=EPS),
         reads=[SC.key], writes=[SC.key])
    s.op("dve", lambda e: e.reciprocal(out=SC[:, 2:3], in_=SC[:, 1:2]), reads=[SC.key], writes=[SC.key])
    s.op("act", lambda e: e.activation(out=XB[:, 0:D], in_=X[:, 0:D], func=AF.Identity, scale=SC[:, 2:3]),
         reads=[X.key, SC.key], writes=[XB.key])
    for cg in range(0, DC, 8):
        n = min(8, DC - cg)
        PT = b.rot("PT", b.PT)

        def tr(e, cg=cg, n=n, PT=PT):
            for c in range(n):
                ins = e.transpose(PT[:, c, :], XB[:, (cg + c) * 128:(cg + c + 1) * 128], b.identb[:])
            return ins
        s.op("pe", tr, reads=[XB.key, b.identb.key], writes=[PT.key])
        if halo_dst is not None:
            for hi, hd in enumerate(halo_dst):
                s.op("dve", lambda e, cg=cg, n=n, PT=PT, hi=hi, hd=hd: e.tensor_tensor(
                    out=hd[:, cg:cg + n, :], in0=PT[:, 0:n, hi:hi + 1],
                    in1=gT[:, cg:cg + n].unsqueeze(2), op=ALU.mult),
                    reads=[PT.key, gkey], writes=[dstkey])
            continue
        s.op("dve", lambda e, cg=cg, n=n, PT=PT: e.tensor_tensor(
            out=dst[:, cg:cg + n, :], in0=PT[:, 0:n, :],
            in1=gT[:, cg:cg + n].unsqueeze(2).to_broadcast([128, n, 128]), op=ALU.mult),
            reads=[PT.key, gkey], writes=[dstkey])


def linear(b, mode, hv, hkey, ntok, w, col0, ncols, evac):
    s = b.s
    KC = hv.shape[1]
    wv = w.rearrange("(c p) n -> p c n", p=128)
    CBW = 512 if KC <= 16 else 128
    for cb in range(0, ncols, CBW):
        nb = min(CBW, ncols - cb)
        W = b.rot("W", b.W)
        Wv = W[:, 0:KC * CBW].rearrange("p (c n) -> p c n", c=KC)
        b.load("pool", Wv[:, :, 0:nb], wv[:, :, col0 + cb:col0 + cb + nb], W.key, reads=[], writes=[W.key])
        if mode == "fm":
            for ct in range(0, nb, 128):
                ncl = min(128, nb - ct)
                for ts in range(0, ntok, 512):
                    nt = min(512, ntok - ts)
                    P = b.rot("P", b.P)

                    def mm(e, P=P, ct=ct, ncl=ncl, ts=ts, nt=nt, Wv=Wv):
                        for c in range(KC):
                            ins = e.matmul(P[0:ncl, 0:nt], Wv[:, c, ct:ct + ncl], hv[:, c, ts:ts + nt],
                                           start=(c == 0), stop=(c == KC - 1))
                        return ins
                    s.op("pe", mm, reads=[W.key, hkey], writes=[P.key])
                    evac(P, col0 + cb + ct, ncl, ts, nt)
        else:
            for tt in range(0, ntok, 128):
                P = b.rot("P", b.P)

                def mm(e, P=P, tt=tt, nb=nb, Wv=Wv):
                    for c in range(KC):
                        ins = e.matmul(P[:, 0:nb], hv[:, c, tt:tt + 128], Wv[:, c, 0:nb],
                                       start=(c == 0), stop=(c == KC - 1))
                    return ins
                s.op("pe", mm, reads=[W.key, hkey], writes=[P.key])
                evac(P, col0 + cb, nb, tt, 128)


def phase_A(b, stack):
    cfg, s = b.cfg, b.s
    T, TB, D, DC, GH, MH = cfg.T, cfg.TBLK, cfg.D, cfg.DC, cfg.GH, cfg.MH
    d = b.dram
    x, w = d["x"], d["ev_w_in"]
    for tb in range(T // TB):
        H = b.rot("H", b.H)
        hv = H[:, 0:DC * TB].rearrange("p (c t) -> p c t", c=DC)
        for tt in range(TB // 128):
            t0 = tb * TB + tt * 128
            rmsnorm_T(b, x[t0:t0 + 128, :], b.gmix[:, 0:DC], b.gmix.key, hv[:, :, tt * 128:(tt + 1) * 128], H.key)
        tok0 = tb * TB

        def ev_fm(dst, row0, scale=None):
            def f(P, c0, ncl, ts, nt):
                S = b.rot("SBF", b.SBF)
                b.copy(b.evac_eng(), S[0:ncl, 0:nt], P[0:ncl, 0:nt], [P.key], [S.key], scale=scale)
                b.store("sp", dst(c0 - row0, ncl, tok0 + ts, nt), S[0:ncl, 0:nt], S.key, [S.key], [dst.__name__])
            return f

        def ev_tm(dstname, col_base, dt):
            def f(P, c0, nb, tt, nt):
                S = b.rot("SBF", b.SBF) if dt == BF16 else b.rot("SF", b.SF)
                b.copy(b.evac_eng(), S[:, 0:nb], P[:, 0:nb], [P.key], [S.key])
                b.store("sp", d[dstname][tok0 + tt:tok0 + tt + 128, c0 - col_base:c0 - col_base + nb], S[:, 0:nb],
                        S.key, [S.key], [dstname])
            return f

        def QKVA_T(r, n, t, nt):
            return d["QKVA_T"][r:r + n, 1 + t:1 + t + nt]

        def QB_T(r, n, t, nt):
            return d["QB_T"][r:r + n, t:t + nt]

        def KB_T(r, n, t, nt):
            return d["KB_T"][r:r + n, t:t + nt]
        linear(b, "fm", hv, H.key, TB, w[0], 0, cfg.A_QKV, ev_fm(QKVA_T, 0))
        linear(b, "tm", hv, H.key, TB, w[0], cfg.c1, cfg.A_Z, ev_tm("Z", cfg.c1, BF16))
        linear(b, "tm", hv, H.key, TB, w[0], cfg.c2, cfg.A_G, ev_tm("GA", cfg.c2, F32))
        linear(b, "fm", hv, H.key, TB, w[0], cfg.c3, MH * 128, ev_fm(QB_T, cfg.c3, scale=128 ** -0.5))
        linear(b, "fm", hv, H.key, TB, w[0], cfg.c3 + MH * 128, MH * 128, ev_fm(KB_T, cfg.c3 + MH * 128))
        linear(b, "tm", hv, H.key, TB, w[0], cfg.c3 + MH * 128, MH * 384, ev_tm("KVB", cfg.c3 + MH * 128, BF16))
        linear(b, "tm", hv, H.key, TB, w[0], cfg.c4, cfg.B_O, ev_tm("OB", cfg.c4, BF16))
        linear(b, "tm", hv, H.key, TB, w[0], cfg.c5, cfg.B_G, ev_tm("GB", cfg.c5, F32))


def declare_io(b, stack):
    cfg = b.cfg
    T, D = cfg.T, cfg.D
    b.din("x", [T, D])
    b.din("norm_mix", [2, D])
    b.din("ev_w_in", [1, D, cfg.EVEN_IN])
    b.din("gdn_conv", [1, 3, cfg.A_QKV])
    b.din("gdn_a_log", [1, 2, cfg.GH])
    b.din("gdn_dt_bias", [1, 2, cfg.GH])
    b.din("gdn_norm", [1, 128])
    b.din("ml_gate_bias", [1, 2, 2, cfg.MH])
    b.din("ml_norm", [1, cfg.MH * 256])
    b.din("ev_w_out", [1, cfg.EVEN_MIX, D])
    b.din("od_w_in", [1, D, cfg.ODD_IN])
    b.din("od_w_out", [1, cfg.ODD_MIX, D])
    b.din("norm_ffn", [2, D])
    b.din("ffn_w_up", [2, D, 2 * cfg.DFF])
    b.din("ffn_conv", [2, 3, cfg.DFF])
    b.din("ffn_conv_b", [2, cfg.DFF])
    b.din("ffn_w_down", [2, cfg.DFF, D])
    b.din("norm_final", [D])
    b.din("kmask", [T + 2048], BF16)
    b.din("pos", [T])
    b.dout("y", [T, D])
    GH, MH = cfg.GH, cfg.MH
    b.dscr("QKVA_T", [cfg.A_QKV, T + 2], BF16)
    b.dscr("Z", [T, cfg.A_Z], BF16)
    b.dscr("GA", [T, cfg.A_G], F32)
    b.dscr("QB_T", [MH * 128, T], BF16)
    b.dscr("KB_T", [MH * 128, T], BF16)
    b.dscr("KVB", [T, MH * 384], BF16)
    b.dscr("OB", [T, cfg.B_O], BF16)
    b.dscr("GB", [T, cfg.B_G], F32)
    b.dscr("OA0", [T, GH * 128], F32)
    b.dscr("OA1", [T, GH * 128], F32)
    b.dscr("HB0", [T, MH * 256], F32)
    b.dscr("HB1", [T, MH * 256], F32)
    b.dscr("XL1", [T, D], F32)
    b.dscr("XL2", [T, D], F32)
    b.dscr("XL3", [T, D], F32)
    b.dscr("XL4", [T, D], F32)
    NQ = 3 * cfg.SLOTS * 128
    b.dscr("QT1", [NQ, T], BF16)
    b.dscr("KT1", [NQ, T + 2048], BF16)
    b.dscr("VT1", [T + 2048, NQ], BF16)
    b.dscr("MIXT", [D, T], BF16)
    b.din("ropec", [32, 2])
    b.din("qvalid", [T])
    DC = cfg.DC
    b.gmix = b.sb(stack, "gmix", [128, 2 * DC], F32)
    b.gffn = b.sb(stack, "gffn", [128, 2 * DC], F32)
    b.gfin = b.sb(stack, "gfin", [128, DC], F32)
    d = b.dram
    with b.nc.allow_non_contiguous_dma(reason="tiny param transposes"):
        pass
    for l in range(2):
        b.s.dma("sp", lambda e, l=l: e.dma_start(out=b.gmix[:, l * DC:(l + 1) * DC],
                                                 in_=d["norm_mix"][l].rearrange("(c p) -> p c", p=128),
                                                 allow_slow_non_contiguous=True), "gmix", writes=["gmix"])
        b.s.dma("sp", lambda e, l=l: e.dma_start(out=b.gffn[:, l * DC:(l + 1) * DC],
                                                 in_=d["norm_ffn"][l].rearrange("(c p) -> p c", p=128),
                                                 allow_slow_non_contiguous=True), "gffn", writes=["gffn"])
    b.s.dma("sp", lambda e: e.dma_start(out=b.gfin[:, 0:DC], in_=d["norm_final"].rearrange("(c p) -> p c", p=128),
                                        allow_slow_non_contiguous=True), "gfin", writes=["gfin"])


def build(cfg, phases, debug_outs=()):
    from contextlib import ExitStack
    b = B(cfg, debug_outs)
    stack = ExitStack()
    with stack:
        declare_io(b, stack)
        build_common(b, stack)
        for ph in phases:
            with ExitStack() as pst:
                b.pid = getattr(b, "pid", 0) + 1
                ph(b, pst)
                b.s.barrier()
        b.s.finish()
        n, nd = b.s.emit(stack)
        print("ops", n, "dma sems", nd)
    return b


def mm1(b, P, n, lhsT, rhs, reads, m=128):
    return b.s.op("pe", lambda e: e.matmul(P[0:m, 0:n], lhsT, rhs, start=True, stop=True), reads=reads, writes=[P.key])


class _Stop(Exception):
    pass


def chk(n):
    import os
    if os.environ.get("GDN_STOP", "") == str(n):
        raise _Stop()


def phase_gdn(b, stack):
    try:
        phase_gdn_(b, stack)
    except _Stop:
        print("GDN stopped early")


def phase_gdn_(b, stack):
    cfg, s, d = b.cfg, b.s, b.dram
    T, GH = cfg.T, cfg.GH
    NCH = T // 128
    tmp = b.gtmp
    wide = [b.sb(stack, "gw%d" % i, [128, 384], F32) for i in range(6)]
    xin = [b.sb(stack, "gx%d" % i, [128, 3, 130], BF16) for i in range(4)]
    gt = [b.sb(stack, "gt%d" % i, [128, 4 * GH], F32) for i in range(4)]
    gqv = [b.sb(stack, "gqv%d" % i, [128, 128], F32) for i in range(4)]
    gsm = [b.sb(stack, "gs%d" % i, [128, 6 * GH], F32) for i in range(4)]
    col = [b.sb(stack, "gc%d" % i, [128, 8], F32) for i in range(16)]
    S = {(h, dd, p): b.sb(stack, "S%d_%d_%d" % (h, dd, p), [128, 128], F32) for h in range(GH) for dd in range(2)
         for p in range(2)}
    cw = b.sb(stack, "gcw", [128, GH, 9], F32)
    dg = b.sb(stack, "gdg", [128, GH * 9, 128], BF16)
    ea = b.sb(stack, "gea", [128, 2, GH], F32)
    dtb = b.sb(stack, "gdtb", [128, 2, GH], F32)
    BMf, BMb, SMf, SMb = b.BMf, b.BMb, b.SMf, b.SMb
    T_ = lambda: b.rot("gtmp", tmp)
    C_ = lambda: b.rot("gcol", col)
    PS = lambda: b.rot("P", b.P)
    for h in range(GH):
        for a in range(3):
            r0 = a * GH * 128 + h * 128
            s.dma("sp", lambda e, h=h, a=a, r0=r0: e.dma_start(
                out=cw[:, h, a * 3:(a + 1) * 3], in_=d["gdn_conv"][0][:, r0:r0 + 128].rearrange("j p -> p j"),
                allow_slow_non_contiguous=True), cw.key, writes=[cw.key])
    s.dma("sp", lambda e: e.dma_start(out=ea[:], in_=d["gdn_a_log"][0:1].partition_broadcast(128)), ea.key, writes=[ea.key])
    s.dma("sp", lambda e: e.dma_start(out=dtb[:], in_=d["gdn_dt_bias"][0:1].partition_broadcast(128)), dtb.key,
          writes=[dtb.key])
    s.op("act", lambda e: e.activation(out=ea[:], in_=ea[:], func=AF.Exp), reads=[ea.key], writes=[ea.key])
    for h in range(GH):
        for a in range(3):
            for j in range(3):
                s.op("dve", lambda e, h=h, a=a, j=j: e.tensor_scalar(
                    out=dg[:, h * 9 + a * 3 + j, :], in0=b.identb[:], scalar1=cw[:, h, a * 3 + j:a * 3 + j + 1],
                    scalar2=None, op0=ALU.mult), reads=[b.identb.key, cw.key], writes=[dg.key])
        for dd in range(2):
            s.op("pool", lambda e, h=h, dd=dd: e.memset(S[(h, dd, 0)][:], 0.0), writes=[S[(h, dd, 0)].key])
    for r in range(0, cfg.A_QKV, 128):
        b.store("sp", d["QKVA_T"][r:r + 128, 0:1], b.zerob[:, 0:1], "zpad", [b.zerob.key], ["QKVA_T"])
        b.store("sp", d["QKVA_T"][r:r + 128, T + 1:T + 2], b.zerob[:, 0:1], "zpad", [b.zerob.key], ["QKVA_T"])
    qkv3 = d["QKVA_T"].rearrange("(a h p) t -> p a h t", a=3, h=GH)
    import os
    GM = os.environ.get("GDN_MODE", "")
    if GM == "pre":
        return
    if GM == "one":
        NCH = 1

    for step in range(NCH):
        for dd in range(2):
            c = step if dd == 0 else NCH - 1 - step
            t0 = c * 128
            Tri = b.U if dd == 0 else b.L
            BM = BMf if dd == 0 else BMb
            SM = SMf if dd == 0 else SMb
            GT = b.rot("ggt", gt)
            GS = b.rot("ggs", gsm)
            b.load("sp", GT[:], d["GA"][t0:t0 + 128, :], GT.key, [], [GT.key], dr=["GA"])
            QV = b.rot("gqv", gqv)
            b.load("sp", QV[:], d["qvalid"][t0:t0 + 128].partition_broadcast(128), QV.key, [], [QV.key])
            s.op("dve", lambda e, GT=GT, GS=GS, dd=dd: e.tensor_tensor(
                out=GS[:, 0:GH], in0=GT[:, dd * GH:(dd + 1) * GH], in1=dtb[:, dd, :], op=ALU.add),
                reads=[GT.key, dtb.key], writes=[GS.key])
            s.op("act", lambda e, GS=GS: e.activation(out=GS[:, 0:GH], in_=GS[:, 0:GH], func=AF.Exp),
                 reads=[GS.key], writes=[GS.key])
            s.op("act", lambda e, GS=GS: e.activation(out=GS[:, 0:GH], in_=GS[:, 0:GH], func=AF.Ln, bias=1.0),
                 reads=[GS.key], writes=[GS.key])
            s.op("dve", lambda e, GS=GS, dd=dd: e.scalar_tensor_tensor(
                out=GS[:, GH:2 * GH], in0=GS[:, 0:GH], scalar=-1.0, in1=ea[:, dd, :], op0=ALU.mult, op1=ALU.mult),
                reads=[GS.key, ea.key], writes=[GS.key])
            s.op("act", lambda e, GS=GS, GT=GT, dd=dd: e.activation(
                out=GS[:, 2 * GH:3 * GH], in_=GT[:, (2 + dd) * GH:(3 + dd) * GH], func=AF.Sigmoid),
                reads=[GT.key], writes=[GS.key])
            s.op("dve", lambda e, GS=GS: e.tensor_scalar(out=GS[:, 3 * GH:4 * GH], in0=GS[:, 2 * GH:3 * GH], scalar1=-1.0,
                                                        scalar2=None, op0=ALU.mult), reads=[GS.key], writes=[GS.key])
            chk(1)
            for h in range(GH):
                gcolv = GS[:, GH + h:GH + h + 1]
                beta = GS[:, 2 * GH + h:2 * GH + h + 1]
                nbeta = GS[:, 3 * GH + h:3 * GH + h + 1]
                XI = b.rot("gxin", xin)
                b.load("sp", XI[:], qkv3[:, :, h, t0:t0 + 130], XI.key, [], [XI.key], dr=["QKVA_T"])
                Pc = PS()

                def conv(e, XI=XI, Pc=Pc, h=h):
                    for a in range(3):
                        for j in range(3):
                            ins = e.matmul(Pc[:, a * 128:(a + 1) * 128], dg[:, h * 9 + a * 3 + j, :], XI[:, a, j:j + 128],
                                           start=(j == 0), stop=(j == 2))
                    return ins
                s.op("pe", conv, reads=[XI.key, dg.key], writes=[Pc.key])
                SL = b.rot("gwide", wide)
                s.op("act", lambda e, SL=SL, Pc=Pc: e.activation(out=SL[:], in_=Pc[:, 0:384], func=AF.Silu),
                     reads=[Pc.key], writes=[SL.key])
                s.op("dve", lambda e, SL=SL, QV=QV: e.tensor_tensor(
                    out=SL[:].rearrange("p (a t) -> p a t", a=3), in0=SL[:].rearrange("p (a t) -> p a t", a=3),
                    in1=QV[:].unsqueeze(1).to_broadcast([128, 3, 128]), op=ALU.mult),
                    reads=[SL.key, QV.key], writes=[SL.key])
                chk(2)
                SQ = b.rot("gwide", wide)
                s.op("act", lambda e, SL=SL, SQ=SQ: e.activation(out=SQ[:, 0:256], in_=SL[:, 0:256], func=AF.Square),
                     reads=[SL.key], writes=[SQ.key])
                Pn = PS()
                mm1(b, Pn, 256, b.onesf[:], SQ[:, 0:256], [b.onesf.key, SQ.key])
                s.op("act", lambda e, SQ=SQ, Pn=Pn: e.activation(out=SQ[:, 0:256], in_=Pn[:, 0:256], func=AF.Sqrt, bias=EPS),
                     reads=[Pn.key], writes=[SQ.key])
                s.op("dve", lambda e, SQ=SQ: e.reciprocal(out=SQ[:, 0:256], in_=SQ[:, 0:256]), reads=[SQ.key], writes=[SQ.key])
                QK = b.rot("gwide", wide)
                s.op("dve", lambda e, QK=QK, SL=SL, SQ=SQ: e.scalar_tensor_tensor(
                    out=QK[:, 0:128], in0=SL[:, 0:128], scalar=128 ** -0.5, in1=SQ[:, 0:128], op0=ALU.mult, op1=ALU.mult),
                    reads=[SL.key, SQ.key], writes=[QK.key])
                s.op("dve", lambda e, QK=QK, SL=SL, SQ=SQ: e.tensor_tensor(
                    out=QK[:, 128:256], in0=SL[:, 128:256], in1=SQ[:, 128:256], op=ALU.mult),
                    reads=[SL.key, SQ.key, QK.key], writes=[QK.key])
                qT, kT = QK[:, 0:128], QK[:, 128:256]
                chk(3)
                GB_ = T_()
                s.op("dve", lambda e, GB_=GB_, gcolv=gcolv: e.tensor_scalar(out=GB_[:], in0=b.onesf[:], scalar1=gcolv,
                                                                          scalar2=None, op0=ALU.mult),
                     reads=[b.onesf.key, GS.key], writes=[GB_.key])
                Pg = PS()

                def cums(e, Pg=Pg, GB_=GB_, Tri=Tri, BM=BM, gcolv=gcolv):
                    e.matmul(Pg[:, 0:128], GB_[:], Tri[:], start=True, stop=False)
                    e.matmul(Pg[:, 0:128], b.identf[:], BM[:], start=False, stop=True)
                    e.matmul(Pg[:, 128:129], Tri[:], gcolv, start=True, stop=True)
                    return e.matmul(Pg[:, 160:161], GB_[:], b.onesf[:, 0:1], start=True, stop=True)
                s.op("pe", cums, reads=[GB_.key, Tri.key, BM.key, b.identf.key, GS.key, b.onesf.key], writes=[Pg.key])
                CL = C_()
                s.op("dve", lambda e, CL=CL, Pg=Pg: e.tensor_copy(out=CL[:, 0:1], in_=Pg[:, 128:129]),
                     reads=[Pg.key], writes=[CL.key])
                s.op("dve", lambda e, CL=CL, Pg=Pg: e.tensor_copy(out=CL[:, 1:2], in_=Pg[:, 160:161]),
                     reads=[Pg.key, CL.key], writes=[CL.key])
                E = T_()
                s.op("act", lambda e, E=E, Pg=Pg, CL=CL: e.activation(out=E[:], in_=Pg[:, 0:128], func=AF.Exp, scale=-1.0,
                                                                     bias=CL[:, 0:1]), reads=[Pg.key, CL.key], writes=[E.key])
                s.op("act", lambda e, CL=CL: e.activation(out=CL[:, 2:4], in_=CL[:, 0:2], func=AF.Exp),
                     reads=[CL.key], writes=[CL.key])
                s.op("act", lambda e, CL=CL: e.activation(out=CL[:, 4:5], in_=CL[:, 0:1], func=AF.Exp, scale=-1.0,
                                                         bias=CL[:, 1:2]), reads=[CL.key], writes=[CL.key])
                s.op("dve", lambda e, CL=CL, beta=beta: e.tensor_tensor(out=CL[:, 5:6], in0=CL[:, 2:3], in1=beta, op=ALU.mult),
                     reads=[CL.key, GS.key], writes=[CL.key])
                chk(4)
                Pt = PS()

                def trkv(e, Pt=Pt, kT=kT, SL=SL):
                    e.transpose(Pt[:, 0:128], kT, b.identf[:])
                    return e.transpose(Pt[:, 128:256], SL[:, 256:384], b.identf[:])
                s.op("pe", trkv, reads=[QK.key, SL.key, b.identf.key], writes=[Pt.key])
                chk(41)
                KBG, KDEC, VB = T_(), T_(), T_()
                s.op("dve", lambda e, KBG=KBG, Pt=Pt, CL=CL: e.tensor_scalar(out=KBG[:], in0=Pt[:, 0:128], scalar1=CL[:, 5:6],
                                                                           scalar2=None, op0=ALU.mult),
                     reads=[Pt.key, CL.key], writes=[KBG.key])
                chk(42)
                s.op("act", lambda e, KDEC=KDEC, Pt=Pt, CL=CL: e.activation(out=KDEC[:], in_=Pt[:, 0:128], func=AF.Identity,
                                                                          scale=CL[:, 4:5]),
                     reads=[Pt.key, CL.key], writes=[KDEC.key])
                chk(43)
                s.op("dve", lambda e, VB=VB, Pt=Pt, beta=beta: e.tensor_scalar(out=VB[:], in0=Pt[:, 128:256], scalar1=beta,
                                                                             scalar2=None, op0=ALU.mult),
                     reads=[Pt.key, GS.key], writes=[VB.key])
                chk(5)
                Pk = PS()

                def gqk(e, Pk=Pk, kT=kT, qT=qT):
                    e.matmul(Pk[:, 0:128], kT, kT, start=True, stop=True)
                    return e.matmul(Pk[:, 128:256], kT, qT, start=True, stop=True)
                s.op("pe", gqk, reads=[QK.key], writes=[Pk.key])
                ES = T_()
                s.op("dve", lambda e, ES=ES, E=E, SM=SM: e.tensor_tensor(out=ES[:], in0=E[:], in1=SM[:], op=ALU.mult),
                     reads=[E.key, SM.key], writes=[ES.key])
                M = T_()
                GG = T_()
                s.op("act", lambda e, GG=GG, Pk=Pk, nbeta=nbeta: e.activation(out=GG[:], in_=Pk[:, 0:128], func=AF.Identity,
                                                                           scale=nbeta),
                     reads=[Pk.key, GS.key], writes=[GG.key])
                s.op("dve", lambda e, M=M, GG=GG, ES=ES: e.tensor_tensor(out=M[:], in0=GG[:], in1=ES[:], op=ALU.mult),
                     reads=[GG.key, ES.key], writes=[M.key])
                chk(51)
                Pe = PS()

                def trne(e, Pe=Pe, M=M, E=E):
                    e.transpose(Pe[:, 0:128], M[:], b.identf[:])
                    return e.transpose(Pe[:, 128:256], E[:], b.identf[:])
                s.op("pe", trne, reads=[M.key, E.key, b.identf.key], writes=[Pe.key])
                MT = T_()
                b.copy("act", MT[:], Pe[:, 0:128], [Pe.key], [MT.key])
                PP = T_()
                s.op("dve", lambda e, PP=PP, Pe=Pe: e.tensor_tensor(out=PP[:], in0=Pe[:, 0:128], in1=b.identf[:], op=ALU.add),
                     reads=[Pe.key, b.identf.key], writes=[PP.key])
                AT = T_()
                s.op("dve", lambda e, AT=AT, Pe=Pe, Pk=Pk: e.tensor_copy(out=AT[:], in_=Pe[:, 128:256]),
                     reads=[Pe.key], writes=[AT.key])
                s.op("dve", lambda e, AT=AT, Pk=Pk: e.tensor_tensor(out=AT[:], in0=Pk[:, 128:256], in1=AT[:], op=ALU.mult),
                     reads=[Pk.key, AT.key], writes=[AT.key])
                chk(52)
                for k in range(1, 7):
                    chk(52 + k)
                    Pm = PS()

                    def sq(e, Pm=Pm, M=M, MT=MT, k=k):
                        ins = e.matmul(Pm[:, 0:128], MT[:], M[:], start=True, stop=True)
                        if k < 6:
                            ins = e.matmul(Pm[:, 128:256], M[:], MT[:], start=True, stop=True)
                        return ins
                    s.op("pe", sq, reads=[M.key, MT.key], writes=[Pm.key])
                    M2 = T_()
                    b.copy("act", M2[:], Pm[:, 0:128], [Pm.key], [M2.key])
                    if k < 6:
                        MT2 = T_()
                        b.copy("dve", MT2[:], Pm[:, 128:256], [Pm.key], [MT2.key])
                    Pp = PS()
                    mm1(b, Pp, 128, M2[:], PP[:], [M2.key, PP.key])
                    PP2 = T_()
                    s.op("dve", lambda e, PP2=PP2, Pp=Pp, PP=PP: e.tensor_tensor(out=PP2[:], in0=Pp[:, 0:128], in1=PP[:],
                                                                              op=ALU.add),
                         reads=[Pp.key, PP.key], writes=[PP2.key])
                    PP = PP2
                    M = M2
                    if k < 6:
                        MT = MT2
                chk(6)
                Pw = PS()

                def wu(e, Pw=Pw, KBG=KBG, PP=PP, VB=VB):
                    e.matmul(Pw[:, 0:128], KBG[:], PP[:], start=True, stop=True)
                    return e.matmul(Pw[:, 128:256], PP[:], VB[:], start=True, stop=True)
                s.op("pe", wu, reads=[KBG.key, PP.key, VB.key], writes=[Pw.key])
                WT, UU = T_(), T_()
                b.copy("act", WT[:], Pw[:, 0:128], [Pw.key], [WT.key])
                b.copy("dve", UU[:], Pw[:, 128:256], [Pw.key], [UU.key])
                chk(7)
                Sc = S[(h, dd, step % 2)]
                Sn = S[(h, dd, (step + 1) % 2)]
                Pr = PS()

                def r1(e, Pr=Pr, WT=WT, Sc=Sc, qT=qT):
                    e.matmul(Pr[:, 0:128], WT[:], Sc[:], start=True, stop=True)
                    return e.matmul(Pr[:, 128:256], qT, Sc[:], start=True, stop=True)
                s.op("pe", r1, reads=[WT.key, Sc.key, QK.key], writes=[Pr.key])
                VN = T_()
                s.op("dve", lambda e, VN=VN, UU=UU, Pr=Pr: e.tensor_tensor(out=VN[:], in0=UU[:], in1=Pr[:, 0:128],
                                                                        op=ALU.subtract),
                     reads=[UU.key, Pr.key], writes=[VN.key])
                OT = T_()
                s.op("act", lambda e, OT=OT, Pr=Pr, CL=CL: e.activation(out=OT[:], in_=Pr[:, 128:256], func=AF.Identity,
                                                                       scale=CL[:, 2:3]),
                     reads=[Pr.key, CL.key], writes=[OT.key])
                Po = PS()

                def r2(e, Po=Po, AT=AT, VN=VN, KDEC=KDEC):
                    e.matmul(Po[:, 0:128], AT[:], VN[:], start=True, stop=True)
                    return e.matmul(Po[:, 128:256], KDEC[:], VN[:], start=True, stop=True)
                s.op("pe", r2, reads=[AT.key, VN.key, KDEC.key], writes=[Po.key])
                OO = T_()
                s.op("dve", lambda e, OO=OO, OT=OT, Po=Po: e.tensor_tensor(out=OO[:], in0=OT[:], in1=Po[:, 0:128], op=ALU.add),
                     reads=[OT.key, Po.key], writes=[OO.key])
                SS = T_()
                s.op("act", lambda e, SS=SS, Sc=Sc, CL=CL: e.activation(out=SS[:], in_=Sc[:], func=AF.Identity, scale=CL[:, 3:4]),
                     reads=[Sc.key, CL.key], writes=[SS.key])
                s.op("dve", lambda e, Sn=Sn, SS=SS, Po=Po: e.tensor_tensor(out=Sn[:], in0=SS[:], in1=Po[:, 128:256], op=ALU.add),
                     reads=[SS.key, Po.key], writes=[Sn.key])
                b.store("sp", d["OA%d" % dd][t0:t0 + 128, h * 128:(h + 1) * 128], OO[:], OO.key, [OO.key], ["OA%d" % dd])


def phase_mlstm(b, stack):
    cfg, s, d = b.cfg, b.s, b.dram
    T, MH = cfg.T, cfg.MH
    NCH = T // 128
    tmp = b.gtmp
    T_ = lambda: b.rot("gtmp", tmp)
    col = [b.sb(stack, "mc%d" % i, [128, 16], F32) for i in range(12)]
    C_ = lambda: b.rot("mcol", col)
    gt = [b.sb(stack, "mgt%d" % i, [128, 4 * MH], F32) for i in range(4)]
    gs = [b.sb(stack, "mgs%d" % i, [128, 4 * MH], F32) for i in range(4)]
    kvt = [b.sb(stack, "mkv%d" % i, [128, MH * 384], BF16) for i in range(3)]
    qkt = [b.sb(stack, "mqk%d" % i, [128, 2, 128], BF16) for i in range(4)]
    w257 = [b.sb(stack, "mw%d" % i, [128, 264], F32) for i in range(8)]
    W_ = lambda: b.rot("mw", w257)
    Cst = {(h, dd, p): b.sb(stack, "C%d_%d_%d" % (h, dd, p), [128, 264], F32) for h in range(MH) for dd in range(2)
           for p in range(2)}
    Mst = {(h, dd, p): b.sb(stack, "M%d_%d_%d" % (h, dd, p), [128, 2], F32) for h in range(MH) for dd in range(2)
           for p in range(2)}
    mlb = b.sb(stack, "mlb", [128, 4 * MH], F32)
    PS = lambda: b.rot("P", b.P)
    BMf, BMb = b.BMf, b.BMb
    s.dma("sp", lambda e: e.dma_start(out=mlb[:], in_=d["ml_gate_bias"].rearrange("a k d h -> a (k d h)").partition_broadcast(128)),
          mlb.key, writes=[mlb.key])
    for h in range(MH):
        for dd in range(2):
            s.op("pool", lambda e, h=h, dd=dd: e.memset(Cst[(h, dd, 0)][:], 0.0), writes=[Cst[(h, dd, 0)].key])
            s.op("pool", lambda e, h=h, dd=dd: e.memset(Mst[(h, dd, 0)][:], 0.0), writes=[Mst[(h, dd, 0)].key])
    qb3 = d["QB_T"].rearrange("(h p) t -> p h t", h=MH)
    kb3 = d["KB_T"].rearrange("(h p) t -> p h t", h=MH)
    for step in range(NCH):
        for dd in range(2):
            c = step if dd == 0 else NCH - 1 - step
            t0 = c * 128
            Tri = b.U if dd == 0 else b.L
            BM = BMf if dd == 0 else BMb
            GT = b.rot("mgt", gt)
            GS = b.rot("mgs", gs)
            b.load("sp", GT[:], d["GB"][t0:t0 + 128, :], GT.key, [], [GT.key], dr=["GB"])
            s.op("dve", lambda e, GT=GT: e.tensor_tensor(out=GT[:], in0=GT[:], in1=mlb[:], op=ALU.add),
                 reads=[GT.key, mlb.key], writes=[GT.key])
            s.op("dve", lambda e, GT=GT, GS=GS, dd=dd: e.tensor_copy(out=GS[:, 0:MH], in_=GT[:, dd * MH:(dd + 1) * MH]),
                 reads=[GT.key], writes=[GS.key])
            s.op("dve", lambda e, GT=GT, GS=GS, dd=dd: e.tensor_scalar(out=GS[:, MH:2 * MH], in0=GT[:, dd * MH:(dd + 1) * MH],
                                                                    scalar1=-1.0, scalar2=None, op0=ALU.mult),
                 reads=[GT.key, GS.key], writes=[GS.key])
            s.op("act", lambda e, GT=GT, GS=GS, dd=dd: e.activation(out=GS[:, 2 * MH:3 * MH],
                                                                 in_=GT[:, (2 + dd) * MH:(3 + dd) * MH], func=AF.Exp, scale=-1.0),
                 reads=[GT.key, GS.key], writes=[GS.key])
            s.op("act", lambda e, GS=GS: e.activation(out=GS[:, 2 * MH:3 * MH], in_=GS[:, 2 * MH:3 * MH], func=AF.Ln, bias=1.0),
                 reads=[GS.key], writes=[GS.key])
            s.op("dve", lambda e, GS=GS: e.tensor_scalar(out=GS[:, 2 * MH:3 * MH], in0=GS[:, 2 * MH:3 * MH], scalar1=-1.0,
                                                        scalar2=None, op0=ALU.mult), reads=[GS.key], writes=[GS.key])
            KV = b.rot("mkv", kvt)
            b.load("sp", KV[:], d["KVB"][t0:t0 + 128, :], KV.key, [], [KV.key], dr=["KVB"])
            for h in range(MH):
                igc = GS[:, h:h + 1]
                nigc = GS[:, MH + h:MH + h + 1]
                lfc = GS[:, 2 * MH + h:2 * MH + h + 1]
                Mc, Mn = Mst[(h, dd, step % 2)], Mst[(h, dd, (step + 1) % 2)]
                Cc, Cn = Cst[(h, dd, step % 2)], Cst[(h, dd, (step + 1) % 2)]
                QKb = b.rot("mqk", qkt)
                b.load("sp", QKb[:, 0, :], qb3[:, h, t0:t0 + 128], QKb.key, [], [QKb.key], dr=["QB_T"])
                b.load("sp", QKb[:, 1, :], kb3[:, h, t0:t0 + 128], QKb.key, [], [QKb.key], dr=["KB_T"])
                QK = W_()
                s.op("act", lambda e, QK=QK, QKb=QKb: e.activation(out=QK[:, 0:256],
                                                                  in_=QKb[:].rearrange("p a t -> p (a t)"), func=AF.Copy),
                     reads=[QKb.key], writes=[QK.key])
                qT, kT = QK[:, 0:128], QK[:, 128:256]
                VP = W_()
                s.op("dve", lambda e, VP=VP, KV=KV, h=h: e.tensor_copy(out=VP[:, 0:256],
                                                                    in_=KV[:, MH * 128 + h * 256:MH * 128 + (h + 1) * 256]),
                     reads=[KV.key], writes=[VP.key])
                s.op("dve", lambda e, VP=VP: e.memset(VP[:, 256:257], 1.0), reads=[VP.key], writes=[VP.key])
                LFB, NIB = T_(), T_()
                s.op("dve", lambda e, LFB=LFB, lfc=lfc: e.tensor_scalar(out=LFB[:], in0=b.onesf[:], scalar1=lfc, scalar2=None,
                                                                     op0=ALU.mult), reads=[b.onesf.key, GS.key], writes=[LFB.key])
                s.op("dve", lambda e, NIB=NIB, nigc=nigc: e.tensor_scalar(out=NIB[:], in0=b.onesf[:], scalar1=nigc, scalar2=None,
                                                                       op0=ALU.mult), reads=[b.onesf.key, GS.key], writes=[NIB.key])
                Pg = PS()

                def cums(e, Pg=Pg, LFB=LFB, NIB=NIB, Tri=Tri, BM=BM, lfc=lfc):
                    e.matmul(Pg[:, 0:128], LFB[:], Tri[:], start=True, stop=False)
                    e.matmul(Pg[:, 0:128], NIB[:], b.identf[:], start=False, stop=False)
                    e.matmul(Pg[:, 0:128], b.identf[:], BM[:], start=False, stop=True)
                    e.matmul(Pg[:, 128:256], LFB[:], Tri[:], start=True, stop=False)
                    e.matmul(Pg[:, 128:256], NIB[:], b.identf[:], start=False, stop=True)
                    e.matmul(Pg[:, 256:257], Tri[:], lfc, start=True, stop=True)
                    return e.matmul(Pg[:, 288:289], LFB[:], b.onesf[:, 0:1], start=True, stop=True)
                s.op("pe", cums, reads=[LFB.key, NIB.key, Tri.key, BM.key, b.identf.key, GS.key, b.onesf.key], writes=[Pg.key])
                CL = C_()
                s.op("dve", lambda e, CL=CL, Pg=Pg: e.tensor_copy(out=CL[:, 0:1], in_=Pg[:, 256:257]), reads=[Pg.key], writes=[CL.key])
                s.op("dve", lambda e, CL=CL, Pg=Pg: e.tensor_copy(out=CL[:, 1:2], in_=Pg[:, 288:289]),
                     reads=[Pg.key, CL.key], writes=[CL.key])
                s.op("dve", lambda e, CL=CL, Pg=Pg: e.tensor_reduce(out=CL[:, 2:3], in_=Pg[:, 0:128], axis=AX.X, op=ALU.min),
                     reads=[Pg.key, CL.key], writes=[CL.key])
                s.op("dve", lambda e, CL=CL, Pg=Pg: e.tensor_reduce(out=CL[:, 3:4], in_=Pg[:, 128:256], axis=AX.X, op=ALU.min),
                     reads=[Pg.key, CL.key], writes=[CL.key])
                s.op("dve", lambda e, CL=CL, Mc=Mc: e.scalar_tensor_tensor(out=CL[:, 4:5], in0=CL[:, 2:3], scalar=-1.0, in1=Mc[:, 0:1],
                                                                        op0=ALU.mult, op1=ALU.max),
                     reads=[CL.key, Mc.key], writes=[CL.key])
                s.op("dve", lambda e, CL=CL, Mc=Mc: e.scalar_tensor_tensor(out=CL[:, 9:10], in0=CL[:, 3:4], scalar=-1.0, in1=Mc[:, 0:1],
                                                                        op0=ALU.mult, op1=ALU.max),
                     reads=[CL.key, Mc.key], writes=[CL.key])
                s.op("dve", lambda e, CL=CL: e.tensor_scalar(out=CL[:, 5:6], in0=CL[:, 4:5], scalar1=-1.0, scalar2=None, op0=ALU.mult),
                     reads=[CL.key], writes=[CL.key])
                s.op("dve", lambda e, CL=CL: e.tensor_scalar(out=CL[:, 10:11], in0=CL[:, 9:10], scalar1=-1.0, scalar2=None, op0=ALU.mult),
                     reads=[CL.key], writes=[CL.key])
                s.op("dve", lambda e, CL=CL: e.tensor_tensor(out=CL[:, 7:8], in0=CL[:, 0:1], in1=CL[:, 4:5], op=ALU.add),
                     reads=[CL.key], writes=[CL.key])
                s.op("dve", lambda e, CL=CL, igc=igc: e.tensor_tensor(out=CL[:, 12:13], in0=CL[:, 0:1], in1=igc, op=ALU.subtract),
                     reads=[CL.key, GS.key], writes=[CL.key])
                s.op("dve", lambda e, CL=CL, Mn=Mn: e.tensor_tensor(out=Mn[:, 0:1], in0=CL[:, 1:2], in1=CL[:, 9:10], op=ALU.add),
                     reads=[CL.key], writes=[Mn.key])
                EW = T_()
                s.op("act", lambda e, EW=EW, Pg=Pg, CL=CL: e.activation(out=EW[:], in_=Pg[:, 0:128], func=AF.Exp, scale=-1.0,
                                                                       bias=CL[:, 5:6]), reads=[Pg.key, CL.key], writes=[EW.key])
                s.op("act", lambda e, CL=CL, Mc=Mc: e.activation(out=CL[:, 6:7], in_=CL[:, 4:5], func=AF.Exp, scale=-1.0,
                                                                bias=Mc[:, 0:1]), reads=[CL.key, Mc.key], writes=[CL.key])
                s.op("act", lambda e, CL=CL: e.activation(out=CL[:, 8:9], in_=CL[:, 7:8], func=AF.Exp, scale=-1.0),
                     reads=[CL.key], writes=[CL.key])
                s.op("act", lambda e, CL=CL, Mc=Mc: e.activation(out=CL[:, 11:12], in_=CL[:, 9:10], func=AF.Exp, scale=-1.0,
                                                                bias=Mc[:, 0:1]), reads=[CL.key, Mc.key], writes=[CL.key])
                s.op("act", lambda e, CL=CL: e.activation(out=CL[:, 13:14], in_=CL[:, 12:13], func=AF.Exp, scale=-1.0,
                                                         bias=CL[:, 10:11]), reads=[CL.key], writes=[CL.key])
                Pq = PS()
                mm1(b, Pq, 128, qT, kT, [QK.key])
                WI = T_()
                s.op("dve", lambda e, WI=WI, EW=EW, Pq=Pq: e.tensor_tensor(out=WI[:], in0=Pq[:, 0:128], in1=EW[:], op=ALU.mult),
                     reads=[Pq.key, EW.key], writes=[WI.key])
                Pt = PS()
                s.op("pe", lambda e, Pt=Pt, WI=WI: e.transpose(Pt[:, 0:128], WI[:], b.identf[:]),
                     reads=[WI.key, b.identf.key], writes=[Pt.key])
                WIT = T_()
                b.copy("act", WIT[:], Pt[:, 0:128], [Pt.key], [WIT.key])
                WK = T_()
                s.op("dve", lambda e, WK=WK, KV=KV, CL=CL, h=h: e.tensor_scalar(out=WK[:], in0=KV[:, h * 128:(h + 1) * 128],
                                                                             scalar1=CL[:, 13:14], scalar2=None, op0=ALU.mult),
                     reads=[KV.key, CL.key], writes=[WK.key])
                Pa = PS()
                mm1(b, Pa, 257, qT, Cc[:, 0:257], [QK.key, Cc.key])
                T1 = W_()
                s.op("act", lambda e, T1=T1, Pa=Pa, CL=CL: e.activation(out=T1[:, 0:257], in_=Pa[:, 0:257], func=AF.Identity,
                                                                       scale=CL[:, 6:7]), reads=[Pa.key, CL.key], writes=[T1.key])
                Pb = PS()
                mm1(b, Pb, 257, WIT[:], VP[:, 0:257], [WIT.key, VP.key])
                ND = W_()
                s.op("dve", lambda e, ND=ND, T1=T1, Pb=Pb: e.tensor_tensor(out=ND[:, 0:257], in0=Pb[:, 0:257], in1=T1[:, 0:257],
                                                                        op=ALU.add), reads=[Pb.key, T1.key], writes=[ND.key])
                CD = C_()
                s.op("dve", lambda e, CD=CD, ND=ND: e.scalar_tensor_tensor(out=CD[:, 0:1], in0=ND[:, 256:257], scalar=-1.0,
                                                                          in1=ND[:, 256:257], op0=ALU.mult, op1=ALU.max),
                     reads=[ND.key], writes=[CD.key])
                s.op("dve", lambda e, CD=CD, CL=CL: e.tensor_tensor(out=CD[:, 1:2], in0=CD[:, 0:1], in1=CL[:, 8:9], op=ALU.max),
                     reads=[CD.key, CL.key], writes=[CD.key])
                s.op("dve", lambda e, CD=CD: e.reciprocal(out=CD[:, 2:3], in_=CD[:, 1:2]), reads=[CD.key], writes=[CD.key])
                HO = W_()
                s.op("dve", lambda e, HO=HO, ND=ND, CD=CD: e.tensor_scalar(out=HO[:, 0:256], in0=ND[:, 0:256], scalar1=CD[:, 2:3],
                                                                        scalar2=None, op0=ALU.mult),
                     reads=[ND.key, CD.key], writes=[HO.key])
                b.store("sp", d["HB%d" % dd][t0:t0 + 128, h * 256:(h + 1) * 256], HO[:, 0:256], HO.key, [HO.key], ["HB%d" % dd])
                Pc = PS()
                mm1(b, Pc, 257, WK[:], VP[:, 0:257], [WK.key, VP.key])
                T2 = W_()
                s.op("act", lambda e, T2=T2, Cc=Cc, CL=CL: e.activation(out=T2[:, 0:257], in_=Cc[:, 0:257], func=AF.Identity,
                                                                       scale=CL[:, 11:12]), reads=[Cc.key, CL.key], writes=[T2.key])
                s.op("dve", lambda e, Cn=Cn, T2=T2, Pc=Pc: e.tensor_tensor(out=Cn[:, 0:257], in0=Pc[:, 0:257], in1=T2[:, 0:257],
                                                                        op=ALU.add), reads=[Pc.key, T2.key], writes=[Cn.key])


def transpose_to_fm(b, XBt, dst, dstkey, ncols):
    s = b.s
    NCk = ncols // 128
    for cg in range(0, NCk, 8):
        n = min(8, NCk - cg)
        PT = b.rot("PT", b.PT)

        def tr(e, cg=cg, n=n, PT=PT):
            for c in range(n):
                ins = e.transpose(PT[:, c, :], XBt[:, (cg + c) * 128:(cg + c + 1) * 128], b.identb[:])
            return ins
        s.op("pe", tr, reads=[XBt.key, b.identb.key], writes=[PT.key])
        b.copy(b.evac_eng(), dst[:, cg:cg + n, :], PT[:, 0:n, :], [PT.key], [dstkey])


def outproj_residual(b, hv, hkey, ntok, w, resid, dst, tok0, final_store=False):
    s, d = b.s, b.dram
    D = b.cfg.D

    def ev(P, c0, nb, tt, nt):
        R = b.rot("SF", b.SF)
        b.load("sp", R[:, 0:nb], d[resid][tok0 + tt:tok0 + tt + 128, c0:c0 + nb], R.key, [], [R.key], dr=[resid])
        O = b.rot("SF", b.SF)
        s.op("dve", lambda e: e.tensor_tensor(out=O[:, 0:nb], in0=P[:, 0:nb], in1=R[:, 0:nb], op=ALU.add),
             reads=[P.key, R.key], writes=[O.key])
        b.store("sp", d[dst][tok0 + tt:tok0 + tt + 128, c0:c0 + nb], O[:, 0:nb], O.key, [O.key], [dst], final=final_store)
    linear(b, "tm", hv, hkey, ntok, w, 0, D, ev)


def phase_mixout0(b, stack):
    cfg, s, d = b.cfg, b.s, b.dram
    T, D, DC, GH, MH = cfg.T, cfg.D, cfg.DC, cfg.GH, cfg.MH
    WA, WB = GH * 128, MH * 256
    TBM = 512
    gfull = b.sb(stack, "gfull", [128, D], F32)
    for h in range(GH):
        s.dma("sp", lambda e, h=h: e.dma_start(out=gfull[:, h * 128:(h + 1) * 128], in_=d["gdn_norm"][0:1, :].partition_broadcast(128)),
              gfull.key, writes=[gfull.key])
    s.dma("sp", lambda e: e.dma_start(out=gfull[:, WA:D], in_=d["ml_norm"][0:1, :].partition_broadcast(128)),
          gfull.key, writes=[gfull.key])
    rsb = [b.sb(stack, "rsb%d" % i, [128, 2 * (GH + MH)], F32) for i in range(2)]
    for tb in range(T // TBM):
        H = b.H[0]
        hv = H[:, 0:DC * TBM].rearrange("p (c t) -> p c t", c=DC)
        for tt in range(TBM // 128):
            t0 = tb * TBM + tt * 128
            XA, XC = b.X[0], b.X[1]
            ZB, MB = b.XB[0], b.XB[1]
            RS = b.rot("rsb", rsb)
            b.load("sp", XA[:, 0:WA], d["OA0"][t0:t0 + 128, :], XA.key, [], [XA.key], dr=["OA0"])
            b.load("sp", XA[:, WA:D], d["HB0"][t0:t0 + 128, :], XA.key, [], [XA.key], dr=["HB0"])
            b.load("sp", XC[:, 0:WA], d["OA1"][t0:t0 + 128, :], XC.key, [], [XC.key], dr=["OA1"])
            b.load("sp", XC[:, WA:D], d["HB1"][t0:t0 + 128, :], XC.key, [], [XC.key], dr=["HB1"])
            b.load("sp", ZB[:, 0:WA], d["Z"][t0:t0 + 128, :], ZB.key, [], [ZB.key], dr=["Z"])
            b.load("sp", ZB[:, WA:D], d["OB"][t0:t0 + 128, :], ZB.key, [], [ZB.key], dr=["OB"])
            s.op("dve", lambda e, XA=XA, XC=XC: e.tensor_tensor(out=XA[:, 0:D], in0=XA[:, 0:D], in1=XC[:, 0:D], op=ALU.add),
                 reads=[XA.key, XC.key], writes=[XA.key])
            s.op("act", lambda e, XA=XA, XC=XC: e.activation(out=XC[:, 0:D], in_=XA[:, 0:D], func=AF.Square),
                 reads=[XA.key], writes=[XC.key])
            s.op("dve", lambda e, XC=XC, RS=RS: e.tensor_reduce(out=RS[:, 0:GH], in_=XC[:, 0:WA].rearrange("p (h k) -> p h k", h=GH),
                                                              axis=AX.X, op=ALU.add), reads=[XC.key], writes=[RS.key])
            s.op("dve", lambda e, XC=XC, RS=RS: e.tensor_reduce(out=RS[:, GH:GH + MH],
                                                              in_=XC[:, WA:D].rearrange("p (h k) -> p h k", h=MH),
                                                              axis=AX.X, op=ALU.add), reads=[XC.key, RS.key], writes=[RS.key])
            s.op("act", lambda e, RS=RS: e.activation(out=RS[:, 0:GH], in_=RS[:, 0:GH], func=AF.Sqrt, scale=1.0 / 128, bias=EPS),
                 reads=[RS.key], writes=[RS.key])
            s.op("act", lambda e, RS=RS: e.activation(out=RS[:, GH:GH + MH], in_=RS[:, GH:GH + MH], func=AF.Sqrt, scale=1.0 / 256,
                                                     bias=EPS), reads=[RS.key], writes=[RS.key])
            s.op("dve", lambda e, RS=RS: e.reciprocal(out=RS[:, GH + MH:2 * (GH + MH)], in_=RS[:, 0:GH + MH]),
                 reads=[RS.key], writes=[RS.key])
            s.op("dve", lambda e, XA=XA, RS=RS: e.tensor_tensor(
                out=XA[:, 0:WA].rearrange("p (h k) -> p h k", h=GH), in0=XA[:, 0:WA].rearrange("p (h k) -> p h k", h=GH),
                in1=RS[:, GH + MH:2 * GH + MH].unsqueeze(2).to_broadcast([128, GH, 128]), op=ALU.mult),
                reads=[XA.key, RS.key], writes=[XA.key])
            s.op("dve", lambda e, XA=XA, RS=RS: e.tensor_tensor(
                out=XA[:, WA:D].rearrange("p (h k) -> p h k", h=MH), in0=XA[:, WA:D].rearrange("p (h k) -> p h k", h=MH),
                in1=RS[:, 2 * GH + MH:2 * (GH + MH)].unsqueeze(2).to_broadcast([128, MH, 256]), op=ALU.mult),
                reads=[XA.key, RS.key], writes=[XA.key])
            s.op("dve", lambda e, XA=XA: e.tensor_tensor(out=XA[:, 0:D], in0=XA[:, 0:D], in1=gfull[:], op=ALU.mult),
                 reads=[XA.key, gfull.key], writes=[XA.key])
            s.op("act", lambda e, XC=XC, ZB=ZB: e.activation(out=XC[:, 0:WA], in_=ZB[:, 0:WA], func=AF.Silu),
                 reads=[ZB.key], writes=[XC.key])
            s.op("act", lambda e, XC=XC, ZB=ZB: e.activation(out=XC[:, WA:D], in_=ZB[:, WA:D], func=AF.Sigmoid),
                 reads=[ZB.key, XC.key], writes=[XC.key])
            s.op("dve", lambda e, XA=XA, XC=XC, MB=MB: e.tensor_tensor(out=MB[:, 0:D], in0=XA[:, 0:D], in1=XC[:, 0:D], op=ALU.mult),
                 reads=[XA.key, XC.key], writes=[MB.key])
            transpose_to_fm(b, MB, hv[:, :, tt * 128:(tt + 1) * 128], H.key, D)
        outproj_residual(b, hv, H.key, TBM, d["ev_w_out"][0], "x", "XL1", tb * TBM)


def phase_ffn(layer, src, dst):
    def ph(b, stack):
        cfg, s, d = b.cfg, b.s, b.dram
        T, D, DC, DFF = cfg.T, cfg.D, cfg.DC, cfg.DFF
        FC = DFF // 128
        TBF = 512
        pre = "f%d_" % b.pid
        cwt = b.sb(stack, pre + "cw", [128, FC, 4], F32)
        for j in range(3):
            s.dma("sp", lambda e, j=j: e.dma_start(out=cwt[:, :, j:j + 1],
                                                   in_=d["ffn_conv"][layer, j].rearrange("(c p o) -> p c o", p=128, o=1),
                                                   allow_slow_non_contiguous=True), cwt.key, writes=[cwt.key])
        s.dma("sp", lambda e: e.dma_start(out=cwt[:, :, 3:4], in_=d["ffn_conv_b"][layer].rearrange("(c p o) -> p c o", p=128, o=1),
                                          allow_slow_non_contiguous=True), cwt.key, writes=[cwt.key])
        hn2 = b.sb(stack, pre + "hn", [128, DC, TBF + 2], BF16)
        actT = b.H[0]
        av = actT[:, 0:FC * TBF].rearrange("p (c t) -> p c t", c=FC)
        gbuf = [b.sb(stack, pre + "g%d" % i, [128, TBF + 2], F32) for i in range(2)]
        cbuf = [b.sb(stack, pre + "c%d" % i, [128, TBF], F32) for i in range(2)]
        halo = b.sb(stack, pre + "halo", [128, D], F32)
        w_up, w_dn = d["ffn_w_up"][layer], d["ffn_w_down"][layer]
        wupv = w_up.rearrange("(c p) n -> p c n", p=128)
        gsrc = b.gffn[:, layer * DC:(layer + 1) * DC]
        for tb in range(T // TBF):
            tok0 = tb * TBF
            for tt in range(TBF // 128):
                t0 = tok0 + tt * 128
                rmsnorm_T(b, d[src][t0:t0 + 128, :], gsrc, b.gffn.key, hn2[:, :, 1 + tt * 128:1 + (tt + 1) * 128], hn2.key,
                          src_reads=[], dr=[src])
            s.op("pool", lambda e: e.memset(halo[:], 0.0), writes=[halo.key])
            if tok0 > 0:
                b.load("sp", halo[0:1, :], d[src][tok0 - 1:tok0, :], halo.key, [], [halo.key], dr=[src])
            if tok0 + TBF < T:
                b.load("sp", halo[1:2, :], d[src][tok0 + TBF:tok0 + TBF + 1, :], halo.key, [], [halo.key], dr=[src])
            rmsnorm_T(b, None, gsrc, b.gffn.key, None, hn2.key, preloaded=halo,
                      halo_dst=(hn2[:, :, 0:1], hn2[:, :, TBF + 1:TBF + 2]))
            for fb in range(0, DFF, 512):
                nb = min(512, DFF - fb)
                Wg = b.rot("W", b.W)
                Wgv = Wg[:, 0:DC * 512].rearrange("p (c n) -> p c n", c=DC)
                b.load("pool", Wgv[:, :, 0:nb], wupv[:, :, fb:fb + nb], Wg.key, [], [Wg.key])
                Wv = b.rot("W", b.W)
                Wvv = Wv[:, 0:DC * 512].rearrange("p (c n) -> p c n", c=DC)
                b.load("pool", Wvv[:, :, 0:nb], wupv[:, :, DFF + fb:DFF + fb + nb], Wv.key, [], [Wv.key])
                for ft in range(0, nb, 128):
                    f = (fb + ft) // 128
                    G = b.rot(pre + "g", gbuf)
                    half = (TBF + 2) // 2
                    for (c0, c1) in ((0, half), (half, TBF + 2)):
                        P = b.rot("P", b.P)

                        def mm(e, P=P, c0=c0, c1=c1, ft=ft, Wgv=Wgv):
                            for c in range(DC):
                                ins = e.matmul(P[:, 0:c1 - c0], Wgv[:, c, ft:ft + 128], hn2[:, c, c0:c1], start=(c == 0),
                                               stop=(c == DC - 1))
                            return ins
                        s.op("pe", mm, reads=[Wg.key, hn2.key], writes=[P.key])
                        b.copy(b.evac_eng(), G[:, c0:c1], P[:, 0:c1 - c0], [P.key], [G.key])
                    Cv = b.rot(pre + "c", cbuf)
                    s.op("dve", lambda e, Cv=Cv, G=G, f=f: e.tensor_scalar(out=Cv[:], in0=G[:, 0:TBF], scalar1=cwt[:, f, 0:1],
                                                                        scalar2=None, op0=ALU.mult),
                         reads=[G.key, cwt.key], writes=[Cv.key])
                    s.op("dve", lambda e, Cv=Cv, G=G, f=f: e.scalar_tensor_tensor(out=Cv[:], in0=G[:, 1:TBF + 1], scalar=cwt[:, f, 1:2],
                                                                               in1=Cv[:], op0=ALU.mult, op1=ALU.add),
                         reads=[G.key, cwt.key, Cv.key], writes=[Cv.key])
                    s.op("dve", lambda e, Cv=Cv, G=G, f=f: e.scalar_tensor_tensor(out=Cv[:], in0=G[:, 2:TBF + 2], scalar=cwt[:, f, 2:3],
                                                                               in1=Cv[:], op0=ALU.mult, op1=ALU.add),
                         reads=[G.key, cwt.key, Cv.key], writes=[Cv.key])
                    s.op("act", lambda e, Cv=Cv, f=f: e.activation(out=Cv[:], in_=Cv[:], func=AF.Silu, bias=cwt[:, f, 3:4]),
                         reads=[Cv.key, cwt.key], writes=[Cv.key])
                    P = b.rot("P", b.P)

                    def mv(e, P=P, ft=ft, Wvv=Wvv):
                        for c in range(DC):
                            ins = e.matmul(P[:, 0:TBF], Wvv[:, c, ft:ft + 128], hn2[:, c, 1:TBF + 1], start=(c == 0), stop=(c == DC - 1))
                        return ins
                    s.op("pe", mv, reads=[Wv.key, hn2.key], writes=[P.key])
                    s.op("dve", lambda e, P=P, Cv=Cv, f=f: e.tensor_tensor(out=av[:, f, :], in0=P[:, 0:TBF], in1=Cv[:], op=ALU.mult),
                         reads=[P.key, Cv.key], writes=[actT.key])
            outproj_residual(b, av, actT.key, TBF, w_dn, src, dst, tok0)
    return ph


PADK = 1024
TWO_PI = 6.283185307179586


def rope_consts():
    inv = (500000.0 ** (-np.arange(0, 32, 2, dtype=np.float32) / 32)).astype(np.float32)
    c = np.zeros((32, 2), np.float32)
    c[:, 0] = np.concatenate([inv, inv])
    c[0:16, 1] = -TWO_PI
    c[16:32, 1] = TWO_PI
    return c


def phase_qkv1(b, stack):
    cfg, s, d = b.cfg, b.s, b.dram
    T, TB, D, DC, S_ = cfg.T, cfg.TBLK, cfg.D, cfg.DC, cfg.SLOTS
    NQ = 3 * S_ * 128
    w = d["od_w_in"][0]
    ropec = b.sb(stack, "ropec_sb", [32, 4], F32)
    s.dma("sp", lambda e: e.dma_start(out=ropec[:, 0:2], in_=d["ropec"][:, :]), ropec.key, writes=[ropec.key])
    permT = b.sb(stack, "permT", [128, 32], BF16)
    s.op("pool", lambda e: e.memset(permT[:], 0.0), writes=[permT.key])
    s.op("pool", lambda e: e.affine_select(out=permT[:, 0:16], in_=permT[:, 0:16], pattern=[[-1, 16]], compare_op=ALU.not_equal,
                                          fill=1.0, base=-16, channel_multiplier=1), reads=[permT.key], writes=[permT.key])
    s.op("pool", lambda e: e.affine_select(out=permT[:, 16:32], in_=permT[:, 16:32], pattern=[[-1, 16]], compare_op=ALU.not_equal,
                                          fill=1.0, base=0, channel_multiplier=1), reads=[permT.key], writes=[permT.key])
    ang = [b.sb(stack, "ang%d" % i, [32, 512], F32) for i in range(2)]
    cst = [b.sb(stack, "cst%d" % i, [32, 2, 512], F32) for i in range(2)]
    cki = [b.sb(stack, "cki%d" % i, [32, 2, 512], mybir.dt.int32) for i in range(2)]
    ckf = [b.sb(stack, "ckf%d" % i, [32, 2, 512], F32) for i in range(2)]
    qa = [b.sb(stack, "qa%d" % i, [128, 512], BF16) for i in range(3)]
    qr = [b.sb(stack, "qr%d" % i, [128, 512], BF16) for i in range(3)]
    rt = [b.sb(stack, "rt%d" % i, [32, 512], F32) for i in range(3)]
    for r in range(0, NQ, 128):
        for (c0) in (0, PADK + T):
            for cc in range(0, PADK, 512):
                b.store("sp", d["KT1"][r:r + 128, c0 + cc:c0 + cc + 512], b.zerob[:, 0:512], "zpad1", [b.zerob.key], ["KT1"])
    for (r0) in (0, PADK + T):
        for rr in range(0, PADK, 128):
            for cc in range(0, NQ, 512):
                b.store("sp", d["VT1"][r0 + rr:r0 + rr + 128, cc:cc + 512], b.zerob[:, 0:512], "zpad1", [b.zerob.key], ["VT1"])
    for tb in range(T // TB):
        H = b.H[0]
        hv = H[:, 0:DC * TB].rearrange("p (c t) -> p c t", c=DC)
        tok0 = tb * TB
        for tt in range(TB // 128):
            t0 = tok0 + tt * 128
            rmsnorm_T(b, d["XL2"][t0:t0 + 128, :], b.gmix[:, DC:2 * DC], b.gmix.key, hv[:, :, tt * 128:(tt + 1) * 128], H.key,
                      dr=["XL2"])
        tabs = {}
        for ts in range(0, TB, 512):
            A = b.rot("ang", ang)
            CS = b.rot("cst", cst)
            b.load("sp", A[:], d["pos"][tok0 + ts:tok0 + ts + 512].partition_broadcast(32), A.key, [], [A.key])
            s.op("dve", lambda e, A=A: e.tensor_scalar(out=A[:], in0=A[:], scalar1=ropec[:, 0:1], scalar2=None, op0=ALU.mult),
                 reads=[A.key, ropec.key], writes=[A.key])
            KI = b.rot("cki", cki)
            KF = b.rot("ckf", ckf)
            s.op("dve", lambda e, A=A, CS=CS: e.tensor_scalar(out=CS[:, 0, :], in0=A[:], scalar1=1.0 / TWO_PI, scalar2=0.25,
                                                            op0=ALU.mult, op1=ALU.add), reads=[A.key], writes=[CS.key])
            s.op("dve", lambda e, A=A, CS=CS: e.tensor_scalar(out=CS[:, 1, :], in0=A[:], scalar1=1.0 / TWO_PI, scalar2=None,
                                                            op0=ALU.mult), reads=[A.key, CS.key], writes=[CS.key])
            s.op("dve", lambda e, CS=CS, KI=KI: e.tensor_copy(out=KI[:], in_=CS[:]), reads=[CS.key], writes=[KI.key])
            s.op("dve", lambda e, KF=KF, KI=KI: e.tensor_copy(out=KF[:], in_=KI[:]), reads=[KI.key], writes=[KF.key])
            s.op("dve", lambda e, CS=CS, KF=KF: e.tensor_tensor(out=CS[:], in0=CS[:], in1=KF[:], op=ALU.subtract),
                 reads=[CS.key, KF.key], writes=[CS.key])
            s.op("act", lambda e, CS=CS: e.activation(out=CS[:, 0, :], in_=CS[:, 0, :], func=AF.Sin, scale=TWO_PI),
                 reads=[CS.key], writes=[CS.key])
            s.op("act", lambda e, CS=CS: e.activation(out=CS[:, 1, :], in_=CS[:, 1, :], func=AF.Sin, scale=ropec[:, 1:2]),
                 reads=[CS.key, ropec.key], writes=[CS.key])
            tabs[ts] = CS

        def ev_rope(dstname, row_base, col_off):
            def f(P, c0, ncl, ts, nt):
                CS = tabs[ts]
                A_ = b.rot("qa", qa)
                R_ = b.rot("qr", qr)
                TT = b.rot("rt", rt)
                b.copy("act", A_[:, 0:nt], P[:, 0:nt], [P.key], [A_.key])
                b.copy("dve", R_[:, 0:nt], P[:, 0:nt], [P.key], [R_.key])
                P2 = b.rot("P", b.P)
                s.op("pe", lambda e: e.matmul(P2[0:32, 0:nt], permT[:, 0:32], A_[:, 0:nt], start=True, stop=True),
                     reads=[permT.key, A_.key], writes=[P2.key])
                s.op("dve", lambda e: e.tensor_tensor(out=TT[:, 0:nt], in0=P2[0:32, 0:nt], in1=CS[:, 1, 0:nt], op=ALU.mult),
                     reads=[P2.key, CS.key], writes=[TT.key])
                s.op("dve", lambda e: e.tensor_tensor(out=R_[0:32, 0:nt], in0=A_[0:32, 0:nt], in1=CS[:, 0, 0:nt], op=ALU.mult),
                     reads=[A_.key, CS.key, R_.key], writes=[R_.key])
                s.op("dve", lambda e: e.tensor_tensor(out=R_[0:32, 0:nt], in0=R_[0:32, 0:nt], in1=TT[:, 0:nt], op=ALU.add),
                     reads=[R_.key, TT.key], writes=[R_.key])
                b.store("sp", d[dstname][c0 - row_base:c0 - row_base + ncl, col_off + tok0 + ts:col_off + tok0 + ts + nt],
                        R_[0:ncl, 0:nt], R_.key, [R_.key], [dstname])
            return f

        def ev_v(P, c0, nb, tt, nt):
            S = b.rot("SBF", b.SBF)
            b.copy(b.evac_eng(), S[:, 0:nb], P[:, 0:nb], [P.key], [S.key])
            b.store("sp", d["VT1"][PADK + tok0 + tt:PADK + tok0 + tt + 128, c0 - 2 * NQ:c0 - 2 * NQ + nb], S[:, 0:nb], S.key,
                    [S.key], ["VT1"])
        linear(b, "fm", hv, H.key, TB, w, 0, NQ, ev_rope("QT1", 0, 0))
        linear(b, "fm", hv, H.key, TB, w, NQ, NQ, ev_rope("KT1", NQ, PADK))
        linear(b, "tm", hv, H.key, TB, w, 2 * NQ, NQ, ev_v)


def phase_attn(b, stack):
    cfg, s, d = b.cfg, b.s, b.dram
    T, S_ = cfg.T, cfg.SLOTS
    DILS = (1, 4, 16)
    SBLK = 2048
    NSB = T // SBLK
    sc = 128 ** -0.5
    qb_ = [b.sb(stack, "aq%d" % i, [128, SBLK], BF16) for i in range(2)]
    kb_ = [b.sb(stack, "ak%d" % i, [128, SBLK + 2048], BF16) for i in range(1)]
    kmb = [b.sb(stack, "akmb%d" % i, [1, SBLK + 2048], BF16) for i in range(1)]
    vt = [b.sb(stack, "av%d" % i, [128, 2, 128], BF16) for i in range(6)]
    pt = [b.sb(stack, "ap%d" % i, [128, 256], BF16) for i in range(4)]
    osum = [b.sb(stack, "aos%d" % i, [128, SBLK], F32) for i in range(1)]
    dsum = [b.sb(stack, "ads%d" % i, [128, SBLK], F32) for i in range(1)]
    mo = [b.sb(stack, "amo%d" % i, [128, SBLK], BF16) for i in range(2)]
    qv = b.sb(stack, "aqv", [128, SBLK], F32)
    mab = b.sb(stack, "amab", [128, 256], BF16)
    s.op("dve", lambda e: e.tensor_copy(out=mab[:, 0:128], in_=b.L[:]), reads=[b.L.key], writes=[mab.key])
    s.op("dve", lambda e: e.tensor_copy(out=mab[:, 128:256], in_=b.U[:]), reads=[b.U.key, mab.key], writes=[mab.key])
    for slot in range(S_):
        for sb_ in range(NSB):
            tokS = sb_ * SBLK
            OS = b.rot("aos", osum)
            DS = b.rot("ads", dsum)
            first = True
            for g in (2, 1, 0):
                dil = DILS[g]
                row0 = (g * S_ + slot) * 128
                halo = 64 * dil
                nk = SBLK + 2 * halo
                Q = b.rot("aq", qb_)
                Kt = b.rot("ak", kb_)
                b.load("sp", Q[:], d["QT1"][row0:row0 + 128, tokS:tokS + SBLK], Q.key, [], [Q.key], dr=["QT1"])
                k0 = PADK + tokS - halo
                b.load("sp", Kt[:, 0:nk], d["KT1"][row0:row0 + 128, k0:k0 + nk], Kt.key, [], [Kt.key], dr=["KT1"])
                KMB = b.rot("akmb", kmb)
                b.load("sp", KMB[0:1, 0:nk], d["kmask"][k0:k0 + nk].rearrange("(o n) -> o n", o=1), KMB.key, [], [KMB.key])
                blk = 128 * dil
                for bi in range(SBLK // blk):
                    for r in range(dil):
                        qsl = slice(bi * blk + r, bi * blk + r + 127 * dil + 1, dil)
                        kA = slice(bi * blk + r, bi * blk + r + 127 * dil + 1, dil)
                        kB = slice(bi * blk + blk + r, bi * blk + blk + r + 127 * dil + 1, dil)
                        V = b.rot("av", vt)
                        tA = k0 + bi * blk + r
                        vrowsA = d["VT1"][tA:tA + 127 * dil + 1:dil, row0:row0 + 128]
                        vrowsB = d["VT1"][tA + blk:tA + blk + 127 * dil + 1:dil, row0:row0 + 128]
                        b.load("sp", V[:, 0, :], vrowsA, V.key, [], [V.key], dr=["VT1"])
                        b.load("sp", V[:, 1, :], vrowsB, V.key, [], [V.key], dr=["VT1"])
                        Ps = b.rot("P", b.P)

                        def sc_mm(e, Ps=Ps, Kt=Kt, Q=Q, KMB=KMB, qsl=qsl, kA=kA, kB=kB):
                            e.matmul(Ps[:, 0:128], Kt[:, kA], Q[:, qsl], start=True, stop=False)
                            e.matmul(Ps[:, 0:128], KMB[0:1, kA], b.onesb[0:1, 0:128], start=False, stop=True)
                            e.matmul(Ps[:, 128:256], Kt[:, kB], Q[:, qsl], start=True, stop=False)
                            return e.matmul(Ps[:, 128:256], KMB[0:1, kB], b.onesb[0:1, 0:128], start=False, stop=True)
                        s.op("pe", sc_mm, reads=[Kt.key, Q.key, KMB.key, b.onesb.key], writes=[Ps.key])
                        PT_ = b.rot("ap", pt)
                        s.op("act", lambda e, PT_=PT_, Ps=Ps: e.activation(out=PT_[:], in_=Ps[:, 0:256], func=AF.Exp, scale=sc),
                             reads=[Ps.key], writes=[PT_.key])
                        s.op("dve", lambda e, PT_=PT_: e.tensor_tensor(out=PT_[:], in0=PT_[:], in1=mab[:], op=ALU.mult),
                             reads=[PT_.key, mab.key], writes=[PT_.key])
                        Po = b.rot("P", b.P)

                        def pv_mm(e, Po=Po, V=V, PT_=PT_):
                            e.matmul(Po[:, 0:128], V[:, 0, :], PT_[:, 0:128], start=True, stop=False)
                            e.matmul(Po[:, 0:128], V[:, 1, :], PT_[:, 128:256], start=False, stop=True)
                            e.matmul(Po[:, 128:256], b.onesb[:], PT_[:, 0:128], start=True, stop=False)
                            return e.matmul(Po[:, 128:256], b.onesb[:], PT_[:, 128:256], start=False, stop=True)
                        s.op("pe", pv_mm, reads=[V.key, PT_.key, b.onesb.key], writes=[Po.key])
                        if first:
                            s.op("act", lambda e, OS=OS, Po=Po, qsl=qsl: e.activation(out=OS[:, qsl], in_=Po[:, 0:128], func=AF.Copy),
                                 reads=[Po.key], writes=[OS.key])
                            s.op("dve", lambda e, DS=DS, Po=Po, qsl=qsl: e.tensor_copy(out=DS[:, qsl], in_=Po[:, 128:256]),
                                 reads=[Po.key], writes=[DS.key])
                        else:
                            s.op("dve", lambda e, OS=OS, Po=Po, qsl=qsl: e.tensor_tensor(out=OS[:, qsl], in0=Po[:, 0:128], in1=OS[:, qsl],
                                                                                    op=ALU.add), reads=[Po.key, OS.key], writes=[OS.key])
                            s.op("dve", lambda e, DS=DS, Po=Po, qsl=qsl: e.tensor_tensor(out=DS[:, qsl], in0=Po[:, 128:256], in1=DS[:, qsl],
                                                                                    op=ALU.add), reads=[Po.key, DS.key], writes=[DS.key])
                first = False
            MO = b.rot("amo", mo)
            b.load("sp", qv[:], d["qvalid"][tokS:tokS + SBLK].partition_broadcast(128), qv.key, [], [qv.key])
            s.op("dve", lambda e, DS=DS: e.tensor_scalar(out=DS[:], in0=DS[:], scalar1=1e-30, scalar2=None, op0=ALU.max),
                 reads=[DS.key], writes=[DS.key])
            s.op("dve", lambda e, DS=DS: e.reciprocal(out=DS[:], in_=DS[:]), reads=[DS.key], writes=[DS.key])
            s.op("dve", lambda e, DS=DS: e.tensor_tensor(out=DS[:], in0=DS[:], in1=qv[:], op=ALU.mult),
                 reads=[DS.key, qv.key], writes=[DS.key])
            s.op("dve", lambda e, MO=MO, OS=OS, DS=DS: e.tensor_tensor(out=MO[:], in0=OS[:], in1=DS[:], op=ALU.mult),
                 reads=[OS.key, DS.key], writes=[MO.key])
            b.store("sp", d["MIXT"][slot * 128:(slot + 1) * 128, tokS:tokS + SBLK], MO[:], MO.key, [MO.key], ["MIXT"])


def phase_mixout1(b, stack):
    cfg, s, d = b.cfg, b.s, b.dram
    T, D, DC = cfg.T, cfg.D, cfg.DC
    TBM = 512
    mv = d["MIXT"].rearrange("(c p) t -> p c t", p=128)
    for tb in range(T // TBM):
        H = b.H[0]
        hv = H[:, 0:DC * TBM].rearrange("p (c t) -> p c t", c=DC)
        b.load("sp", hv, mv[:, :, tb * TBM:(tb + 1) * TBM], H.key, [], [H.key], dr=["MIXT"])
        outproj_residual(b, hv, H.key, TBM, d["od_w_out"][0], "XL2", "XL3", tb * TBM)


def phase_final(b, stack):
    cfg, s, d = b.cfg, b.s, b.dram
    T, D = cfg.T, cfg.D
    gf = b.sb(stack, "gfinb", [128, D], F32)
    s.dma("sp", lambda e: e.dma_start(out=gf[:], in_=d["norm_final"].rearrange("(o n) -> o n", o=1).partition_broadcast(128)),
          gf.key, writes=[gf.key])
    yo = [b.sb(stack, "yo%d" % i, [128, D], F32) for i in range(2)]
    for t0 in range(0, T, 128):
        X = b.rot("X", b.X)
        XB = b.rot("XB", b.XB)
        SC = b.rot("SC", b.SC)
        Y = b.rot("yo", yo)
        b.load("sp", X[:, 0:D], d["XL4"][t0:t0 + 128, :], X.key, [], [X.key], dr=["XL4"])
        s.op("act", lambda e, X=X, XB=XB, SC=SC: e.activation(out=XB[:, 0:D], in_=X[:, 0:D], func=AF.Square, accum_out=SC[:, 0:1]),
             reads=[X.key], writes=[XB.key, SC.key])
        s.op("act", lambda e, SC=SC: e.activation(out=SC[:, 1:2], in_=SC[:, 0:1], func=AF.Sqrt, scale=1.0 / D, bias=EPS),
             reads=[SC.key], writes=[SC.key])
        s.op("dve", lambda e, SC=SC: e.reciprocal(out=SC[:, 2:3], in_=SC[:, 1:2]), reads=[SC.key], writes=[SC.key])
        s.op("dve", lambda e, X=X, SC=SC, Y=Y: e.scalar_tensor_tensor(out=Y[:], in0=X[:, 0:D], scalar=SC[:, 2:3], in1=gf[:],
                                                                   op0=ALU.mult, op1=ALU.mult),
             reads=[X.key, SC.key, gf.key], writes=[Y.key])
        b.store("sp", d["y"][t0:t0 + 128, :], Y[:], Y.key, [Y.key], ["y"], final=True)


def all_phases():
    return [phase_A, phase_gdn, phase_mlstm, phase_mixout0, phase_ffn(0, "XL1", "XL2"), phase_qkv1, phase_attn, phase_mixout1,
            phase_ffn(1, "XL3", "XL4"), phase_final]


N_CORES = 4


def kernel(**inputs):
    import ml_dtypes
    cfg = Cfg()
    T = cfg.T
    xp = np.asarray(inputs["x_prompt"], np.float32)
    xs = np.asarray(inputs["x_sample"], np.float32)
    shared = {k: np.ascontiguousarray(np.asarray(v, np.float32)) for k, v in inputs.items()
              if k not in ("x_prompt", "x_sample")}
    in_maps = []
    lens = []
    for c in range(N_CORES):
        seq = xp[c] if c < 2 else xs[c - 2]
        n = seq.shape[0]
        x = np.zeros((T, cfg.D), np.float32)
        x[:n] = seq
        km = np.full((T + 2048,), -BIG, np.float32)
        km[PADK:PADK + n] = 0.0
        m = dict(shared)
        m["x"] = x
        m["kmask"] = km.astype(ml_dtypes.bfloat16)
        m["pos"] = np.arange(T, dtype=np.float32)
        m["ropec"] = rope_consts()
        qv = np.zeros((T,), np.float32)
        qv[:n] = 1.0
        m["qvalid"] = qv
        in_maps.append(m)
        lens.append(n)
    b = build(cfg, all_phases())
    res = run_bass_kernel_spmd(b.nc, in_maps, core_ids=list(range(N_CORES)))
    ys = [np.asarray(res.results[c]["y"], np.float32)[:lens[c]] for c in range(N_CORES)]
    return (np.stack(ys[0:2], 0), np.stack(ys[2:4], 0))
```

```python
import numpy as np
import concourse.bass as bass
import concourse.mybir as mybir
from concourse.bass_utils import run_bass_kernel_spmd

F32 = mybir.dt.float32
BF16 = mybir.dt.bfloat16
AF = mybir.ActivationFunctionType
ALU = mybir.AluOpType
AX = mybir.AxisListType

EPS = 1e-6
BIG = 30000.0


class Cfg:
    def __init__(self, D=2048, GH=8, MH=4, SLOTS=16, DFF=5632, T=16384, TBLK=1024):
        self.D, self.GH, self.MH, self.SLOTS, self.DFF, self.T, self.TBLK = D, GH, MH, SLOTS, DFF, T, TBLK
        self.DC = D // 128
        self.A_QKV = GH * 384
        self.A_Z = GH * 128
        self.A_G = 4 * GH
        self.B_QKV = MH * 512
        self.B_O = MH * 256
        self.B_G = 4 * MH
        self.c1 = self.A_QKV
        self.c2 = self.c1 + self.A_Z
        self.c3 = self.c2 + self.A_G
        self.c4 = self.c3 + self.B_QKV
        self.c5 = self.c4 + self.B_O
        self.EVEN_IN = self.c5 + self.B_G
        self.EVEN_MIX = GH * 128 + MH * 256
        self.ODD_IN = 9 * SLOTS * 128
        self.ODD_MIX = SLOTS * 128
        assert self.EVEN_MIX == D and self.ODD_MIX == D


class Sched:
    ENG = ("pe", "act", "dve", "pool", "sp")

    def __init__(self, nc):
        self.nc = nc
        self.ops = {e: [] for e in self.ENG}
        self.last_w = {}
        self.reads = {}
        self.waited = {}
        self.dma_val = {}
        self.dma_last = {}
        self.final_events = []
        self.dwl = {}
        self.excl = set()
        self.sem_slot = {}
        self.sem_free = []
        self.n_slots = 0

    def _deps(self, eng, reads, writes):
        deps = []
        for k in reads:
            if k in self.last_w:
                deps.append(self.last_w[k])
        for k in writes:
            if k in self.last_w:
                deps.append(self.last_w[k])
            deps.extend(self.reads.get(k, ()))
        return deps

    def _add_waits(self, eng, deps):
        waits = []
        for ev in deps:
            kind, key, val = ev
            if kind == "eng" and key == eng and eng == "pe":
                continue
            wk = (eng, kind, key)
            if self.waited.get(wk, -1) >= val:
                continue
            self.waited[wk] = val
            waits.append(ev)
            if kind == "eng":
                self.ops[key][val]["inc"] = True
        return waits

    def _commit(self, ev, reads, writes):
        for k in writes:
            self.last_w[k] = ev
            self.reads[k] = []
        for k in reads:
            self.reads.setdefault(k, []).append(ev)

    def op(self, eng, fn, reads=(), writes=()):
        ex = [k for k in reads if k in self.excl]
        if ex:
            reads = [k for k in reads if k not in self.excl]
            writes = list(writes) + [k for k in ex if k not in writes]
        deps = self._deps(eng, reads, writes)
        waits = self._add_waits(eng, deps)
        idx = len(self.ops[eng])
        self.ops[eng].append(dict(waits=waits, fn=fn, inc=False, dma=None))
        ev = ("eng", eng, idx)
        self._commit(ev, reads, writes)
        return ev

    def dma(self, q, fn, semkey, reads=(), writes=(), final=False, dr=(), dw=()):
        deps = self._deps(q, reads, writes)
        if semkey not in self.sem_slot:
            if self.sem_free:
                self.sem_slot[semkey] = self.sem_free.pop(0)
            else:
                self.sem_slot[semkey] = self.n_slots
                self.n_slots += 1
        semkey = self.sem_slot[semkey]
        if semkey in self.dma_last:
            deps.append(self.dma_last[semkey])
        for k in dr:
            deps.extend(self.dwl.get(k, {}).values())
        for k in dw:
            deps.extend(self.reads.get(k, ()))
        waits = self._add_waits(q, deps)
        v = self.dma_val.get(semkey, 0) + 16
        self.dma_val[semkey] = v
        self.ops[q].append(dict(waits=waits, fn=fn, inc=False, dma=(semkey, v)))
        ev = ("dma", semkey, v)
        self.dma_last[semkey] = ev
        self._commit(ev, list(reads) + list(dr), writes)
        for k in dw:
            self.dwl.setdefault(k, {})[semkey] = ev
            self.reads[k] = []
        if final:
            self.final_events.append(ev)
        return ev

    def barrier(self):
        evs = []
        for e in self.ENG:
            if self.ops[e]:
                for idx in range(len(self.ops[e]) - 1, -1, -1):
                    o = self.ops[e][idx]
                    if o["fn"] is not None and o["dma"] is None:
                        evs.append(("eng", e, idx))
                        break
        evs.extend(self.dma_last.values())
        for f in self.ENG:
            waits = self._add_waits(f, evs)
            self.ops[f].append(dict(waits=waits, fn=None, inc=False, dma=None))
        self.sem_free.extend(sorted(set(self.sem_slot.values())))
        self.sem_slot = {}

    def finish(self):
        waits = self._add_waits("sp", self.final_events)
        self.ops["sp"].append(dict(waits=waits, fn=None, inc=False, dma=None))

    def emit(self, stack):
        nc = self.nc
        esem = {e: stack.enter_context(nc.semaphore("s_" + e)) for e in self.ENG}
        dsem = {}
        for k in range(self.n_slots):
            dsem[k] = stack.enter_context(nc.semaphore("d_%d" % k))
        cnt = {}
        for e in self.ENG:
            c = 0
            arr = []
            for o in self.ops[e]:
                if o["inc"]:
                    c += 1
                arr.append(c)
            cnt[e] = arr
        block = stack.enter_context(nc.Block())

        def run(e, engobj):
            for o in self.ops[e]:
                for (kind, key, val) in o["waits"]:
                    if kind == "eng":
                        engobj.wait_ge(esem[key], cnt[key][val])
                    else:
                        engobj.wait_ge(dsem[key], val)
                if o["fn"] is None:
                    continue
                ins = o["fn"](engobj)
                if o["dma"] is not None:
                    ins.then_inc(dsem[o["dma"][0]], 16)
                elif o["inc"]:
                    ins.then_inc(esem[e], 1)

        block.sync(lambda g: run("sp", g))
        block.scalar(lambda g: run("act", g))
        block.vector(lambda g: run("dve", g))
        block.gpsimd(lambda g: run("pool", g))
        block.tensor(lambda g: run("pe", g))
        n = {e: len(self.ops[e]) for e in self.ENG}
        return n, len(dsem)


class Buf:
    def __init__(self, t, key):
        self.t, self.key = t, key

    def __getitem__(self, k):
        return self.t[k]


class B:
    def __init__(self, cfg, debug_outs=()):
        self.cfg = cfg
        self.nc = bass.Bass("TRN2", target_bir_lowering=False)
        self.s = Sched(self.nc)
        self.debug_outs = set(debug_outs)
        self.rr = {}
        self.dram = {}

    def din(self, name, shape, dt=F32):
        self.dram[name] = self.nc.dram_tensor(name, list(shape), dt, kind="ExternalInput").ap()
        return self.dram[name]

    def dout(self, name, shape, dt=F32):
        self.dram[name] = self.nc.dram_tensor(name, list(shape), dt, kind="ExternalOutput").ap()
        return self.dram[name]

    def dscr(self, name, shape, dt):
        kind = "ExternalOutput" if name in self.debug_outs else "Internal"
        self.dram[name] = self.nc.dram_tensor(name, list(shape), dt, kind=kind).ap()
        return self.dram[name]

    def sb(self, stack, name, shape, dt):
        t = stack.enter_context(self.nc.sbuf_tensor(name, list(shape), dt))
        return Buf(t, name)

    def ps(self, stack, name, shape, dt):
        t = stack.enter_context(self.nc.psum_tensor(name, list(shape), dt))
        self.s.excl.add(name)
        return Buf(t, name)

    def rot(self, name, lst):
        i = self.rr.get(name, 0)
        self.rr[name] = i + 1
        return lst[i % len(lst)]

    def evac_eng(self):
        return self.rot("evac", ["act", "dve"])

    def copy(self, eng, out, in_, reads, writes, scale=None):
        if eng == "act":
            if scale is None:
                fn = lambda e: e.activation(out=out, in_=in_, func=AF.Copy)
            else:
                fn = lambda e: e.activation(out=out, in_=in_, func=AF.Identity, scale=scale)
        else:
            if scale is None:
                fn = lambda e: e.tensor_copy(out=out, in_=in_)
            else:
                fn = lambda e: e.tensor_scalar(out=out, in0=in_, scalar1=scale, scalar2=None, op0=ALU.mult)
        return self.s.op(eng, fn, reads=reads, writes=writes)

    def load(self, q, out, in_, semkey, reads, writes, dr=()):
        return self.s.dma(q, lambda e: e.dma_start(out=out, in_=in_, allow_slow_non_contiguous=True), semkey, reads=reads, writes=writes, dr=dr)

    def store(self, q, out, in_, semkey, reads, dw, final=False):
        q = "pool"
        return self.s.dma(q, lambda e: e.dma_start(out=out, in_=in_, allow_slow_non_contiguous=True), semkey, reads=reads, writes=(), dw=dw,
                          final=final)


def build_common(b, stack):
    cfg = b.cfg
    s = b.s
    b.identb = b.sb(stack, "identb", [128, 128], BF16)
    b.identf = b.sb(stack, "identf", [128, 128], F32)
    b.U = b.sb(stack, "U", [128, 128], F32)
    b.L = b.sb(stack, "L", [128, 128], F32)
    b.onesf = b.sb(stack, "onesf", [128, 128], F32)
    b.onesb = b.sb(stack, "onesb", [128, 128], BF16)
    b.zerob = b.sb(stack, "zerob", [128, 512], BF16)

    def mk_tri(buf, cmp_, dt_fill=1.0):
        pass

    def init_ident(buf):
        s.op("pool", lambda e: e.memset(buf[:], 0.0), writes=[buf.key])
        s.op("pool", lambda e: e.affine_select(out=buf[:], in_=buf[:], pattern=[[-1, 128]], compare_op=ALU.not_equal,
                                              fill=1.0, base=0, channel_multiplier=1), reads=[buf.key], writes=[buf.key])
    init_ident(b.identb)
    init_ident(b.identf)
    s.op("pool", lambda e: e.memset(b.onesf[:], 1.0), writes=[b.onesf.key])
    s.op("pool", lambda e: e.memset(b.onesb[:], 1.0), writes=[b.onesb.key])
    s.op("pool", lambda e: e.memset(b.zerob[:], 0.0), writes=[b.zerob.key])
    s.op("pool", lambda e: e.memset(b.U[:], 1.0), writes=[b.U.key])
    s.op("pool", lambda e: e.affine_select(out=b.U[:], in_=b.U[:], pattern=[[1, 128]], compare_op=ALU.is_ge,
                                          fill=0.0, base=0, channel_multiplier=-1), reads=[b.U.key], writes=[b.U.key])
    s.op("pool", lambda e: e.memset(b.L[:], 1.0), writes=[b.L.key])
    s.op("pool", lambda e: e.affine_select(out=b.L[:], in_=b.L[:], pattern=[[-1, 128]], compare_op=ALU.is_ge,
                                          fill=0.0, base=0, channel_multiplier=1), reads=[b.L.key], writes=[b.L.key])
    b.gtmp = [b.sb(stack, "g%d" % i, [128, 128], F32) for i in range(36)]
    b.BMf = b.sb(stack, "BMf", [128, 128], F32)
    b.BMb = b.sb(stack, "BMb", [128, 128], F32)
    b.SMf = b.sb(stack, "SMf", [128, 128], F32)
    b.SMb = b.sb(stack, "SMb", [128, 128], F32)
    SMf, SMb, BMf, BMb = b.SMf, b.SMb, b.BMf, b.BMb
    s.op("dve", lambda e: e.tensor_scalar(out=SMf[:], in0=b.U[:], scalar1=-1.0, scalar2=1.0, op0=ALU.mult, op1=ALU.add),
         reads=[b.U.key], writes=[SMf.key])
    s.op("dve", lambda e: e.tensor_scalar(out=SMb[:], in0=b.L[:], scalar1=-1.0, scalar2=1.0, op0=ALU.mult, op1=ALU.add),
         reads=[b.L.key], writes=[SMb.key])
    s.op("dve", lambda e: e.tensor_scalar(out=BMf[:], in0=SMb[:], scalar1=BIG, scalar2=None, op0=ALU.mult),
         reads=[SMb.key], writes=[BMf.key])
    s.op("dve", lambda e: e.tensor_scalar(out=BMb[:], in0=SMf[:], scalar1=BIG, scalar2=None, op0=ALU.mult),
         reads=[SMf.key], writes=[BMb.key])
    b.W = [b.sb(stack, "W%d" % i, [128, 8192], BF16) for i in range(2)]
    b.X = [b.sb(stack, "X%d" % i, [128, 2048], F32) for i in range(2)]
    b.XB = [b.sb(stack, "XB%d" % i, [128, 2048], BF16) for i in range(2)]
    b.SF = [b.sb(stack, "SF%d" % i, [128, 512], F32) for i in range(4)]
    b.SBF = [b.sb(stack, "SBF%d" % i, [128, 512], BF16) for i in range(4)]
    b.SC = [b.sb(stack, "SC%d" % i, [128, 8], F32) for i in range(8)]
    b.P = [b.ps(stack, "P%d" % i, [128, 512], F32) for i in range(6)]
    b.PT = [b.ps(stack, "PT%d" % i, [128, 8, 128], BF16) for i in range(2)]


def rmsnorm_T(b, src, gT, gkey, dst, dstkey, src_reads=(), dr=(), preloaded=None, halo_dst=None):
    cfg, s = b.cfg, b.s
    D, DC = cfg.D, cfg.DC
    XB = b.rot("XB", b.XB)
    SC = b.rot("SC", b.SC)
    if preloaded is not None:
        X = preloaded
    else:
        X = b.rot("X", b.X)
        b.load("sp", X[:, 0:D], src, X.key, reads=src_reads, writes=[X.key], dr=dr)
    s.op("act", lambda e: e.activation(out=XB[:, 0:D], in_=X[:, 0:D], func=AF.Square, accum_out=SC[:, 0:1]),
         reads=[X.key], writes=[XB.key, SC.key])
    s.op("act", lambda e: e.activation(out=SC[:, 1:2], in_=SC[:, 0:1], func=AF.Sqrt, scale=1.0 / D, bias=EPS),
         reads=[SC.key], writes=[SC.key])
    s.op("dve", lambda e: e.reciprocal(out=SC[:, 2:3], in_=SC[:, 1:2]), reads=[SC.key], writes=[SC.key])
    s.op("act", lambda e: e.activation(out=XB[:, 0:D], in_=X[:, 0:D], func=AF.Identity, scale=SC[:, 2:3]),
         reads=[X.key, SC.key], writes=[XB.key])
    for cg in range(0, DC, 8):
        n = min(8, DC - cg)
        PT = b.rot("PT", b.PT)

        def tr(e, cg=cg, n=n, PT=PT):
            for c in range(n):
                ins = e.transpose(PT[:, c, :], XB[:, (cg + c) * 128:(cg + c + 1) * 128], b.identb[:])
            return ins
        s.op("pe", tr, reads=[XB.key, b.identb.key], writes=[PT.key])
        if halo_dst is not None:
            for hi, hd in enumerate(halo_dst):
                s.op("dve", lambda e, cg=cg, n=n, PT=PT, hi=hi, hd=hd: e.tensor_tensor(
                    out=hd[:, cg:cg + n, :], in0=PT[:, 0:n, hi:hi + 1],
                    in1=gT[:, cg:cg + n].unsqueeze(2), op=ALU.mult),
                    reads=[PT.key, gkey], writes=[dstkey])
            continue
        s.op("dve", lambda e, cg=cg, n=n, PT=PT: e.tensor_tensor(
            out=dst[:, cg:cg + n, :], in0=PT[:, 0:n, :],
            in1=gT[:, cg:cg + n].unsqueeze(2).to_broadcast([128, n, 128]), op=ALU.mult),
            reads=[PT.key, gkey], writes=[dstkey])


def linear(b, mode, hv, hkey, ntok, w, col0, ncols, evac):
    s = b.s
    KC = hv.shape[1]
    wv = w.rearrange("(c p) n -> p c n", p=128)
    CBW = 512 if KC <= 16 else 128
    for cb in range(0, ncols, CBW):
        nb = min(CBW, ncols - cb)
        W = b.rot("W", b.W)
        Wv = W[:, 0:KC * CBW].rearrange("p (c n) -> p c n", c=KC)
        b.load("sp", Wv[:, :, 0:nb], wv[:, :, col0 + cb:col0 + cb + nb], W.key, reads=[], writes=[W.key], dr=[w.tensor.name])
        if mode == "fm":
            for ct in range(0, nb, 128):
                ncl = min(128, nb - ct)
                for ts in range(0, ntok, 512):
                    nt = min(512, ntok - ts)
                    P = b.rot("P", b.P)

                    def mm(e, P=P, ct=ct, ncl=ncl, ts=ts, nt=nt, Wv=Wv):
                        for c in range(KC):
                            ins = e.matmul(P[0:ncl, 0:nt], Wv[:, c, ct:ct + ncl], hv[:, c, ts:ts + nt],
                                           start=(c == 0), stop=(c == KC - 1))
                        return ins
                    s.op("pe", mm, reads=[W.key, hkey], writes=[P.key])
                    evac(P, col0 + cb + ct, ncl, ts, nt)
        else:
            for tt in range(0, ntok, 128):
                P = b.rot("P", b.P)

                def mm(e, P=P, tt=tt, nb=nb, Wv=Wv):
                    for c in range(KC):
                        ins = e.matmul(P[:, 0:nb], hv[:, c, tt:tt + 128], Wv[:, c, 0:nb],
                                       start=(c == 0), stop=(c == KC - 1))
                    return ins
                s.op("pe", mm, reads=[W.key, hkey], writes=[P.key])
                evac(P, col0 + cb, nb, tt, 128)


def phase_A(b, stack):
    b.H = [b.sb(stack, "H_%d" % b.pid, [128, 24576], BF16)]
    cfg, s = b.cfg, b.s
    T, TB, D, DC, GH, MH = cfg.T, cfg.TBLK, cfg.D, cfg.DC, cfg.GH, cfg.MH
    d = b.dram
    x, w = d["x"], d["b_ev_w_in"]
    for tb in range(T // TB):
        H = b.rot("H", b.H)
        hv = H[:, 0:DC * TB].rearrange("p (c t) -> p c t", c=DC)
        for tt in range(TB // 128):
            t0 = tb * TB + tt * 128
            rmsnorm_T(b, x[t0:t0 + 128, :], b.gmix[:, 0:DC], b.gmix.key, hv[:, :, tt * 128:(tt + 1) * 128], H.key)
        tok0 = tb * TB

        def ev_fm(dst, row0, scale=None):
            def f(P, c0, ncl, ts, nt):
                S = b.rot("SBF", b.SBF)
                b.copy(b.evac_eng(), S[0:ncl, 0:nt], P[0:ncl, 0:nt], [P.key], [S.key], scale=scale)
                b.store("sp", dst(c0 - row0, ncl, tok0 + ts, nt), S[0:ncl, 0:nt], S.key, [S.key], [dst.__name__])
            return f

        def ev_tm(dstname, col_base, dt):
            def f(P, c0, nb, tt, nt):
                S = b.rot("SBF", b.SBF) if dt == BF16 else b.rot("SF", b.SF)
                b.copy(b.evac_eng(), S[:, 0:nb], P[:, 0:nb], [P.key], [S.key])
                b.store("sp", d[dstname][tok0 + tt:tok0 + tt + 128, c0 - col_base:c0 - col_base + nb], S[:, 0:nb],
                        S.key, [S.key], [dstname])
            return f

        def QKVA_T(r, n, t, nt):
            return d["QKVA_T"][r:r + n, 1 + t:1 + t + nt]

        def QB_T(r, n, t, nt):
            return d["QB_T"][r:r + n, t:t + nt]

        def KB_T(r, n, t, nt):
            return d["KB_T"][r:r + n, t:t + nt]
        linear(b, "fm", hv, H.key, TB, w[0], 0, cfg.A_QKV, ev_fm(QKVA_T, 0))
        linear(b, "tm", hv, H.key, TB, w[0], cfg.c1, cfg.A_Z, ev_tm("Z", cfg.c1, BF16))
        linear(b, "tm", hv, H.key, TB, w[0], cfg.c2, cfg.A_G, ev_tm("GA", cfg.c2, F32))
        linear(b, "fm", hv, H.key, TB, w[0], cfg.c3, MH * 128, ev_fm(QB_T, cfg.c3, scale=128 ** -0.5))
        linear(b, "fm", hv, H.key, TB, w[0], cfg.c3 + MH * 128, MH * 128, ev_fm(KB_T, cfg.c3 + MH * 128))
        linear(b, "tm", hv, H.key, TB, w[0], cfg.c3 + MH * 128, MH * 384, ev_tm("KVB", cfg.c3 + MH * 128, BF16))
        linear(b, "tm", hv, H.key, TB, w[0], cfg.c4, cfg.B_O, ev_tm("OB", cfg.c4, BF16))
        linear(b, "tm", hv, H.key, TB, w[0], cfg.c5, cfg.B_G, ev_tm("GB", cfg.c5, F32))


def declare_io(b, stack):
    cfg = b.cfg
    T, D = cfg.T, cfg.D
    b.din("x", [T, D])
    b.din("norm_mix", [2, D])
    b.din("ev_w_in", [1, D, cfg.EVEN_IN])
    b.din("gdn_conv", [1, 3, cfg.A_QKV])
    b.din("gdn_a_log", [1, 2, cfg.GH])
    b.din("gdn_dt_bias", [1, 2, cfg.GH])
    b.din("gdn_norm", [1, 128])
    b.din("ml_gate_bias", [1, 2, 2, cfg.MH])
    b.din("ml_norm", [1, cfg.MH * 256])
    b.din("ev_w_out", [1, cfg.EVEN_MIX, D])
    b.din("od_w_in", [1, D, cfg.ODD_IN])
    b.din("od_w_out", [1, cfg.ODD_MIX, D])
    b.din("norm_ffn", [2, D])
    b.din("ffn_w_up", [2, D, 2 * cfg.DFF])
    b.din("ffn_conv", [2, 3, cfg.DFF])
    b.din("ffn_conv_b", [2, cfg.DFF])
    b.din("ffn_w_down", [2, cfg.DFF, D])
    b.din("norm_final", [D])
    for wn, shp in (("ev_w_in", [1, D, cfg.EVEN_IN]), ("ev_w_out", [1, cfg.EVEN_MIX, D]), ("od_w_in", [1, D, cfg.ODD_IN]),
                    ("od_w_out", [1, cfg.ODD_MIX, D]), ("ffn_w_up", [2, D, 2 * cfg.DFF]), ("ffn_w_down", [2, cfg.DFF, D])):
        b.dscr("b_" + wn, shp, BF16)
    b.din("kmask", [T + 2048], BF16)
    b.din("pos", [T])
    b.dout("y", [T, D])
    GH, MH = cfg.GH, cfg.MH
    b.dscr("QKVA_T", [cfg.A_QKV, T + 2], BF16)
    b.dscr("Z", [T, cfg.A_Z], BF16)
    b.dscr("GA", [T, cfg.A_G], F32)
    b.dscr("QB_T", [MH * 128, T], BF16)
    b.dscr("KB_T", [MH * 128, T], BF16)
    b.dscr("KVB", [T, MH * 384], BF16)
    b.dscr("OB", [T, cfg.B_O], BF16)
    b.dscr("GB", [T, cfg.B_G], F32)
    b.dscr("OA0", [T, GH * 128], F32)
    b.dscr("OA1", [T, GH * 128], F32)
    b.dscr("HB0", [T, MH * 256], F32)
    b.dscr("HB1", [T, MH * 256], F32)
    b.dscr("XL1", [T, D], F32)
    b.dscr("XL2", [T, D], F32)
    b.dscr("XL3", [T, D], F32)
    b.dscr("XL4", [T, D], F32)
    NQ = 3 * cfg.SLOTS * 128
    b.dscr("QT1", [NQ, T], BF16)
    b.dscr("KT1", [NQ, T + 2048], BF16)
    b.dscr("VT1", [T + 2048, NQ], BF16)
    b.dscr("MIXT", [D, T], BF16)
    b.din("ropec", [32, 2])
    b.din("qvalid", [T])
    DC = cfg.DC
    b.gmix = b.sb(stack, "gmix", [128, 2 * DC], F32)
    b.gffn = b.sb(stack, "gffn", [128, 2 * DC], F32)
    b.gfin = b.sb(stack, "gfin", [128, DC], F32)
    d = b.dram
    with b.nc.allow_non_contiguous_dma(reason="tiny param transposes"):
        pass
    for l in range(2):
        b.s.dma("sp", lambda e, l=l: e.dma_start(out=b.gmix[:, l * DC:(l + 1) * DC],
                                                 in_=d["norm_mix"][l].rearrange("(c p) -> p c", p=128),
                                                 allow_slow_non_contiguous=True), "gmix", writes=["gmix"])
        b.s.dma("sp", lambda e, l=l: e.dma_start(out=b.gffn[:, l * DC:(l + 1) * DC],
                                                 in_=d["norm_ffn"][l].rearrange("(c p) -> p c", p=128),
                                                 allow_slow_non_contiguous=True), "gffn", writes=["gffn"])
    b.s.dma("sp", lambda e: e.dma_start(out=b.gfin[:, 0:DC], in_=d["norm_final"].rearrange("(c p) -> p c", p=128),
                                        allow_slow_non_contiguous=True), "gfin", writes=["gfin"])


def phase_wcast(b, stack):
    d = b.dram
    for wn in ("ev_w_in", "ffn_w_up", "ffn_w_down", "ev_w_out", "od_w_in", "od_w_out"):
        src, dst = d[wn], d["b_" + wn]
        L, R, C = src.shape
        for l in range(L):
            for r0 in range(0, R, 512):
                r1 = min(R, r0 + 512)
                b.s.dma("pool", lambda e, l=l, r0=r0, r1=r1, src=src, dst=dst: e.dma_start(out=dst[l, r0:r1, :], in_=src[l, r0:r1, :]),
                        "wcast", dw=["b_" + wn])


def build(cfg, phases, debug_outs=()):
    from contextlib import ExitStack
    b = B(cfg, debug_outs)
    stack = ExitStack()
    with stack:
        declare_io(b, stack)
        build_common(b, stack)
        for ph in phases:
            with ExitStack() as pst:
                b.pid = getattr(b, "pid", 0) + 1
                ph(b, pst)
                b.s.barrier()
        b.s.finish()
        n, nd = b.s.emit(stack)
        print("ops", n, "dma sems", nd)
    return b


def run_interleaved(b, groups, group_pre, unit_fn, width):
    pending = []
    gi = 0
    active = {}
    busy = {}
    free = list(range(width))
    while True:
        while free and (pending or gi < len(groups)):
            if not pending:
                pending = list(group_pre(*groups[gi]))
                gi += 1
            chain = (pending[0][0], pending[0][2])
            if chain in busy.values():
                break
            slot = free.pop(0)
            busy[slot] = chain
            active[slot] = unit_fn(slot, *pending.pop(0))
        if not active:
            break
        for slot in sorted(active):
            try:
                next(active[slot])
            except StopIteration:
                del active[slot]
                del busy[slot]
                free.append(slot)


def mm1(b, P, n, lhsT, rhs, reads, m=128):
    return b.s.op("pe", lambda e: e.matmul(P[0:m, 0:n], lhsT, rhs, start=True, stop=True), reads=reads, writes=[P.key])


class _Stop(Exception):
    pass


def chk(n):
    import os
    if os.environ.get("GDN_STOP", "") == str(n):
        raise _Stop()


def phase_gdn(b, stack):
    try:
        phase_gdn_(b, stack)
    except _Stop:
        print("GDN stopped early")


def phase_gdn_(b, stack):
    cfg, s, d = b.cfg, b.s, b.dram
    T, GH = cfg.T, cfg.GH
    NCH = T // 128
    tmps = [b.gtmp] + [[b.sb(stack, "g%d_%d" % (j, i), [128, 128], F32) for i in range(36)] for j in (1, 2)]
    wides = [[b.sb(stack, "gw%d_%d" % (j, i), [128, 384], F32) for i in range(3)] for j in range(3)]
    xins = [[b.sb(stack, "gx%d_%d" % (j, i), [128, 3, 130], BF16) for i in range(2)] for j in range(3)]
    gt = [b.sb(stack, "gt%d" % i, [128, 4 * GH], F32) for i in range(4)]
    gqv = [b.sb(stack, "gqv%d" % i, [128, 128], F32) for i in range(4)]
    gsm = [b.sb(stack, "gs%d" % i, [128, 6 * GH], F32) for i in range(4)]
    cols = [[b.sb(stack, "gc%d_%d" % (j, i), [128, 8], F32) for i in range(4)] for j in range(3)]
    S = {(h, dd, p): b.sb(stack, "S%d_%d_%d" % (h, dd, p), [128, 128], F32) for h in range(GH) for dd in range(2)
         for p in range(2)}
    cw = b.sb(stack, "gcw", [128, GH, 9], F32)
    dg = b.sb(stack, "gdg", [128, GH * 9, 128], BF16)
    ea = b.sb(stack, "gea", [128, 2, GH], F32)
    dtb = b.sb(stack, "gdtb", [128, 2, GH], F32)
    BMf, BMb, SMf, SMb = b.BMf, b.BMb, b.SMf, b.SMb
    for h in range(GH):
        for a in range(3):
            r0 = a * GH * 128 + h * 128
            s.dma("sp", lambda e, h=h, a=a, r0=r0: e.dma_start(
                out=cw[:, h, a * 3:(a + 1) * 3], in_=d["gdn_conv"][0][:, r0:r0 + 128].rearrange("j p -> p j"),
                allow_slow_non_contiguous=True), cw.key, writes=[cw.key])
    s.dma("sp", lambda e: e.dma_start(out=ea[:], in_=d["gdn_a_log"][0:1].partition_broadcast(128)), ea.key, writes=[ea.key])
    s.dma("sp", lambda e: e.dma_start(out=dtb[:], in_=d["gdn_dt_bias"][0:1].partition_broadcast(128)), dtb.key,
          writes=[dtb.key])
    s.op("act", lambda e: e.activation(out=ea[:], in_=ea[:], func=AF.Exp), reads=[ea.key], writes=[ea.key])
    for h in range(GH):
        for a in range(3):
            for j in range(3):
                s.op("dve", lambda e, h=h, a=a, j=j: e.tensor_scalar(
                    out=dg[:, h * 9 + a * 3 + j, :], in0=b.identb[:], scalar1=cw[:, h, a * 3 + j:a * 3 + j + 1],
                    scalar2=None, op0=ALU.mult), reads=[b.identb.key, cw.key], writes=[dg.key])
        for dd in range(2):
            s.op("pool", lambda e, h=h, dd=dd: e.memset(S[(h, dd, 0)][:], 0.0), writes=[S[(h, dd, 0)].key])
    for r in range(0, cfg.A_QKV, 128):
        b.store("sp", d["QKVA_T"][r:r + 128, 0:1], b.zerob[:, 0:1], "zpad", [b.zerob.key], ["QKVA_T"])
        b.store("sp", d["QKVA_T"][r:r + 128, T + 1:T + 2], b.zerob[:, 0:1], "zpad", [b.zerob.key], ["QKVA_T"])
    qkv3 = d["QKVA_T"].rearrange("(a h p) t -> p a h t", a=3, h=GH)
    units_q = [(step, dd) for step in range(NCH) for dd in range(2)]

    def group_pre(step, dd):
        if True:
            c = step if dd == 0 else NCH - 1 - step
            t0 = c * 128
            Tri = b.U if dd == 0 else b.L
            BM = BMf if dd == 0 else BMb
            SM = SMf if dd == 0 else SMb
            GT = b.rot("ggt", gt)
            GS = b.rot("ggs", gsm)
            b.load("sp", GT[:], d["GA"][t0:t0 + 128, :], GT.key, [], [GT.key], dr=["GA"])
            QV = b.rot("gqv", gqv)
            b.load("sp", QV[:], d["qvalid"][t0:t0 + 128].partition_broadcast(128), QV.key, [], [QV.key])
            s.op("dve", lambda e, GT=GT, GS=GS, dd=dd: e.tensor_tensor(
                out=GS[:, 0:GH], in0=GT[:, dd * GH:(dd + 1) * GH], in1=dtb[:, dd, :], op=ALU.add),
                reads=[GT.key, dtb.key], writes=[GS.key])
            s.op("act", lambda e, GS=GS: e.activation(out=GS[:, 0:GH], in_=GS[:, 0:GH], func=AF.Exp),
                 reads=[GS.key], writes=[GS.key])
            s.op("act", lambda e, GS=GS: e.activation(out=GS[:, 0:GH], in_=GS[:, 0:GH], func=AF.Ln, bias=1.0),
                 reads=[GS.key], writes=[GS.key])
            s.op("dve", lambda e, GS=GS, dd=dd: e.scalar_tensor_tensor(
                out=GS[:, GH:2 * GH], in0=GS[:, 0:GH], scalar=-1.0, in1=ea[:, dd, :], op0=ALU.mult, op1=ALU.mult),
                reads=[GS.key, ea.key], writes=[GS.key])
            s.op("act", lambda e, GS=GS, GT=GT, dd=dd: e.activation(
                out=GS[:, 2 * GH:3 * GH], in_=GT[:, (2 + dd) * GH:(3 + dd) * GH], func=AF.Sigmoid),
                reads=[GT.key], writes=[GS.key])
            s.op("dve", lambda e, GS=GS: e.tensor_scalar(out=GS[:, 3 * GH:4 * GH], in0=GS[:, 2 * GH:3 * GH], scalar1=-1.0,
                                                        scalar2=None, op0=ALU.mult), reads=[GS.key], writes=[GS.key])
            return [(h, step, dd, c, t0, Tri, BM, SM, GT, GS, QV, None) for h in range(GH)]
    def gdn_unit(slot, h, step, dd, c, t0, Tri, BM, SM, GT, GS, QV, KV):
        T_ = lambda: b.rot("gtmp%d" % slot, tmps[slot])
        C_ = lambda: b.rot("gcol%d" % slot, cols[slot])
        PS = lambda: b.rot("Ps%d" % slot, b.P[2 * slot:2 * slot + 2])
        wide = wides[slot]
        xin = xins[slot]
        gcolv = GS[:, GH + h:GH + h + 1]
        beta = GS[:, 2 * GH + h:2 * GH + h + 1]
        nbeta = GS[:, 3 * GH + h:3 * GH + h + 1]
        XI = b.rot("gxin%d" % slot, xin)
        b.load("sp", XI[:], qkv3[:, :, h, t0:t0 + 130], XI.key, [], [XI.key], dr=["QKVA_T"])
        yield
        Pc = PS()

        def conv(e, XI=XI, Pc=Pc, h=h):
            for a in range(3):
                for j in range(3):
                    ins = e.matmul(Pc[:, a * 128:(a + 1) * 128], dg[:, h * 9 + a * 3 + j, :], XI[:, a, j:j + 128],
                                   start=(j == 0), stop=(j == 2))
            return ins
        s.op("pe", conv, reads=[XI.key, dg.key], writes=[Pc.key])
        yield
        SL = b.rot("gwide%d" % slot, wide)
        s.op("act", lambda e, SL=SL, Pc=Pc: e.activation(out=SL[:], in_=Pc[:, 0:384], func=AF.Silu),
             reads=[Pc.key], writes=[SL.key])
        yield
        s.op("dve", lambda e, SL=SL, QV=QV: e.tensor_tensor(
            out=SL[:].rearrange("p (a t) -> p a t", a=3), in0=SL[:].rearrange("p (a t) -> p a t", a=3),
            in1=QV[:].unsqueeze(1).to_broadcast([128, 3, 128]), op=ALU.mult),
            reads=[SL.key, QV.key], writes=[SL.key])
        yield
        chk(2)
        SQ = b.rot("gwide%d" % slot, wide)
        s.op("act", lambda e, SL=SL, SQ=SQ: e.activation(out=SQ[:, 0:256], in_=SL[:, 0:256], func=AF.Square),
             reads=[SL.key], writes=[SQ.key])
        yield
        Pn = PS()
        mm1(b, Pn, 256, b.onesf[:], SQ[:, 0:256], [b.onesf.key, SQ.key])
        yield
        s.op("act", lambda e, SQ=SQ, Pn=Pn: e.activation(out=SQ[:, 0:256], in_=Pn[:, 0:256], func=AF.Sqrt, bias=EPS),
             reads=[Pn.key], writes=[SQ.key])
        yield
        s.op("dve", lambda e, SQ=SQ: e.reciprocal(out=SQ[:, 0:256], in_=SQ[:, 0:256]), reads=[SQ.key], writes=[SQ.key])
        yield
        QK = b.rot("gwide%d" % slot, wide)
        s.op("dve", lambda e, QK=QK, SL=SL, SQ=SQ: e.scalar_tensor_tensor(
            out=QK[:, 0:128], in0=SL[:, 0:128], scalar=128 ** -0.5, in1=SQ[:, 0:128], op0=ALU.mult, op1=ALU.mult),
            reads=[SL.key, SQ.key], writes=[QK.key])
        yield
        s.op("dve", lambda e, QK=QK, SL=SL, SQ=SQ: e.tensor_tensor(
            out=QK[:, 128:256], in0=SL[:, 128:256], in1=SQ[:, 128:256], op=ALU.mult),
            reads=[SL.key, SQ.key, QK.key], writes=[QK.key])
        yield
        qT, kT = QK[:, 0:128], QK[:, 128:256]
        chk(3)
        GB_ = T_()
        s.op("dve", lambda e, GB_=GB_, gcolv=gcolv: e.tensor_scalar(out=GB_[:], in0=b.onesf[:], scalar1=gcolv,
                                                                  scalar2=None, op0=ALU.mult),
             reads=[b.onesf.key, GS.key], writes=[GB_.key])
        yield
        Pg = PS()

        def cums(e, Pg=Pg, GB_=GB_, Tri=Tri, BM=BM, gcolv=gcolv):
            e.matmul(Pg[:, 0:128], GB_[:], Tri[:], start=True, stop=False)
            e.matmul(Pg[:, 0:128], b.identf[:], BM[:], start=False, stop=True)
            e.matmul(Pg[:, 128:129], Tri[:], gcolv, start=True, stop=True)
            return e.matmul(Pg[:, 160:161], GB_[:], b.onesf[:, 0:1], start=True, stop=True)
        s.op("pe", cums, reads=[GB_.key, Tri.key, BM.key, b.identf.key, GS.key, b.onesf.key], writes=[Pg.key])
        yield
        CL = C_()
        s.op("dve", lambda e, CL=CL, Pg=Pg: e.tensor_copy(out=CL[:, 0:1], in_=Pg[:, 128:129]),
             reads=[Pg.key], writes=[CL.key])
        yield
        s.op("dve", lambda e, CL=CL, Pg=Pg: e.tensor_copy(out=CL[:, 1:2], in_=Pg[:, 160:161]),
             reads=[Pg.key, CL.key], writes=[CL.key])
        yield
        E = T_()
        s.op("act", lambda e, E=E, Pg=Pg, CL=CL: e.activation(out=E[:], in_=Pg[:, 0:128], func=AF.Exp, scale=-1.0,
                                                             bias=CL[:, 0:1]), reads=[Pg.key, CL.key], writes=[E.key])
        yield
        s.op("act", lambda e, CL=CL: e.activation(out=CL[:, 2:4], in_=CL[:, 0:2], func=AF.Exp),
             reads=[CL.key], writes=[CL.key])
        yield
        s.op("act", lambda e, CL=CL: e.activation(out=CL[:, 4:5], in_=CL[:, 0:1], func=AF.Exp, scale=-1.0,
                                                 bias=CL[:, 1:2]), reads=[CL.key], writes=[CL.key])
        yield
        s.op("dve", lambda e, CL=CL, beta=beta: e.tensor_tensor(out=CL[:, 5:6], in0=CL[:, 2:3], in1=beta, op=ALU.mult),
             reads=[CL.key, GS.key], writes=[CL.key])
        yield
        chk(4)
        Pt = PS()

        def trkv(e, Pt=Pt, kT=kT, SL=SL):
            e.transpose(Pt[:, 0:128], kT, b.identf[:])
            return e.transpose(Pt[:, 128:256], SL[:, 256:384], b.identf[:])
        s.op("pe", trkv, reads=[QK.key, SL.key, b.identf.key], writes=[Pt.key])
        yield
        chk(41)
        KBG, KDEC, VB = T_(), T_(), T_()
        s.op("dve", lambda e, KBG=KBG, Pt=Pt, CL=CL: e.tensor_scalar(out=KBG[:], in0=Pt[:, 0:128], scalar1=CL[:, 5:6],
                                                                   scalar2=None, op0=ALU.mult),
             reads=[Pt.key, CL.key], writes=[KBG.key])
        yield
        chk(42)
        s.op("act", lambda e, KDEC=KDEC, Pt=Pt, CL=CL: e.activation(out=KDEC[:], in_=Pt[:, 0:128], func=AF.Identity,
                                                                  scale=CL[:, 4:5]),
             reads=[Pt.key, CL.key], writes=[KDEC.key])
        yield
        chk(43)
        s.op("dve", lambda e, VB=VB, Pt=Pt, beta=beta: e.tensor_scalar(out=VB[:], in0=Pt[:, 128:256], scalar1=beta,
                                                                     scalar2=None, op0=ALU.mult),
             reads=[Pt.key, GS.key], writes=[VB.key])
        yield
        chk(5)
        Pk = PS()

        def gqk(e, Pk=Pk, kT=kT, qT=qT):
            e.matmul(Pk[:, 0:128], kT, kT, start=True, stop=True)
            return e.matmul(Pk[:, 128:256], kT, qT, start=True, stop=True)
        s.op("pe", gqk, reads=[QK.key], writes=[Pk.key])
        yield
        ES = T_()
        s.op("dve", lambda e, ES=ES, E=E, SM=SM: e.tensor_tensor(out=ES[:], in0=E[:], in1=SM[:], op=ALU.mult),
             reads=[E.key, SM.key], writes=[ES.key])
        yield
        M = T_()
        GG = T_()
        s.op("act", lambda e, GG=GG, Pk=Pk, nbeta=nbeta: e.activation(out=GG[:], in_=Pk[:, 0:128], func=AF.Identity,
                                                                   scale=nbeta),
             reads=[Pk.key, GS.key], writes=[GG.key])
        yield
        s.op("dve", lambda e, M=M, GG=GG, ES=ES: e.tensor_tensor(out=M[:], in0=GG[:], in1=ES[:], op=ALU.mult),
             reads=[GG.key, ES.key], writes=[M.key])
        yield
        chk(51)
        Pe = PS()

        def trne(e, Pe=Pe, M=M, E=E):
            e.transpose(Pe[:, 0:128], M[:], b.identf[:])
            return e.transpose(Pe[:, 128:256], E[:], b.identf[:])
        s.op("pe", trne, reads=[M.key, E.key, b.identf.key], writes=[Pe.key])
        yield
        MT = T_()
        b.copy("act", MT[:], Pe[:, 0:128], [Pe.key], [MT.key])
        yield
        PP = T_()
        s.op("dve", lambda e, PP=PP, Pe=Pe: e.tensor_tensor(out=PP[:], in0=Pe[:, 0:128], in1=b.identf[:], op=ALU.add),
             reads=[Pe.key, b.identf.key], writes=[PP.key])
        yield
        AT = T_()
        s.op("dve", lambda e, AT=AT, Pe=Pe, Pk=Pk: e.tensor_copy(out=AT[:], in_=Pe[:, 128:256]),
             reads=[Pe.key], writes=[AT.key])
        yield
        s.op("dve", lambda e, AT=AT, Pk=Pk: e.tensor_tensor(out=AT[:], in0=Pk[:, 128:256], in1=AT[:], op=ALU.mult),
             reads=[Pk.key, AT.key], writes=[AT.key])
        yield
        chk(52)
        for k in range(1, 7):
            chk(52 + k)
            Pm = PS()

            def sq(e, Pm=Pm, M=M, MT=MT, k=k):
                ins = e.matmul(Pm[:, 0:128], MT[:], M[:], start=True, stop=True)
                if k < 6:
                    ins = e.matmul(Pm[:, 128:256], M[:], MT[:], start=True, stop=True)
                return ins
            s.op("pe", sq, reads=[M.key, MT.key], writes=[Pm.key])
            M2 = T_()
            b.copy("act", M2[:], Pm[:, 0:128], [Pm.key], [M2.key])
            if k < 6:
                MT2 = T_()
                b.copy("dve", MT2[:], Pm[:, 128:256], [Pm.key], [MT2.key])
            Pp = PS()
            mm1(b, Pp, 128, M2[:], PP[:], [M2.key, PP.key])
            PP2 = T_()
            s.op("dve", lambda e, PP2=PP2, Pp=Pp, PP=PP: e.tensor_tensor(out=PP2[:], in0=Pp[:, 0:128], in1=PP[:],
                                                                      op=ALU.add),
                 reads=[Pp.key, PP.key], writes=[PP2.key])
            PP = PP2
            M = M2
            if k < 6:
                MT = MT2
        chk(6)
        Pw = PS()

        def wu(e, Pw=Pw, KBG=KBG, PP=PP, VB=VB):
            e.matmul(Pw[:, 0:128], KBG[:], PP[:], start=True, stop=True)
            return e.matmul(Pw[:, 128:256], PP[:], VB[:], start=True, stop=True)
        s.op("pe", wu, reads=[KBG.key, PP.key, VB.key], writes=[Pw.key])
        yield
        WT, UU = T_(), T_()
        b.copy("act", WT[:], Pw[:, 0:128], [Pw.key], [WT.key])
        yield
        b.copy("dve", UU[:], Pw[:, 128:256], [Pw.key], [UU.key])
        yield
        chk(7)
        Sc = S[(h, dd, step % 2)]
        Sn = S[(h, dd, (step + 1) % 2)]
        Pr = PS()

        def r1(e, Pr=Pr, WT=WT, Sc=Sc, qT=qT):
            e.matmul(Pr[:, 0:128], WT[:], Sc[:], start=True, stop=True)
            return e.matmul(Pr[:, 128:256], qT, Sc[:], start=True, stop=True)
        s.op("pe", r1, reads=[WT.key, Sc.key, QK.key], writes=[Pr.key])
        yield
        VN = T_()
        s.op("dve", lambda e, VN=VN, UU=UU, Pr=Pr: e.tensor_tensor(out=VN[:], in0=UU[:], in1=Pr[:, 0:128],
                                                                op=ALU.subtract),
             reads=[UU.key, Pr.key], writes=[VN.key])
        yield
        OT = T_()
        s.op("act", lambda e, OT=OT, Pr=Pr, CL=CL: e.activation(out=OT[:], in_=Pr[:, 128:256], func=AF.Identity,
                                                               scale=CL[:, 2:3]),
             reads=[Pr.key, CL.key], writes=[OT.key])
        yield
        Po = PS()

        def r2(e, Po=Po, AT=AT, VN=VN, KDEC=KDEC):
            e.matmul(Po[:, 0:128], AT[:], VN[:], start=True, stop=True)
            return e.matmul(Po[:, 128:256], KDEC[:], VN[:], start=True, stop=True)
        s.op("pe", r2, reads=[AT.key, VN.key, KDEC.key], writes=[Po.key])
        yield
        OO = T_()
        s.op("dve", lambda e, OO=OO, OT=OT, Po=Po: e.tensor_tensor(out=OO[:], in0=OT[:], in1=Po[:, 0:128], op=ALU.add),
             reads=[OT.key, Po.key], writes=[OO.key])
        yield
        SS = T_()
        s.op("act", lambda e, SS=SS, Sc=Sc, CL=CL: e.activation(out=SS[:], in_=Sc[:], func=AF.Identity, scale=CL[:, 3:4]),
             reads=[Sc.key, CL.key], writes=[SS.key])
        yield
        s.op("dve", lambda e, Sn=Sn, SS=SS, Po=Po: e.tensor_tensor(out=Sn[:], in0=SS[:], in1=Po[:, 128:256], op=ALU.add),
             reads=[SS.key, Po.key], writes=[Sn.key])
        yield
        b.store("sp", d["OA%d" % dd][t0:t0 + 128, h * 128:(h + 1) * 128], OO[:], OO.key, [OO.key], ["OA%d" % dd])
        yield
    run_interleaved(b, units_q, group_pre, gdn_unit, 3)


def phase_mlstm(b, stack):
    cfg, s, d = b.cfg, b.s, b.dram
    T, MH = cfg.T, cfg.MH
    NCH = T // 128
    tmps = [b.gtmp] + [[b.sb(stack, "mt%d_%d" % (j, i), [128, 128], F32) for i in range(12)] for j in (1, 2)]
    cols = [[b.sb(stack, "mc%d_%d" % (j, i), [128, 16], F32) for i in range(4)] for j in range(3)]
    gt = [b.sb(stack, "mgt%d" % i, [128, 4 * MH], F32) for i in range(4)]
    gs = [b.sb(stack, "mgs%d" % i, [128, 4 * MH], F32) for i in range(4)]
    kvt = [b.sb(stack, "mkv%d" % i, [128, MH * 384], BF16) for i in range(3)]
    qkts = [[b.sb(stack, "mqk%d_%d" % (j, i), [128, 2, 128], BF16) for i in range(2)] for j in range(3)]
    w257s = [[b.sb(stack, "mw%d_%d" % (j, i), [128, 264], F32) for i in range(8)] for j in range(3)]
    Cst = {(h, dd, p): b.sb(stack, "C%d_%d_%d" % (h, dd, p), [128, 264], F32) for h in range(MH) for dd in range(2)
           for p in range(2)}
    Mst = {(h, dd, p): b.sb(stack, "M%d_%d_%d" % (h, dd, p), [128, 2], F32) for h in range(MH) for dd in range(2)
           for p in range(2)}
    mlb = b.sb(stack, "mlb", [128, 4 * MH], F32)
    BMf, BMb = b.BMf, b.BMb
    s.dma("sp", lambda e: e.dma_start(out=mlb[:], in_=d["ml_gate_bias"].rearrange("a k d h -> a (k d h)").partition_broadcast(128)),
          mlb.key, writes=[mlb.key])
    for h in range(MH):
        for dd in range(2):
            s.op("pool", lambda e, h=h, dd=dd: e.memset(Cst[(h, dd, 0)][:], 0.0), writes=[Cst[(h, dd, 0)].key])
            s.op("pool", lambda e, h=h, dd=dd: e.memset(Mst[(h, dd, 0)][:], 0.0), writes=[Mst[(h, dd, 0)].key])
    qb3 = d["QB_T"].rearrange("(h p) t -> p h t", h=MH)
    kb3 = d["KB_T"].rearrange("(h p) t -> p h t", h=MH)
    units_q = [(step, dd) for step in range(NCH) for dd in range(2)]

    def group_pre(step, dd):
        if True:
            c = step if dd == 0 else NCH - 1 - step
            t0 = c * 128
            Tri = b.U if dd == 0 else b.L
            BM = BMf if dd == 0 else BMb
            GT = b.rot("mgt", gt)
            GS = b.rot("mgs", gs)
            b.load("sp", GT[:], d["GB"][t0:t0 + 128, :], GT.key, [], [GT.key], dr=["GB"])
            s.op("dve", lambda e, GT=GT: e.tensor_tensor(out=GT[:], in0=GT[:], in1=mlb[:], op=ALU.add),
                 reads=[GT.key, mlb.key], writes=[GT.key])
            s.op("dve", lambda e, GT=GT, GS=GS, dd=dd: e.tensor_copy(out=GS[:, 0:MH], in_=GT[:, dd * MH:(dd + 1) * MH]),
                 reads=[GT.key], writes=[GS.key])
            s.op("dve", lambda e, GT=GT, GS=GS, dd=dd: e.tensor_scalar(out=GS[:, MH:2 * MH], in0=GT[:, dd * MH:(dd + 1) * MH],
                                                                    scalar1=-1.0, scalar2=None, op0=ALU.mult),
                 reads=[GT.key, GS.key], writes=[GS.key])
            s.op("act", lambda e, GT=GT, GS=GS, dd=dd: e.activation(out=GS[:, 2 * MH:3 * MH],
                                                                 in_=GT[:, (2 + dd) * MH:(3 + dd) * MH], func=AF.Exp, scale=-1.0),
                 reads=[GT.key, GS.key], writes=[GS.key])
            s.op("act", lambda e, GS=GS: e.activation(out=GS[:, 2 * MH:3 * MH], in_=GS[:, 2 * MH:3 * MH], func=AF.Ln, bias=1.0),
                 reads=[GS.key], writes=[GS.key])
            s.op("dve", lambda e, GS=GS: e.tensor_scalar(out=GS[:, 2 * MH:3 * MH], in0=GS[:, 2 * MH:3 * MH], scalar1=-1.0,
                                                        scalar2=None, op0=ALU.mult), reads=[GS.key], writes=[GS.key])
            KV = b.rot("mkv", kvt)
            b.load("sp", KV[:], d["KVB"][t0:t0 + 128, :], KV.key, [], [KV.key], dr=["KVB"])
            return [(h, step, dd, c, t0, Tri, BM, None, GT, GS, None, KV) for h in range(MH)]
    def ml_unit(slot, h, step, dd, c, t0, Tri, BM, SM, GT, GS, QV, KV):
        T_ = lambda: b.rot("gtmp%d" % slot, tmps[slot])
        C_ = lambda: b.rot("mcol%d" % slot, cols[slot])
        PS = lambda: b.rot("Ps%d" % slot, b.P[2 * slot:2 * slot + 2])
        W_ = lambda: b.rot("mw%d" % slot, w257s[slot])
        qkt = qkts[slot]
        igc = GS[:, h:h + 1]
        nigc = GS[:, MH + h:MH + h + 1]
        lfc = GS[:, 2 * MH + h:2 * MH + h + 1]
        Mc, Mn = Mst[(h, dd, step % 2)], Mst[(h, dd, (step + 1) % 2)]
        Cc, Cn = Cst[(h, dd, step % 2)], Cst[(h, dd, (step + 1) % 2)]
        QKb = b.rot("mqk%d" % slot, qkt)
        b.load("sp", QKb[:, 0, :], qb3[:, h, t0:t0 + 128], QKb.key, [], [QKb.key], dr=["QB_T"])
        yield
        b.load("sp", QKb[:, 1, :], kb3[:, h, t0:t0 + 128], QKb.key, [], [QKb.key], dr=["KB_T"])
        yield
        QK = W_()
        s.op("act", lambda e, QK=QK, QKb=QKb: e.activation(out=QK[:, 0:256],
                                                          in_=QKb[:].rearrange("p a t -> p (a t)"), func=AF.Copy),
             reads=[QKb.key], writes=[QK.key])
        yield
        qT, kT = QK[:, 0:128], QK[:, 128:256]
        VP = W_()
        s.op("dve", lambda e, VP=VP, KV=KV, h=h: e.tensor_copy(out=VP[:, 0:256],
                                                            in_=KV[:, MH * 128 + h * 256:MH * 128 + (h + 1) * 256]),
             reads=[KV.key], writes=[VP.key])
        yield
        s.op("dve", lambda e, VP=VP: e.memset(VP[:, 256:257], 1.0), reads=[VP.key], writes=[VP.key])
        yield
        LFB, NIB = T_(), T_()
        s.op("dve", lambda e, LFB=LFB, lfc=lfc: e.tensor_scalar(out=LFB[:], in0=b.onesf[:], scalar1=lfc, scalar2=None,
                                                             op0=ALU.mult), reads=[b.onesf.key, GS.key], writes=[LFB.key])
        yield
        s.op("dve", lambda e, NIB=NIB, nigc=nigc: e.tensor_scalar(out=NIB[:], in0=b.onesf[:], scalar1=nigc, scalar2=None,
                                                               op0=ALU.mult), reads=[b.onesf.key, GS.key], writes=[NIB.key])
        yield
        Pg = PS()

        def cums(e, Pg=Pg, LFB=LFB, NIB=NIB, Tri=Tri, BM=BM, lfc=lfc):
            e.matmul(Pg[:, 0:128], LFB[:], Tri[:], start=True, stop=False)
            e.matmul(Pg[:, 0:128], NIB[:], b.identf[:], start=False, stop=False)
            e.matmul(Pg[:, 0:128], b.identf[:], BM[:], start=False, stop=True)
            e.matmul(Pg[:, 128:256], LFB[:], Tri[:], start=True, stop=False)
            e.matmul(Pg[:, 128:256], NIB[:], b.identf[:], start=False, stop=True)
            e.matmul(Pg[:, 256:257], Tri[:], lfc, start=True, stop=True)
            return e.matmul(Pg[:, 288:289], LFB[:], b.onesf[:, 0:1], start=True, stop=True)
        s.op("pe", cums, reads=[LFB.key, NIB.key, Tri.key, BM.key, b.identf.key, GS.key, b.onesf.key], writes=[Pg.key])
        yield
        CL = C_()
        s.op("dve", lambda e, CL=CL, Pg=Pg: e.tensor_copy(out=CL[:, 0:1], in_=Pg[:, 256:257]), reads=[Pg.key], writes=[CL.key])
        yield
        s.op("dve", lambda e, CL=CL, Pg=Pg: e.tensor_copy(out=CL[:, 1:2], in_=Pg[:, 288:289]),
             reads=[Pg.key, CL.key], writes=[CL.key])
        yield
        s.op("dve", lambda e, CL=CL, Pg=Pg: e.tensor_reduce(out=CL[:, 2:3], in_=Pg[:, 0:128], axis=AX.X, op=ALU.min),
             reads=[Pg.key, CL.key], writes=[CL.key])
        yield
        s.op("dve", lambda e, CL=CL, Pg=Pg: e.tensor_reduce(out=CL[:, 3:4], in_=Pg[:, 128:256], axis=AX.X, op=ALU.min),
             reads=[Pg.key, CL.key], writes=[CL.key])
        yield
        s.op("dve", lambda e, CL=CL, Mc=Mc: e.scalar_tensor_tensor(out=CL[:, 4:5], in0=CL[:, 2:3], scalar=-1.0, in1=Mc[:, 0:1],
                                                                op0=ALU.mult, op1=ALU.max),
             reads=[CL.key, Mc.key], writes=[CL.key])
        yield
        s.op("dve", lambda e, CL=CL, Mc=Mc: e.scalar_tensor_tensor(out=CL[:, 9:10], in0=CL[:, 3:4], scalar=-1.0, in1=Mc[:, 0:1],
                                                                op0=ALU.mult, op1=ALU.max),
             reads=[CL.key, Mc.key], writes=[CL.key])
        yield
        s.op("dve", lambda e, CL=CL: e.tensor_scalar(out=CL[:, 5:6], in0=CL[:, 4:5], scalar1=-1.0, scalar2=None, op0=ALU.mult),
             reads=[CL.key], writes=[CL.key])
        yield
        s.op("dve", lambda e, CL=CL: e.tensor_scalar(out=CL[:, 10:11], in0=CL[:, 9:10], scalar1=-1.0, scalar2=None, op0=ALU.mult),
             reads=[CL.key], writes=[CL.key])
        yield
        s.op("dve", lambda e, CL=CL: e.tensor_tensor(out=CL[:, 7:8], in0=CL[:, 0:1], in1=CL[:, 4:5], op=ALU.add),
             reads=[CL.key], writes=[CL.key])
        yield
        s.op("dve", lambda e, CL=CL, igc=igc: e.tensor_tensor(out=CL[:, 12:13], in0=CL[:, 0:1], in1=igc, op=ALU.subtract),
             reads=[CL.key, GS.key], writes=[CL.key])
        yield
        s.op("dve", lambda e, CL=CL, Mn=Mn: e.tensor_tensor(out=Mn[:, 0:1], in0=CL[:, 1:2], in1=CL[:, 9:10], op=ALU.add),
             reads=[CL.key], writes=[Mn.key])
        yield
        EW = T_()
        s.op("act", lambda e, EW=EW, Pg=Pg, CL=CL: e.activation(out=EW[:], in_=Pg[:, 0:128], func=AF.Exp, scale=-1.0,
                                                               bias=CL[:, 5:6]), reads=[Pg.key, CL.key], writes=[EW.key])
        yield
        s.op("act", lambda e, CL=CL, Mc=Mc: e.activation(out=CL[:, 6:7], in_=CL[:, 4:5], func=AF.Exp, scale=-1.0,
                                                        bias=Mc[:, 0:1]), reads=[CL.key, Mc.key], writes=[CL.key])
        yield
        s.op("act", lambda e, CL=CL: e.activation(out=CL[:, 8:9], in_=CL[:, 7:8], func=AF.Exp, scale=-1.0),
             reads=[CL.key], writes=[CL.key])
        yield
        s.op("act", lambda e, CL=CL, Mc=Mc: e.activation(out=CL[:, 11:12], in_=CL[:, 9:10], func=AF.Exp, scale=-1.0,
                                                        bias=Mc[:, 0:1]), reads=[CL.key, Mc.key], writes=[CL.key])
        yield
        s.op("act", lambda e, CL=CL: e.activation(out=CL[:, 13:14], in_=CL[:, 12:13], func=AF.Exp, scale=-1.0,
                                                 bias=CL[:, 10:11]), reads=[CL.key], writes=[CL.key])
        yield
        Pq = PS()
        mm1(b, Pq, 128, qT, kT, [QK.key])
        yield
        WI = T_()
        s.op("dve", lambda e, WI=WI, EW=EW, Pq=Pq: e.tensor_tensor(out=WI[:], in0=Pq[:, 0:128], in1=EW[:], op=ALU.mult),
             reads=[Pq.key, EW.key], writes=[WI.key])
        yield
        Pt = PS()
        s.op("pe", lambda e, Pt=Pt, WI=WI: e.transpose(Pt[:, 0:128], WI[:], b.identf[:]),
             reads=[WI.key, b.identf.key], writes=[Pt.key])
        yield
        WIT = T_()
        b.copy("act", WIT[:], Pt[:, 0:128], [Pt.key], [WIT.key])
        yield
        WK = T_()
        s.op("dve", lambda e, WK=WK, KV=KV, CL=CL, h=h: e.tensor_scalar(out=WK[:], in0=KV[:, h * 128:(h + 1) * 128],
                                                                     scalar1=CL[:, 13:14], scalar2=None, op0=ALU.mult),
             reads=[KV.key, CL.key], writes=[WK.key])
        yield
        Pa = PS()
        mm1(b, Pa, 257, qT, Cc[:, 0:257], [QK.key, Cc.key])
        yield
        T1 = W_()
        s.op("act", lambda e, T1=T1, Pa=Pa, CL=CL: e.activation(out=T1[:, 0:257], in_=Pa[:, 0:257], func=AF.Identity,
                                                               scale=CL[:, 6:7]), reads=[Pa.key, CL.key], writes=[T1.key])
        yield
        Pb = PS()
        mm1(b, Pb, 257, WIT[:], VP[:, 0:257], [WIT.key, VP.key])
        yield
        ND = W_()
        s.op("dve", lambda e, ND=ND, T1=T1, Pb=Pb: e.tensor_tensor(out=ND[:, 0:257], in0=Pb[:, 0:257], in1=T1[:, 0:257],
                                                                op=ALU.add), reads=[Pb.key, T1.key], writes=[ND.key])
        yield
        CD = C_()
        s.op("dve", lambda e, CD=CD, ND=ND: e.scalar_tensor_tensor(out=CD[:, 0:1], in0=ND[:, 256:257], scalar=-1.0,
                                                                  in1=ND[:, 256:257], op0=ALU.mult, op1=ALU.max),
             reads=[ND.key], writes=[CD.key])
        yield
        s.op("dve", lambda e, CD=CD, CL=CL: e.tensor_tensor(out=CD[:, 1:2], in0=CD[:, 0:1], in1=CL[:, 8:9], op=ALU.max),
             reads=[CD.key, CL.key], writes=[CD.key])
        yield
        s.op("dve", lambda e, CD=CD: e.reciprocal(out=CD[:, 2:3], in_=CD[:, 1:2]), reads=[CD.key], writes=[CD.key])
        yield
        HO = W_()
        s.op("dve", lambda e, HO=HO, ND=ND, CD=CD: e.tensor_scalar(out=HO[:, 0:256], in0=ND[:, 0:256], scalar1=CD[:, 2:3],
                                                                scalar2=None, op0=ALU.mult),
             reads=[ND.key, CD.key], writes=[HO.key])
        yield
        b.store("sp", d["HB%d" % dd][t0:t0 + 128, h * 256:(h + 1) * 256], HO[:, 0:256], HO.key, [HO.key], ["HB%d" % dd])
        yield
        Pc = PS()
        mm1(b, Pc, 257, WK[:], VP[:, 0:257], [WK.key, VP.key])
        yield
        T2 = W_()
        s.op("act", lambda e, T2=T2, Cc=Cc, CL=CL: e.activation(out=T2[:, 0:257], in_=Cc[:, 0:257], func=AF.Identity,
                                                               scale=CL[:, 11:12]), reads=[Cc.key, CL.key], writes=[T2.key])
        yield
        s.op("dve", lambda e, Cn=Cn, T2=T2, Pc=Pc: e.tensor_tensor(out=Cn[:, 0:257], in0=Pc[:, 0:257], in1=T2[:, 0:257],
                                                                op=ALU.add), reads=[Pc.key, T2.key], writes=[Cn.key])
        yield
    run_interleaved(b, units_q, group_pre, ml_unit, 3)


def transpose_to_fm(b, XBt, dst, dstkey, ncols):
    s = b.s
    NCk = ncols // 128
    for cg in range(0, NCk, 8):
        n = min(8, NCk - cg)
        PT = b.rot("PT", b.PT)

        def tr(e, cg=cg, n=n, PT=PT):
            for c in range(n):
                ins = e.transpose(PT[:, c, :], XBt[:, (cg + c) * 128:(cg + c + 1) * 128], b.identb[:])
            return ins
        s.op("pe", tr, reads=[XBt.key, b.identb.key], writes=[PT.key])
        b.copy(b.evac_eng(), dst[:, cg:cg + n, :], PT[:, 0:n, :], [PT.key], [dstkey])


def outproj_residual(b, hv, hkey, ntok, w, resid, dst, tok0, final_store=False):
    s, d = b.s, b.dram
    D = b.cfg.D

    def ev(P, c0, nb, tt, nt):
        R = b.rot("SF", b.SF)
        b.load("sp", R[:, 0:nb], d[resid][tok0 + tt:tok0 + tt + 128, c0:c0 + nb], R.key, [], [R.key], dr=[resid])
        O = b.rot("SF", b.SF)
        s.op("dve", lambda e: e.tensor_tensor(out=O[:, 0:nb], in0=P[:, 0:nb], in1=R[:, 0:nb], op=ALU.add),
             reads=[P.key, R.key], writes=[O.key])
        b.store("sp", d[dst][tok0 + tt:tok0 + tt + 128, c0:c0 + nb], O[:, 0:nb], O.key, [O.key], [dst], final=final_store)
    linear(b, "tm", hv, hkey, ntok, w, 0, D, ev)


def phase_mixout0(b, stack):
    b.H = [b.sb(stack, "H_%d" % b.pid, [128, 24576], BF16)]
    cfg, s, d = b.cfg, b.s, b.dram
    T, D, DC, GH, MH = cfg.T, cfg.D, cfg.DC, cfg.GH, cfg.MH
    WA, WB = GH * 128, MH * 256
    TBM = 512
    gfull = b.sb(stack, "gfull", [128, D], F32)
    for h in range(GH):
        s.dma("sp", lambda e, h=h: e.dma_start(out=gfull[:, h * 128:(h + 1) * 128], in_=d["gdn_norm"][0:1, :].partition_broadcast(128)),
              gfull.key, writes=[gfull.key])
    s.dma("sp", lambda e: e.dma_start(out=gfull[:, WA:D], in_=d["ml_norm"][0:1, :].partition_broadcast(128)),
          gfull.key, writes=[gfull.key])
    rsb = [b.sb(stack, "rsb%d" % i, [128, 2 * (GH + MH)], F32) for i in range(2)]
    for tb in range(T // TBM):
        H = b.H[0]
        hv = H[:, 0:DC * TBM].rearrange("p (c t) -> p c t", c=DC)
        for tt in range(TBM // 128):
            t0 = tb * TBM + tt * 128
            XA, XC = b.X[0], b.X[1]
            ZB, MB = b.XB[0], b.XB[1]
            RS = b.rot("rsb", rsb)
            b.load("sp", XA[:, 0:WA], d["OA0"][t0:t0 + 128, :], XA.key, [], [XA.key], dr=["OA0"])
            b.load("sp", XA[:, WA:D], d["HB0"][t0:t0 + 128, :], XA.key, [], [XA.key], dr=["HB0"])
            b.load("sp", XC[:, 0:WA], d["OA1"][t0:t0 + 128, :], XC.key, [], [XC.key], dr=["OA1"])
            b.load("sp", XC[:, WA:D], d["HB1"][t0:t0 + 128, :], XC.key, [], [XC.key], dr=["HB1"])
            b.load("sp", ZB[:, 0:WA], d["Z"][t0:t0 + 128, :], ZB.key, [], [ZB.key], dr=["Z"])
            b.load("sp", ZB[:, WA:D], d["OB"][t0:t0 + 128, :], ZB.key, [], [ZB.key], dr=["OB"])
            s.op("dve", lambda e, XA=XA, XC=XC: e.tensor_tensor(out=XA[:, 0:D], in0=XA[:, 0:D], in1=XC[:, 0:D], op=ALU.add),
                 reads=[XA.key, XC.key], writes=[XA.key])
            s.op("act", lambda e, XA=XA, XC=XC: e.activation(out=XC[:, 0:D], in_=XA[:, 0:D], func=AF.Square),
                 reads=[XA.key], writes=[XC.key])
            s.op("dve", lambda e, XC=XC, RS=RS: e.tensor_reduce(out=RS[:, 0:GH], in_=XC[:, 0:WA].rearrange("p (h k) -> p h k", h=GH),
                                                              axis=AX.X, op=ALU.add), reads=[XC.key], writes=[RS.key])
            s.op("dve", lambda e, XC=XC, RS=RS: e.tensor_reduce(out=RS[:, GH:GH + MH],
                                                              in_=XC[:, WA:D].rearrange("p (h k) -> p h k", h=MH),
                                                              axis=AX.X, op=ALU.add), reads=[XC.key, RS.key], writes=[RS.key])
            s.op("act", lambda e, RS=RS: e.activation(out=RS[:, 0:GH], in_=RS[:, 0:GH], func=AF.Sqrt, scale=1.0 / 128, bias=EPS),
                 reads=[RS.key], writes=[RS.key])
            s.op("act", lambda e, RS=RS: e.activation(out=RS[:, GH:GH + MH], in_=RS[:, GH:GH + MH], func=AF.Sqrt, scale=1.0 / 256,
                                                     bias=EPS), reads=[RS.key], writes=[RS.key])
            s.op("dve", lambda e, RS=RS: e.reciprocal(out=RS[:, GH + MH:2 * (GH + MH)], in_=RS[:, 0:GH + MH]),
                 reads=[RS.key], writes=[RS.key])
            s.op("dve", lambda e, XA=XA, RS=RS: e.tensor_tensor(
                out=XA[:, 0:WA].rearrange("p (h k) -> p h k", h=GH), in0=XA[:, 0:WA].rearrange("p (h k) -> p h k", h=GH),
                in1=RS[:, GH + MH:2 * GH + MH].unsqueeze(2).to_broadcast([128, GH, 128]), op=ALU.mult),
                reads=[XA.key, RS.key], writes=[XA.key])
            s.op("dve", lambda e, XA=XA, RS=RS: e.tensor_tensor(
                out=XA[:, WA:D].rearrange("p (h k) -> p h k", h=MH), in0=XA[:, WA:D].rearrange("p (h k) -> p h k", h=MH),
                in1=RS[:, 2 * GH + MH:2 * (GH + MH)].unsqueeze(2).to_broadcast([128, MH, 256]), op=ALU.mult),
                reads=[XA.key, RS.key], writes=[XA.key])
            s.op("dve", lambda e, XA=XA: e.tensor_tensor(out=XA[:, 0:D], in0=XA[:, 0:D], in1=gfull[:], op=ALU.mult),
                 reads=[XA.key, gfull.key], writes=[XA.key])
            s.op("act", lambda e, XC=XC, ZB=ZB: e.activation(out=XC[:, 0:WA], in_=ZB[:, 0:WA], func=AF.Silu),
                 reads=[ZB.key], writes=[XC.key])
            s.op("act", lambda e, XC=XC, ZB=ZB: e.activation(out=XC[:, WA:D], in_=ZB[:, WA:D], func=AF.Sigmoid),
                 reads=[ZB.key, XC.key], writes=[XC.key])
            s.op("dve", lambda e, XA=XA, XC=XC, MB=MB: e.tensor_tensor(out=MB[:, 0:D], in0=XA[:, 0:D], in1=XC[:, 0:D], op=ALU.mult),
                 reads=[XA.key, XC.key], writes=[MB.key])
            transpose_to_fm(b, MB, hv[:, :, tt * 128:(tt + 1) * 128], H.key, D)
        outproj_residual(b, hv, H.key, TBM, d["b_ev_w_out"][0], "x", "XL1", tb * TBM)


def phase_ffn(layer, src, dst):
    def ph(b, stack):
        b.H = [b.sb(stack, "H_%d" % b.pid, [128, 24576], BF16)]
        cfg, s, d = b.cfg, b.s, b.dram
        T, D, DC, DFF = cfg.T, cfg.D, cfg.DC, cfg.DFF
        FC = DFF // 128
        TBF = 512
        pre = "f%d_" % b.pid
        cwt = b.sb(stack, pre + "cw", [128, FC, 4], F32)
        for j in range(3):
            s.dma("sp", lambda e, j=j: e.dma_start(out=cwt[:, :, j:j + 1],
                                                   in_=d["ffn_conv"][layer, j].rearrange("(c p o) -> p c o", p=128, o=1),
                                                   allow_slow_non_contiguous=True), cwt.key, writes=[cwt.key])
        s.dma("sp", lambda e: e.dma_start(out=cwt[:, :, 3:4], in_=d["ffn_conv_b"][layer].rearrange("(c p o) -> p c o", p=128, o=1),
                                          allow_slow_non_contiguous=True), cwt.key, writes=[cwt.key])
        hn2 = b.sb(stack, pre + "hn", [128, DC, TBF + 2], BF16)
        actT = b.H[0]
        av = actT[:, 0:FC * TBF].rearrange("p (c t) -> p c t", c=FC)
        gbuf = [b.sb(stack, pre + "g%d" % i, [128, TBF + 2], F32) for i in range(2)]
        cbuf = [b.sb(stack, pre + "c%d" % i, [128, TBF], F32) for i in range(2)]
        halo = b.sb(stack, pre + "halo", [128, D], F32)
        w_up, w_dn = d["b_ffn_w_up"][layer], d["b_ffn_w_down"][layer]
        wupv = w_up.rearrange("(c p) n -> p c n", p=128)
        gsrc = b.gffn[:, layer * DC:(layer + 1) * DC]
        for tb in range(T // TBF):
            tok0 = tb * TBF
            for tt in range(TBF // 128):
                t0 = tok0 + tt * 128
                rmsnorm_T(b, d[src][t0:t0 + 128, :], gsrc, b.gffn.key, hn2[:, :, 1 + tt * 128:1 + (tt + 1) * 128], hn2.key,
                          src_reads=[], dr=[src])
            s.op("pool", lambda e: e.memset(halo[:], 0.0), writes=[halo.key])
            if tok0 > 0:
                b.load("sp", halo[0:1, :], d[src][tok0 - 1:tok0, :], halo.key, [], [halo.key], dr=[src])
            if tok0 + TBF < T:
                b.load("sp", halo[1:2, :], d[src][tok0 + TBF:tok0 + TBF + 1, :], halo.key, [], [halo.key], dr=[src])
            rmsnorm_T(b, None, gsrc, b.gffn.key, None, hn2.key, preloaded=halo,
                      halo_dst=(hn2[:, :, 0:1], hn2[:, :, TBF + 1:TBF + 2]))
            for fb in range(0, DFF, 512):
                nb = min(512, DFF - fb)
                Wg = b.rot("W", b.W)
                Wgv = Wg[:, 0:DC * 512].rearrange("p (c n) -> p c n", c=DC)
                b.load("sp", Wgv[:, :, 0:nb], wupv[:, :, fb:fb + nb], Wg.key, [], [Wg.key], dr=["b_ffn_w_up"])
                Wv = b.rot("W", b.W)
                Wvv = Wv[:, 0:DC * 512].rearrange("p (c n) -> p c n", c=DC)
                b.load("sp", Wvv[:, :, 0:nb], wupv[:, :, DFF + fb:DFF + fb + nb], Wv.key, [], [Wv.key], dr=["b_ffn_w_up"])
                for ft in range(0, nb, 128):
                    f = (fb + ft) // 128
                    G = b.rot(pre + "g", gbuf)
                    half = (TBF + 2) // 2
                    for (c0, c1) in ((0, half), (half, TBF + 2)):
                        P = b.rot("P", b.P)

                        def mm(e, P=P, c0=c0, c1=c1, ft=ft, Wgv=Wgv):
                            for c in range(DC):
                                ins = e.matmul(P[:, 0:c1 - c0], Wgv[:, c, ft:ft + 128], hn2[:, c, c0:c1], start=(c == 0),
                                               stop=(c == DC - 1))
                            return ins
                        s.op("pe", mm, reads=[Wg.key, hn2.key], writes=[P.key])
                        b.copy(b.evac_eng(), G[:, c0:c1], P[:, 0:c1 - c0], [P.key], [G.key])
                    Cv = b.rot(pre + "c", cbuf)
                    s.op("dve", lambda e, Cv=Cv, G=G, f=f: e.tensor_scalar(out=Cv[:], in0=G[:, 0:TBF], scalar1=cwt[:, f, 0:1],
                                                                        scalar2=None, op0=ALU.mult),
                         reads=[G.key, cwt.key], writes=[Cv.key])
                    s.op("dve", lambda e, Cv=Cv, G=G, f=f: e.scalar_tensor_tensor(out=Cv[:], in0=G[:, 1:TBF + 1], scalar=cwt[:, f, 1:2],
                                                                               in1=Cv[:], op0=ALU.mult, op1=ALU.add),
                         reads=[G.key, cwt.key, Cv.key], writes=[Cv.key])
                    s.op("dve", lambda e, Cv=Cv, G=G, f=f: e.scalar_tensor_tensor(out=Cv[:], in0=G[:, 2:TBF + 2], scalar=cwt[:, f, 2:3],
                                                                               in1=Cv[:], op0=ALU.mult, op1=ALU.add),
                         reads=[G.key, cwt.key, Cv.key], writes=[Cv.key])
                    s.op("act", lambda e, Cv=Cv, f=f: e.activation(out=Cv[:], in_=Cv[:], func=AF.Silu, bias=cwt[:, f, 3:4]),
                         reads=[Cv.key, cwt.key], writes=[Cv.key])
                    P = b.rot("P", b.P)

                    def mv(e, P=P, ft=ft, Wvv=Wvv):
                        for c in range(DC):
                            ins = e.matmul(P[:, 0:TBF], Wvv[:, c, ft:ft + 128], hn2[:, c, 1:TBF + 1], start=(c == 0), stop=(c == DC - 1))
                        return ins
                    s.op("pe", mv, reads=[Wv.key, hn2.key], writes=[P.key])
                    s.op("dve", lambda e, P=P, Cv=Cv, f=f: e.tensor_tensor(out=av[:, f, :], in0=P[:, 0:TBF], in1=Cv[:], op=ALU.mult),
                         reads=[P.key, Cv.key], writes=[actT.key])
            outproj_residual(b, av, actT.key, TBF, w_dn, src, dst, tok0)
    return ph


PADK = 1024
TWO_PI = 6.283185307179586


def rope_consts():
    inv = (500000.0 ** (-np.arange(0, 32, 2, dtype=np.float32) / 32)).astype(np.float32)
    c = np.zeros((32, 2), np.float32)
    c[:, 0] = np.concatenate([inv, inv])
    c[0:16, 1] = -TWO_PI
    c[16:32, 1] = TWO_PI
    return c


def phase_qkv1(b, stack):
    b.H = [b.sb(stack, "H_%d" % b.pid, [128, 24576], BF16)]
    cfg, s, d = b.cfg, b.s, b.dram
    T, TB, D, DC, S_ = cfg.T, cfg.TBLK, cfg.D, cfg.DC, cfg.SLOTS
    NQ = 3 * S_ * 128
    w = d["b_od_w_in"][0]
    ropec = b.sb(stack, "ropec_sb", [32, 4], F32)
    s.dma("sp", lambda e: e.dma_start(out=ropec[:, 0:2], in_=d["ropec"][:, :]), ropec.key, writes=[ropec.key])
    permT = b.sb(stack, "permT", [128, 32], BF16)
    s.op("pool", lambda e: e.memset(permT[:], 0.0), writes=[permT.key])
    s.op("pool", lambda e: e.affine_select(out=permT[:, 0:16], in_=permT[:, 0:16], pattern=[[-1, 16]], compare_op=ALU.not_equal,
                                          fill=1.0, base=-16, channel_multiplier=1), reads=[permT.key], writes=[permT.key])
    s.op("pool", lambda e: e.affine_select(out=permT[:, 16:32], in_=permT[:, 16:32], pattern=[[-1, 16]], compare_op=ALU.not_equal,
                                          fill=1.0, base=0, channel_multiplier=1), reads=[permT.key], writes=[permT.key])
    ang = [b.sb(stack, "ang%d" % i, [32, 512], F32) for i in range(2)]
    cst = [b.sb(stack, "cst%d" % i, [32, 2, 512], F32) for i in range(2)]
    cki = [b.sb(stack, "cki%d" % i, [32, 2, 512], mybir.dt.int32) for i in range(2)]
    ckf = [b.sb(stack, "ckf%d" % i, [32, 2, 512], F32) for i in range(2)]
    qa = [b.sb(stack, "qa%d" % i, [128, 512], BF16) for i in range(3)]
    qr = [b.sb(stack, "qr%d" % i, [128, 512], BF16) for i in range(3)]
    rt = [b.sb(stack, "rt%d" % i, [32, 512], F32) for i in range(3)]
    for r in range(0, NQ, 128):
        for (c0) in (0, PADK + T):
            for cc in range(0, PADK, 512):
                b.store("sp", d["KT1"][r:r + 128, c0 + cc:c0 + cc + 512], b.zerob[:, 0:512], "zpad1", [b.zerob.key], ["KT1"])
    for (r0) in (0, PADK + T):
        for rr in range(0, PADK, 128):
            for cc in range(0, NQ, 512):
                b.store("sp", d["VT1"][r0 + rr:r0 + rr + 128, cc:cc + 512], b.zerob[:, 0:512], "zpad1", [b.zerob.key], ["VT1"])
    for tb in range(T // TB):
        H = b.H[0]
        hv = H[:, 0:DC * TB].rearrange("p (c t) -> p c t", c=DC)
        tok0 = tb * TB
        for tt in range(TB // 128):
            t0 = tok0 + tt * 128
            rmsnorm_T(b, d["XL2"][t0:t0 + 128, :], b.gmix[:, DC:2 * DC], b.gmix.key, hv[:, :, tt * 128:(tt + 1) * 128], H.key,
                      dr=["XL2"])
        tabs = {}
        for ts in range(0, TB, 512):
            A = b.rot("ang", ang)
            CS = b.rot("cst", cst)
            b.load("sp", A[:], d["pos"][tok0 + ts:tok0 + ts + 512].partition_broadcast(32), A.key, [], [A.key])
            s.op("dve", lambda e, A=A: e.tensor_scalar(out=A[:], in0=A[:], scalar1=ropec[:, 0:1], scalar2=None, op0=ALU.mult),
                 reads=[A.key, ropec.key], writes=[A.key])
            KI = b.rot("cki", cki)
            KF = b.rot("ckf", ckf)
            s.op("dve", lambda e, A=A, CS=CS: e.tensor_scalar(out=CS[:, 0, :], in0=A[:], scalar1=1.0 / TWO_PI, scalar2=0.25,
                                                            op0=ALU.mult, op1=ALU.add), reads=[A.key], writes=[CS.key])
            s.op("dve", lambda e, A=A, CS=CS: e.tensor_scalar(out=CS[:, 1, :], in0=A[:], scalar1=1.0 / TWO_PI, scalar2=None,
                                                            op0=ALU.mult), reads=[A.key, CS.key], writes=[CS.key])
            s.op("dve", lambda e, CS=CS, KI=KI: e.tensor_copy(out=KI[:], in_=CS[:]), reads=[CS.key], writes=[KI.key])
            s.op("dve", lambda e, KF=KF, KI=KI: e.tensor_copy(out=KF[:], in_=KI[:]), reads=[KI.key], writes=[KF.key])
            s.op("dve", lambda e, CS=CS, KF=KF: e.tensor_tensor(out=CS[:], in0=CS[:], in1=KF[:], op=ALU.subtract),
                 reads=[CS.key, KF.key], writes=[CS.key])
            s.op("act", lambda e, CS=CS: e.activation(out=CS[:, 0, :], in_=CS[:, 0, :], func=AF.Sin, scale=TWO_PI),
                 reads=[CS.key], writes=[CS.key])
            s.op("act", lambda e, CS=CS: e.activation(out=CS[:, 1, :], in_=CS[:, 1, :], func=AF.Sin, scale=ropec[:, 1:2]),
                 reads=[CS.key, ropec.key], writes=[CS.key])
            tabs[ts] = CS

        def ev_rope(dstname, row_base, col_off):
            def f(P, c0, ncl, ts, nt):
                CS = tabs[ts]
                A_ = b.rot("qa", qa)
                R_ = b.rot("qr", qr)
                TT = b.rot("rt", rt)
                b.copy("act", A_[:, 0:nt], P[:, 0:nt], [P.key], [A_.key])
                b.copy("dve", R_[:, 0:nt], P[:, 0:nt], [P.key], [R_.key])
                P2 = b.rot("P", b.P)
                s.op("pe", lambda e: e.matmul(P2[0:32, 0:nt], permT[:, 0:32], A_[:, 0:nt], start=True, stop=True),
                     reads=[permT.key, A_.key], writes=[P2.key])
                s.op("dve", lambda e: e.tensor_tensor(out=TT[:, 0:nt], in0=P2[0:32, 0:nt], in1=CS[:, 1, 0:nt], op=ALU.mult),
                     reads=[P2.key, CS.key], writes=[TT.key])
                s.op("dve", lambda e: e.tensor_tensor(out=R_[0:32, 0:nt], in0=A_[0:32, 0:nt], in1=CS[:, 0, 0:nt], op=ALU.mult),
                     reads=[A_.key, CS.key, R_.key], writes=[R_.key])
                s.op("dve", lambda e: e.tensor_tensor(out=R_[0:32, 0:nt], in0=R_[0:32, 0:nt], in1=TT[:, 0:nt], op=ALU.add),
                     reads=[R_.key, TT.key], writes=[R_.key])
                b.store("sp", d[dstname][c0 - row_base:c0 - row_base + ncl, col_off + tok0 + ts:col_off + tok0 + ts + nt],
                        R_[0:ncl, 0:nt], R_.key, [R_.key], [dstname])
            return f

        def ev_v(P, c0, nb, tt, nt):
            S = b.rot("SBF", b.SBF)
            b.copy(b.evac_eng(), S[:, 0:nb], P[:, 0:nb], [P.key], [S.key])
            b.store("sp", d["VT1"][PADK + tok0 + tt:PADK + tok0 + tt + 128, c0 - 2 * NQ:c0 - 2 * NQ + nb], S[:, 0:nb], S.key,
                    [S.key], ["VT1"])
        linear(b, "fm", hv, H.key, TB, w, 0, NQ, ev_rope("QT1", 0, 0))
        linear(b, "fm", hv, H.key, TB, w, NQ, NQ, ev_rope("KT1", NQ, PADK))
        linear(b, "tm", hv, H.key, TB, w, 2 * NQ, NQ, ev_v)


def phase_attn(b, stack):
    cfg, s, d = b.cfg, b.s, b.dram
    T, S_ = cfg.T, cfg.SLOTS
    DILS = (1, 4, 16)
    SBLK = 2048
    NSB = T // SBLK
    sc = 128 ** -0.5
    qb_ = [b.sb(stack, "aq%d" % i, [128, SBLK], BF16) for i in range(2)]
    kb_ = [b.sb(stack, "ak%d" % i, [128, SBLK + 2048], BF16) for i in range(1)]
    kmb = [b.sb(stack, "akmb%d" % i, [1, SBLK + 2048], BF16) for i in range(1)]
    vt = [b.sb(stack, "av%d" % i, [128, 2, 128], BF16) for i in range(6)]
    pt = [b.sb(stack, "ap%d" % i, [128, 256], BF16) for i in range(4)]
    osum = [b.sb(stack, "aos%d" % i, [128, SBLK], F32) for i in range(1)]
    dsum = [b.sb(stack, "ads%d" % i, [128, SBLK], F32) for i in range(1)]
    mo = [b.sb(stack, "amo%d" % i, [128, SBLK], BF16) for i in range(2)]
    qv = b.sb(stack, "aqv", [128, SBLK], F32)
    mab = b.sb(stack, "amab", [128, 256], BF16)
    s.op("dve", lambda e: e.tensor_copy(out=mab[:, 0:128], in_=b.L[:]), reads=[b.L.key], writes=[mab.key])
    s.op("dve", lambda e: e.tensor_copy(out=mab[:, 128:256], in_=b.U[:]), reads=[b.U.key, mab.key], writes=[mab.key])
    for slot in range(S_):
        for sb_ in range(NSB):
            tokS = sb_ * SBLK
            OS = b.rot("aos", osum)
            DS = b.rot("ads", dsum)
            units = []
            for g in (2, 1, 0):
                dil = DILS[g]
                blk = 128 * dil
                for bi in range(SBLK // blk):
                    for r in range(dil):
                        units.append((g, bi, r))
            state = {}

            def stage1(u):
                g, bi, r = u
                dil = DILS[g]
                blk = 128 * dil
                row0 = (g * S_ + slot) * 128
                halo = 64 * dil
                nk = SBLK + 2 * halo
                k0 = PADK + tokS - halo
                if (bi, r) == (0, 0):
                    Q = b.rot("aq", qb_)
                    Kt = b.rot("ak", kb_)
                    KMB = b.rot("akmb", kmb)
                    b.load("sp", Q[:], d["QT1"][row0:row0 + 128, tokS:tokS + SBLK], Q.key, [], [Q.key], dr=["QT1"])
                    b.load("sp", Kt[:, 0:nk], d["KT1"][row0:row0 + 128, k0:k0 + nk], Kt.key, [], [Kt.key], dr=["KT1"])
                    b.load("sp", KMB[0:1, 0:nk], d["kmask"][k0:k0 + nk].rearrange("(o n) -> o n", o=1), KMB.key, [], [KMB.key])
                    state["qk"] = (Q, Kt, KMB)
                Q, Kt, KMB = state["qk"]
                qsl = slice(bi * blk + r, bi * blk + r + 127 * dil + 1, dil)
                kA = slice(bi * blk + r, bi * blk + r + 127 * dil + 1, dil)
                kB = slice(bi * blk + blk + r, bi * blk + blk + r + 127 * dil + 1, dil)
                V = b.rot("av", vt)
                tA = k0 + bi * blk + r
                vrowsA = d["VT1"][tA:tA + 127 * dil + 1:dil, row0:row0 + 128]
                vrowsB = d["VT1"][tA + blk:tA + blk + 127 * dil + 1:dil, row0:row0 + 128]
                b.load("sp", V[:, 0, :], vrowsA, V.key, [], [V.key], dr=["VT1"])
                b.load("sp", V[:, 1, :], vrowsB, V.key, [], [V.key], dr=["VT1"])
                Ps = b.rot("Pa", b.P[0:3])

                def sc_mm(e):
                    e.matmul(Ps[:, 0:128], Kt[:, kA], Q[:, qsl], start=True, stop=False)
                    e.matmul(Ps[:, 0:128], KMB[0:1, kA], b.onesb[0:1, 0:128], start=False, stop=True)
                    e.matmul(Ps[:, 128:256], Kt[:, kB], Q[:, qsl], start=True, stop=False)
                    return e.matmul(Ps[:, 128:256], KMB[0:1, kB], b.onesb[0:1, 0:128], start=False, stop=True)
                s.op("pe", sc_mm, reads=[Kt.key, Q.key, KMB.key, b.onesb.key], writes=[Ps.key])
                PT_ = b.rot("ap", pt)
                s.op("act", lambda e: e.activation(out=PT_[:], in_=Ps[:, 0:256], func=AF.Exp, scale=sc),
                     reads=[Ps.key], writes=[PT_.key])
                s.op("dve", lambda e: e.tensor_tensor(out=PT_[:], in0=PT_[:], in1=mab[:], op=ALU.mult),
                     reads=[PT_.key, mab.key], writes=[PT_.key])
                return (V, PT_, qsl, g)

            def stage2(st):
                V, PT_, qsl, g = st
                Po = b.rot("Pb", b.P[3:6])

                def pv_mm(e):
                    e.matmul(Po[:, 0:128], V[:, 0, :], PT_[:, 0:128], start=True, stop=False)
                    e.matmul(Po[:, 0:128], V[:, 1, :], PT_[:, 128:256], start=False, stop=True)
                    e.matmul(Po[:, 128:256], b.onesb[:], PT_[:, 0:128], start=True, stop=False)
                    return e.matmul(Po[:, 128:256], b.onesb[:], PT_[:, 128:256], start=False, stop=True)
                s.op("pe", pv_mm, reads=[V.key, PT_.key, b.onesb.key], writes=[Po.key])
                if g == 2:
                    s.op("act", lambda e: e.activation(out=OS[:, qsl], in_=Po[:, 0:128], func=AF.Copy),
                         reads=[Po.key], writes=[OS.key])
                    s.op("dve", lambda e: e.tensor_copy(out=DS[:, qsl], in_=Po[:, 128:256]), reads=[Po.key], writes=[DS.key])
                else:
                    s.op("dve", lambda e: e.tensor_tensor(out=OS[:, qsl], in0=Po[:, 0:128], in1=OS[:, qsl], op=ALU.add),
                         reads=[Po.key, OS.key], writes=[OS.key])
                    s.op("dve", lambda e: e.tensor_tensor(out=DS[:, qsl], in0=Po[:, 128:256], in1=DS[:, qsl], op=ALU.add),
                         reads=[Po.key, DS.key], writes=[DS.key])
            prev = None
            for u in units:
                cur = stage1(u)
                if prev is not None:
                    stage2(prev)
                prev = cur
            stage2(prev)
            MO = b.rot("amo", mo)
            b.load("sp", qv[:], d["qvalid"][tokS:tokS + SBLK].partition_broadcast(128), qv.key, [], [qv.key])
            s.op("dve", lambda e, DS=DS: e.tensor_scalar(out=DS[:], in0=DS[:], scalar1=1e-30, scalar2=None, op0=ALU.max),
                 reads=[DS.key], writes=[DS.key])
            s.op("dve", lambda e, DS=DS: e.reciprocal(out=DS[:], in_=DS[:]), reads=[DS.key], writes=[DS.key])
            s.op("dve", lambda e, DS=DS: e.tensor_tensor(out=DS[:], in0=DS[:], in1=qv[:], op=ALU.mult),
                 reads=[DS.key, qv.key], writes=[DS.key])
            s.op("dve", lambda e, MO=MO, OS=OS, DS=DS: e.tensor_tensor(out=MO[:], in0=OS[:], in1=DS[:], op=ALU.mult),
                 reads=[OS.key, DS.key], writes=[MO.key])
            b.store("sp", d["MIXT"][slot * 128:(slot + 1) * 128, tokS:tokS + SBLK], MO[:], MO.key, [MO.key], ["MIXT"])


def phase_mixout1(b, stack):
    b.H = [b.sb(stack, "H_%d" % b.pid, [128, 24576], BF16)]
    cfg, s, d = b.cfg, b.s, b.dram
    T, D, DC = cfg.T, cfg.D, cfg.DC
    TBM = 512
    mv = d["MIXT"].rearrange("(c p) t -> p c t", p=128)
    for tb in range(T // TBM):
        H = b.H[0]
        hv = H[:, 0:DC * TBM].rearrange("p (c t) -> p c t", c=DC)
        b.load("sp", hv, mv[:, :, tb * TBM:(tb + 1) * TBM], H.key, [], [H.key], dr=["MIXT"])
        outproj_residual(b, hv, H.key, TBM, d["b_od_w_out"][0], "XL2", "XL3", tb * TBM)


def phase_final(b, stack):
    cfg, s, d = b.cfg, b.s, b.dram
    T, D = cfg.T, cfg.D
    gf = b.sb(stack, "gfinb", [128, D], F32)
    s.dma("sp", lambda e: e.dma_start(out=gf[:], in_=d["norm_final"].rearrange("(o n) -> o n", o=1).partition_broadcast(128)),
          gf.key, writes=[gf.key])
    yo = [b.sb(stack, "yo%d" % i, [128, D], F32) for i in range(2)]
    for t0 in range(0, T, 128):
        X = b.rot("X", b.X)
        XB = b.rot("XB", b.XB)
        SC = b.rot("SC", b.SC)
        Y = b.rot("yo", yo)
        b.load("sp", X[:, 0:D], d["XL4"][t0:t0 + 128, :], X.key, [], [X.key], dr=["XL4"])
        s.op("act", lambda e, X=X, XB=XB, SC=SC: e.activation(out=XB[:, 0:D], in_=X[:, 0:D], func=AF.Square, accum_out=SC[:, 0:1]),
             reads=[X.key], writes=[XB.key, SC.key])
        s.op("act", lambda e, SC=SC: e.activation(out=SC[:, 1:2], in_=SC[:, 0:1], func=AF.Sqrt, scale=1.0 / D, bias=EPS),
             reads=[SC.key], writes=[SC.key])
        s.op("dve", lambda e, SC=SC: e.reciprocal(out=SC[:, 2:3], in_=SC[:, 1:2]), reads=[SC.key], writes=[SC.key])
        s.op("dve", lambda e, X=X, SC=SC, Y=Y: e.scalar_tensor_tensor(out=Y[:], in0=X[:, 0:D], scalar=SC[:, 2:3], in1=gf[:],
                                                                   op0=ALU.mult, op1=ALU.mult),
             reads=[X.key, SC.key, gf.key], writes=[Y.key])
        b.store("sp", d["y"][t0:t0 + 128, :], Y[:], Y.key, [Y.key], ["y"], final=True)


def all_phases():
    return [phase_wcast, phase_A, phase_gdn, phase_mlstm, phase_mixout0, phase_ffn(0, "XL1", "XL2"), phase_qkv1, phase_attn, phase_mixout1,
            phase_ffn(1, "XL3", "XL4"), phase_final]


N_CORES = 4


def kernel(**inputs):
    import ml_dtypes
    cfg = Cfg()
    T = cfg.T
    xp = np.asarray(inputs["x_prompt"], np.float32)
    xs = np.asarray(inputs["x_sample"], np.float32)
    shared = {k: np.ascontiguousarray(np.asarray(v, np.float32)) for k, v in inputs.items()
              if k not in ("x_prompt", "x_sample")}
    in_maps = []
    lens = []
    for c in range(N_CORES):
        seq = xp[c] if c < 2 else xs[c - 2]
        n = seq.shape[0]
        x = np.zeros((T, cfg.D), np.float32)
        x[:n] = seq
        km = np.full((T + 2048,), -BIG, np.float32)
        km[PADK:PADK + n] = 0.0
        m = dict(shared)
        m["x"] = x
        m["kmask"] = km.astype(ml_dtypes.bfloat16)
        m["pos"] = np.arange(T, dtype=np.float32)
        m["ropec"] = rope_consts()
        qv = np.zeros((T,), np.float32)
        qv[:n] = 1.0
        m["qvalid"] = qv
        in_maps.append(m)
        lens.append(n)
    b = build(cfg, all_phases())
    res = run_bass_kernel_spmd(b.nc, in_maps, core_ids=list(range(N_CORES)))
    ys = [np.asarray(res.results[c]["y"], np.float32)[:lens[c]] for c in range(N_CORES)]
    return (np.stack(ys[0:2], 0), np.stack(ys[2:4], 0))
```

```python
import numpy as np
import concourse.bass as bass
import concourse.mybir as mybir
from concourse.bass_utils import run_bass_kernel_spmd

F32 = mybir.dt.float32
BF16 = mybir.dt.bfloat16
AF = mybir.ActivationFunctionType
ALU = mybir.AluOpType
AX = mybir.AxisListType

EPS = 1e-6
BIG = 30000.0


class Cfg:
    def __init__(self, D=2048, GH=8, MH=4, SLOTS=16, DFF=5632, T=16384, TBLK=1024):
        self.D, self.GH, self.MH, self.SLOTS, self.DFF, self.T, self.TBLK = D, GH, MH, SLOTS, DFF, T, TBLK
        self.DC = D // 128
        self.A_QKV = GH * 384
        self.A_Z = GH * 128
        self.A_G = 4 * GH
        self.B_QKV = MH * 512
        self.B_O = MH * 256
        self.B_G = 4 * MH
        self.c1 = self.A_QKV
        self.c2 = self.c1 + self.A_Z
        self.c3 = self.c2 + self.A_G
        self.c4 = self.c3 + self.B_QKV
        self.c5 = self.c4 + self.B_O
        self.EVEN_IN = self.c5 + self.B_G
        self.EVEN_MIX = GH * 128 + MH * 256
        self.ODD_IN = 9 * SLOTS * 128
        self.ODD_MIX = SLOTS * 128
        assert self.EVEN_MIX == D and self.ODD_MIX == D


class Sched:
    ENG = ("pe", "act", "dve", "pool", "sp")

    def __init__(self, nc):
        self.nc = nc
        self.ops = {e: [] for e in self.ENG}
        self.last_w = {}
        self.reads = {}
        self.waited = {}
        self.dma_val = {}
        self.dma_last = {}
        self.final_events = []
        self.dwl = {}
        self.excl = set()
        self.sem_slot = {}
        self.sem_free = []
        self.n_slots = 0

    def _deps(self, eng, reads, writes):
        deps = []
        for k in reads:
            if k in self.last_w:
                deps.append(self.last_w[k])
        for k in writes:
            if k in self.last_w:
                deps.append(self.last_w[k])
            deps.extend(self.reads.get(k, ()))
        return deps

    def _add_waits(self, eng, deps):
        waits = []
        for ev in deps:
            kind, key, val = ev
            if kind == "eng" and key == eng and eng == "pe":
                continue
            wk = (eng, kind, key)
            if self.waited.get(wk, -1) >= val:
                continue
            self.waited[wk] = val
            waits.append(ev)
            if kind == "eng":
                self.ops[key][val]["inc"] = True
        return waits

    def _commit(self, ev, reads, writes):
        for k in writes:
            self.last_w[k] = ev
            self.reads[k] = []
        for k in reads:
            self.reads.setdefault(k, []).append(ev)

    def op(self, eng, fn, reads=(), writes=()):
        ex = [k for k in reads if k in self.excl]
        if ex:
            reads = [k for k in reads if k not in self.excl]
            writes = list(writes) + [k for k in ex if k not in writes]
        deps = self._deps(eng, reads, writes)
        waits = self._add_waits(eng, deps)
        idx = len(self.ops[eng])
        self.ops[eng].append(dict(waits=waits, fn=fn, inc=False, dma=None))
        ev = ("eng", eng, idx)
        self._commit(ev, reads, writes)
        return ev

    def dma(self, q, fn, semkey, reads=(), writes=(), final=False, dr=(), dw=()):
        deps = self._deps(q, reads, writes)
        if semkey not in self.sem_slot:
            if self.sem_free:
                self.sem_slot[semkey] = self.sem_free.pop(0)
            else:
                self.sem_slot[semkey] = self.n_slots
                self.n_slots += 1
        semkey = self.sem_slot[semkey]
        if semkey in self.dma_last:
            deps.append(self.dma_last[semkey])
        for k in dr:
            deps.extend(self.dwl.get(k, {}).values())
        for k in dw:
            deps.extend(self.reads.get(k, ()))
        waits = self._add_waits(q, deps)
        v = self.dma_val.get(semkey, 0) + 16
        self.dma_val[semkey] = v
        self.ops[q].append(dict(waits=waits, fn=fn, inc=False, dma=(semkey, v)))
        ev = ("dma", semkey, v)
        self.dma_last[semkey] = ev
        self._commit(ev, list(reads) + list(dr), writes)
        for k in dw:
            self.dwl.setdefault(k, {})[semkey] = ev
            self.reads[k] = []
        if final:
            self.final_events.append(ev)
        return ev

    def barrier(self):
        evs = []
        for e in self.ENG:
            if self.ops[e]:
                for idx in range(len(self.ops[e]) - 1, -1, -1):
                    o = self.ops[e][idx]
                    if o["fn"] is not None and o["dma"] is None:
                        evs.append(("eng", e, idx))
                        break
        evs.extend(self.dma_last.values())
        for f in self.ENG:
            waits = self._add_waits(f, evs)
            self.ops[f].append(dict(waits=waits, fn=None, inc=False, dma=None))
        self.sem_free.extend(sorted(set(self.sem_slot.values())))
        self.sem_slot = {}

    def finish(self):
        waits = self._add_waits("sp", self.final_events)
        self.ops["sp"].append(dict(waits=waits, fn=None, inc=False, dma=None))

    def emit(self, stack):
        nc = self.nc
        esem = {e: stack.enter_context(nc.semaphore("s_" + e)) for e in self.ENG}
        dsem = {}
        for k in range(self.n_slots):
            dsem[k] = stack.enter_context(nc.semaphore("d_%d" % k))
        cnt = {}
        for e in self.ENG:
            c = 0
            arr = []
            for o in self.ops[e]:
                if o["inc"]:
                    c += 1
                arr.append(c)
            cnt[e] = arr
        block = stack.enter_context(nc.Block())

        def run(e, engobj):
            for o in self.ops[e]:
                for (kind, key, val) in o["waits"]:
                    if kind == "eng":
                        engobj.wait_ge(esem[key], cnt[key][val])
                    else:
                        engobj.wait_ge(dsem[key], val)
                if o["fn"] is None:
                    continue
                ins = o["fn"](engobj)
                if o["dma"] is not None:
                    ins.then_inc(dsem[o["dma"][0]], 16)
                elif o["inc"]:
                    ins.then_inc(esem[e], 1)

        block.sync(lambda g: run("sp", g))
        block.scalar(lambda g: run("act", g))
        block.vector(lambda g: run("dve", g))
        block.gpsimd(lambda g: run("pool", g))
        block.tensor(lambda g: run("pe", g))
        n = {e: len(self.ops[e]) for e in self.ENG}
        return n, len(dsem)


class Buf:
    def __init__(self, t, key):
        self.t, self.key = t, key

    def __getitem__(self, k):
        return self.t[k]


class B:
    def __init__(self, cfg, debug_outs=()):
        self.cfg = cfg
        self.nc = bass.Bass("TRN2", target_bir_lowering=False)
        self.s = Sched(self.nc)
        self.debug_outs = set(debug_outs)
        self.rr = {}
        self.dram = {}

    def din(self, name, shape, dt=F32):
        self.dram[name] = self.nc.dram_tensor(name, list(shape), dt, kind="ExternalInput").ap()
        return self.dram[name]

    def dout(self, name, shape, dt=F32):
        self.dram[name] = self.nc.dram_tensor(name, list(shape), dt, kind="ExternalOutput").ap()
        return self.dram[name]

    def dscr(self, name, shape, dt):
        kind = "ExternalOutput" if name in self.debug_outs else "Internal"
        self.dram[name] = self.nc.dram_tensor(name, list(shape), dt, kind=kind).ap()
        return self.dram[name]

    def sb(self, stack, name, shape, dt):
        t = stack.enter_context(self.nc.sbuf_tensor(name, list(shape), dt))
        return Buf(t, name)

    def ps(self, stack, name, shape, dt):
        t = stack.enter_context(self.nc.psum_tensor(name, list(shape), dt))
        self.s.excl.add(name)
        return Buf(t, name)

    def rot(self, name, lst):
        i = self.rr.get(name, 0)
        self.rr[name] = i + 1
        return lst[i % len(lst)]

    def evac_eng(self):
        return self.rot("evac", ["act", "dve"])

    def copy(self, eng, out, in_, reads, writes, scale=None):
        if eng == "act":
            if scale is None:
                fn = lambda e: e.activation(out=out, in_=in_, func=AF.Copy)
            else:
                fn = lambda e: e.activation(out=out, in_=in_, func=AF.Identity, scale=scale)
        else:
            if scale is None:
                fn = lambda e: e.tensor_copy(out=out, in_=in_)
            else:
                fn = lambda e: e.tensor_scalar(out=out, in0=in_, scalar1=scale, scalar2=None, op0=ALU.mult)
        return self.s.op(eng, fn, reads=reads, writes=writes)

    def load(self, q, out, in_, semkey, reads, writes, dr=()):
        return self.s.dma(q, lambda e: e.dma_start(out=out, in_=in_, allow_slow_non_contiguous=True), semkey, reads=reads, writes=writes, dr=dr)

    def store(self, q, out, in_, semkey, reads, dw, final=False):
        q = "pool"
        return self.s.dma(q, lambda e: e.dma_start(out=out, in_=in_, allow_slow_non_contiguous=True), semkey, reads=reads, writes=(), dw=dw,
                          final=final)


def build_common(b, stack):
    cfg = b.cfg
    s = b.s
    b.identb = b.sb(stack, "identb", [128, 128], BF16)
    b.identf = b.sb(stack, "identf", [128, 128], F32)
    b.U = b.sb(stack, "U", [128, 128], F32)
    b.L = b.sb(stack, "L", [128, 128], F32)
    b.onesf = b.sb(stack, "onesf", [128, 128], F32)
    b.onesb = b.sb(stack, "onesb", [128, 128], BF16)
    b.zerob = b.sb(stack, "zerob", [128, 512], BF16)

    def mk_tri(buf, cmp_, dt_fill=1.0):
        pass

    def init_ident(buf):
        s.op("pool", lambda e: e.memset(buf[:], 0.0), writes=[buf.key])
        s.op("pool", lambda e: e.affine_select(out=buf[:], in_=buf[:], pattern=[[-1, 128]], compare_op=ALU.not_equal,
                                              fill=1.0, base=0, channel_multiplier=1), reads=[buf.key], writes=[buf.key])
    init_ident(b.identb)
    init_ident(b.identf)
    s.op("pool", lambda e: e.memset(b.onesf[:], 1.0), writes=[b.onesf.key])
    s.op("pool", lambda e: e.memset(b.onesb[:], 1.0), writes=[b.onesb.key])
    s.op("pool", lambda e: e.memset(b.zerob[:], 0.0), writes=[b.zerob.key])
    s.op("pool", lambda e: e.memset(b.U[:], 1.0), writes=[b.U.key])
    s.op("pool", lambda e: e.affine_select(out=b.U[:], in_=b.U[:], pattern=[[1, 128]], compare_op=ALU.is_ge,
                                          fill=0.0, base=0, channel_multiplier=-1), reads=[b.U.key], writes=[b.U.key])
    s.op("pool", lambda e: e.memset(b.L[:], 1.0), writes=[b.L.key])
    s.op("pool", lambda e: e.affine_select(out=b.L[:], in_=b.L[:], pattern=[[-1, 128]], compare_op=ALU.is_ge,
                                          fill=0.0, base=0, channel_multiplier=1), reads=[b.L.key], writes=[b.L.key])
    b.gtmp = [b.sb(stack, "g%d" % i, [128, 128], F32) for i in range(36)]
    b.BMf = b.sb(stack, "BMf", [128, 128], F32)
    b.BMb = b.sb(stack, "BMb", [128, 128], F32)
    b.SMf = b.sb(stack, "SMf", [128, 128], F32)
    b.SMb = b.sb(stack, "SMb", [128, 128], F32)
    SMf, SMb, BMf, BMb = b.SMf, b.SMb, b.BMf, b.BMb
    s.op("dve", lambda e: e.tensor_scalar(out=SMf[:], in0=b.U[:], scalar1=-1.0, scalar2=1.0, op0=ALU.mult, op1=ALU.add),
         reads=[b.U.key], writes=[SMf.key])
    s.op("dve", lambda e: e.tensor_scalar(out=SMb[:], in0=b.L[:], scalar1=-1.0, scalar2=1.0, op0=ALU.mult, op1=ALU.add),
         reads=[b.L.key], writes=[SMb.key])
    s.op("dve", lambda e: e.tensor_scalar(out=BMf[:], in0=SMb[:], scalar1=BIG, scalar2=None, op0=ALU.mult),
         reads=[SMb.key], writes=[BMf.key])
    s.op("dve", lambda e: e.tensor_scalar(out=BMb[:], in0=SMf[:], scalar1=BIG, scalar2=None, op0=ALU.mult),
         reads=[SMf.key], writes=[BMb.key])
    b.P = [b.ps(stack, "P%d" % i, [128, 512], F32) for i in range(8)]

    class _PTView:
        def __init__(self, buf):
            self.key = buf.key
            self.v = buf[:, :].bitcast(BF16).rearrange("p (c t) -> p c t", c=8)

        def __getitem__(self, k):
            return self.v[k]
    b.PT = [_PTView(b.P[6]), _PTView(b.P[7])]
    b.PD = b.P[0:6]


def dense_bufs(b, stack, need_h=True):
    p = "p%d_" % b.pid
    if need_h:
        b.H = [b.sb(stack, p + "H", [128, 24576], BF16)]
    b.W = [b.sb(stack, p + "W%d" % i, [128, 8192], BF16) for i in range(2)]
    b.X = [b.sb(stack, p + "X%d" % i, [128, 2048], F32) for i in range(2)]
    b.XB = [b.sb(stack, p + "XB%d" % i, [128, 2048], BF16) for i in range(2)]
    b.SF = [b.sb(stack, p + "SF%d" % i, [128, 512], F32) for i in range(4)]
    b.SBF = [b.sb(stack, p + "SBF%d" % i, [128, 512], BF16) for i in range(4)]
    b.SC = [b.sb(stack, p + "SC%d" % i, [128, 8], F32) for i in range(8)]


def rmsnorm_T(b, src, gT, gkey, dst, dstkey, src_reads=(), dr=(), preloaded=None, halo_dst=None):
    cfg, s = b.cfg, b.s
    D, DC = cfg.D, cfg.DC
    XB = b.rot("XB", b.XB)
    SC = b.rot("SC", b.SC)
    if preloaded is not None:
        X = preloaded
    else:
        X = b.rot("X", b.X)
        b.load("sp", X[:, 0:D], src, X.key, reads=src_reads, writes=[X.key], dr=dr)
    s.op("act", lambda e: e.activation(out=XB[:, 0:D], in_=X[:, 0:D], func=AF.Square, accum_out=SC[:, 0:1]),
         reads=[X.key], writes=[XB.key, SC.key])
    s.op("act", lambda e: e.activation(out=SC[:, 1:2], in_=SC[:, 0:1], func=AF.Sqrt, scale=1.0 / D, bias=EPS),
         reads=[SC.key], writes=[SC.key])
    s.op("dve", lambda e: e.reciprocal(out=SC[:, 2:3], in_=SC[:, 1:2]), reads=[SC.key], writes=[SC.key])
    s.op("act", lambda e: e.activation(out=XB[:, 0:D], in_=X[:, 0:D], func=AF.Identity, scale=SC[:, 2:3]),
         reads=[X.key, SC.key], writes=[XB.key])
    for cg in range(0, DC, 8):
        n = min(8, DC - cg)
        PT = b.rot("PT", b.PT)

        def tr(e, cg=cg, n=n, PT=PT):
            for c in range(n):
                ins = e.transpose(PT[:, c, :], XB[:, (cg + c) * 128:(cg + c + 1) * 128], b.identb[:])
            return ins
        s.op("pe", tr, reads=[XB.key, b.identb.key], writes=[PT.key])
        if halo_dst is not None:
            for hi, hd in enumerate(halo_dst):
                s.op("dve", lambda e, cg=cg, n=n, PT=PT, hi=hi, hd=hd: e.tensor_tensor(
                    out=hd[:, cg:cg + n, :], in0=PT[:, 0:n, hi:hi + 1],
                    in1=gT[:, cg:cg + n].unsqueeze(2), op=ALU.mult),
                    reads=[PT.key, gkey], writes=[dstkey])
            continue
        s.op("dve", lambda e, cg=cg, n=n, PT=PT: e.tensor_tensor(
            out=dst[:, cg:cg + n, :], in0=PT[:, 0:n, :],
            in1=gT[:, cg:cg + n].unsqueeze(2).to_broadcast([128, n, 128]), op=ALU.mult),
            reads=[PT.key, gkey], writes=[dstkey])


def linear(b, mode, hv, hkey, ntok, w, col0, ncols, evac):
    s = b.s
    KC = hv.shape[1]
    wv = w.rearrange("(c p) n -> p c n", p=128)
    CBW = 512 if KC <= 16 else 128
    for cb in range(0, ncols, CBW):
        nb = min(CBW, ncols - cb)
        W = b.rot("W", b.W)
        Wv = W[:, 0:KC * CBW].rearrange("p (c n) -> p c n", c=KC)
        b.load("sp", Wv[:, :, 0:nb], wv[:, :, col0 + cb:col0 + cb + nb], W.key, reads=[], writes=[W.key], dr=[w.tensor.name])
        if mode == "fm":
            for ct in range(0, nb, 128):
                ncl = min(128, nb - ct)
                for ts in range(0, ntok, 512):
                    nt = min(512, ntok - ts)
                    P = b.rot("P", b.PD)

                    def mm(e, P=P, ct=ct, ncl=ncl, ts=ts, nt=nt, Wv=Wv):
                        for c in range(KC):
                            ins = e.matmul(P[0:ncl, 0:nt], Wv[:, c, ct:ct + ncl], hv[:, c, ts:ts + nt],
                                           start=(c == 0), stop=(c == KC - 1))
                        return ins
                    s.op("pe", mm, reads=[W.key, hkey], writes=[P.key])
                    evac(P, col0 + cb + ct, ncl, ts, nt)
        else:
            for tt in range(0, ntok, 128):
                P = b.rot("P", b.PD)

                def mm(e, P=P, tt=tt, nb=nb, Wv=Wv):
                    for c in range(KC):
                        ins = e.matmul(P[:, 0:nb], hv[:, c, tt:tt + 128], Wv[:, c, 0:nb],
                                       start=(c == 0), stop=(c == KC - 1))
                    return ins
                s.op("pe", mm, reads=[W.key, hkey], writes=[P.key])
                evac(P, col0 + cb, nb, tt, 128)


def phase_A(b, stack):
    dense_bufs(b, stack)
    cfg, s = b.cfg, b.s
    T, TB, D, DC, GH, MH = cfg.T, cfg.TBLK, cfg.D, cfg.DC, cfg.GH, cfg.MH
    d = b.dram
    x, w = d["x"], d["b_ev_w_in"]
    for tb in range(T // TB):
        H = b.rot("H", b.H)
        hv = H[:, 0:DC * TB].rearrange("p (c t) -> p c t", c=DC)
        for tt in range(TB // 128):
            t0 = tb * TB + tt * 128
            rmsnorm_T(b, x[t0:t0 + 128, :], b.gmix[:, 0:DC], b.gmix.key, hv[:, :, tt * 128:(tt + 1) * 128], H.key)
        tok0 = tb * TB

        def ev_fm(dst, row0, scale=None):
            def f(P, c0, ncl, ts, nt):
                S = b.rot("SBF", b.SBF)
                b.copy(b.evac_eng(), S[0:ncl, 0:nt], P[0:ncl, 0:nt], [P.key], [S.key], scale=scale)
                b.store("sp", dst(c0 - row0, ncl, tok0 + ts, nt), S[0:ncl, 0:nt], S.key, [S.key], [dst.__name__])
            return f

        def ev_tm(dstname, col_base, dt):
            def f(P, c0, nb, tt, nt):
                S = b.rot("SBF", b.SBF) if dt == BF16 else b.rot("SF", b.SF)
                b.copy(b.evac_eng(), S[:, 0:nb], P[:, 0:nb], [P.key], [S.key])
                b.store("sp", d[dstname][tok0 + tt:tok0 + tt + 128, c0 - col_base:c0 - col_base + nb], S[:, 0:nb],
                        S.key, [S.key], [dstname])
            return f

        def QKVA_T(r, n, t, nt):
            return d["QKVA_T"][r:r + n, 1 + t:1 + t + nt]

        def QB_T(r, n, t, nt):
            return d["QB_T"][r:r + n, t:t + nt]

        def KB_T(r, n, t, nt):
            return d["KB_T"][r:r + n, t:t + nt]
        linear(b, "fm", hv, H.key, TB, w[0], 0, cfg.A_QKV, ev_fm(QKVA_T, 0))
        linear(b, "tm", hv, H.key, TB, w[0], cfg.c1, cfg.A_Z, ev_tm("Z", cfg.c1, BF16))
        linear(b, "tm", hv, H.key, TB, w[0], cfg.c2, cfg.A_G, ev_tm("GA", cfg.c2, F32))
        linear(b, "fm", hv, H.key, TB, w[0], cfg.c3, MH * 128, ev_fm(QB_T, cfg.c3, scale=128 ** -0.5))
        linear(b, "fm", hv, H.key, TB, w[0], cfg.c3 + MH * 128, MH * 128, ev_fm(KB_T, cfg.c3 + MH * 128))
        linear(b, "tm", hv, H.key, TB, w[0], cfg.c3 + MH * 128, MH * 384, ev_tm("KVB", cfg.c3 + MH * 128, BF16))
        linear(b, "tm", hv, H.key, TB, w[0], cfg.c4, cfg.B_O, ev_tm("OB", cfg.c4, BF16))
        linear(b, "tm", hv, H.key, TB, w[0], cfg.c5, cfg.B_G, ev_tm("GB", cfg.c5, F32))


def declare_io(b, stack):
    cfg = b.cfg
    T, D = cfg.T, cfg.D
    b.din("x", [T, D])
    b.din("norm_mix", [2, D])
    b.din("ev_w_in", [1, D, cfg.EVEN_IN])
    b.din("gdn_conv", [1, 3, cfg.A_QKV])
    b.din("gdn_a_log", [1, 2, cfg.GH])
    b.din("gdn_dt_bias", [1, 2, cfg.GH])
    b.din("gdn_norm", [1, 128])
    b.din("ml_gate_bias", [1, 2, 2, cfg.MH])
    b.din("ml_norm", [1, cfg.MH * 256])
    b.din("ev_w_out", [1, cfg.EVEN_MIX, D])
    b.din("od_w_in", [1, D, cfg.ODD_IN])
    b.din("od_w_out", [1, cfg.ODD_MIX, D])
    b.din("norm_ffn", [2, D])
    b.din("ffn_w_up", [2, D, 2 * cfg.DFF])
    b.din("ffn_conv", [2, 3, cfg.DFF])
    b.din("ffn_conv_b", [2, cfg.DFF])
    b.din("ffn_w_down", [2, cfg.DFF, D])
    b.din("norm_final", [D])
    for wn, shp in (("ev_w_in", [1, D, cfg.EVEN_IN]), ("ev_w_out", [1, cfg.EVEN_MIX, D]), ("od_w_in", [1, D, cfg.ODD_IN]),
                    ("od_w_out", [1, cfg.ODD_MIX, D]), ("ffn_w_up", [2, D, 2 * cfg.DFF]), ("ffn_w_down", [2, cfg.DFF, D])):
        b.dscr("b_" + wn, shp, BF16)
    b.din("kmask", [T + 2048], BF16)
    b.din("pos", [T])
    b.dout("y", [T, D])
    GH, MH = cfg.GH, cfg.MH
    b.dscr("QKVA_T", [cfg.A_QKV, T + 2], BF16)
    b.dscr("Z", [T, cfg.A_Z], BF16)
    b.dscr("GA", [T, cfg.A_G], F32)
    b.dscr("QB_T", [MH * 128, T], BF16)
    b.dscr("KB_T", [MH * 128, T], BF16)
    b.dscr("KVB", [T, MH * 384], BF16)
    b.dscr("OB", [T, cfg.B_O], BF16)
    b.dscr("GB", [T, cfg.B_G], F32)
    b.dscr("OA0", [T, GH * 128], F32)
    b.dscr("OA1", [T, GH * 128], F32)
    b.dscr("HB0", [T, MH * 256], F32)
    b.dscr("HB1", [T, MH * 256], F32)
    b.dscr("XL1", [T, D], F32)
    b.dscr("XL2", [T, D], F32)
    b.dscr("XL3", [T, D], F32)
    b.dscr("XL4", [T, D], F32)
    NQ = 3 * cfg.SLOTS * 128
    b.dscr("QT1", [NQ, T], BF16)
    b.dscr("KT1", [NQ, T + 2048], BF16)
    b.dscr("VT1", [T + 2048, NQ], BF16)
    b.dscr("MIXT", [D, T], BF16)
    b.din("ropec", [32, 2])
    b.din("qvalid", [T])
    DC = cfg.DC
    b.gmix = b.sb(stack, "gmix", [128, 2 * DC], F32)
    b.gffn = b.sb(stack, "gffn", [128, 2 * DC], F32)
    b.gfin = b.sb(stack, "gfin", [128, DC], F32)
    d = b.dram
    with b.nc.allow_non_contiguous_dma(reason="tiny param transposes"):
        pass
    for l in range(2):
        b.s.dma("sp", lambda e, l=l: e.dma_start(out=b.gmix[:, l * DC:(l + 1) * DC],
                                                 in_=d["norm_mix"][l].rearrange("(c p) -> p c", p=128),
                                                 allow_slow_non_contiguous=True), "gmix", writes=["gmix"])
        b.s.dma("sp", lambda e, l=l: e.dma_start(out=b.gffn[:, l * DC:(l + 1) * DC],
                                                 in_=d["norm_ffn"][l].rearrange("(c p) -> p c", p=128),
                                                 allow_slow_non_contiguous=True), "gffn", writes=["gffn"])
    b.s.dma("sp", lambda e: e.dma_start(out=b.gfin[:, 0:DC], in_=d["norm_final"].rearrange("(c p) -> p c", p=128),
                                        allow_slow_non_contiguous=True), "gfin", writes=["gfin"])


def phase_wcast(b, stack):
    d = b.dram
    for wn in ("ev_w_in", "ffn_w_up", "ffn_w_down", "ev_w_out", "od_w_in", "od_w_out"):
        src, dst = d[wn], d["b_" + wn]
        L, R, C = src.shape
        for l in range(L):
            for r0 in range(0, R, 512):
                r1 = min(R, r0 + 512)
                b.s.dma("pool", lambda e, l=l, r0=r0, r1=r1, src=src, dst=dst: e.dma_start(out=dst[l, r0:r1, :], in_=src[l, r0:r1, :]),
                        "wcast", dw=["b_" + wn])


def build(cfg, phases, debug_outs=()):
    from contextlib import ExitStack
    b = B(cfg, debug_outs)
    stack = ExitStack()
    with stack:
        declare_io(b, stack)
        build_common(b, stack)
        for ph in phases:
            with ExitStack() as pst:
                b.pid = getattr(b, "pid", 0) + 1
                ph(b, pst)
                b.s.barrier()
        b.s.finish()
        n, nd = b.s.emit(stack)
        print("ops", n, "dma sems", nd)
    return b


class PHalf:
    def __init__(self, buf, off):
        self.buf, self.off, self.key = buf, off, buf.key

    def __getitem__(self, k):
        rows, cols = k
        assert cols.step is None
        return self.buf[rows, self.off + cols.start:self.off + cols.stop]


def run_interleaved(b, groups, group_pre, unit_fn, width):
    pending = []
    gi = 0
    active = {}
    busy = {}
    free = list(range(width))
    while True:
        while free and (pending or gi < len(groups)):
            if not pending:
                pending = list(group_pre(*groups[gi]))
                gi += 1
            chain = (pending[0][0], pending[0][2])
            if chain in busy.values():
                break
            slot = free.pop(0)
            busy[slot] = chain
            active[slot] = unit_fn(slot, *pending.pop(0))
        if not active:
            break
        for slot in sorted(active):
            try:
                next(active[slot])
            except StopIteration:
                del active[slot]
                del busy[slot]
                free.append(slot)


def mm1(b, P, n, lhsT, rhs, reads, m=128):
    return b.s.op("pe", lambda e: e.matmul(P[0:m, 0:n], lhsT, rhs, start=True, stop=True), reads=reads, writes=[P.key])


class _Stop(Exception):
    pass


def chk(n):
    import os
    if os.environ.get("GDN_STOP", "") == str(n):
        raise _Stop()


def phase_gdn(b, stack):
    try:
        phase_gdn_(b, stack)
    except _Stop:
        print("GDN stopped early")


def phase_gdn_(b, stack):
    cfg, s, d = b.cfg, b.s, b.dram
    T, GH = cfg.T, cfg.GH
    NCH = T // 128
    NSL = 4
    tmps = [b.gtmp] + [[b.sb(stack, "g%d_%d" % (j, i), [128, 128], F32) for i in range(36)] for j in range(1, NSL)]
    wides = [[b.sb(stack, "gw%d_%d" % (j, i), [128, 384], F32) for i in range(3)] for j in range(NSL)]
    xins = [[b.sb(stack, "gx%d_%d" % (j, i), [128, 3, 130], BF16) for i in range(2)] for j in range(NSL)]
    gt = [b.sb(stack, "gt%d" % i, [128, 4 * GH], F32) for i in range(4)]
    gqv = [b.sb(stack, "gqv%d" % i, [128, 128], F32) for i in range(4)]
    gsm = [b.sb(stack, "gs%d" % i, [128, 6 * GH], F32) for i in range(4)]
    cols = [[b.sb(stack, "gc%d_%d" % (j, i), [128, 8], F32) for i in range(4)] for j in range(NSL)]
    oos = [[b.sb(stack, "go%d_%d" % (j, i), [128, 128], F32) for i in range(2)] for j in range(NSL)]
    S = {(h, dd, p): b.sb(stack, "S%d_%d_%d" % (h, dd, p), [128, 128], F32) for h in range(GH) for dd in range(2)
         for p in range(2)}
    cw = b.sb(stack, "gcw", [128, GH, 9], F32)
    dg = b.sb(stack, "gdg", [128, GH * 9, 128], BF16)
    ea = b.sb(stack, "gea", [128, 2, GH], F32)
    dtb = b.sb(stack, "gdtb", [128, 2, GH], F32)
    BMf, BMb, SMf, SMb = b.BMf, b.BMb, b.SMf, b.SMb
    for h in range(GH):
        for a in range(3):
            r0 = a * GH * 128 + h * 128
            s.dma("sp", lambda e, h=h, a=a, r0=r0: e.dma_start(
                out=cw[:, h, a * 3:(a + 1) * 3], in_=d["gdn_conv"][0][:, r0:r0 + 128].rearrange("j p -> p j"),
                allow_slow_non_contiguous=True), cw.key, writes=[cw.key])
    s.dma("sp", lambda e: e.dma_start(out=ea[:], in_=d["gdn_a_log"][0:1].partition_broadcast(128)), ea.key, writes=[ea.key])
    s.dma("sp", lambda e: e.dma_start(out=dtb[:], in_=d["gdn_dt_bias"][0:1].partition_broadcast(128)), dtb.key,
          writes=[dtb.key])
    s.op("act", lambda e: e.activation(out=ea[:], in_=ea[:], func=AF.Exp), reads=[ea.key], writes=[ea.key])
    for h in range(GH):
        for a in range(3):
            for j in range(3):
                s.op("dve", lambda e, h=h, a=a, j=j: e.tensor_scalar(
                    out=dg[:, h * 9 + a * 3 + j, :], in0=b.identb[:], scalar1=cw[:, h, a * 3 + j:a * 3 + j + 1],
                    scalar2=None, op0=ALU.mult), reads=[b.identb.key, cw.key], writes=[dg.key])
        for dd in range(2):
            s.op("pool", lambda e, h=h, dd=dd: e.memset(S[(h, dd, 0)][:], 0.0), writes=[S[(h, dd, 0)].key])
    for r in range(0, cfg.A_QKV, 128):
        b.store("sp", d["QKVA_T"][r:r + 128, 0:1], b.zerob[:, 0:1], "zpad", [b.zerob.key], ["QKVA_T"])
        b.store("sp", d["QKVA_T"][r:r + 128, T + 1:T + 2], b.zerob[:, 0:1], "zpad", [b.zerob.key], ["QKVA_T"])
    qkv3 = d["QKVA_T"].rearrange("(a h p) t -> p a h t", a=3, h=GH)
    units_q = [(step, dd) for step in range(NCH) for dd in range(2)]

    def group_pre(step, dd):
        if True:
            c = step if dd == 0 else NCH - 1 - step
            t0 = c * 128
            Tri = b.U if dd == 0 else b.L
            BM = BMf if dd == 0 else BMb
            SM = SMf if dd == 0 else SMb
            GT = b.rot("ggt", gt)
            GS = b.rot("ggs", gsm)
            b.load("sp", GT[:], d["GA"][t0:t0 + 128, :], GT.key, [], [GT.key], dr=["GA"])
            QV = b.rot("gqv", gqv)
            b.load("sp", QV[:], d["qvalid"][t0:t0 + 128].partition_broadcast(128), QV.key, [], [QV.key])
            s.op("dve", lambda e, GT=GT, GS=GS, dd=dd: e.tensor_tensor(
                out=GS[:, 0:GH], in0=GT[:, dd * GH:(dd + 1) * GH], in1=dtb[:, dd, :], op=ALU.add),
                reads=[GT.key, dtb.key], writes=[GS.key])
            s.op("act", lambda e, GS=GS: e.activation(out=GS[:, 0:GH], in_=GS[:, 0:GH], func=AF.Exp),
                 reads=[GS.key], writes=[GS.key])
            s.op("act", lambda e, GS=GS: e.activation(out=GS[:, 0:GH], in_=GS[:, 0:GH], func=AF.Ln, bias=1.0),
                 reads=[GS.key], writes=[GS.key])
            s.op("dve", lambda e, GS=GS, dd=dd: e.scalar_tensor_tensor(
                out=GS[:, GH:2 * GH], in0=GS[:, 0:GH], scalar=-1.0, in1=ea[:, dd, :], op0=ALU.mult, op1=ALU.mult),
                reads=[GS.key, ea.key], writes=[GS.key])
            s.op("act", lambda e, GS=GS, GT=GT, dd=dd: e.activation(
                out=GS[:, 2 * GH:3 * GH], in_=GT[:, (2 + dd) * GH:(3 + dd) * GH], func=AF.Sigmoid),
                reads=[GT.key], writes=[GS.key])
            s.op("dve", lambda e, GS=GS: e.tensor_scalar(out=GS[:, 3 * GH:4 * GH], in0=GS[:, 2 * GH:3 * GH], scalar1=-1.0,
                                                        scalar2=None, op0=ALU.mult), reads=[GS.key], writes=[GS.key])
            return [(h, step, dd, c, t0, Tri, BM, SM, GT, GS, QV, None) for h in range(GH)]
    def gdn_unit(slot, h, step, dd, c, t0, Tri, BM, SM, GT, GS, QV, KV):
        T_ = lambda: b.rot("gtmp%d" % slot, tmps[slot])
        C_ = lambda: b.rot("gcol%d" % slot, cols[slot])
        PS = lambda: b.rot("Ps%d" % slot, b.P[2 * slot:2 * slot + 2])
        wide = wides[slot]
        xin = xins[slot]
        gcolv = GS[:, GH + h:GH + h + 1]
        beta = GS[:, 2 * GH + h:2 * GH + h + 1]
        nbeta = GS[:, 3 * GH + h:3 * GH + h + 1]
        XI = b.rot("gxin%d" % slot, xin)
        b.load("sp", XI[:], qkv3[:, :, h, t0:t0 + 130], XI.key, [], [XI.key], dr=["QKVA_T"])
        yield
        Pc = PS()

        def conv(e, XI=XI, Pc=Pc, h=h):
            for a in range(3):
                for j in range(3):
                    ins = e.matmul(Pc[:, a * 128:(a + 1) * 128], dg[:, h * 9 + a * 3 + j, :], XI[:, a, j:j + 128],
                                   start=(j == 0), stop=(j == 2))
            return ins
        s.op("pe", conv, reads=[XI.key, dg.key], writes=[Pc.key])
        yield
        SL = b.rot("gwide%d" % slot, wide)
        s.op("act", lambda e, SL=SL, Pc=Pc: e.activation(out=SL[:], in_=Pc[:, 0:384], func=AF.Silu),
             reads=[Pc.key], writes=[SL.key])
        yield
        s.op("dve", lambda e, SL=SL, QV=QV: e.tensor_tensor(
            out=SL[:].rearrange("p (a t) -> p a t", a=3), in0=SL[:].rearrange("p (a t) -> p a t", a=3),
            in1=QV[:].unsqueeze(1).to_broadcast([128, 3, 128]), op=ALU.mult),
            reads=[SL.key, QV.key], writes=[SL.key])
        yield
        chk(2)
        SQ = b.rot("gwide%d" % slot, wide)
        s.op("act", lambda e, SL=SL, SQ=SQ: e.activation(out=SQ[:, 0:256], in_=SL[:, 0:256], func=AF.Square),
             reads=[SL.key], writes=[SQ.key])
        yield
        Pn = PS()
        mm1(b, Pn, 256, b.onesf[:], SQ[:, 0:256], [b.onesf.key, SQ.key])
        yield
        s.op("act", lambda e, SQ=SQ, Pn=Pn: e.activation(out=SQ[:, 0:256], in_=Pn[:, 0:256], func=AF.Sqrt, bias=EPS),
             reads=[Pn.key], writes=[SQ.key])
        yield
        s.op("dve", lambda e, SQ=SQ: e.reciprocal(out=SQ[:, 0:256], in_=SQ[:, 0:256]), reads=[SQ.key], writes=[SQ.key])
        yield
        QK = b.rot("gwide%d" % slot, wide)
        s.op("dve", lambda e, QK=QK, SL=SL, SQ=SQ: e.scalar_tensor_tensor(
            out=QK[:, 0:128], in0=SL[:, 0:128], scalar=128 ** -0.5, in1=SQ[:, 0:128], op0=ALU.mult, op1=ALU.mult),
            reads=[SL.key, SQ.key], writes=[QK.key])
        yield
        s.op("dve", lambda e, QK=QK, SL=SL, SQ=SQ: e.tensor_tensor(
            out=QK[:, 128:256], in0=SL[:, 128:256], in1=SQ[:, 128:256], op=ALU.mult),
            reads=[SL.key, SQ.key, QK.key], writes=[QK.key])
        yield
        qT, kT = QK[:, 0:128], QK[:, 128:256]
        chk(3)
        GB_ = T_()
        s.op("dve", lambda e, GB_=GB_, gcolv=gcolv: e.tensor_scalar(out=GB_[:], in0=b.onesf[:], scalar1=gcolv,
                                                                  scalar2=None, op0=ALU.mult),
             reads=[b.onesf.key, GS.key], writes=[GB_.key])
        yield
        Pg = PS()

        def cums(e, Pg=Pg, GB_=GB_, Tri=Tri, BM=BM, gcolv=gcolv):
            e.matmul(Pg[:, 0:128], GB_[:], Tri[:], start=True, stop=False)
            e.matmul(Pg[:, 0:128], b.identf[:], BM[:], start=False, stop=True)
            e.matmul(Pg[:, 128:129], Tri[:], gcolv, start=True, stop=True)
            return e.matmul(Pg[:, 160:161], GB_[:], b.onesf[:, 0:1], start=True, stop=True)
        s.op("pe", cums, reads=[GB_.key, Tri.key, BM.key, b.identf.key, GS.key, b.onesf.key], writes=[Pg.key])
        yield
        CL = C_()
        s.op("dve", lambda e, CL=CL, Pg=Pg: e.tensor_copy(out=CL[:, 0:1], in_=Pg[:, 128:129]),
             reads=[Pg.key], writes=[CL.key])
        yield
        s.op("dve", lambda e, CL=CL, Pg=Pg: e.tensor_copy(out=CL[:, 1:2], in_=Pg[:, 160:161]),
             reads=[Pg.key, CL.key], writes=[CL.key])
        yield
        E = T_()
        s.op("act", lambda e, E=E, Pg=Pg, CL=CL: e.activation(out=E[:], in_=Pg[:, 0:128], func=AF.Exp, scale=-1.0,
                                                             bias=CL[:, 0:1]), reads=[Pg.key, CL.key], writes=[E.key])
        yield
        s.op("act", lambda e, CL=CL: e.activation(out=CL[:, 2:4], in_=CL[:, 0:2], func=AF.Exp),
             reads=[CL.key], writes=[CL.key])
        yield
        s.op("act", lambda e, CL=CL: e.activation(out=CL[:, 4:5], in_=CL[:, 0:1], func=AF.Exp, scale=-1.0,
                                                 bias=CL[:, 1:2]), reads=[CL.key], writes=[CL.key])
        yield
        s.op("dve", lambda e, CL=CL, beta=beta: e.tensor_tensor(out=CL[:, 5:6], in0=CL[:, 2:3], in1=beta, op=ALU.mult),
             reads=[CL.key, GS.key], writes=[CL.key])
        yield
        chk(4)
        Pt = PS()

        def trkv(e, Pt=Pt, kT=kT, SL=SL):
            e.transpose(Pt[:, 0:128], kT, b.identf[:])
            return e.transpose(Pt[:, 128:256], SL[:, 256:384], b.identf[:])
        s.op("pe", trkv, reads=[QK.key, SL.key, b.identf.key], writes=[Pt.key])
        yield
        chk(41)
        KBG, KDEC, VB = T_(), T_(), T_()
        s.op("dve", lambda e, KBG=KBG, Pt=Pt, CL=CL: e.tensor_scalar(out=KBG[:], in0=Pt[:, 0:128], scalar1=CL[:, 5:6],
                                                                   scalar2=None, op0=ALU.mult),
             reads=[Pt.key, CL.key], writes=[KBG.key])
        yield
        chk(42)
        s.op("act", lambda e, KDEC=KDEC, Pt=Pt, CL=CL: e.activation(out=KDEC[:], in_=Pt[:, 0:128], func=AF.Identity,
                                                                  scale=CL[:, 4:5]),
             reads=[Pt.key, CL.key], writes=[KDEC.key])
        yield
        chk(43)
        s.op("dve", lambda e, VB=VB, Pt=Pt, beta=beta: e.tensor_scalar(out=VB[:], in0=Pt[:, 128:256], scalar1=beta,
                                                                     scalar2=None, op0=ALU.mult),
             reads=[Pt.key, GS.key], writes=[VB.key])
        yield
        chk(5)
        Pk = PS()

        def gqk(e, Pk=Pk, kT=kT, qT=qT):
            e.matmul(Pk[:, 0:128], kT, kT, start=True, stop=True)
            return e.matmul(Pk[:, 128:256], kT, qT, start=True, stop=True)
        s.op("pe", gqk, reads=[QK.key], writes=[Pk.key])
        yield
        ES = T_()
        s.op("dve", lambda e, ES=ES, E=E, SM=SM: e.tensor_tensor(out=ES[:], in0=E[:], in1=SM[:], op=ALU.mult),
             reads=[E.key, SM.key], writes=[ES.key])
        yield
        M = T_()
        GG = T_()
        s.op("act", lambda e, GG=GG, Pk=Pk, nbeta=nbeta: e.activation(out=GG[:], in_=Pk[:, 0:128], func=AF.Identity,
                                                                   scale=nbeta),
             reads=[Pk.key, GS.key], writes=[GG.key])
        yield
        s.op("dve", lambda e, M=M, GG=GG, ES=ES: e.tensor_tensor(out=M[:], in0=GG[:], in1=ES[:], op=ALU.mult),
             reads=[GG.key, ES.key], writes=[M.key])
        yield
        chk(51)
        Pe = PS()

        def trne(e, Pe=Pe, M=M, E=E):
            e.transpose(Pe[:, 0:128], M[:], b.identf[:])
            return e.transpose(Pe[:, 128:256], E[:], b.identf[:])
        s.op("pe", trne, reads=[M.key, E.key, b.identf.key], writes=[Pe.key])
        yield
        MT = T_()
        b.copy("act", MT[:], Pe[:, 0:128], [Pe.key], [MT.key])
        yield
        PP = T_()
        s.op("dve", lambda e, PP=PP, Pe=Pe: e.tensor_tensor(out=PP[:], in0=Pe[:, 0:128], in1=b.identf[:], op=ALU.add),
             reads=[Pe.key, b.identf.key], writes=[PP.key])
        yield
        AT = T_()
        s.op("dve", lambda e, AT=AT, Pe=Pe, Pk=Pk: e.tensor_copy(out=AT[:], in_=Pe[:, 128:256]),
             reads=[Pe.key], writes=[AT.key])
        yield
        s.op("dve", lambda e, AT=AT, Pk=Pk: e.tensor_tensor(out=AT[:], in0=Pk[:, 128:256], in1=AT[:], op=ALU.mult),
             reads=[Pk.key, AT.key], writes=[AT.key])
        yield
        chk(52)
        for k in range(1, 7):
            chk(52 + k)
            Pm = PS()

            def sq(e, Pm=Pm, M=M, MT=MT, k=k):
                ins = e.matmul(Pm[:, 0:128], MT[:], M[:], start=True, stop=True)
                if k < 6:
                    ins = e.matmul(Pm[:, 128:256], M[:], MT[:], start=True, stop=True)
                return ins
            s.op("pe", sq, reads=[M.key, MT.key], writes=[Pm.key])
            M2 = T_()
            b.copy("act", M2[:], Pm[:, 0:128], [Pm.key], [M2.key])
            if k < 6:
                MT2 = T_()
                b.copy("dve", MT2[:], Pm[:, 128:256], [Pm.key], [MT2.key])
            Pp = PS()
            mm1(b, Pp, 128, M2[:], PP[:], [M2.key, PP.key])
            PP2 = T_()
            s.op("dve", lambda e, PP2=PP2, Pp=Pp, PP=PP: e.tensor_tensor(out=PP2[:], in0=Pp[:, 0:128], in1=PP[:],
                                                                      op=ALU.add),
                 reads=[Pp.key, PP.key], writes=[PP2.key])
            PP = PP2
            M = M2
            if k < 6:
                MT = MT2
        chk(6)
        Pw = PS()

        def wu(e, Pw=Pw, KBG=KBG, PP=PP, VB=VB):
            e.matmul(Pw[:, 0:128], KBG[:], PP[:], start=True, stop=True)
            return e.matmul(Pw[:, 128:256], PP[:], VB[:], start=True, stop=True)
        s.op("pe", wu, reads=[KBG.key, PP.key, VB.key], writes=[Pw.key])
        yield
        WT, UU = T_(), T_()
        b.copy("act", WT[:], Pw[:, 0:128], [Pw.key], [WT.key])
        yield
        b.copy("dve", UU[:], Pw[:, 128:256], [Pw.key], [UU.key])
        yield
        chk(7)
        Sc = S[(h, dd, step % 2)]
        Sn = S[(h, dd, (step + 1) % 2)]
        Pr = PS()

        def r1(e, Pr=Pr, WT=WT, Sc=Sc, qT=qT):
            e.matmul(Pr[:, 0:128], WT[:], Sc[:], start=True, stop=True)
            return e.matmul(Pr[:, 128:256], qT, Sc[:], start=True, stop=True)
        s.op("pe", r1, reads=[WT.key, Sc.key, QK.key], writes=[Pr.key])
        yield
        VN = T_()
        s.op("dve", lambda e, VN=VN, UU=UU, Pr=Pr: e.tensor_tensor(out=VN[:], in0=UU[:], in1=Pr[:, 0:128],
                                                                op=ALU.subtract),
             reads=[UU.key, Pr.key], writes=[VN.key])
        yield
        OT = T_()
        s.op("act", lambda e, OT=OT, Pr=Pr, CL=CL: e.activation(out=OT[:], in_=Pr[:, 128:256], func=AF.Identity,
                                                               scale=CL[:, 2:3]),
             reads=[Pr.key, CL.key], writes=[OT.key])
        yield
        Po = PS()

        def r2(e, Po=Po, AT=AT, VN=VN, KDEC=KDEC):
            e.matmul(Po[:, 0:128], AT[:], VN[:], start=True, stop=True)
            return e.matmul(Po[:, 128:256], KDEC[:], VN[:], start=True, stop=True)
        s.op("pe", r2, reads=[AT.key, VN.key, KDEC.key], writes=[Po.key])
        yield
        OO = b.rot("goo%d" % slot, oos[slot])
        s.op("dve", lambda e, OO=OO, OT=OT, Po=Po: e.tensor_tensor(out=OO[:], in0=OT[:], in1=Po[:, 0:128], op=ALU.add),
             reads=[OT.key, Po.key], writes=[OO.key])
        yield
        SS = T_()
        s.op("act", lambda e, SS=SS, Sc=Sc, CL=CL: e.activation(out=SS[:], in_=Sc[:], func=AF.Identity, scale=CL[:, 3:4]),
             reads=[Sc.key, CL.key], writes=[SS.key])
        yield
        s.op("dve", lambda e, Sn=Sn, SS=SS, Po=Po: e.tensor_tensor(out=Sn[:], in0=SS[:], in1=Po[:, 128:256], op=ALU.add),
             reads=[SS.key, Po.key], writes=[Sn.key])
        yield
        b.store("sp", d["OA%d" % dd][t0:t0 + 128, h * 128:(h + 1) * 128], OO[:], OO.key, [OO.key], ["OA%d" % dd])
        yield
    run_interleaved(b, units_q, group_pre, gdn_unit, NSL)


def phase_mlstm(b, stack):
    cfg, s, d = b.cfg, b.s, b.dram
    T, MH = cfg.T, cfg.MH
    NCH = T // 128
    tmps = [b.gtmp] + [[b.sb(stack, "mt%d_%d" % (j, i), [128, 128], F32) for i in range(12)] for j in (1, 2, 3)]
    cols = [[b.sb(stack, "mc%d_%d" % (j, i), [128, 16], F32) for i in range(4)] for j in range(4)]
    gt = [b.sb(stack, "mgt%d" % i, [128, 4 * MH], F32) for i in range(4)]
    gs = [b.sb(stack, "mgs%d" % i, [128, 4 * MH], F32) for i in range(4)]
    kvt = [b.sb(stack, "mkv%d" % i, [128, MH * 384], BF16) for i in range(3)]
    qkts = [[b.sb(stack, "mqk%d_%d" % (j, i), [128, 2, 128], BF16) for i in range(2)] for j in range(4)]
    w257s = [[b.sb(stack, "mw%d_%d" % (j, i), [128, 264], F32) for i in range(8)] for j in range(4)]
    Cst = {(h, dd, p): b.sb(stack, "C%d_%d_%d" % (h, dd, p), [128, 264], F32) for h in range(MH) for dd in range(2)
           for p in range(2)}
    Mst = {(h, dd, p): b.sb(stack, "M%d_%d_%d" % (h, dd, p), [128, 2], F32) for h in range(MH) for dd in range(2)
           for p in range(2)}
    mlb = b.sb(stack, "mlb", [128, 4 * MH], F32)
    BMf, BMb = b.BMf, b.BMb
    s.dma("sp", lambda e: e.dma_start(out=mlb[:], in_=d["ml_gate_bias"].rearrange("a k d h -> a (k d h)").partition_broadcast(128)),
          mlb.key, writes=[mlb.key])
    for h in range(MH):
        for dd in range(2):
            s.op("pool", lambda e, h=h, dd=dd: e.memset(Cst[(h, dd, 0)][:], 0.0), writes=[Cst[(h, dd, 0)].key])
            s.op("pool", lambda e, h=h, dd=dd: e.memset(Mst[(h, dd, 0)][:], 0.0), writes=[Mst[(h, dd, 0)].key])
    qb3 = d["QB_T"].rearrange("(h p) t -> p h t", h=MH)
    kb3 = d["KB_T"].rearrange("(h p) t -> p h t", h=MH)
    units_q = [(step, dd) for step in range(NCH) for dd in range(2)]

    def group_pre(step, dd):
        if True:
            c = step if dd == 0 else NCH - 1 - step
            t0 = c * 128
            Tri = b.U if dd == 0 else b.L
            BM = BMf if dd == 0 else BMb
            GT = b.rot("mgt", gt)
            GS = b.rot("mgs", gs)
            b.load("sp", GT[:], d["GB"][t0:t0 + 128, :], GT.key, [], [GT.key], dr=["GB"])
            s.op("dve", lambda e, GT=GT: e.tensor_tensor(out=GT[:], in0=GT[:], in1=mlb[:], op=ALU.add),
                 reads=[GT.key, mlb.key], writes=[GT.key])
            s.op("dve", lambda e, GT=GT, GS=GS, dd=dd: e.tensor_copy(out=GS[:, 0:MH], in_=GT[:, dd * MH:(dd + 1) * MH]),
                 reads=[GT.key], writes=[GS.key])
            s.op("dve", lambda e, GT=GT, GS=GS, dd=dd: e.tensor_scalar(out=GS[:, MH:2 * MH], in0=GT[:, dd * MH:(dd + 1) * MH],
                                                                    scalar1=-1.0, scalar2=None, op0=ALU.mult),
                 reads=[GT.key, GS.key], writes=[GS.key])
            s.op("act", lambda e, GT=GT, GS=GS, dd=dd: e.activation(out=GS[:, 2 * MH:3 * MH],
                                                                 in_=GT[:, (2 + dd) * MH:(3 + dd) * MH], func=AF.Exp, scale=-1.0),
                 reads=[GT.key, GS.key], writes=[GS.key])
            s.op("act", lambda e, GS=GS: e.activation(out=GS[:, 2 * MH:3 * MH], in_=GS[:, 2 * MH:3 * MH], func=AF.Ln, bias=1.0),
                 reads=[GS.key], writes=[GS.key])
            s.op("dve", lambda e, GS=GS: e.tensor_scalar(out=GS[:, 2 * MH:3 * MH], in0=GS[:, 2 * MH:3 * MH], scalar1=-1.0,
                                                        scalar2=None, op0=ALU.mult), reads=[GS.key], writes=[GS.key])
            KV = b.rot("mkv", kvt)
            b.load("sp", KV[:], d["KVB"][t0:t0 + 128, :], KV.key, [], [KV.key], dr=["KVB"])
            return [(h, step, dd, c, t0, Tri, BM, None, GT, GS, None, KV) for h in range(MH)]
    def ml_unit(slot, h, step, dd, c, t0, Tri, BM, SM, GT, GS, QV, KV):
        T_ = lambda: b.rot("gtmp%d" % slot, tmps[slot])
        C_ = lambda: b.rot("mcol%d" % slot, cols[slot])
        PS = lambda: b.rot("Ps%d" % slot, b.P[2 * slot:2 * slot + 2])
        W_ = lambda: b.rot("mw%d" % slot, w257s[slot])
        qkt = qkts[slot]
        igc = GS[:, h:h + 1]
        nigc = GS[:, MH + h:MH + h + 1]
        lfc = GS[:, 2 * MH + h:2 * MH + h + 1]
        Mc, Mn = Mst[(h, dd, step % 2)], Mst[(h, dd, (step + 1) % 2)]
        Cc, Cn = Cst[(h, dd, step % 2)], Cst[(h, dd, (step + 1) % 2)]
        QKb = b.rot("mqk%d" % slot, qkt)
        b.load("sp", QKb[:, 0, :], qb3[:, h, t0:t0 + 128], QKb.key, [], [QKb.key], dr=["QB_T"])
        yield
        b.load("sp", QKb[:, 1, :], kb3[:, h, t0:t0 + 128], QKb.key, [], [QKb.key], dr=["KB_T"])
        yield
        QK = W_()
        s.op("act", lambda e, QK=QK, QKb=QKb: e.activation(out=QK[:, 0:256],
                                                          in_=QKb[:].rearrange("p a t -> p (a t)"), func=AF.Copy),
             reads=[QKb.key], writes=[QK.key])
        yield
        qT, kT = QK[:, 0:128], QK[:, 128:256]
        VP = W_()
        s.op("dve", lambda e, VP=VP, KV=KV, h=h: e.tensor_copy(out=VP[:, 0:256],
                                                            in_=KV[:, MH * 128 + h * 256:MH * 128 + (h + 1) * 256]),
             reads=[KV.key], writes=[VP.key])
        yield
        s.op("dve", lambda e, VP=VP: e.memset(VP[:, 256:257], 1.0), reads=[VP.key], writes=[VP.key])
        yield
        LFB, NIB = T_(), T_()
        s.op("dve", lambda e, LFB=LFB, lfc=lfc: e.tensor_scalar(out=LFB[:], in0=b.onesf[:], scalar1=lfc, scalar2=None,
                                                             op0=ALU.mult), reads=[b.onesf.key, GS.key], writes=[LFB.key])
        yield
        s.op("dve", lambda e, NIB=NIB, nigc=nigc: e.tensor_scalar(out=NIB[:], in0=b.onesf[:], scalar1=nigc, scalar2=None,
                                                               op0=ALU.mult), reads=[b.onesf.key, GS.key], writes=[NIB.key])
        yield
        Pg = PS()

        def cums(e, Pg=Pg, LFB=LFB, NIB=NIB, Tri=Tri, BM=BM, lfc=lfc):
            e.matmul(Pg[:, 0:128], LFB[:], Tri[:], start=True, stop=False)
            e.matmul(Pg[:, 0:128], NIB[:], b.identf[:], start=False, stop=False)
            e.matmul(Pg[:, 0:128], b.identf[:], BM[:], start=False, stop=True)
            e.matmul(Pg[:, 128:256], LFB[:], Tri[:], start=True, stop=False)
            e.matmul(Pg[:, 128:256], NIB[:], b.identf[:], start=False, stop=True)
            e.matmul(Pg[:, 256:257], Tri[:], lfc, start=True, stop=True)
            return e.matmul(Pg[:, 288:289], LFB[:], b.onesf[:, 0:1], start=True, stop=True)
        s.op("pe", cums, reads=[LFB.key, NIB.key, Tri.key, BM.key, b.identf.key, GS.key, b.onesf.key], writes=[Pg.key])
        yield
        CL = C_()
        s.op("dve", lambda e, CL=CL, Pg=Pg: e.tensor_copy(out=CL[:, 0:1], in_=Pg[:, 256:257]), reads=[Pg.key], writes=[CL.key])
        yield
        s.op("dve", lambda e, CL=CL, Pg=Pg: e.tensor_copy(out=CL[:, 1:2], in_=Pg[:, 288:289]),
             reads=[Pg.key, CL.key], writes=[CL.key])
        yield
        s.op("dve", lambda e, CL=CL, Pg=Pg: e.tensor_reduce(out=CL[:, 2:3], in_=Pg[:, 0:128], axis=AX.X, op=ALU.min),
             reads=[Pg.key, CL.key], writes=[CL.key])
        yield
        s.op("dve", lambda e, CL=CL, Pg=Pg: e.tensor_reduce(out=CL[:, 3:4], in_=Pg[:, 128:256], axis=AX.X, op=ALU.min),
             reads=[Pg.key, CL.key], writes=[CL.key])
        yield
        s.op("dve", lambda e, CL=CL, Mc=Mc: e.scalar_tensor_tensor(out=CL[:, 4:5], in0=CL[:, 2:3], scalar=-1.0, in1=Mc[:, 0:1],
                                                                op0=ALU.mult, op1=ALU.max),
             reads=[CL.key, Mc.key], writes=[CL.key])
        yield
        s.op("dve", lambda e, CL=CL, Mc=Mc: e.scalar_tensor_tensor(out=CL[:, 9:10], in0=CL[:, 3:4], scalar=-1.0, in1=Mc[:, 0:1],
                                                                op0=ALU.mult, op1=ALU.max),
             reads=[CL.key, Mc.key], writes=[CL.key])
        yield
        s.op("dve", lambda e, CL=CL: e.tensor_scalar(out=CL[:, 5:6], in0=CL[:, 4:5], scalar1=-1.0, scalar2=None, op0=ALU.mult),
             reads=[CL.key], writes=[CL.key])
        yield
        s.op("dve", lambda e, CL=CL: e.tensor_scalar(out=CL[:, 10:11], in0=CL[:, 9:10], scalar1=-1.0, scalar2=None, op0=ALU.mult),
             reads=[CL.key], writes=[CL.key])
        yield
        s.op("dve", lambda e, CL=CL: e.tensor_tensor(out=CL[:, 7:8], in0=CL[:, 0:1], in1=CL[:, 4:5], op=ALU.add),
             reads=[CL.key], writes=[CL.key])
        yield
        s.op("dve", lambda e, CL=CL, igc=igc: e.tensor_tensor(out=CL[:, 12:13], in0=CL[:, 0:1], in1=igc, op=ALU.subtract),
             reads=[CL.key, GS.key], writes=[CL.key])
        yield
        s.op("dve", lambda e, CL=CL, Mn=Mn: e.tensor_tensor(out=Mn[:, 0:1], in0=CL[:, 1:2], in1=CL[:, 9:10], op=ALU.add),
             reads=[CL.key], writes=[Mn.key])
        yield
        EW = T_()
        s.op("act", lambda e, EW=EW, Pg=Pg, CL=CL: e.activation(out=EW[:], in_=Pg[:, 0:128], func=AF.Exp, scale=-1.0,
                                                               bias=CL[:, 5:6]), reads=[Pg.key, CL.key], writes=[EW.key])
        yield
        s.op("act", lambda e, CL=CL, Mc=Mc: e.activation(out=CL[:, 6:7], in_=CL[:, 4:5], func=AF.Exp, scale=-1.0,
                                                        bias=Mc[:, 0:1]), reads=[CL.key, Mc.key], writes=[CL.key])
        yield
        s.op("act", lambda e, CL=CL: e.activation(out=CL[:, 8:9], in_=CL[:, 7:8], func=AF.Exp, scale=-1.0),
             reads=[CL.key], writes=[CL.key])
        yield
        s.op("act", lambda e, CL=CL, Mc=Mc: e.activation(out=CL[:, 11:12], in_=CL[:, 9:10], func=AF.Exp, scale=-1.0,
                                                        bias=Mc[:, 0:1]), reads=[CL.key, Mc.key], writes=[CL.key])
        yield
        s.op("act", lambda e, CL=CL: e.activation(out=CL[:, 13:14], in_=CL[:, 12:13], func=AF.Exp, scale=-1.0,
                                                 bias=CL[:, 10:11]), reads=[CL.key], writes=[CL.key])
        yield
        Pq = PS()
        mm1(b, Pq, 128, qT, kT, [QK.key])
        yield
        WI = T_()
        s.op("dve", lambda e, WI=WI, EW=EW, Pq=Pq: e.tensor_tensor(out=WI[:], in0=Pq[:, 0:128], in1=EW[:], op=ALU.mult),
             reads=[Pq.key, EW.key], writes=[WI.key])
        yield
        Pt = PS()
        s.op("pe", lambda e, Pt=Pt, WI=WI: e.transpose(Pt[:, 0:128], WI[:], b.identf[:]),
             reads=[WI.key, b.identf.key], writes=[Pt.key])
        yield
        WIT = T_()
        b.copy("act", WIT[:], Pt[:, 0:128], [Pt.key], [WIT.key])
        yield
        WK = T_()
        s.op("dve", lambda e, WK=WK, KV=KV, CL=CL, h=h: e.tensor_scalar(out=WK[:], in0=KV[:, h * 128:(h + 1) * 128],
                                                                     scalar1=CL[:, 13:14], scalar2=None, op0=ALU.mult),
             reads=[KV.key, CL.key], writes=[WK.key])
        yield
        Pa = PS()
        mm1(b, Pa, 257, qT, Cc[:, 0:257], [QK.key, Cc.key])
        yield
        T1 = W_()
        s.op("act", lambda e, T1=T1, Pa=Pa, CL=CL: e.activation(out=T1[:, 0:257], in_=Pa[:, 0:257], func=AF.Identity,
                                                               scale=CL[:, 6:7]), reads=[Pa.key, CL.key], writes=[T1.key])
        yield
        Pb = PS()
        mm1(b, Pb, 257, WIT[:], VP[:, 0:257], [WIT.key, VP.key])
        yield
        ND = W_()
        s.op("dve", lambda e, ND=ND, T1=T1, Pb=Pb: e.tensor_tensor(out=ND[:, 0:257], in0=Pb[:, 0:257], in1=T1[:, 0:257],
                                                                op=ALU.add), reads=[Pb.key, T1.key], writes=[ND.key])
        yield
        CD = C_()
        s.op("dve", lambda e, CD=CD, ND=ND: e.scalar_tensor_tensor(out=CD[:, 0:1], in0=ND[:, 256:257], scalar=-1.0,
                                                                  in1=ND[:, 256:257], op0=ALU.mult, op1=ALU.max),
             reads=[ND.key], writes=[CD.key])
        yield
        s.op("dve", lambda e, CD=CD, CL=CL: e.tensor_tensor(out=CD[:, 1:2], in0=CD[:, 0:1], in1=CL[:, 8:9], op=ALU.max),
             reads=[CD.key, CL.key], writes=[CD.key])
        yield
        s.op("dve", lambda e, CD=CD: e.reciprocal(out=CD[:, 2:3], in_=CD[:, 1:2]), reads=[CD.key], writes=[CD.key])
        yield
        HO = W_()
        s.op("dve", lambda e, HO=HO, ND=ND, CD=CD: e.tensor_scalar(out=HO[:, 0:256], in0=ND[:, 0:256], scalar1=CD[:, 2:3],
                                                                scalar2=None, op0=ALU.mult),
             reads=[ND.key, CD.key], writes=[HO.key])
        yield
        b.store("sp", d["HB%d" % dd][t0:t0 + 128, h * 256:(h + 1) * 256], HO[:, 0:256], HO.key, [HO.key], ["HB%d" % dd])
        yield
        Pc = PS()
        mm1(b, Pc, 257, WK[:], VP[:, 0:257], [WK.key, VP.key])
        yield
        T2 = W_()
        s.op("act", lambda e, T2=T2, Cc=Cc, CL=CL: e.activation(out=T2[:, 0:257], in_=Cc[:, 0:257], func=AF.Identity,
                                                               scale=CL[:, 11:12]), reads=[Cc.key, CL.key], writes=[T2.key])
        yield
        s.op("dve", lambda e, Cn=Cn, T2=T2, Pc=Pc: e.tensor_tensor(out=Cn[:, 0:257], in0=Pc[:, 0:257], in1=T2[:, 0:257],
                                                                op=ALU.add), reads=[Pc.key, T2.key], writes=[Cn.key])
        yield
    run_interleaved(b, units_q, group_pre, ml_unit, 4)


def transpose_to_fm(b, XBt, dst, dstkey, ncols):
    s = b.s
    NCk = ncols // 128
    for cg in range(0, NCk, 8):
        n = min(8, NCk - cg)
        PT = b.rot("PT", b.PT)

        def tr(e, cg=cg, n=n, PT=PT):
            for c in range(n):
                ins = e.transpose(PT[:, c, :], XBt[:, (cg + c) * 128:(cg + c + 1) * 128], b.identb[:])
            return ins
        s.op("pe", tr, reads=[XBt.key, b.identb.key], writes=[PT.key])
        b.copy(b.evac_eng(), dst[:, cg:cg + n, :], PT[:, 0:n, :], [PT.key], [dstkey])


def outproj_residual(b, hv, hkey, ntok, w, resid, dst, tok0, final_store=False):
    s, d = b.s, b.dram
    D = b.cfg.D

    def ev(P, c0, nb, tt, nt):
        R = b.rot("SF", b.SF)
        b.load("sp", R[:, 0:nb], d[resid][tok0 + tt:tok0 + tt + 128, c0:c0 + nb], R.key, [], [R.key], dr=[resid])
        O = b.rot("SF", b.SF)
        s.op("dve", lambda e: e.tensor_tensor(out=O[:, 0:nb], in0=P[:, 0:nb], in1=R[:, 0:nb], op=ALU.add),
             reads=[P.key, R.key], writes=[O.key])
        b.store("sp", d[dst][tok0 + tt:tok0 + tt + 128, c0:c0 + nb], O[:, 0:nb], O.key, [O.key], [dst], final=final_store)
    linear(b, "tm", hv, hkey, ntok, w, 0, D, ev)


def phase_mixout0(b, stack):
    dense_bufs(b, stack)
    cfg, s, d = b.cfg, b.s, b.dram
    T, D, DC, GH, MH = cfg.T, cfg.D, cfg.DC, cfg.GH, cfg.MH
    WA, WB = GH * 128, MH * 256
    TBM = 512
    gfull = b.sb(stack, "gfull", [128, D], F32)
    for h in range(GH):
        s.dma("sp", lambda e, h=h: e.dma_start(out=gfull[:, h * 128:(h + 1) * 128], in_=d["gdn_norm"][0:1, :].partition_broadcast(128)),
              gfull.key, writes=[gfull.key])
    s.dma("sp", lambda e: e.dma_start(out=gfull[:, WA:D], in_=d["ml_norm"][0:1, :].partition_broadcast(128)),
          gfull.key, writes=[gfull.key])
    rsb = [b.sb(stack, "rsb%d" % i, [128, 2 * (GH + MH)], F32) for i in range(2)]
    for tb in range(T // TBM):
        H = b.H[0]
        hv = H[:, 0:DC * TBM].rearrange("p (c t) -> p c t", c=DC)
        for tt in range(TBM // 128):
            t0 = tb * TBM + tt * 128
            XA, XC = b.X[0], b.X[1]
            ZB, MB = b.XB[0], b.XB[1]
            RS = b.rot("rsb", rsb)
            b.load("sp", XA[:, 0:WA], d["OA0"][t0:t0 + 128, :], XA.key, [], [XA.key], dr=["OA0"])
            b.load("sp", XA[:, WA:D], d["HB0"][t0:t0 + 128, :], XA.key, [], [XA.key], dr=["HB0"])
            b.load("sp", XC[:, 0:WA], d["OA1"][t0:t0 + 128, :], XC.key, [], [XC.key], dr=["OA1"])
            b.load("sp", XC[:, WA:D], d["HB1"][t0:t0 + 128, :], XC.key, [], [XC.key], dr=["HB1"])
            b.load("sp", ZB[:, 0:WA], d["Z"][t0:t0 + 128, :], ZB.key, [], [ZB.key], dr=["Z"])
            b.load("sp", ZB[:, WA:D], d["OB"][t0:t0 + 128, :], ZB.key, [], [ZB.key], dr=["OB"])
            s.op("dve", lambda e, XA=XA, XC=XC: e.tensor_tensor(out=XA[:, 0:D], in0=XA[:, 0:D], in1=XC[:, 0:D], op=ALU.add),
                 reads=[XA.key, XC.key], writes=[XA.key])
            s.op("act", lambda e, XA=XA, XC=XC: e.activation(out=XC[:, 0:D], in_=XA[:, 0:D], func=AF.Square),
                 reads=[XA.key], writes=[XC.key])
            s.op("dve", lambda e, XC=XC, RS=RS: e.tensor_reduce(out=RS[:, 0:GH], in_=XC[:, 0:WA].rearrange("p (h k) -> p h k", h=GH),
                                                              axis=AX.X, op=ALU.add), reads=[XC.key], writes=[RS.key])
            s.op("dve", lambda e, XC=XC, RS=RS: e.tensor_reduce(out=RS[:, GH:GH + MH],
                                                              in_=XC[:, WA:D].rearrange("p (h k) -> p h k", h=MH),
                                                              axis=AX.X, op=ALU.add), reads=[XC.key, RS.key], writes=[RS.key])
            s.op("act", lambda e, RS=RS: e.activation(out=RS[:, 0:GH], in_=RS[:, 0:GH], func=AF.Sqrt, scale=1.0 / 128, bias=EPS),
                 reads=[RS.key], writes=[RS.key])
            s.op("act", lambda e, RS=RS: e.activation(out=RS[:, GH:GH + MH], in_=RS[:, GH:GH + MH], func=AF.Sqrt, scale=1.0 / 256,
                                                     bias=EPS), reads=[RS.key], writes=[RS.key])
            s.op("dve", lambda e, RS=RS: e.reciprocal(out=RS[:, GH + MH:2 * (GH + MH)], in_=RS[:, 0:GH + MH]),
                 reads=[RS.key], writes=[RS.key])
            s.op("dve", lambda e, XA=XA, RS=RS: e.tensor_tensor(
                out=XA[:, 0:WA].rearrange("p (h k) -> p h k", h=GH), in0=XA[:, 0:WA].rearrange("p (h k) -> p h k", h=GH),
                in1=RS[:, GH + MH:2 * GH + MH].unsqueeze(2).to_broadcast([128, GH, 128]), op=ALU.mult),
                reads=[XA.key, RS.key], writes=[XA.key])
            s.op("dve", lambda e, XA=XA, RS=RS: e.tensor_tensor(
                out=XA[:, WA:D].rearrange("p (h k) -> p h k", h=MH), in0=XA[:, WA:D].rearrange("p (h k) -> p h k", h=MH),
                in1=RS[:, 2 * GH + MH:2 * (GH + MH)].unsqueeze(2).to_broadcast([128, MH, 256]), op=ALU.mult),
                reads=[XA.key, RS.key], writes=[XA.key])
            s.op("dve", lambda e, XA=XA: e.tensor_tensor(out=XA[:, 0:D], in0=XA[:, 0:D], in1=gfull[:], op=ALU.mult),
                 reads=[XA.key, gfull.key], writes=[XA.key])
            s.op("act", lambda e, XC=XC, ZB=ZB: e.activation(out=XC[:, 0:WA], in_=ZB[:, 0:WA], func=AF.Silu),
                 reads=[ZB.key], writes=[XC.key])
            s.op("act", lambda e, XC=XC, ZB=ZB: e.activation(out=XC[:, WA:D], in_=ZB[:, WA:D], func=AF.Sigmoid),
                 reads=[ZB.key, XC.key], writes=[XC.key])
            s.op("dve", lambda e, XA=XA, XC=XC, MB=MB: e.tensor_tensor(out=MB[:, 0:D], in0=XA[:, 0:D], in1=XC[:, 0:D], op=ALU.mult),
                 reads=[XA.key, XC.key], writes=[MB.key])
            transpose_to_fm(b, MB, hv[:, :, tt * 128:(tt + 1) * 128], H.key, D)
        outproj_residual(b, hv, H.key, TBM, d["b_ev_w_out"][0], "x", "XL1", tb * TBM)


def phase_ffn(layer, src, dst):
    def ph(b, stack):
        dense_bufs(b, stack)
        cfg, s, d = b.cfg, b.s, b.dram
        T, D, DC, DFF = cfg.T, cfg.D, cfg.DC, cfg.DFF
        FC = DFF // 128
        TBF = 512
        pre = "f%d_" % b.pid
        cwt = b.sb(stack, pre + "cw", [128, FC, 4], F32)
        for j in range(3):
            s.dma("sp", lambda e, j=j: e.dma_start(out=cwt[:, :, j:j + 1],
                                                   in_=d["ffn_conv"][layer, j].rearrange("(c p o) -> p c o", p=128, o=1),
                                                   allow_slow_non_contiguous=True), cwt.key, writes=[cwt.key])
        s.dma("sp", lambda e: e.dma_start(out=cwt[:, :, 3:4], in_=d["ffn_conv_b"][layer].rearrange("(c p o) -> p c o", p=128, o=1),
                                          allow_slow_non_contiguous=True), cwt.key, writes=[cwt.key])
        hn2 = b.sb(stack, pre + "hn", [128, DC, TBF + 2], BF16)
        actT = b.H[0]
        av = actT[:, 0:FC * TBF].rearrange("p (c t) -> p c t", c=FC)
        gbuf = [b.sb(stack, pre + "g%d" % i, [128, TBF + 2], F32) for i in range(2)]
        cbuf = [b.sb(stack, pre + "c%d" % i, [128, TBF], F32) for i in range(2)]
        halo = b.sb(stack, pre + "halo", [128, D], F32)
        w_up, w_dn = d["b_ffn_w_up"][layer], d["b_ffn_w_down"][layer]
        wupv = w_up.rearrange("(c p) n -> p c n", p=128)
        gsrc = b.gffn[:, layer * DC:(layer + 1) * DC]
        for tb in range(T // TBF):
            tok0 = tb * TBF
            for tt in range(TBF // 128):
                t0 = tok0 + tt * 128
                rmsnorm_T(b, d[src][t0:t0 + 128, :], gsrc, b.gffn.key, hn2[:, :, 1 + tt * 128:1 + (tt + 1) * 128], hn2.key,
                          src_reads=[], dr=[src])
            s.op("pool", lambda e: e.memset(halo[:], 0.0), writes=[halo.key])
            if tok0 > 0:
                b.load("sp", halo[0:1, :], d[src][tok0 - 1:tok0, :], halo.key, [], [halo.key], dr=[src])
            if tok0 + TBF < T:
                b.load("sp", halo[1:2, :], d[src][tok0 + TBF:tok0 + TBF + 1, :], halo.key, [], [halo.key], dr=[src])
            rmsnorm_T(b, None, gsrc, b.gffn.key, None, hn2.key, preloaded=halo,
                      halo_dst=(hn2[:, :, 0:1], hn2[:, :, TBF + 1:TBF + 2]))
            for fb in range(0, DFF, 512):
                nb = min(512, DFF - fb)
                Wg = b.rot("W", b.W)
                Wgv = Wg[:, 0:DC * 512].rearrange("p (c n) -> p c n", c=DC)
                b.load("sp", Wgv[:, :, 0:nb], wupv[:, :, fb:fb + nb], Wg.key, [], [Wg.key], dr=["b_ffn_w_up"])
                Wv = b.rot("W", b.W)
                Wvv = Wv[:, 0:DC * 512].rearrange("p (c n) -> p c n", c=DC)
                b.load("sp", Wvv[:, :, 0:nb], wupv[:, :, DFF + fb:DFF + fb + nb], Wv.key, [], [Wv.key], dr=["b_ffn_w_up"])
                for ft in range(0, nb, 128):
                    f = (fb + ft) // 128
                    G = b.rot(pre + "g", gbuf)
                    half = (TBF + 2) // 2
                    for (c0, c1) in ((0, half), (half, TBF + 2)):
                        P = b.rot("P", b.PD)

                        def mm(e, P=P, c0=c0, c1=c1, ft=ft, Wgv=Wgv):
                            for c in range(DC):
                                ins = e.matmul(P[:, 0:c1 - c0], Wgv[:, c, ft:ft + 128], hn2[:, c, c0:c1], start=(c == 0),
                                               stop=(c == DC - 1))
                            return ins
                        s.op("pe", mm, reads=[Wg.key, hn2.key], writes=[P.key])
                        b.copy(b.evac_eng(), G[:, c0:c1], P[:, 0:c1 - c0], [P.key], [G.key])
                    Cv = b.rot(pre + "c", cbuf)
                    s.op("dve", lambda e, Cv=Cv, G=G, f=f: e.tensor_scalar(out=Cv[:], in0=G[:, 0:TBF], scalar1=cwt[:, f, 0:1],
                                                                        scalar2=None, op0=ALU.mult),
                         reads=[G.key, cwt.key], writes=[Cv.key])
                    s.op("dve", lambda e, Cv=Cv, G=G, f=f: e.scalar_tensor_tensor(out=Cv[:], in0=G[:, 1:TBF + 1], scalar=cwt[:, f, 1:2],
                                                                               in1=Cv[:], op0=ALU.mult, op1=ALU.add),
                         reads=[G.key, cwt.key, Cv.key], writes=[Cv.key])
                    s.op("dve", lambda e, Cv=Cv, G=G, f=f: e.scalar_tensor_tensor(out=Cv[:], in0=G[:, 2:TBF + 2], scalar=cwt[:, f, 2:3],
                                                                               in1=Cv[:], op0=ALU.mult, op1=ALU.add),
                         reads=[G.key, cwt.key, Cv.key], writes=[Cv.key])
                    s.op("act", lambda e, Cv=Cv, f=f: e.activation(out=Cv[:], in_=Cv[:], func=AF.Silu, bias=cwt[:, f, 3:4]),
                         reads=[Cv.key, cwt.key], writes=[Cv.key])
                    P = b.rot("P", b.PD)

                    def mv(e, P=P, ft=ft, Wvv=Wvv):
                        for c in range(DC):
                            ins = e.matmul(P[:, 0:TBF], Wvv[:, c, ft:ft + 128], hn2[:, c, 1:TBF + 1], start=(c == 0), stop=(c == DC - 1))
                        return ins
                    s.op("pe", mv, reads=[Wv.key, hn2.key], writes=[P.key])
                    s.op("dve", lambda e, P=P, Cv=Cv, f=f: e.tensor_tensor(out=av[:, f, :], in0=P[:, 0:TBF], in1=Cv[:], op=ALU.mult),
                         reads=[P.key, Cv.key], writes=[actT.key])
            outproj_residual(b, av, actT.key, TBF, w_dn, src, dst, tok0)
    return ph


PADK = 1024
TWO_PI = 6.283185307179586


def rope_consts():
    inv = (500000.0 ** (-np.arange(0, 32, 2, dtype=np.float32) / 32)).astype(np.float32)
    c = np.zeros((32, 2), np.float32)
    c[:, 0] = np.concatenate([inv, inv])
    c[0:16, 1] = -TWO_PI
    c[16:32, 1] = TWO_PI
    return c


def phase_qkv1(b, stack):
    dense_bufs(b, stack)
    cfg, s, d = b.cfg, b.s, b.dram
    T, TB, D, DC, S_ = cfg.T, cfg.TBLK, cfg.D, cfg.DC, cfg.SLOTS
    NQ = 3 * S_ * 128
    w = d["b_od_w_in"][0]
    ropec = b.sb(stack, "ropec_sb", [32, 4], F32)
    s.dma("sp", lambda e: e.dma_start(out=ropec[:, 0:2], in_=d["ropec"][:, :]), ropec.key, writes=[ropec.key])
    permT = b.sb(stack, "permT", [128, 32], BF16)
    s.op("pool", lambda e: e.memset(permT[:], 0.0), writes=[permT.key])
    s.op("pool", lambda e: e.affine_select(out=permT[:, 0:16], in_=permT[:, 0:16], pattern=[[-1, 16]], compare_op=ALU.not_equal,
                                          fill=1.0, base=-16, channel_multiplier=1), reads=[permT.key], writes=[permT.key])
    s.op("pool", lambda e: e.affine_select(out=permT[:, 16:32], in_=permT[:, 16:32], pattern=[[-1, 16]], compare_op=ALU.not_equal,
                                          fill=1.0, base=0, channel_multiplier=1), reads=[permT.key], writes=[permT.key])
    ang = [b.sb(stack, "ang%d" % i, [32, 512], F32) for i in range(2)]
    cst = [b.sb(stack, "cst%d" % i, [32, 2, 512], F32) for i in range(2)]
    cki = [b.sb(stack, "cki%d" % i, [32, 2, 512], mybir.dt.int32) for i in range(2)]
    ckf = [b.sb(stack, "ckf%d" % i, [32, 2, 512], F32) for i in range(2)]
    qa = [b.sb(stack, "qa%d" % i, [128, 512], BF16) for i in range(3)]
    qr = [b.sb(stack, "qr%d" % i, [128, 512], BF16) for i in range(3)]
    rt = [b.sb(stack, "rt%d" % i, [32, 512], F32) for i in range(3)]
    for r in range(0, NQ, 128):
        for (c0) in (0, PADK + T):
            for cc in range(0, PADK, 512):
                b.store("sp", d["KT1"][r:r + 128, c0 + cc:c0 + cc + 512], b.zerob[:, 0:512], "zpad1", [b.zerob.key], ["KT1"])
    for (r0) in (0, PADK + T):
        for rr in range(0, PADK, 128):
            for cc in range(0, NQ, 512):
                b.store("sp", d["VT1"][r0 + rr:r0 + rr + 128, cc:cc + 512], b.zerob[:, 0:512], "zpad1", [b.zerob.key], ["VT1"])
    for tb in range(T // TB):
        H = b.H[0]
        hv = H[:, 0:DC * TB].rearrange("p (c t) -> p c t", c=DC)
        tok0 = tb * TB
        for tt in range(TB // 128):
            t0 = tok0 + tt * 128
            rmsnorm_T(b, d["XL2"][t0:t0 + 128, :], b.gmix[:, DC:2 * DC], b.gmix.key, hv[:, :, tt * 128:(tt + 1) * 128], H.key,
                      dr=["XL2"])
        tabs = {}
        for ts in range(0, TB, 512):
            A = b.rot("ang", ang)
            CS = b.rot("cst", cst)
            b.load("sp", A[:], d["pos"][tok0 + ts:tok0 + ts + 512].partition_broadcast(32), A.key, [], [A.key])
            s.op("dve", lambda e, A=A: e.tensor_scalar(out=A[:], in0=A[:], scalar1=ropec[:, 0:1], scalar2=None, op0=ALU.mult),
                 reads=[A.key, ropec.key], writes=[A.key])
            KI = b.rot("cki", cki)
            KF = b.rot("ckf", ckf)
            s.op("dve", lambda e, A=A, CS=CS: e.tensor_scalar(out=CS[:, 0, :], in0=A[:], scalar1=1.0 / TWO_PI, scalar2=0.25,
                                                            op0=ALU.mult, op1=ALU.add), reads=[A.key], writes=[CS.key])
            s.op("dve", lambda e, A=A, CS=CS: e.tensor_scalar(out=CS[:, 1, :], in0=A[:], scalar1=1.0 / TWO_PI, scalar2=None,
                                                            op0=ALU.mult), reads=[A.key, CS.key], writes=[CS.key])
            s.op("dve", lambda e, CS=CS, KI=KI: e.tensor_copy(out=KI[:], in_=CS[:]), reads=[CS.key], writes=[KI.key])
            s.op("dve", lambda e, KF=KF, KI=KI: e.tensor_copy(out=KF[:], in_=KI[:]), reads=[KI.key], writes=[KF.key])
            s.op("dve", lambda e, CS=CS, KF=KF: e.tensor_tensor(out=CS[:], in0=CS[:], in1=KF[:], op=ALU.subtract),
                 reads=[CS.key, KF.key], writes=[CS.key])
            s.op("act", lambda e, CS=CS: e.activation(out=CS[:, 0, :], in_=CS[:, 0, :], func=AF.Sin, scale=TWO_PI),
                 reads=[CS.key], writes=[CS.key])
            s.op("act", lambda e, CS=CS: e.activation(out=CS[:, 1, :], in_=CS[:, 1, :], func=AF.Sin, scale=ropec[:, 1:2]),
                 reads=[CS.key, ropec.key], writes=[CS.key])
            tabs[ts] = CS

        def ev_rope(dstname, row_base, col_off):
            def f(P, c0, ncl, ts, nt):
                CS = tabs[ts]
                A_ = b.rot("qa", qa)
                R_ = b.rot("qr", qr)
                TT = b.rot("rt", rt)
                b.copy("act", A_[:, 0:nt], P[:, 0:nt], [P.key], [A_.key])
                b.copy("dve", R_[:, 0:nt], P[:, 0:nt], [P.key], [R_.key])
                P2 = b.rot("P", b.PD)
                s.op("pe", lambda e: e.matmul(P2[0:32, 0:nt], permT[:, 0:32], A_[:, 0:nt], start=True, stop=True),
                     reads=[permT.key, A_.key], writes=[P2.key])
                s.op("dve", lambda e: e.tensor_tensor(out=TT[:, 0:nt], in0=P2[0:32, 0:nt], in1=CS[:, 1, 0:nt], op=ALU.mult),
                     reads=[P2.key, CS.key], writes=[TT.key])
                s.op("dve", lambda e: e.tensor_tensor(out=R_[0:32, 0:nt], in0=A_[0:32, 0:nt], in1=CS[:, 0, 0:nt], op=ALU.mult),
                     reads=[A_.key, CS.key, R_.key], writes=[R_.key])
                s.op("dve", lambda e: e.tensor_tensor(out=R_[0:32, 0:nt], in0=R_[0:32, 0:nt], in1=TT[:, 0:nt], op=ALU.add),
                     reads=[R_.key, TT.key], writes=[R_.key])
                b.store("sp", d[dstname][c0 - row_base:c0 - row_base + ncl, col_off + tok0 + ts:col_off + tok0 + ts + nt],
                        R_[0:ncl, 0:nt], R_.key, [R_.key], [dstname])
            return f

        def ev_v(P, c0, nb, tt, nt):
            S = b.rot("SBF", b.SBF)
            b.copy(b.evac_eng(), S[:, 0:nb], P[:, 0:nb], [P.key], [S.key])
            b.store("sp", d["VT1"][PADK + tok0 + tt:PADK + tok0 + tt + 128, c0 - 2 * NQ:c0 - 2 * NQ + nb], S[:, 0:nb], S.key,
                    [S.key], ["VT1"])
        linear(b, "fm", hv, H.key, TB, w, 0, NQ, ev_rope("QT1", 0, 0))
        linear(b, "fm", hv, H.key, TB, w, NQ, NQ, ev_rope("KT1", NQ, PADK))
        linear(b, "tm", hv, H.key, TB, w, 2 * NQ, NQ, ev_v)


def phase_attn(b, stack):
    cfg, s, d = b.cfg, b.s, b.dram
    T, S_ = cfg.T, cfg.SLOTS
    DILS = (1, 4, 16)
    SBLK = 2048
    NSB = T // SBLK
    sc = 128 ** -0.5
    qb_ = [b.sb(stack, "aq%d" % i, [128, SBLK], BF16) for i in range(2)]
    kb_ = [b.sb(stack, "ak%d" % i, [128, SBLK + 2048], BF16) for i in range(1)]
    kmb = [b.sb(stack, "akmb%d" % i, [1, SBLK + 2048], BF16) for i in range(1)]
    vt = [b.sb(stack, "av%d" % i, [128, 2, 128], BF16) for i in range(8)]
    pt = [b.sb(stack, "ap%d" % i, [128, 256], BF16) for i in range(6)]
    osum = [b.sb(stack, "aos%d" % i, [128, SBLK], F32) for i in range(1)]
    dsum = [b.sb(stack, "ads%d" % i, [128, SBLK], F32) for i in range(1)]
    mo = [b.sb(stack, "amo%d" % i, [128, SBLK], BF16) for i in range(2)]
    qv = b.sb(stack, "aqv", [128, SBLK], F32)
    mab = b.sb(stack, "amab", [128, 256], BF16)
    s.op("dve", lambda e: e.tensor_copy(out=mab[:, 0:128], in_=b.L[:]), reads=[b.L.key], writes=[mab.key])
    s.op("dve", lambda e: e.tensor_copy(out=mab[:, 128:256], in_=b.U[:]), reads=[b.U.key, mab.key], writes=[mab.key])
    for slot in range(S_):
        for sb_ in range(NSB):
            tokS = sb_ * SBLK
            OS = b.rot("aos", osum)
            DS = b.rot("ads", dsum)
            units = []
            for g in (2, 1, 0):
                dil = DILS[g]
                blk = 128 * dil
                for bi in range(SBLK // blk):
                    for r in range(dil):
                        units.append((g, bi, r))
            state = {}

            def stage1(u):
                g, bi, r = u
                dil = DILS[g]
                blk = 128 * dil
                row0 = (g * S_ + slot) * 128
                halo = 64 * dil
                nk = SBLK + 2 * halo
                k0 = PADK + tokS - halo
                if (bi, r) == (0, 0):
                    Q = b.rot("aq", qb_)
                    Kt = b.rot("ak", kb_)
                    KMB = b.rot("akmb", kmb)
                    b.load("sp", Q[:], d["QT1"][row0:row0 + 128, tokS:tokS + SBLK], Q.key, [], [Q.key], dr=["QT1"])
                    b.load("sp", Kt[:, 0:nk], d["KT1"][row0:row0 + 128, k0:k0 + nk], Kt.key, [], [Kt.key], dr=["KT1"])
                    b.load("sp", KMB[0:1, 0:nk], d["kmask"][k0:k0 + nk].rearrange("(o n) -> o n", o=1), KMB.key, [], [KMB.key])
                    state["qk"] = (Q, Kt, KMB)
                Q, Kt, KMB = state["qk"]
                qsl = slice(bi * blk + r, bi * blk + r + 127 * dil + 1, dil)
                kA = slice(bi * blk + r, bi * blk + r + 127 * dil + 1, dil)
                kB = slice(bi * blk + blk + r, bi * blk + blk + r + 127 * dil + 1, dil)
                V = b.rot("av", vt)
                tA = k0 + bi * blk + r
                vrowsA = d["VT1"][tA:tA + 127 * dil + 1:dil, row0:row0 + 128]
                vrowsB = d["VT1"][tA + blk:tA + blk + 127 * dil + 1:dil, row0:row0 + 128]
                b.load("sp", V[:, 0, :], vrowsA, V.key, [], [V.key], dr=["VT1"])
                b.load("sp", V[:, 1, :], vrowsB, V.key, [], [V.key], dr=["VT1"])
                Ps = b.rot("Pa", b.P[0:4])

                def sc_mm(e):
                    e.matmul(Ps[:, 0:128], Kt[:, kA], Q[:, qsl], start=True, stop=False)
                    e.matmul(Ps[:, 0:128], KMB[0:1, kA], b.onesb[0:1, 0:128], start=False, stop=True)
                    e.matmul(Ps[:, 128:256], Kt[:, kB], Q[:, qsl], start=True, stop=False)
                    return e.matmul(Ps[:, 128:256], KMB[0:1, kB], b.onesb[0:1, 0:128], start=False, stop=True)
                s.op("pe", sc_mm, reads=[Kt.key, Q.key, KMB.key, b.onesb.key], writes=[Ps.key])
                PT_ = b.rot("ap", pt)
                s.op("act", lambda e: e.activation(out=PT_[:], in_=Ps[:, 0:256], func=AF.Exp, scale=sc),
                     reads=[Ps.key], writes=[PT_.key])
                s.op("dve", lambda e: e.tensor_tensor(out=PT_[:], in0=PT_[:], in1=mab[:], op=ALU.mult),
                     reads=[PT_.key, mab.key], writes=[PT_.key])
                return (V, PT_, qsl, g)

            def stage2(st):
                V, PT_, qsl, g = st
                Po = b.rot("Pb", b.P[4:8])

                def pv_mm(e):
                    e.matmul(Po[:, 0:128], V[:, 0, :], PT_[:, 0:128], start=True, stop=False)
                    e.matmul(Po[:, 0:128], V[:, 1, :], PT_[:, 128:256], start=False, stop=True)
                    e.matmul(Po[:, 128:256], b.onesb[:], PT_[:, 0:128], start=True, stop=False)
                    return e.matmul(Po[:, 128:256], b.onesb[:], PT_[:, 128:256], start=False, stop=True)
                s.op("pe", pv_mm, reads=[V.key, PT_.key, b.onesb.key], writes=[Po.key])
                if g == 2:
                    s.op("act", lambda e: e.activation(out=OS[:, qsl], in_=Po[:, 0:128], func=AF.Copy),
                         reads=[Po.key], writes=[OS.key])
                    s.op("dve", lambda e: e.tensor_copy(out=DS[:, qsl], in_=Po[:, 128:256]), reads=[Po.key], writes=[DS.key])
                else:
                    s.op("dve", lambda e: e.tensor_tensor(out=OS[:, qsl], in0=Po[:, 0:128], in1=OS[:, qsl], op=ALU.add),
                         reads=[Po.key, OS.key], writes=[OS.key])
                    s.op("dve", lambda e: e.tensor_tensor(out=DS[:, qsl], in0=Po[:, 128:256], in1=DS[:, qsl], op=ALU.add),
                         reads=[Po.key, DS.key], writes=[DS.key])
            pend = []
            for u in units:
                pend.append(stage1(u))
                if len(pend) > 2:
                    stage2(pend.pop(0))
            while pend:
                stage2(pend.pop(0))
            MO = b.rot("amo", mo)
            b.load("sp", qv[:], d["qvalid"][tokS:tokS + SBLK].partition_broadcast(128), qv.key, [], [qv.key])
            s.op("dve", lambda e, DS=DS: e.tensor_scalar(out=DS[:], in0=DS[:], scalar1=1e-30, scalar2=None, op0=ALU.max),
                 reads=[DS.key], writes=[DS.key])
            s.op("dve", lambda e, DS=DS: e.reciprocal(out=DS[:], in_=DS[:]), reads=[DS.key], writes=[DS.key])
            s.op("dve", lambda e, DS=DS: e.tensor_tensor(out=DS[:], in0=DS[:], in1=qv[:], op=ALU.mult),
                 reads=[DS.key, qv.key], writes=[DS.key])
            s.op("dve", lambda e, MO=MO, OS=OS, DS=DS: e.tensor_tensor(out=MO[:], in0=OS[:], in1=DS[:], op=ALU.mult),
                 reads=[OS.key, DS.key], writes=[MO.key])
            b.store("sp", d["MIXT"][slot * 128:(slot + 1) * 128, tokS:tokS + SBLK], MO[:], MO.key, [MO.key], ["MIXT"])


def phase_mixout1(b, stack):
    dense_bufs(b, stack)
    cfg, s, d = b.cfg, b.s, b.dram
    T, D, DC = cfg.T, cfg.D, cfg.DC
    TBM = 512
    mv = d["MIXT"].rearrange("(c p) t -> p c t", p=128)
    for tb in range(T // TBM):
        H = b.H[0]
        hv = H[:, 0:DC * TBM].rearrange("p (c t) -> p c t", c=DC)
        b.load("sp", hv, mv[:, :, tb * TBM:(tb + 1) * TBM], H.key, [], [H.key], dr=["MIXT"])
        outproj_residual(b, hv, H.key, TBM, d["b_od_w_out"][0], "XL2", "XL3", tb * TBM)


def phase_final(b, stack):
    dense_bufs(b, stack, need_h=False)
    cfg, s, d = b.cfg, b.s, b.dram
    T, D = cfg.T, cfg.D
    gf = b.sb(stack, "gfinb", [128, D], F32)
    s.dma("sp", lambda e: e.dma_start(out=gf[:], in_=d["norm_final"].rearrange("(o n) -> o n", o=1).partition_broadcast(128)),
          gf.key, writes=[gf.key])
    yo = [b.sb(stack, "yo%d" % i, [128, D], F32) for i in range(2)]
    for t0 in range(0, T, 128):
        X = b.rot("X", b.X)
        XB = b.rot("XB", b.XB)
        SC = b.rot("SC", b.SC)
        Y = b.rot("yo", yo)
        b.load("sp", X[:, 0:D], d["XL4"][t0:t0 + 128, :], X.key, [], [X.key], dr=["XL4"])
        s.op("act", lambda e, X=X, XB=XB, SC=SC: e.activation(out=XB[:, 0:D], in_=X[:, 0:D], func=AF.Square, accum_out=SC[:, 0:1]),
             reads=[X.key], writes=[XB.key, SC.key])
        s.op("act", lambda e, SC=SC: e.activation(out=SC[:, 1:2], in_=SC[:, 0:1], func=AF.Sqrt, scale=1.0 / D, bias=EPS),
             reads=[SC.key], writes=[SC.key])
        s.op("dve", lambda e, SC=SC: e.reciprocal(out=SC[:, 2:3], in_=SC[:, 1:2]), reads=[SC.key], writes=[SC.key])
        s.op("dve", lambda e, X=X, SC=SC, Y=Y: e.scalar_tensor_tensor(out=Y[:], in0=X[:, 0:D], scalar=SC[:, 2:3], in1=gf[:],
                                                                   op0=ALU.mult, op1=ALU.mult),
             reads=[X.key, SC.key, gf.key], writes=[Y.key])
        b.store("sp", d["y"][t0:t0 + 128, :], Y[:], Y.key, [Y.key], ["y"], final=True)


def all_phases():
    return [phase_wcast, phase_A, phase_gdn, phase_mlstm, phase_mixout0, phase_ffn(0, "XL1", "XL2"), phase_qkv1, phase_attn, phase_mixout1,
            phase_ffn(1, "XL3", "XL4"), phase_final]


N_CORES = 4


def kernel(**inputs):
    import ml_dtypes
    cfg = Cfg()
    T = cfg.T
    xp = np.asarray(inputs["x_prompt"], np.float32)
    xs = np.asarray(inputs["x_sample"], np.float32)
    shared = {k: np.ascontiguousarray(np.asarray(v, np.float32)) for k, v in inputs.items()
              if k not in ("x_prompt", "x_sample")}
    in_maps = []
    lens = []
    for c in range(N_CORES):
        seq = xp[c] if c < 2 else xs[c - 2]
        n = seq.shape[0]
        x = np.zeros((T, cfg.D), np.float32)
        x[:n] = seq
        km = np.full((T + 2048,), -BIG, np.float32)
        km[PADK:PADK + n] = 0.0
        m = dict(shared)
        m["x"] = x
        m["kmask"] = km.astype(ml_dtypes.bfloat16)
        m["pos"] = np.arange(T, dtype=np.float32)
        m["ropec"] = rope_consts()
        qv = np.zeros((T,), np.float32)
        qv[:n] = 1.0
        m["qvalid"] = qv
        in_maps.append(m)
        lens.append(n)
    b = build(cfg, all_phases())
    res = run_bass_kernel_spmd(b.nc, in_maps, core_ids=list(range(N_CORES)))
    ys = [np.asarray(res.results[c]["y"], np.float32)[:lens[c]] for c in range(N_CORES)]
    return (np.stack(ys[0:2], 0), np.stack(ys[2:4], 0))
```

```python
import numpy as np
import concourse.bass as bass
import concourse.mybir as mybir
from concourse.bass_utils import run_bass_kernel_spmd

F32 = mybir.dt.float32
BF16 = mybir.dt.bfloat16
AF = mybir.ActivationFunctionType
ALU = mybir.AluOpType
AX = mybir.AxisListType

EPS = 1e-6
BIG = 30000.0


class Cfg:
    def __init__(self, D=2048, GH=8, MH=4, SLOTS=16, DFF=5632, T=16384, TBLK=1024):
        self.D, self.GH, self.MH, self.SLOTS, self.DFF, self.T, self.TBLK = D, GH, MH, SLOTS, DFF, T, TBLK
        self.DC = D // 128
        self.A_QKV = GH * 384
        self.A_Z = GH * 128
        self.A_G = 4 * GH
        self.B_QKV = MH * 512
        self.B_O = MH * 256
        self.B_G = 4 * MH
        self.c1 = self.A_QKV
        self.c2 = self.c1 + self.A_Z
        self.c3 = self.c2 + self.A_G
        self.c4 = self.c3 + self.B_QKV
        self.c5 = self.c4 + self.B_O
        self.EVEN_IN = self.c5 + self.B_G
        self.EVEN_MIX = GH * 128 + MH * 256
        self.ODD_IN = 9 * SLOTS * 128
        self.ODD_MIX = SLOTS * 128
        assert self.EVEN_MIX == D and self.ODD_MIX == D


class Sched:
    ENG = ("pe", "act", "dve", "pool", "sp")

    def __init__(self, nc):
        self.nc = nc
        self.ops = {e: [] for e in self.ENG}
        self.last_w = {}
        self.reads = {}
        self.waited = {}
        self.dma_val = {}
        self.dma_last = {}
        self.final_events = []
        self.dwl = {}
        self.excl = set()
        self.sem_slot = {}
        self.sem_free = []
        self.n_slots = 0

    def _deps(self, eng, reads, writes):
        deps = []
        for k in reads:
            if k in self.last_w:
                deps.append(self.last_w[k])
        for k in writes:
            if k in self.last_w:
                deps.append(self.last_w[k])
            deps.extend(self.reads.get(k, ()))
        return deps

    def _add_waits(self, eng, deps):
        waits = []
        for ev in deps:
            kind, key, val = ev
            if kind == "eng" and key == eng and eng == "pe":
                continue
            wk = (eng, kind, key)
            if self.waited.get(wk, -1) >= val:
                continue
            self.waited[wk] = val
            waits.append(ev)
            if kind == "eng":
                self.ops[key][val]["inc"] = True
        return waits

    def _commit(self, ev, reads, writes):
        for k in writes:
            self.last_w[k] = ev
            self.reads[k] = []
        for k in reads:
            self.reads.setdefault(k, []).append(ev)

    def op(self, eng, fn, reads=(), writes=()):
        ex = [k for k in reads if k in self.excl]
        if ex:
            reads = [k for k in reads if k not in self.excl]
            writes = list(writes) + [k for k in ex if k not in writes]
        deps = self._deps(eng, reads, writes)
        waits = self._add_waits(eng, deps)
        idx = len(self.ops[eng])
        self.ops[eng].append(dict(waits=waits, fn=fn, inc=False, dma=None))
        ev = ("eng", eng, idx)
        self._commit(ev, reads, writes)
        return ev

    def dma(self, q, fn, semkey, reads=(), writes=(), final=False, dr=(), dw=()):
        deps = self._deps(q, reads, writes)
        if semkey not in self.sem_slot:
            if self.sem_free:
                self.sem_slot[semkey] = self.sem_free.pop(0)
            else:
                self.sem_slot[semkey] = self.n_slots
                self.n_slots += 1
        semkey = self.sem_slot[semkey]
        if semkey in self.dma_last:
            deps.append(self.dma_last[semkey])
        for k in dr:
            deps.extend(self.dwl.get(k, {}).values())
        for k in dw:
            deps.extend(self.reads.get(k, ()))
        waits = self._add_waits(q, deps)
        v = self.dma_val.get(semkey, 0) + 16
        self.dma_val[semkey] = v
        self.ops[q].append(dict(waits=waits, fn=fn, inc=False, dma=(semkey, v)))
        ev = ("dma", semkey, v)
        self.dma_last[semkey] = ev
        self._commit(ev, list(reads) + list(dr), writes)
        for k in dw:
            self.dwl.setdefault(k, {})[semkey] = ev
            self.reads[k] = []
        if final:
            self.final_events.append(ev)
        return ev

    def barrier(self):
        evs = []
        for e in self.ENG:
            if self.ops[e]:
                for idx in range(len(self.ops[e]) - 1, -1, -1):
                    o = self.ops[e][idx]
                    if o["fn"] is not None and o["dma"] is None:
                        evs.append(("eng", e, idx))
                        break
        evs.extend(self.dma_last.values())
        for f in self.ENG:
            waits = self._add_waits(f, evs)
            self.ops[f].append(dict(waits=waits, fn=None, inc=False, dma=None))
        self.sem_free.extend(sorted(set(self.sem_slot.values())))
        self.sem_slot = {}

    def finish(self):
        waits = self._add_waits("sp", self.final_events)
        self.ops["sp"].append(dict(waits=waits, fn=None, inc=False, dma=None))

    def emit(self, stack):
        nc = self.nc
        esem = {e: stack.enter_context(nc.semaphore("s_" + e)) for e in self.ENG}
        dsem = {}
        for k in range(self.n_slots):
            dsem[k] = stack.enter_context(nc.semaphore("d_%d" % k))
        cnt = {}
        for e in self.ENG:
            c = 0
            arr = []
            for o in self.ops[e]:
                if o["inc"]:
                    c += 1
                arr.append(c)
            cnt[e] = arr
        block = stack.enter_context(nc.Block())

        def run(e, engobj):
            for o in self.ops[e]:
                for (kind, key, val) in o["waits"]:
                    if kind == "eng":
                        engobj.wait_ge(esem[key], cnt[key][val])
                    else:
                        engobj.wait_ge(dsem[key], val)
                if o["fn"] is None:
                    continue
                ins = o["fn"](engobj)
                if o["dma"] is not None:
                    ins.then_inc(dsem[o["dma"][0]], 16)
                elif o["inc"]:
                    ins.then_inc(esem[e], 1)

        block.sync(lambda g: run("sp", g))
        block.scalar(lambda g: run("act", g))
        block.vector(lambda g: run("dve", g))
        block.gpsimd(lambda g: run("pool", g))
        block.tensor(lambda g: run("pe", g))
        n = {e: len(self.ops[e]) for e in self.ENG}
        return n, len(dsem)


class Buf:
    def __init__(self, t, key):
        self.t, self.key = t, key

    def __getitem__(self, k):
        return self.t[k]


class B:
    def __init__(self, cfg, debug_outs=()):
        self.cfg = cfg
        self.nc = bass.Bass("TRN2", target_bir_lowering=False)
        self.s = Sched(self.nc)
        self.debug_outs = set(debug_outs)
        self.rr = {}
        self.dram = {}

    def din(self, name, shape, dt=F32):
        self.dram[name] = self.nc.dram_tensor(name, list(shape), dt, kind="ExternalInput").ap()
        return self.dram[name]

    def dout(self, name, shape, dt=F32):
        self.dram[name] = self.nc.dram_tensor(name, list(shape), dt, kind="ExternalOutput").ap()
        return self.dram[name]

    def dscr(self, name, shape, dt):
        kind = "ExternalOutput" if name in self.debug_outs else "Internal"
        self.dram[name] = self.nc.dram_tensor(name, list(shape), dt, kind=kind).ap()
        return self.dram[name]

    def sb(self, stack, name, shape, dt):
        t = stack.enter_context(self.nc.sbuf_tensor(name, list(shape), dt))
        return Buf(t, name)

    def ps(self, stack, name, shape, dt):
        t = stack.enter_context(self.nc.psum_tensor(name, list(shape), dt))
        self.s.excl.add(name)
        return Buf(t, name)

    def rot(self, name, lst):
        i = self.rr.get(name, 0)
        self.rr[name] = i + 1
        return lst[i % len(lst)]

    def evac_eng(self):
        return self.rot("evac", ["act", "dve"])

    def copy(self, eng, out, in_, reads, writes, scale=None):
        if eng == "act":
            if scale is None:
                fn = lambda e: e.activation(out=out, in_=in_, func=AF.Copy)
            else:
                fn = lambda e: e.activation(out=out, in_=in_, func=AF.Identity, scale=scale)
        else:
            if scale is None:
                fn = lambda e: e.tensor_copy(out=out, in_=in_)
            else:
                fn = lambda e: e.tensor_scalar(out=out, in0=in_, scalar1=scale, scalar2=None, op0=ALU.mult)
        return self.s.op(eng, fn, reads=reads, writes=writes)

    def load(self, q, out, in_, semkey, reads, writes, dr=()):
        return self.s.dma(q, lambda e: e.dma_start(out=out, in_=in_, allow_slow_non_contiguous=True), semkey, reads=reads, writes=writes, dr=dr)

    def store(self, q, out, in_, semkey, reads, dw, final=False):
        q = "pool"
        return self.s.dma(q, lambda e: e.dma_start(out=out, in_=in_, allow_slow_non_contiguous=True), semkey, reads=reads, writes=(), dw=dw,
                          final=final)


def build_common(b, stack):
    cfg = b.cfg
    s = b.s
    b.identb = b.sb(stack, "identb", [128, 128], BF16)
    b.identf = b.sb(stack, "identf", [128, 128], F32)
    b.U = b.sb(stack, "U", [128, 128], F32)
    b.L = b.sb(stack, "L", [128, 128], F32)
    b.onesf = b.sb(stack, "onesf", [128, 128], F32)
    b.onesb = b.sb(stack, "onesb", [128, 128], BF16)
    b.zerob = b.sb(stack, "zerob", [128, 512], BF16)

    def mk_tri(buf, cmp_, dt_fill=1.0):
        pass

    def init_ident(buf):
        s.op("pool", lambda e: e.memset(buf[:], 0.0), writes=[buf.key])
        s.op("pool", lambda e: e.affine_select(out=buf[:], in_=buf[:], pattern=[[-1, 128]], compare_op=ALU.not_equal,
                                              fill=1.0, base=0, channel_multiplier=1), reads=[buf.key], writes=[buf.key])
    init_ident(b.identb)
    init_ident(b.identf)
    s.op("pool", lambda e: e.memset(b.onesf[:], 1.0), writes=[b.onesf.key])
    s.op("pool", lambda e: e.memset(b.onesb[:], 1.0), writes=[b.onesb.key])
    s.op("pool", lambda e: e.memset(b.zerob[:], 0.0), writes=[b.zerob.key])
    s.op("pool", lambda e: e.memset(b.U[:], 1.0), writes=[b.U.key])
    s.op("pool", lambda e: e.affine_select(out=b.U[:], in_=b.U[:], pattern=[[1, 128]], compare_op=ALU.is_ge,
                                          fill=0.0, base=0, channel_multiplier=-1), reads=[b.U.key], writes=[b.U.key])
    s.op("pool", lambda e: e.memset(b.L[:], 1.0), writes=[b.L.key])
    s.op("pool", lambda e: e.affine_select(out=b.L[:], in_=b.L[:], pattern=[[-1, 128]], compare_op=ALU.is_ge,
                                          fill=0.0, base=0, channel_multiplier=1), reads=[b.L.key], writes=[b.L.key])
    b.gtmp = [b.sb(stack, "g%d" % i, [128, 128], F32) for i in range(36)]
    b.BMf = b.sb(stack, "BMf", [128, 128], F32)
    b.BMb = b.sb(stack, "BMb", [128, 128], F32)
    b.SMf = b.sb(stack, "SMf", [128, 128], F32)
    b.SMb = b.sb(stack, "SMb", [128, 128], F32)
    SMf, SMb, BMf, BMb = b.SMf, b.SMb, b.BMf, b.BMb
    s.op("dve", lambda e: e.tensor_scalar(out=SMf[:], in0=b.U[:], scalar1=-1.0, scalar2=1.0, op0=ALU.mult, op1=ALU.add),
         reads=[b.U.key], writes=[SMf.key])
    s.op("dve", lambda e: e.tensor_scalar(out=SMb[:], in0=b.L[:], scalar1=-1.0, scalar2=1.0, op0=ALU.mult, op1=ALU.add),
         reads=[b.L.key], writes=[SMb.key])
    s.op("dve", lambda e: e.tensor_scalar(out=BMf[:], in0=SMb[:], scalar1=BIG, scalar2=None, op0=ALU.mult),
         reads=[SMb.key], writes=[BMf.key])
    s.op("dve", lambda e: e.tensor_scalar(out=BMb[:], in0=SMf[:], scalar1=BIG, scalar2=None, op0=ALU.mult),
         reads=[SMf.key], writes=[BMb.key])
    b.P = [b.ps(stack, "P%d" % i, [128, 512], F32) for i in range(8)]

    class _PTView:
        def __init__(self, buf):
            self.key = buf.key
            self.v = buf[:, :].bitcast(BF16).rearrange("p (c t) -> p c t", c=8)

        def __getitem__(self, k):
            return self.v[k]
    b.PT = [_PTView(b.P[6]), _PTView(b.P[7])]
    b.PD = b.P[0:6]


def dense_bufs(b, stack, need_h=True):
    p = "p%d_" % b.pid
    if need_h:
        b.H = [b.sb(stack, p + "H", [128, 24576], BF16)]
    b.W = [b.sb(stack, p + "W%d" % i, [128, 8192], BF16) for i in range(2)]
    b.X = [b.sb(stack, p + "X%d" % i, [128, 2048], F32) for i in range(2)]
    b.XB = [b.sb(stack, p + "XB%d" % i, [128, 2048], BF16) for i in range(2)]
    b.SF = [b.sb(stack, p + "SF%d" % i, [128, 512], F32) for i in range(4)]
    b.SBF = [b.sb(stack, p + "SBF%d" % i, [128, 512], BF16) for i in range(4)]
    b.SC = [b.sb(stack, p + "SC%d" % i, [128, 8], F32) for i in range(8)]


def rmsnorm_T(b, src, gT, gkey, dst, dstkey, src_reads=(), dr=(), preloaded=None, halo_dst=None):
    cfg, s = b.cfg, b.s
    D, DC = cfg.D, cfg.DC
    XB = b.rot("XB", b.XB)
    SC = b.rot("SC", b.SC)
    if preloaded is not None:
        X = preloaded
    else:
        X = b.rot("X", b.X)
        b.load("sp", X[:, 0:D], src, X.key, reads=src_reads, writes=[X.key], dr=dr)
    s.op("act", lambda e: e.activation(out=XB[:, 0:D], in_=X[:, 0:D], func=AF.Square, accum_out=SC[:, 0:1]),
         reads=[X.key], writes=[XB.key, SC.key])
    s.op("act", lambda e: e.activation(out=SC[:, 1:2], in_=SC[:, 0:1], func=AF.Sqrt, scale=1.0 / D, bias=EPS),
         reads=[SC.key], writes=[SC.key])
    s.op("dve", lambda e: e.reciprocal(out=SC[:, 2:3], in_=SC[:, 1:2]), reads=[SC.key], writes=[SC.key])
    s.op("act", lambda e: e.activation(out=XB[:, 0:D], in_=X[:, 0:D], func=AF.Identity, scale=SC[:, 2:3]),
         reads=[X.key, SC.key], writes=[XB.key])
    for cg in range(0, DC, 8):
        n = min(8, DC - cg)
        PT = b.rot("PT", b.PT)

        def tr(e, cg=cg, n=n, PT=PT):
            for c in range(n):
                ins = e.transpose(PT[:, c, :], XB[:, (cg + c) * 128:(cg + c + 1) * 128], b.identb[:])
            return ins
        s.op("pe", tr, reads=[XB.key, b.identb.key], writes=[PT.key])
        if halo_dst is not None:
            for hi, hd in enumerate(halo_dst):
                s.op("dve", lambda e, cg=cg, n=n, PT=PT, hi=hi, hd=hd: e.tensor_tensor(
                    out=hd[:, cg:cg + n, :], in0=PT[:, 0:n, hi:hi + 1],
                    in1=gT[:, cg:cg + n].unsqueeze(2), op=ALU.mult),
                    reads=[PT.key, gkey], writes=[dstkey])
            continue
        s.op("dve", lambda e, cg=cg, n=n, PT=PT: e.tensor_tensor(
            out=dst[:, cg:cg + n, :], in0=PT[:, 0:n, :],
            in1=gT[:, cg:cg + n].unsqueeze(2).to_broadcast([128, n, 128]), op=ALU.mult),
            reads=[PT.key, gkey], writes=[dstkey])


def linear(b, mode, hv, hkey, ntok, w, col0, ncols, evac):
    s = b.s
    KC = hv.shape[1]
    wv = w.rearrange("(c p) n -> p c n", p=128)
    CBW = 512 if KC <= 16 else 128
    for cb in range(0, ncols, CBW):
        nb = min(CBW, ncols - cb)
        W = b.rot("W", b.W)
        Wv = W[:, 0:KC * CBW].rearrange("p (c n) -> p c n", c=KC)
        b.load("sp", Wv[:, :, 0:nb], wv[:, :, col0 + cb:col0 + cb + nb], W.key, reads=[], writes=[W.key], dr=[w.tensor.name])
        if mode == "fm":
            for ct in range(0, nb, 128):
                ncl = min(128, nb - ct)
                for ts in range(0, ntok, 512):
                    nt = min(512, ntok - ts)
                    P = b.rot("P", b.PD)

                    def mm(e, P=P, ct=ct, ncl=ncl, ts=ts, nt=nt, Wv=Wv):
                        for c in range(KC):
                            ins = e.matmul(P[0:ncl, 0:nt], Wv[:, c, ct:ct + ncl], hv[:, c, ts:ts + nt],
                                           start=(c == 0), stop=(c == KC - 1))
                        return ins
                    s.op("pe", mm, reads=[W.key, hkey], writes=[P.key])
                    evac(P, col0 + cb + ct, ncl, ts, nt)
        else:
            for tt in range(0, ntok, 128):
                P = b.rot("P", b.PD)

                def mm(e, P=P, tt=tt, nb=nb, Wv=Wv):
                    for c in range(KC):
                        ins = e.matmul(P[:, 0:nb], hv[:, c, tt:tt + 128], Wv[:, c, 0:nb],
                                       start=(c == 0), stop=(c == KC - 1))
                    return ins
                s.op("pe", mm, reads=[W.key, hkey], writes=[P.key])
                evac(P, col0 + cb, nb, tt, 128)


def phase_A(b, stack):
    dense_bufs(b, stack)
    cfg, s = b.cfg, b.s
    T, TB, D, DC, GH, MH = cfg.T, cfg.TBLK, cfg.D, cfg.DC, cfg.GH, cfg.MH
    d = b.dram
    x, w = d["x"], d["b_ev_w_in"]
    for tb in range(T // TB):
        H = b.rot("H", b.H)
        hv = H[:, 0:DC * TB].rearrange("p (c t) -> p c t", c=DC)
        for tt in range(TB // 128):
            t0 = tb * TB + tt * 128
            rmsnorm_T(b, x[t0:t0 + 128, :], b.gmix[:, 0:DC], b.gmix.key, hv[:, :, tt * 128:(tt + 1) * 128], H.key)
        tok0 = tb * TB

        def ev_fm(dst, row0, scale=None):
            def f(P, c0, ncl, ts, nt):
                S = b.rot("SBF", b.SBF)
                b.copy(b.evac_eng(), S[0:ncl, 0:nt], P[0:ncl, 0:nt], [P.key], [S.key], scale=scale)
                b.store("sp", dst(c0 - row0, ncl, tok0 + ts, nt), S[0:ncl, 0:nt], S.key, [S.key], [dst.__name__])
            return f

        def ev_tm(dstname, col_base, dt):
            def f(P, c0, nb, tt, nt):
                S = b.rot("SBF", b.SBF) if dt == BF16 else b.rot("SF", b.SF)
                b.copy(b.evac_eng(), S[:, 0:nb], P[:, 0:nb], [P.key], [S.key])
                b.store("sp", d[dstname][tok0 + tt:tok0 + tt + 128, c0 - col_base:c0 - col_base + nb], S[:, 0:nb],
                        S.key, [S.key], [dstname])
            return f

        def QKVA_T(r, n, t, nt):
            return d["QKVA_T"][r:r + n, 1 + t:1 + t + nt]

        def QB_T(r, n, t, nt):
            return d["QB_T"][r:r + n, t:t + nt]

        def KB_T(r, n, t, nt):
            return d["KB_T"][r:r + n, t:t + nt]
        linear(b, "fm", hv, H.key, TB, w[0], 0, cfg.A_QKV, ev_fm(QKVA_T, 0))
        linear(b, "tm", hv, H.key, TB, w[0], cfg.c1, cfg.A_Z, ev_tm("Z", cfg.c1, BF16))
        linear(b, "tm", hv, H.key, TB, w[0], cfg.c2, cfg.A_G, ev_tm("GA", cfg.c2, F32))
        linear(b, "fm", hv, H.key, TB, w[0], cfg.c3, MH * 128, ev_fm(QB_T, cfg.c3, scale=128 ** -0.5))
        linear(b, "fm", hv, H.key, TB, w[0], cfg.c3 + MH * 128, MH * 128, ev_fm(KB_T, cfg.c3 + MH * 128))
        linear(b, "tm", hv, H.key, TB, w[0], cfg.c3 + MH * 128, MH * 384, ev_tm("KVB", cfg.c3 + MH * 128, BF16))
        linear(b, "tm", hv, H.key, TB, w[0], cfg.c4, cfg.B_O, ev_tm("OB", cfg.c4, BF16))
        linear(b, "tm", hv, H.key, TB, w[0], cfg.c5, cfg.B_G, ev_tm("GB", cfg.c5, F32))


def declare_io(b, stack):
    cfg = b.cfg
    T, D = cfg.T, cfg.D
    b.din("x", [T, D])
    b.din("norm_mix", [2, D])
    b.din("ev_w_in", [1, D, cfg.EVEN_IN])
    b.din("gdn_conv", [1, 3, cfg.A_QKV])
    b.din("gdn_a_log", [1, 2, cfg.GH])
    b.din("gdn_dt_bias", [1, 2, cfg.GH])
    b.din("gdn_norm", [1, 128])
    b.din("ml_gate_bias", [1, 2, 2, cfg.MH])
    b.din("ml_norm", [1, cfg.MH * 256])
    b.din("ev_w_out", [1, cfg.EVEN_MIX, D])
    b.din("od_w_in", [1, D, cfg.ODD_IN])
    b.din("od_w_out", [1, cfg.ODD_MIX, D])
    b.din("norm_ffn", [2, D])
    b.din("ffn_w_up", [2, D, 2 * cfg.DFF])
    b.din("ffn_conv", [2, 3, cfg.DFF])
    b.din("ffn_conv_b", [2, cfg.DFF])
    b.din("ffn_w_down", [2, cfg.DFF, D])
    b.din("norm_final", [D])
    for wn, shp in (("ev_w_in", [1, D, cfg.EVEN_IN]), ("ev_w_out", [1, cfg.EVEN_MIX, D]), ("od_w_in", [1, D, cfg.ODD_IN]),
                    ("od_w_out", [1, cfg.ODD_MIX, D]), ("ffn_w_up", [2, D, 2 * cfg.DFF]), ("ffn_w_down", [2, cfg.DFF, D])):
        b.dscr("b_" + wn, shp, BF16)
    b.din("kmask", [T + 2048], BF16)
    b.din("pos", [T])
    b.dout("y", [T, D])
    GH, MH = cfg.GH, cfg.MH
    b.dscr("QKVA_T", [cfg.A_QKV, T + 2], BF16)
    b.dscr("Z", [T, cfg.A_Z], BF16)
    b.dscr("GA", [T, cfg.A_G], F32)
    b.dscr("QB_T", [MH * 128, T], BF16)
    b.dscr("KB_T", [MH * 128, T], BF16)
    b.dscr("KVB", [T, MH * 384], BF16)
    b.dscr("OB", [T, cfg.B_O], BF16)
    b.dscr("GB", [T, cfg.B_G], F32)
    b.dscr("OA0", [T, GH * 128], F32)
    b.dscr("OA1", [T, GH * 128], F32)
    b.dscr("HB0", [T, MH * 256], F32)
    b.dscr("HB1", [T, MH * 256], F32)
    b.dscr("XL1", [T, D], F32)
    b.dscr("XL2", [T, D], F32)
    b.dscr("XL3", [T, D], F32)
    b.dscr("XL4", [T, D], F32)
    NQ = 3 * cfg.SLOTS * 128
    b.dscr("QT1", [NQ, T], BF16)
    b.dscr("KT1", [NQ, T + 2048], BF16)
    b.dscr("VT1", [T + 2048, NQ], BF16)
    b.dscr("MIXT", [D, T], BF16)
    b.din("ropec", [32, 2])
    b.din("qvalid", [T])
    DC = cfg.DC
    b.gmix = b.sb(stack, "gmix", [128, 2 * DC], F32)
    b.gffn = b.sb(stack, "gffn", [128, 2 * DC], F32)
    b.gfin = b.sb(stack, "gfin", [128, DC], F32)
    d = b.dram
    with b.nc.allow_non_contiguous_dma(reason="tiny param transposes"):
        pass
    for l in range(2):
        b.s.dma("sp", lambda e, l=l: e.dma_start(out=b.gmix[:, l * DC:(l + 1) * DC],
                                                 in_=d["norm_mix"][l].rearrange("(c p) -> p c", p=128),
                                                 allow_slow_non_contiguous=True), "gmix", writes=["gmix"])
        b.s.dma("sp", lambda e, l=l: e.dma_start(out=b.gffn[:, l * DC:(l + 1) * DC],
                                                 in_=d["norm_ffn"][l].rearrange("(c p) -> p c", p=128),
                                                 allow_slow_non_contiguous=True), "gffn", writes=["gffn"])
    b.s.dma("sp", lambda e: e.dma_start(out=b.gfin[:, 0:DC], in_=d["norm_final"].rearrange("(c p) -> p c", p=128),
                                        allow_slow_non_contiguous=True), "gfin", writes=["gfin"])


def phase_wcast(b, stack):
    d = b.dram
    for wn in ("ev_w_in", "ffn_w_up", "ffn_w_down", "ev_w_out", "od_w_in", "od_w_out"):
        src, dst = d[wn], d["b_" + wn]
        L, R, C = src.shape
        for l in range(L):
            for r0 in range(0, R, 512):
                r1 = min(R, r0 + 512)
                b.s.dma("pool", lambda e, l=l, r0=r0, r1=r1, src=src, dst=dst: e.dma_start(out=dst[l, r0:r1, :], in_=src[l, r0:r1, :]),
                        "wcast", dw=["b_" + wn])


def build(cfg, phases, debug_outs=()):
    from contextlib import ExitStack
    b = B(cfg, debug_outs)
    stack = ExitStack()
    with stack:
        declare_io(b, stack)
        build_common(b, stack)
        for ph in phases:
            with ExitStack() as pst:
                b.pid = getattr(b, "pid", 0) + 1
                ph(b, pst)
                b.s.barrier()
        b.s.finish()
        n, nd = b.s.emit(stack)
        print("ops", n, "dma sems", nd)
    return b


class PHalf:
    def __init__(self, buf, off):
        self.buf, self.off, self.key = buf, off, buf.key

    def __getitem__(self, k):
        rows, cols = k
        assert cols.step is None
        return self.buf[rows, self.off + cols.start:self.off + cols.stop]


def run_interleaved(b, groups, group_pre, unit_fn, width):
    pending = []
    gi = 0
    active = {}
    busy = {}
    free = list(range(width))
    while True:
        while free and (pending or gi < len(groups)):
            if not pending:
                pending = list(group_pre(*groups[gi]))
                gi += 1
            chain = (pending[0][0], pending[0][2])
            if chain in busy.values():
                break
            slot = free.pop(0)
            busy[slot] = chain
            active[slot] = unit_fn(slot, *pending.pop(0))
        if not active:
            break
        for slot in sorted(active):
            try:
                next(active[slot])
            except StopIteration:
                del active[slot]
                del busy[slot]
                free.append(slot)


def mm1(b, P, n, lhsT, rhs, reads, m=128):
    return b.s.op("pe", lambda e: e.matmul(P[0:m, 0:n], lhsT, rhs, start=True, stop=True), reads=reads, writes=[P.key])


class _Stop(Exception):
    pass


def chk(n):
    import os
    if os.environ.get("GDN_STOP", "") == str(n):
        raise _Stop()


def phase_gdn(b, stack):
    try:
        phase_gdn_(b, stack)
    except _Stop:
        print("GDN stopped early")


def phase_gdn_(b, stack):
    cfg, s, d = b.cfg, b.s, b.dram
    T, GH = cfg.T, cfg.GH
    NCH = T // 128
    NSL = 4
    tmps = [b.gtmp] + [[b.sb(stack, "g%d_%d" % (j, i), [128, 128], F32) for i in range(36)] for j in range(1, NSL)]
    wides = [[b.sb(stack, "gw%d_%d" % (j, i), [128, 384], F32) for i in range(3)] for j in range(NSL)]
    xins = [[b.sb(stack, "gx%d_%d" % (j, i), [128, 3, 130], BF16) for i in range(2)] for j in range(NSL)]
    gt = [b.sb(stack, "gt%d" % i, [128, 4 * GH], F32) for i in range(4)]
    gqv = [b.sb(stack, "gqv%d" % i, [128, 128], F32) for i in range(4)]
    gsm = [b.sb(stack, "gs%d" % i, [128, 6 * GH], F32) for i in range(4)]
    cols = [[b.sb(stack, "gc%d_%d" % (j, i), [128, 8], F32) for i in range(4)] for j in range(NSL)]
    oos = [[b.sb(stack, "go%d_%d" % (j, i), [128, 128], F32) for i in range(2)] for j in range(NSL)]
    S = {(h, dd, p): b.sb(stack, "S%d_%d_%d" % (h, dd, p), [128, 128], F32) for h in range(GH) for dd in range(2)
         for p in range(2)}
    cw = b.sb(stack, "gcw", [128, GH, 9], F32)
    dg = b.sb(stack, "gdg", [128, GH * 9, 128], BF16)
    ea = b.sb(stack, "gea", [128, 2, GH], F32)
    dtb = b.sb(stack, "gdtb", [128, 2, GH], F32)
    BMf, BMb, SMf, SMb = b.BMf, b.BMb, b.SMf, b.SMb
    for h in range(GH):
        for a in range(3):
            r0 = a * GH * 128 + h * 128
            s.dma("sp", lambda e, h=h, a=a, r0=r0: e.dma_start(
                out=cw[:, h, a * 3:(a + 1) * 3], in_=d["gdn_conv"][0][:, r0:r0 + 128].rearrange("j p -> p j"),
                allow_slow_non_contiguous=True), cw.key, writes=[cw.key])
    s.dma("sp", lambda e: e.dma_start(out=ea[:], in_=d["gdn_a_log"][0:1].partition_broadcast(128)), ea.key, writes=[ea.key])
    s.dma("sp", lambda e: e.dma_start(out=dtb[:], in_=d["gdn_dt_bias"][0:1].partition_broadcast(128)), dtb.key,
          writes=[dtb.key])
    s.op("act", lambda e: e.activation(out=ea[:], in_=ea[:], func=AF.Exp), reads=[ea.key], writes=[ea.key])
    for h in range(GH):
        for a in range(3):
            for j in range(3):
                s.op("dve", lambda e, h=h, a=a, j=j: e.tensor_scalar(
                    out=dg[:, h * 9 + a * 3 + j, :], in0=b.identb[:], scalar1=cw[:, h, a * 3 + j:a * 3 + j + 1],
                    scalar2=None, op0=ALU.mult), reads=[b.identb.key, cw.key], writes=[dg.key])
        for dd in range(2):
            s.op("pool", lambda e, h=h, dd=dd: e.memset(S[(h, dd, 0)][:], 0.0), writes=[S[(h, dd, 0)].key])
    for r in range(0, cfg.A_QKV, 128):
        b.store("sp", d["QKVA_T"][r:r + 128, 0:1], b.zerob[:, 0:1], "zpad", [b.zerob.key], ["QKVA_T"])
        b.store("sp", d["QKVA_T"][r:r + 128, T + 1:T + 2], b.zerob[:, 0:1], "zpad", [b.zerob.key], ["QKVA_T"])
    qkv3 = d["QKVA_T"].rearrange("(a h p) t -> p a h t", a=3, h=GH)
    units_q = [(step, dd) for step in range(NCH) for dd in range(2)]

    def group_pre(step, dd):
        if True:
            c = step if dd == 0 else NCH - 1 - step
            t0 = c * 128
            Tri = b.U if dd == 0 else b.L
            BM = BMf if dd == 0 else BMb
            SM = SMf if dd == 0 else SMb
            GT = b.rot("ggt", gt)
            GS = b.rot("ggs", gsm)
            b.load("sp", GT[:], d["GA"][t0:t0 + 128, :], GT.key, [], [GT.key], dr=["GA"])
            QV = b.rot("gqv", gqv)
            b.load("sp", QV[:], d["qvalid"][t0:t0 + 128].partition_broadcast(128), QV.key, [], [QV.key])
            s.op("dve", lambda e, GT=GT, GS=GS, dd=dd: e.tensor_tensor(
                out=GS[:, 0:GH], in0=GT[:, dd * GH:(dd + 1) * GH], in1=dtb[:, dd, :], op=ALU.add),
                reads=[GT.key, dtb.key], writes=[GS.key])
            s.op("act", lambda e, GS=GS: e.activation(out=GS[:, 0:GH], in_=GS[:, 0:GH], func=AF.Exp),
                 reads=[GS.key], writes=[GS.key])
            s.op("act", lambda e, GS=GS: e.activation(out=GS[:, 0:GH], in_=GS[:, 0:GH], func=AF.Ln, bias=1.0),
                 reads=[GS.key], writes=[GS.key])
            s.op("dve", lambda e, GS=GS, dd=dd: e.scalar_tensor_tensor(
                out=GS[:, GH:2 * GH], in0=GS[:, 0:GH], scalar=-1.0, in1=ea[:, dd, :], op0=ALU.mult, op1=ALU.mult),
                reads=[GS.key, ea.key], writes=[GS.key])
            s.op("act", lambda e, GS=GS, GT=GT, dd=dd: e.activation(
                out=GS[:, 2 * GH:3 * GH], in_=GT[:, (2 + dd) * GH:(3 + dd) * GH], func=AF.Sigmoid),
                reads=[GT.key], writes=[GS.key])
            s.op("dve", lambda e, GS=GS: e.tensor_scalar(out=GS[:, 3 * GH:4 * GH], in0=GS[:, 2 * GH:3 * GH], scalar1=-1.0,
                                                        scalar2=None, op0=ALU.mult), reads=[GS.key], writes=[GS.key])
            return [(h, step, dd, c, t0, Tri, BM, SM, GT, GS, QV, None) for h in range(GH)]
    def gdn_unit(slot, h, step, dd, c, t0, Tri, BM, SM, GT, GS, QV, KV):
        T_ = lambda: b.rot("gtmp%d" % slot, tmps[slot])
        C_ = lambda: b.rot("gcol%d" % slot, cols[slot])
        PS = lambda: b.rot("Ps%d" % slot, b.P[2 * slot:2 * slot + 2])
        wide = wides[slot]
        xin = xins[slot]
        gcolv = GS[:, GH + h:GH + h + 1]
        beta = GS[:, 2 * GH + h:2 * GH + h + 1]
        nbeta = GS[:, 3 * GH + h:3 * GH + h + 1]
        XI = b.rot("gxin%d" % slot, xin)
        b.load("sp", XI[:], qkv3[:, :, h, t0:t0 + 130], XI.key, [], [XI.key], dr=["QKVA_T"])
        yield
        Pc = PS()

        def conv(e, XI=XI, Pc=Pc, h=h):
            for a in range(3):
                for j in range(3):
                    ins = e.matmul(Pc[:, a * 128:(a + 1) * 128], dg[:, h * 9 + a * 3 + j, :], XI[:, a, j:j + 128],
                                   start=(j == 0), stop=(j == 2))
            return ins
        s.op("pe", conv, reads=[XI.key, dg.key], writes=[Pc.key])
        yield
        SL = b.rot("gwide%d" % slot, wide)
        s.op("act", lambda e, SL=SL, Pc=Pc: e.activation(out=SL[:], in_=Pc[:, 0:384], func=AF.Silu),
             reads=[Pc.key], writes=[SL.key])
        yield
        s.op("dve", lambda e, SL=SL, QV=QV: e.tensor_tensor(
            out=SL[:].rearrange("p (a t) -> p a t", a=3), in0=SL[:].rearrange("p (a t) -> p a t", a=3),
            in1=QV[:].unsqueeze(1).to_broadcast([128, 3, 128]), op=ALU.mult),
            reads=[SL.key, QV.key], writes=[SL.key])
        yield
        chk(2)
        SQ = b.rot("gwide%d" % slot, wide)
        s.op("act", lambda e, SL=SL, SQ=SQ: e.activation(out=SQ[:, 0:256], in_=SL[:, 0:256], func=AF.Square),
             reads=[SL.key], writes=[SQ.key])
        yield
        Pn = PS()
        mm1(b, Pn, 256, b.onesf[:], SQ[:, 0:256], [b.onesf.key, SQ.key])
        yield
        s.op("act", lambda e, SQ=SQ, Pn=Pn: e.activation(out=SQ[:, 0:256], in_=Pn[:, 0:256], func=AF.Sqrt, bias=EPS),
             reads=[Pn.key], writes=[SQ.key])
        yield
        s.op("dve", lambda e, SQ=SQ: e.reciprocal(out=SQ[:, 0:256], in_=SQ[:, 0:256]), reads=[SQ.key], writes=[SQ.key])
        yield
        QK = b.rot("gwide%d" % slot, wide)
        s.op("dve", lambda e, QK=QK, SL=SL, SQ=SQ: e.scalar_tensor_tensor(
            out=QK[:, 0:128], in0=SL[:, 0:128], scalar=128 ** -0.5, in1=SQ[:, 0:128], op0=ALU.mult, op1=ALU.mult),
            reads=[SL.key, SQ.key], writes=[QK.key])
        yield
        s.op("dve", lambda e, QK=QK, SL=SL, SQ=SQ: e.tensor_tensor(
            out=QK[:, 128:256], in0=SL[:, 128:256], in1=SQ[:, 128:256], op=ALU.mult),
            reads=[SL.key, SQ.key, QK.key], writes=[QK.key])
        yield
        qT, kT = QK[:, 0:128], QK[:, 128:256]
        chk(3)
        GB_ = T_()
        s.op("dve", lambda e, GB_=GB_, gcolv=gcolv: e.tensor_scalar(out=GB_[:], in0=b.onesf[:], scalar1=gcolv,
                                                                  scalar2=None, op0=ALU.mult),
             reads=[b.onesf.key, GS.key], writes=[GB_.key])
        yield
        Pg = PS()

        def cums(e, Pg=Pg, GB_=GB_, Tri=Tri, BM=BM, gcolv=gcolv):
            e.matmul(Pg[:, 0:128], GB_[:], Tri[:], start=True, stop=False)
            e.matmul(Pg[:, 0:128], b.identf[:], BM[:], start=False, stop=True)
            e.matmul(Pg[:, 128:129], Tri[:], gcolv, start=True, stop=True)
            return e.matmul(Pg[:, 160:161], GB_[:], b.onesf[:, 0:1], start=True, stop=True)
        s.op("pe", cums, reads=[GB_.key, Tri.key, BM.key, b.identf.key, GS.key, b.onesf.key], writes=[Pg.key])
        yield
        CL = C_()
        s.op("dve", lambda e, CL=CL, Pg=Pg: e.tensor_copy(out=CL[:, 0:1], in_=Pg[:, 128:129]),
             reads=[Pg.key], writes=[CL.key])
        yield
        s.op("dve", lambda e, CL=CL, Pg=Pg: e.tensor_copy(out=CL[:, 1:2], in_=Pg[:, 160:161]),
             reads=[Pg.key, CL.key], writes=[CL.key])
        yield
        E = T_()
        s.op("act", lambda e, E=E, Pg=Pg, CL=CL: e.activation(out=E[:], in_=Pg[:, 0:128], func=AF.Exp, scale=-1.0,
                                                             bias=CL[:, 0:1]), reads=[Pg.key, CL.key], writes=[E.key])
        yield
        s.op("act", lambda e, CL=CL: e.activation(out=CL[:, 2:4], in_=CL[:, 0:2], func=AF.Exp),
             reads=[CL.key], writes=[CL.key])
        yield
        s.op("act", lambda e, CL=CL: e.activation(out=CL[:, 4:5], in_=CL[:, 0:1], func=AF.Exp, scale=-1.0,
                                                 bias=CL[:, 1:2]), reads=[CL.key], writes=[CL.key])
        yield
        s.op("dve", lambda e, CL=CL, beta=beta: e.tensor_tensor(out=CL[:, 5:6], in0=CL[:, 2:3], in1=beta, op=ALU.mult),
             reads=[CL.key, GS.key], writes=[CL.key])
        yield
        chk(4)
        Pt = PS()

        def trkv(e, Pt=Pt, kT=kT, SL=SL):
            e.transpose(Pt[:, 0:128], kT, b.identf[:])
            return e.transpose(Pt[:, 128:256], SL[:, 256:384], b.identf[:])
        s.op("pe", trkv, reads=[QK.key, SL.key, b.identf.key], writes=[Pt.key])
        yield
        chk(41)
        KBG, KDEC, VB = T_(), T_(), T_()
        s.op("dve", lambda e, KBG=KBG, Pt=Pt, CL=CL: e.tensor_scalar(out=KBG[:], in0=Pt[:, 0:128], scalar1=CL[:, 5:6],
                                                                   scalar2=None, op0=ALU.mult),
             reads=[Pt.key, CL.key], writes=[KBG.key])
        yield
        chk(42)
        s.op("act", lambda e, KDEC=KDEC, Pt=Pt, CL=CL: e.activation(out=KDEC[:], in_=Pt[:, 0:128], func=AF.Identity,
                                                                  scale=CL[:, 4:5]),
             reads=[Pt.key, CL.key], writes=[KDEC.key])
        yield
        chk(43)
        s.op("dve", lambda e, VB=VB, Pt=Pt, beta=beta: e.tensor_scalar(out=VB[:], in0=Pt[:, 128:256], scalar1=beta,
                                                                     scalar2=None, op0=ALU.mult),
             reads=[Pt.key, GS.key], writes=[VB.key])
        yield
        chk(5)
        Pk = PS()

        def gqk(e, Pk=Pk, kT=kT, qT=qT):
            e.matmul(Pk[:, 0:128], kT, kT, start=True, stop=True)
            return e.matmul(Pk[:, 128:256], kT, qT, start=True, stop=True)
        s.op("pe", gqk, reads=[QK.key], writes=[Pk.key])
        yield
        ES = T_()
        s.op("dve", lambda e, ES=ES, E=E, SM=SM: e.tensor_tensor(out=ES[:], in0=E[:], in1=SM[:], op=ALU.mult),
             reads=[E.key, SM.key], writes=[ES.key])
        yield
        M = T_()
        GG = T_()
        s.op("act", lambda e, GG=GG, Pk=Pk, nbeta=nbeta: e.activation(out=GG[:], in_=Pk[:, 0:128], func=AF.Identity,
                                                                   scale=nbeta),
             reads=[Pk.key, GS.key], writes=[GG.key])
        yield
        s.op("dve", lambda e, M=M, GG=GG, ES=ES: e.tensor_tensor(out=M[:], in0=GG[:], in1=ES[:], op=ALU.mult),
             reads=[GG.key, ES.key], writes=[M.key])
        yield
        chk(51)
        Pe = PS()

        def trne(e, Pe=Pe, M=M, E=E):
            e.transpose(Pe[:, 0:128], M[:], b.identf[:])
            return e.transpose(Pe[:, 128:256], E[:], b.identf[:])
        s.op("pe", trne, reads=[M.key, E.key, b.identf.key], writes=[Pe.key])
        yield
        MT = T_()
        b.copy("act", MT[:], Pe[:, 0:128], [Pe.key], [MT.key])
        yield
        PP = T_()
        s.op("dve", lambda e, PP=PP, Pe=Pe: e.tensor_tensor(out=PP[:], in0=Pe[:, 0:128], in1=b.identf[:], op=ALU.add),
             reads=[Pe.key, b.identf.key], writes=[PP.key])
        yield
        AT = T_()
        s.op("dve", lambda e, AT=AT, Pe=Pe, Pk=Pk: e.tensor_copy(out=AT[:], in_=Pe[:, 128:256]),
             reads=[Pe.key], writes=[AT.key])
        yield
        s.op("dve", lambda e, AT=AT, Pk=Pk: e.tensor_tensor(out=AT[:], in0=Pk[:, 128:256], in1=AT[:], op=ALU.mult),
             reads=[Pk.key, AT.key], writes=[AT.key])
        yield
        chk(52)
        for k in range(1, 7):
            chk(52 + k)
            Pm = PS()

            def sq(e, Pm=Pm, M=M, MT=MT, k=k):
                ins = e.matmul(Pm[:, 0:128], MT[:], M[:], start=True, stop=True)
                if k < 6:
                    ins = e.matmul(Pm[:, 128:256], M[:], MT[:], start=True, stop=True)
                return ins
            s.op("pe", sq, reads=[M.key, MT.key], writes=[Pm.key])
            yield
            M2 = T_()
            b.copy("act", M2[:], Pm[:, 0:128], [Pm.key], [M2.key])
            yield
            if k < 6:
                MT2 = T_()
                b.copy("dve", MT2[:], Pm[:, 128:256], [Pm.key], [MT2.key])
                yield
            Pp = PS()
            mm1(b, Pp, 128, M2[:], PP[:], [M2.key, PP.key])
            yield
            PP2 = T_()
            s.op("dve", lambda e, PP2=PP2, Pp=Pp, PP=PP: e.tensor_tensor(out=PP2[:], in0=Pp[:, 0:128], in1=PP[:],
                                                                      op=ALU.add),
                 reads=[Pp.key, PP.key], writes=[PP2.key])
            yield
            PP = PP2
            M = M2
            if k < 6:
                MT = MT2
        chk(6)
        Pw = PS()

        def wu(e, Pw=Pw, KBG=KBG, PP=PP, VB=VB):
            e.matmul(Pw[:, 0:128], KBG[:], PP[:], start=True, stop=True)
            return e.matmul(Pw[:, 128:256], PP[:], VB[:], start=True, stop=True)
        s.op("pe", wu, reads=[KBG.key, PP.key, VB.key], writes=[Pw.key])
        yield
        WT, UU = T_(), T_()
        b.copy("act", WT[:], Pw[:, 0:128], [Pw.key], [WT.key])
        yield
        b.copy("dve", UU[:], Pw[:, 128:256], [Pw.key], [UU.key])
        yield
        chk(7)
        Sc = S[(h, dd, step % 2)]
        Sn = S[(h, dd, (step + 1) % 2)]
        Pr = PS()

        def r1(e, Pr=Pr, WT=WT, Sc=Sc, qT=qT):
            e.matmul(Pr[:, 0:128], WT[:], Sc[:], start=True, stop=True)
            return e.matmul(Pr[:, 128:256], qT, Sc[:], start=True, stop=True)
        s.op("pe", r1, reads=[WT.key, Sc.key, QK.key], writes=[Pr.key])
        yield
        VN = T_()
        s.op("dve", lambda e, VN=VN, UU=UU, Pr=Pr: e.tensor_tensor(out=VN[:], in0=UU[:], in1=Pr[:, 0:128],
                                                                op=ALU.subtract),
             reads=[UU.key, Pr.key], writes=[VN.key])
        yield
        OT = T_()
        s.op("act", lambda e, OT=OT, Pr=Pr, CL=CL: e.activation(out=OT[:], in_=Pr[:, 128:256], func=AF.Identity,
                                                               scale=CL[:, 2:3]),
             reads=[Pr.key, CL.key], writes=[OT.key])
        yield
        Po = PS()

        def r2(e, Po=Po, AT=AT, VN=VN, KDEC=KDEC):
            e.matmul(Po[:, 0:128], AT[:], VN[:], start=True, stop=True)
            return e.matmul(Po[:, 128:256], KDEC[:], VN[:], start=True, stop=True)
        s.op("pe", r2, reads=[AT.key, VN.key, KDEC.key], writes=[Po.key])
        yield
        OO = b.rot("goo%d" % slot, oos[slot])
        s.op("dve", lambda e, OO=OO, OT=OT, Po=Po: e.tensor_tensor(out=OO[:], in0=OT[:], in1=Po[:, 0:128], op=ALU.add),
             reads=[OT.key, Po.key], writes=[OO.key])
        yield
        SS = T_()
        s.op("act", lambda e, SS=SS, Sc=Sc, CL=CL: e.activation(out=SS[:], in_=Sc[:], func=AF.Identity, scale=CL[:, 3:4]),
             reads=[Sc.key, CL.key], writes=[SS.key])
        yield
        s.op("dve", lambda e, Sn=Sn, SS=SS, Po=Po: e.tensor_tensor(out=Sn[:], in0=SS[:], in1=Po[:, 128:256], op=ALU.add),
             reads=[SS.key, Po.key], writes=[Sn.key])
        yield
        b.store("sp", d["OA%d" % dd][t0:t0 + 128, h * 128:(h + 1) * 128], OO[:], OO.key, [OO.key], ["OA%d" % dd])
        yield
    run_interleaved(b, units_q, group_pre, gdn_unit, NSL)


def phase_mlstm(b, stack):
    cfg, s, d = b.cfg, b.s, b.dram
    T, MH = cfg.T, cfg.MH
    NCH = T // 128
    tmps = [b.gtmp] + [[b.sb(stack, "mt%d_%d" % (j, i), [128, 128], F32) for i in range(12)] for j in (1, 2, 3)]
    cols = [[b.sb(stack, "mc%d_%d" % (j, i), [128, 16], F32) for i in range(4)] for j in range(4)]
    gt = [b.sb(stack, "mgt%d" % i, [128, 4 * MH], F32) for i in range(4)]
    gs = [b.sb(stack, "mgs%d" % i, [128, 4 * MH], F32) for i in range(4)]
    kvt = [b.sb(stack, "mkv%d" % i, [128, MH * 384], BF16) for i in range(3)]
    qkts = [[b.sb(stack, "mqk%d_%d" % (j, i), [128, 2, 128], BF16) for i in range(2)] for j in range(4)]
    w257s = [[b.sb(stack, "mw%d_%d" % (j, i), [128, 264], F32) for i in range(8)] for j in range(4)]
    Cst = {(h, dd, p): b.sb(stack, "C%d_%d_%d" % (h, dd, p), [128, 264], F32) for h in range(MH) for dd in range(2)
           for p in range(2)}
    Mst = {(h, dd, p): b.sb(stack, "M%d_%d_%d" % (h, dd, p), [128, 2], F32) for h in range(MH) for dd in range(2)
           for p in range(2)}
    mlb = b.sb(stack, "mlb", [128, 4 * MH], F32)
    BMf, BMb = b.BMf, b.BMb
    s.dma("sp", lambda e: e.dma_start(out=mlb[:], in_=d["ml_gate_bias"].rearrange("a k d h -> a (k d h)").partition_broadcast(128)),
          mlb.key, writes=[mlb.key])
    for h in range(MH):
        for dd in range(2):
            s.op("pool", lambda e, h=h, dd=dd: e.memset(Cst[(h, dd, 0)][:], 0.0), writes=[Cst[(h, dd, 0)].key])
            s.op("pool", lambda e, h=h, dd=dd: e.memset(Mst[(h, dd, 0)][:], 0.0), writes=[Mst[(h, dd, 0)].key])
    qb3 = d["QB_T"].rearrange("(h p) t -> p h t", h=MH)
    kb3 = d["KB_T"].rearrange("(h p) t -> p h t", h=MH)
    units_q = [(step, dd) for step in range(NCH) for dd in range(2)]

    def group_pre(step, dd):
        if True:
            c = step if dd == 0 else NCH - 1 - step
            t0 = c * 128
            Tri = b.U if dd == 0 else b.L
            BM = BMf if dd == 0 else BMb
            GT = b.rot("mgt", gt)
            GS = b.rot("mgs", gs)
            b.load("sp", GT[:], d["GB"][t0:t0 + 128, :], GT.key, [], [GT.key], dr=["GB"])
            s.op("dve", lambda e, GT=GT: e.tensor_tensor(out=GT[:], in0=GT[:], in1=mlb[:], op=ALU.add),
                 reads=[GT.key, mlb.key], writes=[GT.key])
            s.op("dve", lambda e, GT=GT, GS=GS, dd=dd: e.tensor_copy(out=GS[:, 0:MH], in_=GT[:, dd * MH:(dd + 1) * MH]),
                 reads=[GT.key], writes=[GS.key])
            s.op("dve", lambda e, GT=GT, GS=GS, dd=dd: e.tensor_scalar(out=GS[:, MH:2 * MH], in0=GT[:, dd * MH:(dd + 1) * MH],
                                                                    scalar1=-1.0, scalar2=None, op0=ALU.mult),
                 reads=[GT.key, GS.key], writes=[GS.key])
            s.op("act", lambda e, GT=GT, GS=GS, dd=dd: e.activation(out=GS[:, 2 * MH:3 * MH],
                                                                 in_=GT[:, (2 + dd) * MH:(3 + dd) * MH], func=AF.Exp, scale=-1.0),
                 reads=[GT.key, GS.key], writes=[GS.key])
            s.op("act", lambda e, GS=GS: e.activation(out=GS[:, 2 * MH:3 * MH], in_=GS[:, 2 * MH:3 * MH], func=AF.Ln, bias=1.0),
                 reads=[GS.key], writes=[GS.key])
            s.op("dve", lambda e, GS=GS: e.tensor_scalar(out=GS[:, 2 * MH:3 * MH], in0=GS[:, 2 * MH:3 * MH], scalar1=-1.0,
                                                        scalar2=None, op0=ALU.mult), reads=[GS.key], writes=[GS.key])
            KV = b.rot("mkv", kvt)
            b.load("sp", KV[:], d["KVB"][t0:t0 + 128, :], KV.key, [], [KV.key], dr=["KVB"])
            return [(h, step, dd, c, t0, Tri, BM, None, GT, GS, None, KV) for h in range(MH)]
    def ml_unit(slot, h, step, dd, c, t0, Tri, BM, SM, GT, GS, QV, KV):
        T_ = lambda: b.rot("gtmp%d" % slot, tmps[slot])
        C_ = lambda: b.rot("mcol%d" % slot, cols[slot])
        PS = lambda: b.rot("Ps%d" % slot, b.P[2 * slot:2 * slot + 2])
        W_ = lambda: b.rot("mw%d" % slot, w257s[slot])
        qkt = qkts[slot]
        igc = GS[:, h:h + 1]
        nigc = GS[:, MH + h:MH + h + 1]
        lfc = GS[:, 2 * MH + h:2 * MH + h + 1]
        Mc, Mn = Mst[(h, dd, step % 2)], Mst[(h, dd, (step + 1) % 2)]
        Cc, Cn = Cst[(h, dd, step % 2)], Cst[(h, dd, (step + 1) % 2)]
        QKb = b.rot("mqk%d" % slot, qkt)
        b.load("sp", QKb[:, 0, :], qb3[:, h, t0:t0 + 128], QKb.key, [], [QKb.key], dr=["QB_T"])
        yield
        b.load("sp", QKb[:, 1, :], kb3[:, h, t0:t0 + 128], QKb.key, [], [QKb.key], dr=["KB_T"])
        yield
        QK = W_()
        s.op("act", lambda e, QK=QK, QKb=QKb: e.activation(out=QK[:, 0:256],
                                                          in_=QKb[:].rearrange("p a t -> p (a t)"), func=AF.Copy),
             reads=[QKb.key], writes=[QK.key])
        yield
        qT, kT = QK[:, 0:128], QK[:, 128:256]
        VP = W_()
        s.op("dve", lambda e, VP=VP, KV=KV, h=h: e.tensor_copy(out=VP[:, 0:256],
                                                            in_=KV[:, MH * 128 + h * 256:MH * 128 + (h + 1) * 256]),
             reads=[KV.key], writes=[VP.key])
        yield
        s.op("dve", lambda e, VP=VP: e.memset(VP[:, 256:257], 1.0), reads=[VP.key], writes=[VP.key])
        yield
        LFB, NIB = T_(), T_()
        s.op("dve", lambda e, LFB=LFB, lfc=lfc: e.tensor_scalar(out=LFB[:], in0=b.onesf[:], scalar1=lfc, scalar2=None,
                                                             op0=ALU.mult), reads=[b.onesf.key, GS.key], writes=[LFB.key])
        yield
        s.op("dve", lambda e, NIB=NIB, nigc=nigc: e.tensor_scalar(out=NIB[:], in0=b.onesf[:], scalar1=nigc, scalar2=None,
                                                               op0=ALU.mult), reads=[b.onesf.key, GS.key], writes=[NIB.key])
        yield
        Pg = PS()

        def cums(e, Pg=Pg, LFB=LFB, NIB=NIB, Tri=Tri, BM=BM, lfc=lfc):
            e.matmul(Pg[:, 0:128], LFB[:], Tri[:], start=True, stop=False)
            e.matmul(Pg[:, 0:128], NIB[:], b.identf[:], start=False, stop=False)
            e.matmul(Pg[:, 0:128], b.identf[:], BM[:], start=False, stop=True)
            e.matmul(Pg[:, 128:256], LFB[:], Tri[:], start=True, stop=False)
            e.matmul(Pg[:, 128:256], NIB[:], b.identf[:], start=False, stop=True)
            e.matmul(Pg[:, 256:257], Tri[:], lfc, start=True, stop=True)
            return e.matmul(Pg[:, 288:289], LFB[:], b.onesf[:, 0:1], start=True, stop=True)
        s.op("pe", cums, reads=[LFB.key, NIB.key, Tri.key, BM.key, b.identf.key, GS.key, b.onesf.key], writes=[Pg.key])
        yield
        CL = C_()
        s.op("dve", lambda e, CL=CL, Pg=Pg: e.tensor_copy(out=CL[:, 0:1], in_=Pg[:, 256:257]), reads=[Pg.key], writes=[CL.key])
        yield
        s.op("dve", lambda e, CL=CL, Pg=Pg: e.tensor_copy(out=CL[:, 1:2], in_=Pg[:, 288:289]),
             reads=[Pg.key, CL.key], writes=[CL.key])
        yield
        s.op("dve", lambda e, CL=CL, Pg=Pg: e.tensor_reduce(out=CL[:, 2:3], in_=Pg[:, 0:128], axis=AX.X, op=ALU.min),
             reads=[Pg.key, CL.key], writes=[CL.key])
        yield
        s.op("dve", lambda e, CL=CL, Pg=Pg: e.tensor_reduce(out=CL[:, 3:4], in_=Pg[:, 128:256], axis=AX.X, op=ALU.min),
             reads=[Pg.key, CL.key], writes=[CL.key])
        yield
        s.op("dve", lambda e, CL=CL, Mc=Mc: e.scalar_tensor_tensor(out=CL[:, 4:5], in0=CL[:, 2:3], scalar=-1.0, in1=Mc[:, 0:1],
                                                                op0=ALU.mult, op1=ALU.max),
             reads=[CL.key, Mc.key], writes=[CL.key])
        yield
        s.op("dve", lambda e, CL=CL, Mc=Mc: e.scalar_tensor_tensor(out=CL[:, 9:10], in0=CL[:, 3:4], scalar=-1.0, in1=Mc[:, 0:1],
                                                                op0=ALU.mult, op1=ALU.max),
             reads=[CL.key, Mc.key], writes=[CL.key])
        yield
        s.op("dve", lambda e, CL=CL: e.tensor_scalar(out=CL[:, 5:6], in0=CL[:, 4:5], scalar1=-1.0, scalar2=None, op0=ALU.mult),
             reads=[CL.key], writes=[CL.key])
        yield
        s.op("dve", lambda e, CL=CL: e.tensor_scalar(out=CL[:, 10:11], in0=CL[:, 9:10], scalar1=-1.0, scalar2=None, op0=ALU.mult),
             reads=[CL.key], writes=[CL.key])
        yield
        s.op("dve", lambda e, CL=CL: e.tensor_tensor(out=CL[:, 7:8], in0=CL[:, 0:1], in1=CL[:, 4:5], op=ALU.add),
             reads=[CL.key], writes=[CL.key])
        yield
        s.op("dve", lambda e, CL=CL, igc=igc: e.tensor_tensor(out=CL[:, 12:13], in0=CL[:, 0:1], in1=igc, op=ALU.subtract),
             reads=[CL.key, GS.key], writes=[CL.key])
        yield
        s.op("dve", lambda e, CL=CL, Mn=Mn: e.tensor_tensor(out=Mn[:, 0:1], in0=CL[:, 1:2], in1=CL[:, 9:10], op=ALU.add),
             reads=[CL.key], writes=[Mn.key])
        yield
        EW = T_()
        s.op("act", lambda e, EW=EW, Pg=Pg, CL=CL: e.activation(out=EW[:], in_=Pg[:, 0:128], func=AF.Exp, scale=-1.0,
                                                               bias=CL[:, 5:6]), reads=[Pg.key, CL.key], writes=[EW.key])
        yield
        s.op("act", lambda e, CL=CL, Mc=Mc: e.activation(out=CL[:, 6:7], in_=CL[:, 4:5], func=AF.Exp, scale=-1.0,
                                                        bias=Mc[:, 0:1]), reads=[CL.key, Mc.key], writes=[CL.key])
        yield
        s.op("act", lambda e, CL=CL: e.activation(out=CL[:, 8:9], in_=CL[:, 7:8], func=AF.Exp, scale=-1.0),
             reads=[CL.key], writes=[CL.key])
        yield
        s.op("act", lambda e, CL=CL, Mc=Mc: e.activation(out=CL[:, 11:12], in_=CL[:, 9:10], func=AF.Exp, scale=-1.0,
                                                        bias=Mc[:, 0:1]), reads=[CL.key, Mc.key], writes=[CL.key])
        yield
        s.op("act", lambda e, CL=CL: e.activation(out=CL[:, 13:14], in_=CL[:, 12:13], func=AF.Exp, scale=-1.0,
                                                 bias=CL[:, 10:11]), reads=[CL.key], writes=[CL.key])
        yield
        Pq = PS()
        mm1(b, Pq, 128, qT, kT, [QK.key])
        yield
        WI = T_()
        s.op("dve", lambda e, WI=WI, EW=EW, Pq=Pq: e.tensor_tensor(out=WI[:], in0=Pq[:, 0:128], in1=EW[:], op=ALU.mult),
             reads=[Pq.key, EW.key], writes=[WI.key])
        yield
        Pt = PS()
        s.op("pe", lambda e, Pt=Pt, WI=WI: e.transpose(Pt[:, 0:128], WI[:], b.identf[:]),
             reads=[WI.key, b.identf.key], writes=[Pt.key])
        yield
        WIT = T_()
        b.copy("act", WIT[:], Pt[:, 0:128], [Pt.key], [WIT.key])
        yield
        WK = T_()
        s.op("dve", lambda e, WK=WK, KV=KV, CL=CL, h=h: e.tensor_scalar(out=WK[:], in0=KV[:, h * 128:(h + 1) * 128],
                                                                     scalar1=CL[:, 13:14], scalar2=None, op0=ALU.mult),
             reads=[KV.key, CL.key], writes=[WK.key])
        yield
        Pa = PS()
        mm1(b, Pa, 257, qT, Cc[:, 0:257], [QK.key, Cc.key])
        yield
        T1 = W_()
        s.op("act", lambda e, T1=T1, Pa=Pa, CL=CL: e.activation(out=T1[:, 0:257], in_=Pa[:, 0:257], func=AF.Identity,
                                                               scale=CL[:, 6:7]), reads=[Pa.key, CL.key], writes=[T1.key])
        yield
        Pb = PS()
        mm1(b, Pb, 257, WIT[:], VP[:, 0:257], [WIT.key, VP.key])
        yield
        ND = W_()
        s.op("dve", lambda e, ND=ND, T1=T1, Pb=Pb: e.tensor_tensor(out=ND[:, 0:257], in0=Pb[:, 0:257], in1=T1[:, 0:257],
                                                                op=ALU.add), reads=[Pb.key, T1.key], writes=[ND.key])
        yield
        CD = C_()
        s.op("dve", lambda e, CD=CD, ND=ND: e.scalar_tensor_tensor(out=CD[:, 0:1], in0=ND[:, 256:257], scalar=-1.0,
                                                                  in1=ND[:, 256:257], op0=ALU.mult, op1=ALU.max),
             reads=[ND.key], writes=[CD.key])
        yield
        s.op("dve", lambda e, CD=CD, CL=CL: e.tensor_tensor(out=CD[:, 1:2], in0=CD[:, 0:1], in1=CL[:, 8:9], op=ALU.max),
             reads=[CD.key, CL.key], writes=[CD.key])
        yield
        s.op("dve", lambda e, CD=CD: e.reciprocal(out=CD[:, 2:3], in_=CD[:, 1:2]), reads=[CD.key], writes=[CD.key])
        yield
        HO = W_()
        s.op("dve", lambda e, HO=HO, ND=ND, CD=CD: e.tensor_scalar(out=HO[:, 0:256], in0=ND[:, 0:256], scalar1=CD[:, 2:3],
                                                                scalar2=None, op0=ALU.mult),
             reads=[ND.key, CD.key], writes=[HO.key])
        yield
        b.store("sp", d["HB%d" % dd][t0:t0 + 128, h * 256:(h + 1) * 256], HO[:, 0:256], HO.key, [HO.key], ["HB%d" % dd])
        yield
        Pc = PS()
        mm1(b, Pc, 257, WK[:], VP[:, 0:257], [WK.key, VP.key])
        yield
        T2 = W_()
        s.op("act", lambda e, T2=T2, Cc=Cc, CL=CL: e.activation(out=T2[:, 0:257], in_=Cc[:, 0:257], func=AF.Identity,
                                                               scale=CL[:, 11:12]), reads=[Cc.key, CL.key], writes=[T2.key])
        yield
        s.op("dve", lambda e, Cn=Cn, T2=T2, Pc=Pc: e.tensor_tensor(out=Cn[:, 0:257], in0=Pc[:, 0:257], in1=T2[:, 0:257],
                                                                op=ALU.add), reads=[Pc.key, T2.key], writes=[Cn.key])
        yield
    run_interleaved(b, units_q, group_pre, ml_unit, 4)


def transpose_to_fm(b, XBt, dst, dstkey, ncols):
    s = b.s
    NCk = ncols // 128
    for cg in range(0, NCk, 8):
        n = min(8, NCk - cg)
        PT = b.rot("PT", b.PT)

        def tr(e, cg=cg, n=n, PT=PT):
            for c in range(n):
                ins = e.transpose(PT[:, c, :], XBt[:, (cg + c) * 128:(cg + c + 1) * 128], b.identb[:])
            return ins
        s.op("pe", tr, reads=[XBt.key, b.identb.key], writes=[PT.key])
        b.copy(b.evac_eng(), dst[:, cg:cg + n, :], PT[:, 0:n, :], [PT.key], [dstkey])


def outproj_residual(b, hv, hkey, ntok, w, resid, dst, tok0, final_store=False):
    s, d = b.s, b.dram
    D = b.cfg.D

    def ev(P, c0, nb, tt, nt):
        R = b.rot("SF", b.SF)
        b.load("sp", R[:, 0:nb], d[resid][tok0 + tt:tok0 + tt + 128, c0:c0 + nb], R.key, [], [R.key], dr=[resid])
        O = b.rot("SF", b.SF)
        s.op("dve", lambda e: e.tensor_tensor(out=O[:, 0:nb], in0=P[:, 0:nb], in1=R[:, 0:nb], op=ALU.add),
             reads=[P.key, R.key], writes=[O.key])
        b.store("sp", d[dst][tok0 + tt:tok0 + tt + 128, c0:c0 + nb], O[:, 0:nb], O.key, [O.key], [dst], final=final_store)
    linear(b, "tm", hv, hkey, ntok, w, 0, D, ev)


def phase_mixout0(b, stack):
    dense_bufs(b, stack)
    cfg, s, d = b.cfg, b.s, b.dram
    T, D, DC, GH, MH = cfg.T, cfg.D, cfg.DC, cfg.GH, cfg.MH
    WA, WB = GH * 128, MH * 256
    TBM = 512
    gfull = b.sb(stack, "gfull", [128, D], F32)
    for h in range(GH):
        s.dma("sp", lambda e, h=h: e.dma_start(out=gfull[:, h * 128:(h + 1) * 128], in_=d["gdn_norm"][0:1, :].partition_broadcast(128)),
              gfull.key, writes=[gfull.key])
    s.dma("sp", lambda e: e.dma_start(out=gfull[:, WA:D], in_=d["ml_norm"][0:1, :].partition_broadcast(128)),
          gfull.key, writes=[gfull.key])
    rsb = [b.sb(stack, "rsb%d" % i, [128, 2 * (GH + MH)], F32) for i in range(2)]
    for tb in range(T // TBM):
        H = b.H[0]
        hv = H[:, 0:DC * TBM].rearrange("p (c t) -> p c t", c=DC)
        for tt in range(TBM // 128):
            t0 = tb * TBM + tt * 128
            XA, XC = b.X[0], b.X[1]
            ZB, MB = b.XB[0], b.XB[1]
            RS = b.rot("rsb", rsb)
            b.load("sp", XA[:, 0:WA], d["OA0"][t0:t0 + 128, :], XA.key, [], [XA.key], dr=["OA0"])
            b.load("sp", XA[:, WA:D], d["HB0"][t0:t0 + 128, :], XA.key, [], [XA.key], dr=["HB0"])
            b.load("sp", XC[:, 0:WA], d["OA1"][t0:t0 + 128, :], XC.key, [], [XC.key], dr=["OA1"])
            b.load("sp", XC[:, WA:D], d["HB1"][t0:t0 + 128, :], XC.key, [], [XC.key], dr=["HB1"])
            b.load("sp", ZB[:, 0:WA], d["Z"][t0:t0 + 128, :], ZB.key, [], [ZB.key], dr=["Z"])
            b.load("sp", ZB[:, WA:D], d["OB"][t0:t0 + 128, :], ZB.key, [], [ZB.key], dr=["OB"])
            s.op("dve", lambda e, XA=XA, XC=XC: e.tensor_tensor(out=XA[:, 0:D], in0=XA[:, 0:D], in1=XC[:, 0:D], op=ALU.add),
                 reads=[XA.key, XC.key], writes=[XA.key])
            s.op("act", lambda e, XA=XA, XC=XC: e.activation(out=XC[:, 0:D], in_=XA[:, 0:D], func=AF.Square),
                 reads=[XA.key], writes=[XC.key])
            s.op("dve", lambda e, XC=XC, RS=RS: e.tensor_reduce(out=RS[:, 0:GH], in_=XC[:, 0:WA].rearrange("p (h k) -> p h k", h=GH),
                                                              axis=AX.X, op=ALU.add), reads=[XC.key], writes=[RS.key])
            s.op("dve", lambda e, XC=XC, RS=RS: e.tensor_reduce(out=RS[:, GH:GH + MH],
                                                              in_=XC[:, WA:D].rearrange("p (h k) -> p h k", h=MH),
                                                              axis=AX.X, op=ALU.add), reads=[XC.key, RS.key], writes=[RS.key])
            s.op("act", lambda e, RS=RS: e.activation(out=RS[:, 0:GH], in_=RS[:, 0:GH], func=AF.Sqrt, scale=1.0 / 128, bias=EPS),
                 reads=[RS.key], writes=[RS.key])
            s.op("act", lambda e, RS=RS: e.activation(out=RS[:, GH:GH + MH], in_=RS[:, GH:GH + MH], func=AF.Sqrt, scale=1.0 / 256,
                                                     bias=EPS), reads=[RS.key], writes=[RS.key])
            s.op("dve", lambda e, RS=RS: e.reciprocal(out=RS[:, GH + MH:2 * (GH + MH)], in_=RS[:, 0:GH + MH]),
                 reads=[RS.key], writes=[RS.key])
            s.op("dve", lambda e, XA=XA, RS=RS: e.tensor_tensor(
                out=XA[:, 0:WA].rearrange("p (h k) -> p h k", h=GH), in0=XA[:, 0:WA].rearrange("p (h k) -> p h k", h=GH),
                in1=RS[:, GH + MH:2 * GH + MH].unsqueeze(2).to_broadcast([128, GH, 128]), op=ALU.mult),
                reads=[XA.key, RS.key], writes=[XA.key])
            s.op("dve", lambda e, XA=XA, RS=RS: e.tensor_tensor(
                out=XA[:, WA:D].rearrange("p (h k) -> p h k", h=MH), in0=XA[:, WA:D].rearrange("p (h k) -> p h k", h=MH),
                in1=RS[:, 2 * GH + MH:2 * (GH + MH)].unsqueeze(2).to_broadcast([128, MH, 256]), op=ALU.mult),
                reads=[XA.key, RS.key], writes=[XA.key])
            s.op("dve", lambda e, XA=XA: e.tensor_tensor(out=XA[:, 0:D], in0=XA[:, 0:D], in1=gfull[:], op=ALU.mult),
                 reads=[XA.key, gfull.key], writes=[XA.key])
            s.op("act", lambda e, XC=XC, ZB=ZB: e.activation(out=XC[:, 0:WA], in_=ZB[:, 0:WA], func=AF.Silu),
                 reads=[ZB.key], writes=[XC.key])
            s.op("act", lambda e, XC=XC, ZB=ZB: e.activation(out=XC[:, WA:D], in_=ZB[:, WA:D], func=AF.Sigmoid),
                 reads=[ZB.key, XC.key], writes=[XC.key])
            s.op("dve", lambda e, XA=XA, XC=XC, MB=MB: e.tensor_tensor(out=MB[:, 0:D], in0=XA[:, 0:D], in1=XC[:, 0:D], op=ALU.mult),
                 reads=[XA.key, XC.key], writes=[MB.key])
            transpose_to_fm(b, MB, hv[:, :, tt * 128:(tt + 1) * 128], H.key, D)
        outproj_residual(b, hv, H.key, TBM, d["b_ev_w_out"][0], "x", "XL1", tb * TBM)


def phase_ffn(layer, src, dst):
    def ph(b, stack):
        dense_bufs(b, stack)
        cfg, s, d = b.cfg, b.s, b.dram
        T, D, DC, DFF = cfg.T, cfg.D, cfg.DC, cfg.DFF
        FC = DFF // 128
        TBF = 512
        pre = "f%d_" % b.pid
        cwt = b.sb(stack, pre + "cw", [128, FC, 4], F32)
        for j in range(3):
            s.dma("sp", lambda e, j=j: e.dma_start(out=cwt[:, :, j:j + 1],
                                                   in_=d["ffn_conv"][layer, j].rearrange("(c p o) -> p c o", p=128, o=1),
                                                   allow_slow_non_contiguous=True), cwt.key, writes=[cwt.key])
        s.dma("sp", lambda e: e.dma_start(out=cwt[:, :, 3:4], in_=d["ffn_conv_b"][layer].rearrange("(c p o) -> p c o", p=128, o=1),
                                          allow_slow_non_contiguous=True), cwt.key, writes=[cwt.key])
        hn2 = b.sb(stack, pre + "hn", [128, DC, TBF + 2], BF16)
        actT = b.H[0]
        av = actT[:, 0:FC * TBF].rearrange("p (c t) -> p c t", c=FC)
        gbuf = [b.sb(stack, pre + "g%d" % i, [128, TBF + 2], F32) for i in range(2)]
        cbuf = [b.sb(stack, pre + "c%d" % i, [128, TBF], F32) for i in range(2)]
        halo = b.sb(stack, pre + "halo", [128, D], F32)
        w_up, w_dn = d["b_ffn_w_up"][layer], d["b_ffn_w_down"][layer]
        wupv = w_up.rearrange("(c p) n -> p c n", p=128)
        gsrc = b.gffn[:, layer * DC:(layer + 1) * DC]
        for tb in range(T // TBF):
            tok0 = tb * TBF
            for tt in range(TBF // 128):
                t0 = tok0 + tt * 128
                rmsnorm_T(b, d[src][t0:t0 + 128, :], gsrc, b.gffn.key, hn2[:, :, 1 + tt * 128:1 + (tt + 1) * 128], hn2.key,
                          src_reads=[], dr=[src])
            s.op("pool", lambda e: e.memset(halo[:], 0.0), writes=[halo.key])
            if tok0 > 0:
                b.load("sp", halo[0:1, :], d[src][tok0 - 1:tok0, :], halo.key, [], [halo.key], dr=[src])
            if tok0 + TBF < T:
                b.load("sp", halo[1:2, :], d[src][tok0 + TBF:tok0 + TBF + 1, :], halo.key, [], [halo.key], dr=[src])
            rmsnorm_T(b, None, gsrc, b.gffn.key, None, hn2.key, preloaded=halo,
                      halo_dst=(hn2[:, :, 0:1], hn2[:, :, TBF + 1:TBF + 2]))
            for fb in range(0, DFF, 512):
                nb = min(512, DFF - fb)
                Wg = b.rot("W", b.W)
                Wgv = Wg[:, 0:DC * 512].rearrange("p (c n) -> p c n", c=DC)
                b.load("sp", Wgv[:, :, 0:nb], wupv[:, :, fb:fb + nb], Wg.key, [], [Wg.key], dr=["b_ffn_w_up"])
                Wv = b.rot("W", b.W)
                Wvv = Wv[:, 0:DC * 512].rearrange("p (c n) -> p c n", c=DC)
                b.load("sp", Wvv[:, :, 0:nb], wupv[:, :, DFF + fb:DFF + fb + nb], Wv.key, [], [Wv.key], dr=["b_ffn_w_up"])
                for ft in range(0, nb, 128):
                    f = (fb + ft) // 128
                    G = b.rot(pre + "g", gbuf)
                    half = (TBF + 2) // 2
                    for (c0, c1) in ((0, half), (half, TBF + 2)):
                        P = b.rot("P", b.PD)

                        def mm(e, P=P, c0=c0, c1=c1, ft=ft, Wgv=Wgv):
                            for c in range(DC):
                                ins = e.matmul(P[:, 0:c1 - c0], Wgv[:, c, ft:ft + 128], hn2[:, c, c0:c1], start=(c == 0),
                                               stop=(c == DC - 1))
                            return ins
                        s.op("pe", mm, reads=[Wg.key, hn2.key], writes=[P.key])
                        b.copy(b.evac_eng(), G[:, c0:c1], P[:, 0:c1 - c0], [P.key], [G.key])
                    Cv = b.rot(pre + "c", cbuf)
                    s.op("dve", lambda e, Cv=Cv, G=G, f=f: e.tensor_scalar(out=Cv[:], in0=G[:, 0:TBF], scalar1=cwt[:, f, 0:1],
                                                                        scalar2=None, op0=ALU.mult),
                         reads=[G.key, cwt.key], writes=[Cv.key])
                    s.op("dve", lambda e, Cv=Cv, G=G, f=f: e.scalar_tensor_tensor(out=Cv[:], in0=G[:, 1:TBF + 1], scalar=cwt[:, f, 1:2],
                                                                               in1=Cv[:], op0=ALU.mult, op1=ALU.add),
                         reads=[G.key, cwt.key, Cv.key], writes=[Cv.key])
                    s.op("dve", lambda e, Cv=Cv, G=G, f=f: e.scalar_tensor_tensor(out=Cv[:], in0=G[:, 2:TBF + 2], scalar=cwt[:, f, 2:3],
                                                                               in1=Cv[:], op0=ALU.mult, op1=ALU.add),
                         reads=[G.key, cwt.key, Cv.key], writes=[Cv.key])
                    s.op("act", lambda e, Cv=Cv, f=f: e.activation(out=Cv[:], in_=Cv[:], func=AF.Silu, bias=cwt[:, f, 3:4]),
                         reads=[Cv.key, cwt.key], writes=[Cv.key])
                    P = b.rot("P", b.PD)

                    def mv(e, P=P, ft=ft, Wvv=Wvv):
                        for c in range(DC):
                            ins = e.matmul(P[:, 0:TBF], Wvv[:, c, ft:ft + 128], hn2[:, c, 1:TBF + 1], start=(c == 0), stop=(c == DC - 1))
                        return ins
                    s.op("pe", mv, reads=[Wv.key, hn2.key], writes=[P.key])
                    s.op("dve", lambda e, P=P, Cv=Cv, f=f: e.tensor_tensor(out=av[:, f, :], in0=P[:, 0:TBF], in1=Cv[:], op=ALU.mult),
                         reads=[P.key, Cv.key], writes=[actT.key])
            outproj_residual(b, av, actT.key, TBF, w_dn, src, dst, tok0)
    return ph


PADK = 1024
TWO_PI = 6.283185307179586


def rope_consts():
    inv = (500000.0 ** (-np.arange(0, 32, 2, dtype=np.float32) / 32)).astype(np.float32)
    c = np.zeros((32, 2), np.float32)
    c[:, 0] = np.concatenate([inv, inv])
    c[0:16, 1] = -TWO_PI
    c[16:32, 1] = TWO_PI
    return c


def phase_qkv1(b, stack):
    dense_bufs(b, stack)
    cfg, s, d = b.cfg, b.s, b.dram
    T, TB, D, DC, S_ = cfg.T, cfg.TBLK, cfg.D, cfg.DC, cfg.SLOTS
    NQ = 3 * S_ * 128
    w = d["b_od_w_in"][0]
    ropec = b.sb(stack, "ropec_sb", [32, 4], F32)
    s.dma("sp", lambda e: e.dma_start(out=ropec[:, 0:2], in_=d["ropec"][:, :]), ropec.key, writes=[ropec.key])
    permT = b.sb(stack, "permT", [128, 32], BF16)
    s.op("pool", lambda e: e.memset(permT[:], 0.0), writes=[permT.key])
    s.op("pool", lambda e: e.affine_select(out=permT[:, 0:16], in_=permT[:, 0:16], pattern=[[-1, 16]], compare_op=ALU.not_equal,
                                          fill=1.0, base=-16, channel_multiplier=1), reads=[permT.key], writes=[permT.key])
    s.op("pool", lambda e: e.affine_select(out=permT[:, 16:32], in_=permT[:, 16:32], pattern=[[-1, 16]], compare_op=ALU.not_equal,
                                          fill=1.0, base=0, channel_multiplier=1), reads=[permT.key], writes=[permT.key])
    ang = [b.sb(stack, "ang%d" % i, [32, 512], F32) for i in range(2)]
    cst = [b.sb(stack, "cst%d" % i, [32, 2, 512], F32) for i in range(2)]
    cki = [b.sb(stack, "cki%d" % i, [32, 2, 512], mybir.dt.int32) for i in range(2)]
    ckf = [b.sb(stack, "ckf%d" % i, [32, 2, 512], F32) for i in range(2)]
    qa = [b.sb(stack, "qa%d" % i, [128, 512], BF16) for i in range(3)]
    qr = [b.sb(stack, "qr%d" % i, [128, 512], BF16) for i in range(3)]
    rt = [b.sb(stack, "rt%d" % i, [32, 512], F32) for i in range(3)]
    for r in range(0, NQ, 128):
        for (c0) in (0, PADK + T):
            for cc in range(0, PADK, 512):
                b.store("sp", d["KT1"][r:r + 128, c0 + cc:c0 + cc + 512], b.zerob[:, 0:512], "zpad1", [b.zerob.key], ["KT1"])
    for (r0) in (0, PADK + T):
        for rr in range(0, PADK, 128):
            for cc in range(0, NQ, 512):
                b.store("sp", d["VT1"][r0 + rr:r0 + rr + 128, cc:cc + 512], b.zerob[:, 0:512], "zpad1", [b.zerob.key], ["VT1"])
    for tb in range(T // TB):
        H = b.H[0]
        hv = H[:, 0:DC * TB].rearrange("p (c t) -> p c t", c=DC)
        tok0 = tb * TB
        for tt in range(TB // 128):
            t0 = tok0 + tt * 128
            rmsnorm_T(b, d["XL2"][t0:t0 + 128, :], b.gmix[:, DC:2 * DC], b.gmix.key, hv[:, :, tt * 128:(tt + 1) * 128], H.key,
                      dr=["XL2"])
        tabs = {}
        for ts in range(0, TB, 512):
            A = b.rot("ang", ang)
            CS = b.rot("cst", cst)
            b.load("sp", A[:], d["pos"][tok0 + ts:tok0 + ts + 512].partition_broadcast(32), A.key, [], [A.key])
            s.op("dve", lambda e, A=A: e.tensor_scalar(out=A[:], in0=A[:], scalar1=ropec[:, 0:1], scalar2=None, op0=ALU.mult),
                 reads=[A.key, ropec.key], writes=[A.key])
            KI = b.rot("cki", cki)
            KF = b.rot("ckf", ckf)
            s.op("dve", lambda e, A=A, CS=CS: e.tensor_scalar(out=CS[:, 0, :], in0=A[:], scalar1=1.0 / TWO_PI, scalar2=0.25,
                                                            op0=ALU.mult, op1=ALU.add), reads=[A.key], writes=[CS.key])
            s.op("dve", lambda e, A=A, CS=CS: e.tensor_scalar(out=CS[:, 1, :], in0=A[:], scalar1=1.0 / TWO_PI, scalar2=None,
                                                            op0=ALU.mult), reads=[A.key, CS.key], writes=[CS.key])
            s.op("dve", lambda e, CS=CS, KI=KI: e.tensor_copy(out=KI[:], in_=CS[:]), reads=[CS.key], writes=[KI.key])
            s.op("dve", lambda e, KF=KF, KI=KI: e.tensor_copy(out=KF[:], in_=KI[:]), reads=[KI.key], writes=[KF.key])
            s.op("dve", lambda e, CS=CS, KF=KF: e.tensor_tensor(out=CS[:], in0=CS[:], in1=KF[:], op=ALU.subtract),
                 reads=[CS.key, KF.key], writes=[CS.key])
            s.op("act", lambda e, CS=CS: e.activation(out=CS[:, 0, :], in_=CS[:, 0, :], func=AF.Sin, scale=TWO_PI),
                 reads=[CS.key], writes=[CS.key])
            s.op("act", lambda e, CS=CS: e.activation(out=CS[:, 1, :], in_=CS[:, 1, :], func=AF.Sin, scale=ropec[:, 1:2]),
                 reads=[CS.key, ropec.key], writes=[CS.key])
            tabs[ts] = CS

        def ev_rope(dstname, row_base, col_off):
            def f(P, c0, ncl, ts, nt):
                CS = tabs[ts]
                A_ = b.rot("qa", qa)
                R_ = b.rot("qr", qr)
                TT = b.rot("rt", rt)
                b.copy("act", A_[:, 0:nt], P[:, 0:nt], [P.key], [A_.key])
                b.copy("dve", R_[:, 0:nt], P[:, 0:nt], [P.key], [R_.key])
                P2 = b.rot("P", b.PD)
                s.op("pe", lambda e: e.matmul(P2[0:32, 0:nt], permT[:, 0:32], A_[:, 0:nt], start=True, stop=True),
                     reads=[permT.key, A_.key], writes=[P2.key])
                s.op("dve", lambda e: e.tensor_tensor(out=TT[:, 0:nt], in0=P2[0:32, 0:nt], in1=CS[:, 1, 0:nt], op=ALU.mult),
                     reads=[P2.key, CS.key], writes=[TT.key])
                s.op("dve", lambda e: e.tensor_tensor(out=R_[0:32, 0:nt], in0=A_[0:32, 0:nt], in1=CS[:, 0, 0:nt], op=ALU.mult),
                     reads=[A_.key, CS.key, R_.key], writes=[R_.key])
                s.op("dve", lambda e: e.tensor_tensor(out=R_[0:32, 0:nt], in0=R_[0:32, 0:nt], in1=TT[:, 0:nt], op=ALU.add),
                     reads=[R_.key, TT.key], writes=[R_.key])
                b.store("sp", d[dstname][c0 - row_base:c0 - row_base + ncl, col_off + tok0 + ts:col_off + tok0 + ts + nt],
                        R_[0:ncl, 0:nt], R_.key, [R_.key], [dstname])
            return f

        def ev_v(P, c0, nb, tt, nt):
            S = b.rot("SBF", b.SBF)
            b.copy(b.evac_eng(), S[:, 0:nb], P[:, 0:nb], [P.key], [S.key])
            b.store("sp", d["VT1"][PADK + tok0 + tt:PADK + tok0 + tt + 128, c0 - 2 * NQ:c0 - 2 * NQ + nb], S[:, 0:nb], S.key,
                    [S.key], ["VT1"])
        linear(b, "fm", hv, H.key, TB, w, 0, NQ, ev_rope("QT1", 0, 0))
        linear(b, "fm", hv, H.key, TB, w, NQ, NQ, ev_rope("KT1", NQ, PADK))
        linear(b, "tm", hv, H.key, TB, w, 2 * NQ, NQ, ev_v)


def phase_attn(b, stack):
    cfg, s, d = b.cfg, b.s, b.dram
    T, S_ = cfg.T, cfg.SLOTS
    DILS = (1, 4, 16)
    SBLK = 2048
    NSB = T // SBLK
    sc = 128 ** -0.5
    qb_ = [b.sb(stack, "aq%d" % i, [128, SBLK], BF16) for i in range(2)]
    kb_ = [b.sb(stack, "ak%d" % i, [128, SBLK + 2048], BF16) for i in range(1)]
    kmb = [b.sb(stack, "akmb%d" % i, [1, SBLK + 2048], BF16) for i in range(1)]
    vt = [b.sb(stack, "av%d" % i, [128, 2, 128], BF16) for i in range(8)]
    pt = [b.sb(stack, "ap%d" % i, [128, 256], BF16) for i in range(6)]
    osum = [b.sb(stack, "aos%d" % i, [128, SBLK], F32) for i in range(1)]
    dsum = [b.sb(stack, "ads%d" % i, [128, SBLK], F32) for i in range(1)]
    mo = [b.sb(stack, "amo%d" % i, [128, SBLK], BF16) for i in range(2)]
    qv = b.sb(stack, "aqv", [128, SBLK], F32)
    mab = b.sb(stack, "amab", [128, 256], BF16)
    s.op("dve", lambda e: e.tensor_copy(out=mab[:, 0:128], in_=b.L[:]), reads=[b.L.key], writes=[mab.key])
    s.op("dve", lambda e: e.tensor_copy(out=mab[:, 128:256], in_=b.U[:]), reads=[b.U.key, mab.key], writes=[mab.key])
    for slot in range(S_):
        for sb_ in range(NSB):
            tokS = sb_ * SBLK
            OS = b.rot("aos", osum)
            DS = b.rot("ads", dsum)
            units = []
            for g in (2, 1, 0):
                dil = DILS[g]
                blk = 128 * dil
                for bi in range(SBLK // blk):
                    for r in range(dil):
                        units.append((g, bi, r))
            state = {}

            def stage1(u):
                g, bi, r = u
                dil = DILS[g]
                blk = 128 * dil
                row0 = (g * S_ + slot) * 128
                halo = 64 * dil
                nk = SBLK + 2 * halo
                k0 = PADK + tokS - halo
                if (bi, r) == (0, 0):
                    Q = b.rot("aq", qb_)
                    Kt = b.rot("ak", kb_)
                    KMB = b.rot("akmb", kmb)
                    b.load("sp", Q[:], d["QT1"][row0:row0 + 128, tokS:tokS + SBLK], Q.key, [], [Q.key], dr=["QT1"])
                    b.load("sp", Kt[:, 0:nk], d["KT1"][row0:row0 + 128, k0:k0 + nk], Kt.key, [], [Kt.key], dr=["KT1"])
                    b.load("sp", KMB[0:1, 0:nk], d["kmask"][k0:k0 + nk].rearrange("(o n) -> o n", o=1), KMB.key, [], [KMB.key])
                    state["qk"] = (Q, Kt, KMB)
                Q, Kt, KMB = state["qk"]
                qsl = slice(bi * blk + r, bi * blk + r + 127 * dil + 1, dil)
                kA = slice(bi * blk + r, bi * blk + r + 127 * dil + 1, dil)
                kB = slice(bi * blk + blk + r, bi * blk + blk + r + 127 * dil + 1, dil)
                V = b.rot("av", vt)
                tA = k0 + bi * blk + r
                vrowsA = d["VT1"][tA:tA + 127 * dil + 1:dil, row0:row0 + 128]
                vrowsB = d["VT1"][tA + blk:tA + blk + 127 * dil + 1:dil, row0:row0 + 128]
                b.load("sp", V[:, 0, :], vrowsA, V.key, [], [V.key], dr=["VT1"])
                b.load("sp", V[:, 1, :], vrowsB, V.key, [], [V.key], dr=["VT1"])
                Ps = b.rot("Pa", b.P[0:4])

                def sc_mm(e):
                    e.matmul(Ps[:, 0:128], Kt[:, kA], Q[:, qsl], start=True, stop=False)
                    e.matmul(Ps[:, 0:128], KMB[0:1, kA], b.onesb[0:1, 0:128], start=False, stop=True)
                    e.matmul(Ps[:, 128:256], Kt[:, kB], Q[:, qsl], start=True, stop=False)
                    return e.matmul(Ps[:, 128:256], KMB[0:1, kB], b.onesb[0:1, 0:128], start=False, stop=True)
                s.op("pe", sc_mm, reads=[Kt.key, Q.key, KMB.key, b.onesb.key], writes=[Ps.key])
                PT_ = b.rot("ap", pt)
                s.op("act", lambda e: e.activation(out=PT_[:], in_=Ps[:, 0:256], func=AF.Exp, scale=sc),
                     reads=[Ps.key], writes=[PT_.key])
                s.op("dve", lambda e: e.tensor_tensor(out=PT_[:], in0=PT_[:], in1=mab[:], op=ALU.mult),
                     reads=[PT_.key, mab.key], writes=[PT_.key])
                return (V, PT_, qsl, g)

            def stage2(st):
                V, PT_, qsl, g = st
                Po = b.rot("Pb", b.P[4:8])

                def pv_mm(e):
                    e.matmul(Po[:, 0:128], V[:, 0, :], PT_[:, 0:128], start=True, stop=False)
                    e.matmul(Po[:, 0:128], V[:, 1, :], PT_[:, 128:256], start=False, stop=True)
                    e.matmul(Po[:, 128:256], b.onesb[:], PT_[:, 0:128], start=True, stop=False)
                    return e.matmul(Po[:, 128:256], b.onesb[:], PT_[:, 128:256], start=False, stop=True)
                s.op("pe", pv_mm, reads=[V.key, PT_.key, b.onesb.key], writes=[Po.key])
                if g == 2:
                    s.op("act", lambda e: e.activation(out=OS[:, qsl], in_=Po[:, 0:128], func=AF.Copy),
                         reads=[Po.key], writes=[OS.key])
                    s.op("dve", lambda e: e.tensor_copy(out=DS[:, qsl], in_=Po[:, 128:256]), reads=[Po.key], writes=[DS.key])
                else:
                    s.op("dve", lambda e: e.tensor_tensor(out=OS[:, qsl], in0=Po[:, 0:128], in1=OS[:, qsl], op=ALU.add),
                         reads=[Po.key, OS.key], writes=[OS.key])
                    s.op("dve", lambda e: e.tensor_tensor(out=DS[:, qsl], in0=Po[:, 128:256], in1=DS[:, qsl], op=ALU.add),
                         reads=[Po.key, DS.key], writes=[DS.key])
            pend = []
            for u in units:
                pend.append(stage1(u))
                if len(pend) > 2:
                    stage2(pend.pop(0))
            while pend:
                stage2(pend.pop(0))
            MO = b.rot("amo", mo)
            b.load("sp", qv[:], d["qvalid"][tokS:tokS + SBLK].partition_broadcast(128), qv.key, [], [qv.key])
            s.op("dve", lambda e, DS=DS: e.tensor_scalar(out=DS[:], in0=DS[:], scalar1=1e-30, scalar2=None, op0=ALU.max),
                 reads=[DS.key], writes=[DS.key])
            s.op("dve", lambda e, DS=DS: e.reciprocal(out=DS[:], in_=DS[:]), reads=[DS.key], writes=[DS.key])
            s.op("dve", lambda e, DS=DS: e.tensor_tensor(out=DS[:], in0=DS[:], in1=qv[:], op=ALU.mult),
                 reads=[DS.key, qv.key], writes=[DS.key])
            s.op("dve", lambda e, MO=MO, OS=OS, DS=DS: e.tensor_tensor(out=MO[:], in0=OS[:], in1=DS[:], op=ALU.mult),
                 reads=[OS.key, DS.key], writes=[MO.key])
            b.store("sp", d["MIXT"][slot * 128:(slot + 1) * 128, tokS:tokS + SBLK], MO[:], MO.key, [MO.key], ["MIXT"])


def phase_mixout1(b, stack):
    dense_bufs(b, stack)
    cfg, s, d = b.cfg, b.s, b.dram
    T, D, DC = cfg.T, cfg.D, cfg.DC
    TBM = 512
    mv = d["MIXT"].rearrange("(c p) t -> p c t", p=128)
    for tb in range(T // TBM):
        H = b.H[0]
        hv = H[:, 0:DC * TBM].rearrange("p (c t) -> p c t", c=DC)
        b.load("sp", hv, mv[:, :, tb * TBM:(tb + 1) * TBM], H.key, [], [H.key], dr=["MIXT"])
        outproj_residual(b, hv, H.key, TBM, d["b_od_w_out"][0], "XL2", "XL3", tb * TBM)


def phase_final(b, stack):
    dense_bufs(b, stack, need_h=False)
    cfg, s, d = b.cfg, b.s, b.dram
    T, D = cfg.T, cfg.D
    gf = b.sb(stack, "gfinb", [128, D], F32)
    s.dma("sp", lambda e: e.dma_start(out=gf[:], in_=d["norm_final"].rearrange("(o n) -> o n", o=1).partition_broadcast(128)),
          gf.key, writes=[gf.key])
    yo = [b.sb(stack, "yo%d" % i, [128, D], F32) for i in range(2)]
    for t0 in range(0, T, 128):
        X = b.rot("X", b.X)
        XB = b.rot("XB", b.XB)
        SC = b.rot("SC", b.SC)
        Y = b.rot("yo", yo)
        b.load("sp", X[:, 0:D], d["XL4"][t0:t0 + 128, :], X.key, [], [X.key], dr=["XL4"])
        s.op("act", lambda e, X=X, XB=XB, SC=SC: e.activation(out=XB[:, 0:D], in_=X[:, 0:D], func=AF.Square, accum_out=SC[:, 0:1]),
             reads=[X.key], writes=[XB.key, SC.key])
        s.op("act", lambda e, SC=SC: e.activation(out=SC[:, 1:2], in_=SC[:, 0:1], func=AF.Sqrt, scale=1.0 / D, bias=EPS),
             reads=[SC.key], writes=[SC.key])
        s.op("dve", lambda e, SC=SC: e.reciprocal(out=SC[:, 2:3], in_=SC[:, 1:2]), reads=[SC.key], writes=[SC.key])
        s.op("dve", lambda e, X=X, SC=SC, Y=Y: e.scalar_tensor_tensor(out=Y[:], in0=X[:, 0:D], scalar=SC[:, 2:3], in1=gf[:],
                                                                   op0=ALU.mult, op1=ALU.mult),
             reads=[X.key, SC.key, gf.key], writes=[Y.key])
        b.store("sp", d["y"][t0:t0 + 128, :], Y[:], Y.key, [Y.key], ["y"], final=True)


def all_phases():
    return [phase_wcast, phase_A, phase_gdn, phase_mlstm, phase_mixout0, phase_ffn(0, "XL1", "XL2"), phase_qkv1, phase_attn, phase_mixout1,
            phase_ffn(1, "XL3", "XL4"), phase_final]


N_CORES = 4


def kernel(**inputs):
    import ml_dtypes
    cfg = Cfg()
    T = cfg.T
    xp = np.asarray(inputs["x_prompt"], np.float32)
    xs = np.asarray(inputs["x_sample"], np.float32)
    shared = {k: np.ascontiguousarray(np.asarray(v, np.float32)) for k, v in inputs.items()
              if k not in ("x_prompt", "x_sample")}
    in_maps = []
    lens = []
    for c in range(N_CORES):
        seq = xp[c] if c < 2 else xs[c - 2]
        n = seq.shape[0]
        x = np.zeros((T, cfg.D), np.float32)
        x[:n] = seq
        km = np.full((T + 2048,), -BIG, np.float32)
        km[PADK:PADK + n] = 0.0
        m = dict(shared)
        m["x"] = x
        m["kmask"] = km.astype(ml_dtypes.bfloat16)
        m["pos"] = np.arange(T, dtype=np.float32)
        m["ropec"] = rope_consts()
        qv = np.zeros((T,), np.float32)
        qv[:n] = 1.0
        m["qvalid"] = qv
        in_maps.append(m)
        lens.append(n)
    b = build(cfg, all_phases())
    res = run_bass_kernel_spmd(b.nc, in_maps, core_ids=list(range(N_CORES)))
    ys = [np.asarray(res.results[c]["y"], np.float32)[:lens[c]] for c in range(N_CORES)]
    return (np.stack(ys[0:2], 0), np.stack(ys[2:4], 0))
```

```python
import numpy as np
import concourse.bass as bass
import concourse.mybir as mybir
from concourse.bass_utils import run_bass_kernel_spmd

F32 = mybir.dt.float32
BF16 = mybir.dt.bfloat16
AF = mybir.ActivationFunctionType
ALU = mybir.AluOpType
AX = mybir.AxisListType

EPS = 1e-6
BIG = 30000.0


class Cfg:
    def __init__(self, D=2048, GH=8, MH=4, SLOTS=16, DFF=5632, T=16384, TBLK=1024):
        self.D, self.GH, self.MH, self.SLOTS, self.DFF, self.T, self.TBLK = D, GH, MH, SLOTS, DFF, T, TBLK
        self.DC = D // 128
        self.A_QKV = GH * 384
        self.A_Z = GH * 128
        self.A_G = 4 * GH
        self.B_QKV = MH * 512
        self.B_O = MH * 256
        self.B_G = 4 * MH
        self.c1 = self.A_QKV
        self.c2 = self.c1 + self.A_Z
        self.c3 = self.c2 + self.A_G
        self.c4 = self.c3 + self.B_QKV
        self.c5 = self.c4 + self.B_O
        self.EVEN_IN = self.c5 + self.B_G
        self.EVEN_MIX = GH * 128 + MH * 256
        self.ODD_IN = 9 * SLOTS * 128
        self.ODD_MIX = SLOTS * 128
        assert self.EVEN_MIX == D and self.ODD_MIX == D


class Sched:
    ENG = ("pe", "act", "dve", "pool", "sp")

    def __init__(self, nc):
        self.nc = nc
        self.ops = {e: [] for e in self.ENG}
        self.last_w = {}
        self.reads = {}
        self.waited = {}
        self.dma_val = {}
        self.dma_last = {}
        self.final_events = []
        self.dwl = {}
        self.excl = set()
        self.sem_slot = {}
        self.sem_free = []
        self.n_slots = 0

    def _deps(self, eng, reads, writes):
        deps = []
        for k in reads:
            if k in self.last_w:
                deps.append(self.last_w[k])
        for k in writes:
            if k in self.last_w:
                deps.append(self.last_w[k])
            deps.extend(self.reads.get(k, ()))
        return deps

    def _add_waits(self, eng, deps):
        waits = []
        for ev in deps:
            kind, key, val = ev
            if kind == "eng" and key == eng and eng == "pe":
                continue
            wk = (eng, kind, key)
            if self.waited.get(wk, -1) >= val:
                continue
            self.waited[wk] = val
            waits.append(ev)
            if kind == "eng":
                self.ops[key][val]["inc"] = True
        return waits

    def _commit(self, ev, reads, writes):
        for k in writes:
            self.last_w[k] = ev
            self.reads[k] = []
        for k in reads:
            self.reads.setdefault(k, []).append(ev)

    def op(self, eng, fn, reads=(), writes=()):
        ex = [k for k in reads if k in self.excl]
        if ex:
            reads = [k for k in reads if k not in self.excl]
            writes = list(writes) + [k for k in ex if k not in writes]
        deps = self._deps(eng, reads, writes)
        waits = self._add_waits(eng, deps)
        idx = len(self.ops[eng])
        self.ops[eng].append(dict(waits=waits, fn=fn, inc=False, dma=None))
        ev = ("eng", eng, idx)
        self._commit(ev, reads, writes)
        return ev

    def dma(self, q, fn, semkey, reads=(), writes=(), final=False, dr=(), dw=()):
        deps = self._deps(q, reads, writes)
        if semkey not in self.sem_slot:
            if self.sem_free:
                self.sem_slot[semkey] = self.sem_free.pop(0)
            else:
                self.sem_slot[semkey] = self.n_slots
                self.n_slots += 1
        semkey = self.sem_slot[semkey]
        if semkey in self.dma_last:
            deps.append(self.dma_last[semkey])
        for k in dr:
            deps.extend(self.dwl.get(k, {}).values())
        for k in dw:
            deps.extend(self.reads.get(k, ()))
        waits = self._add_waits(q, deps)
        v = self.dma_val.get(semkey, 0) + 16
        self.dma_val[semkey] = v
        self.ops[q].append(dict(waits=waits, fn=fn, inc=False, dma=(semkey, v)))
        ev = ("dma", semkey, v)
        self.dma_last[semkey] = ev
        self._commit(ev, list(reads) + list(dr), writes)
        for k in dw:
            self.dwl.setdefault(k, {})[semkey] = ev
            self.reads[k] = []
        if final:
            self.final_events.append(ev)
        return ev

    def barrier(self):
        evs = []
        for e in self.ENG:
            if self.ops[e]:
                for idx in range(len(self.ops[e]) - 1, -1, -1):
                    o = self.ops[e][idx]
                    if o["fn"] is not None and o["dma"] is None:
                        evs.append(("eng", e, idx))
                        break
        evs.extend(self.dma_last.values())
        for f in self.ENG:
            waits = self._add_waits(f, evs)
            self.ops[f].append(dict(waits=waits, fn=None, inc=False, dma=None))
        self.sem_free.extend(sorted(set(self.sem_slot.values())))
        self.sem_slot = {}

    def finish(self):
        waits = self._add_waits("sp", self.final_events)
        self.ops["sp"].append(dict(waits=waits, fn=None, inc=False, dma=None))

    def emit(self, stack):
        nc = self.nc
        esem = {e: stack.enter_context(nc.semaphore("s_" + e)) for e in self.ENG}
        dsem = {}
        for k in range(self.n_slots):
            dsem[k] = stack.enter_context(nc.semaphore("d_%d" % k))
        cnt = {}
        for e in self.ENG:
            c = 0
            arr = []
            for o in self.ops[e]:
                if o["inc"]:
                    c += 1
                arr.append(c)
            cnt[e] = arr
        block = stack.enter_context(nc.Block())

        def run(e, engobj):
            for o in self.ops[e]:
                for (kind, key, val) in o["waits"]:
                    if kind == "eng":
                        engobj.wait_ge(esem[key], cnt[key][val])
                    else:
                        engobj.wait_ge(dsem[key], val)
                if o["fn"] is None:
                    continue
                ins = o["fn"](engobj)
                if o["dma"] is not None:
                    ins.then_inc(dsem[o["dma"][0]], 16)
                elif o["inc"]:
                    ins.then_inc(esem[e], 1)

        block.sync(lambda g: run("sp", g))
        block.scalar(lambda g: run("act", g))
        block.vector(lambda g: run("dve", g))
        block.gpsimd(lambda g: run("pool", g))
        block.tensor(lambda g: run("pe", g))
        n = {e: len(self.ops[e]) for e in self.ENG}
        return n, len(dsem)


class Buf:
    def __init__(self, t, key):
        self.t, self.key = t, key

    def __getitem__(self, k):
        return self.t[k]


class B:
    def __init__(self, cfg, debug_outs=()):
        self.cfg = cfg
        self.nc = bass.Bass("TRN2", target_bir_lowering=False)
        self.s = Sched(self.nc)
        self.debug_outs = set(debug_outs)
        self.rr = {}
        self.dram = {}

    def din(self, name, shape, dt=F32):
        self.dram[name] = self.nc.dram_tensor(name, list(shape), dt, kind="ExternalInput").ap()
        return self.dram[name]

    def dout(self, name, shape, dt=F32):
        self.dram[name] = self.nc.dram_tensor(name, list(shape), dt, kind="ExternalOutput").ap()
        return self.dram[name]

    def dscr(self, name, shape, dt):
        kind = "ExternalOutput" if name in self.debug_outs else "Internal"
        self.dram[name] = self.nc.dram_tensor(name, list(shape), dt, kind=kind).ap()
        return self.dram[name]

    def sb(self, stack, name, shape, dt):
        t = stack.enter_context(self.nc.sbuf_tensor(name, list(shape), dt))
        return Buf(t, name)

    def ps(self, stack, name, shape, dt):
        t = stack.enter_context(self.nc.psum_tensor(name, list(shape), dt))
        self.s.excl.add(name)
        return Buf(t, name)

    def rot(self, name, lst):
        i = self.rr.get(name, 0)
        self.rr[name] = i + 1
        return lst[i % len(lst)]

    def evac_eng(self):
        return self.rot("evac", ["act", "dve"])

    def copy(self, eng, out, in_, reads, writes, scale=None):
        if eng == "act":
            if scale is None:
                fn = lambda e: e.activation(out=out, in_=in_, func=AF.Copy)
            else:
                fn = lambda e: e.activation(out=out, in_=in_, func=AF.Identity, scale=scale)
        else:
            if scale is None:
                fn = lambda e: e.tensor_copy(out=out, in_=in_)
            else:
                fn = lambda e: e.tensor_scalar(out=out, in0=in_, scalar1=scale, scalar2=None, op0=ALU.mult)
        return self.s.op(eng, fn, reads=reads, writes=writes)

    def load(self, q, out, in_, semkey, reads, writes, dr=()):
        return self.s.dma(q, lambda e: e.dma_start(out=out, in_=in_, allow_slow_non_contiguous=True), semkey, reads=reads, writes=writes, dr=dr)

    def store(self, q, out, in_, semkey, reads, dw, final=False):
        q = "pool"
        return self.s.dma(q, lambda e: e.dma_start(out=out, in_=in_, allow_slow_non_contiguous=True), semkey, reads=reads, writes=(), dw=dw,
                          final=final)


def build_common(b, stack):
    cfg = b.cfg
    s = b.s
    b.identb = b.sb(stack, "identb", [128, 128], BF16)
    b.identf = b.sb(stack, "identf", [128, 128], F32)
    b.U = b.sb(stack, "U", [128, 128], F32)
    b.L = b.sb(stack, "L", [128, 128], F32)
    b.onesf = b.sb(stack, "onesf", [128, 128], F32)
    b.onesb = b.sb(stack, "onesb", [128, 128], BF16)
    b.zerob = b.sb(stack, "zerob", [128, 512], BF16)

    def mk_tri(buf, cmp_, dt_fill=1.0):
        pass

    def init_ident(buf):
        s.op("pool", lambda e: e.memset(buf[:], 0.0), writes=[buf.key])
        s.op("pool", lambda e: e.affine_select(out=buf[:], in_=buf[:], pattern=[[-1, 128]], compare_op=ALU.not_equal,
                                              fill=1.0, base=0, channel_multiplier=1), reads=[buf.key], writes=[buf.key])
    init_ident(b.identb)
    init_ident(b.identf)
    s.op("pool", lambda e: e.memset(b.onesf[:], 1.0), writes=[b.onesf.key])
    s.op("pool", lambda e: e.memset(b.onesb[:], 1.0), writes=[b.onesb.key])
    s.op("pool", lambda e: e.memset(b.zerob[:], 0.0), writes=[b.zerob.key])
    s.op("pool", lambda e: e.memset(b.U[:], 1.0), writes=[b.U.key])
    s.op("pool", lambda e: e.affine_select(out=b.U[:], in_=b.U[:], pattern=[[1, 128]], compare_op=ALU.is_ge,
                                          fill=0.0, base=0, channel_multiplier=-1), reads=[b.U.key], writes=[b.U.key])
    s.op("pool", lambda e: e.memset(b.L[:], 1.0), writes=[b.L.key])
    s.op("pool", lambda e: e.affine_select(out=b.L[:], in_=b.L[:], pattern=[[-1, 128]], compare_op=ALU.is_ge,
                                          fill=0.0, base=0, channel_multiplier=1), reads=[b.L.key], writes=[b.L.key])
    b.gtmp = [b.sb(stack, "g%d" % i, [128, 128], F32) for i in range(36)]
    b.BMf = b.sb(stack, "BMf", [128, 128], F32)
    b.BMb = b.sb(stack, "BMb", [128, 128], F32)
    b.SMf = b.sb(stack, "SMf", [128, 128], F32)
    b.SMb = b.sb(stack, "SMb", [128, 128], F32)
    SMf, SMb, BMf, BMb = b.SMf, b.SMb, b.BMf, b.BMb
    s.op("dve", lambda e: e.tensor_scalar(out=SMf[:], in0=b.U[:], scalar1=-1.0, scalar2=1.0, op0=ALU.mult, op1=ALU.add),
         reads=[b.U.key], writes=[SMf.key])
    s.op("dve", lambda e: e.tensor_scalar(out=SMb[:], in0=b.L[:], scalar1=-1.0, scalar2=1.0, op0=ALU.mult, op1=ALU.add),
         reads=[b.L.key], writes=[SMb.key])
    s.op("dve", lambda e: e.tensor_scalar(out=BMf[:], in0=SMb[:], scalar1=BIG, scalar2=None, op0=ALU.mult),
         reads=[SMb.key], writes=[BMf.key])
    s.op("dve", lambda e: e.tensor_scalar(out=BMb[:], in0=SMf[:], scalar1=BIG, scalar2=None, op0=ALU.mult),
         reads=[SMf.key], writes=[BMb.key])
    b.P = [b.ps(stack, "P%d" % i, [128, 512], F32) for i in range(8)]

    class _PTView:
        def __init__(self, buf):
            self.key = buf.key
            self.v = buf[:, :].bitcast(BF16).rearrange("p (c t) -> p c t", c=8)

        def __getitem__(self, k):
            return self.v[k]
    b.PT = [_PTView(b.P[6]), _PTView(b.P[7])]
    b.PD = b.P[0:6]


def dense_bufs(b, stack, need_h=True):
    p = "p%d_" % b.pid
    if need_h:
        b.H = [b.sb(stack, p + "H", [128, 24576], BF16)]
    b.W = [b.sb(stack, p + "W%d" % i, [128, 11264], BF16) for i in range(2)]
    b.X = [b.sb(stack, p + "X%d" % i, [128, 2048], F32) for i in range(2)]
    b.XB = [b.sb(stack, p + "XB%d" % i, [128, 2048], BF16) for i in range(2)]
    b.SF = [b.sb(stack, p + "SF%d" % i, [128, 512], F32) for i in range(4)]
    b.SBF = [b.sb(stack, p + "SBF%d" % i, [128, 512], BF16) for i in range(4)]
    b.SC = [b.sb(stack, p + "SC%d" % i, [128, 8], F32) for i in range(8)]


def rmsnorm_T(b, src, gT, gkey, dst, dstkey, src_reads=(), dr=(), preloaded=None, halo_dst=None):
    cfg, s = b.cfg, b.s
    D, DC = cfg.D, cfg.DC
    XB = b.rot("XB", b.XB)
    SC = b.rot("SC", b.SC)
    if preloaded is not None:
        X = preloaded
    else:
        X = b.rot("X", b.X)
        b.load("sp", X[:, 0:D], src, X.key, reads=src_reads, writes=[X.key], dr=dr)
    s.op("act", lambda e: e.activation(out=XB[:, 0:D], in_=X[:, 0:D], func=AF.Square, accum_out=SC[:, 0:1]),
         reads=[X.key], writes=[XB.key, SC.key])
    s.op("act", lambda e: e.activation(out=SC[:, 1:2], in_=SC[:, 0:1], func=AF.Sqrt, scale=1.0 / D, bias=EPS),
         reads=[SC.key], writes=[SC.key])
    s.op("dve", lambda e: e.reciprocal(out=SC[:, 2:3], in_=SC[:, 1:2]), reads=[SC.key], writes=[SC.key])
    s.op("act", lambda e: e.activation(out=XB[:, 0:D], in_=X[:, 0:D], func=AF.Identity, scale=SC[:, 2:3]),
         reads=[X.key, SC.key], writes=[XB.key])
    for cg in range(0, DC, 8):
        n = min(8, DC - cg)
        PT = b.rot("PT", b.PT)

        def tr(e, cg=cg, n=n, PT=PT):
            for c in range(n):
                ins = e.transpose(PT[:, c, :], XB[:, (cg + c) * 128:(cg + c + 1) * 128], b.identb[:])
            return ins
        s.op("pe", tr, reads=[XB.key, b.identb.key], writes=[PT.key])
        if halo_dst is not None:
            for hi, hd in enumerate(halo_dst):
                s.op("dve", lambda e, cg=cg, n=n, PT=PT, hi=hi, hd=hd: e.tensor_tensor(
                    out=hd[:, cg:cg + n, :], in0=PT[:, 0:n, hi:hi + 1],
                    in1=gT[:, cg:cg + n].unsqueeze(2), op=ALU.mult),
                    reads=[PT.key, gkey], writes=[dstkey])
            continue
        s.op("dve", lambda e, cg=cg, n=n, PT=PT: e.tensor_tensor(
            out=dst[:, cg:cg + n, :], in0=PT[:, 0:n, :],
            in1=gT[:, cg:cg + n].unsqueeze(2).to_broadcast([128, n, 128]), op=ALU.mult),
            reads=[PT.key, gkey], writes=[dstkey])


def linear(b, mode, hv, hkey, ntok, w, col0, ncols, evac):
    s = b.s
    KC = hv.shape[1]
    wv = w.rearrange("(c p) n -> p c n", p=128)
    CBW = 512 if KC <= 16 else 256
    for cb in range(0, ncols, CBW):
        nb = min(CBW, ncols - cb)
        W = b.rot("W", b.W)
        Wv = W[:, 0:KC * CBW].rearrange("p (c n) -> p c n", c=KC)
        b.load("sp", Wv[:, :, 0:nb], wv[:, :, col0 + cb:col0 + cb + nb], W.key, reads=[], writes=[W.key], dr=[w.tensor.name])
        if mode == "fm":
            for ct in range(0, nb, 128):
                ncl = min(128, nb - ct)
                for ts in range(0, ntok, 512):
                    nt = min(512, ntok - ts)
                    P = b.rot("P", b.PD)

                    def mm(e, P=P, ct=ct, ncl=ncl, ts=ts, nt=nt, Wv=Wv):
                        for c in range(KC):
                            ins = e.matmul(P[0:ncl, 0:nt], Wv[:, c, ct:ct + ncl], hv[:, c, ts:ts + nt],
                                           start=(c == 0), stop=(c == KC - 1))
                        return ins
                    s.op("pe", mm, reads=[W.key, hkey], writes=[P.key])
                    evac(P, col0 + cb + ct, ncl, ts, nt)
        else:
            for tt in range(0, ntok, 128):
                P = b.rot("P", b.PD)

                def mm(e, P=P, tt=tt, nb=nb, Wv=Wv):
                    for c in range(KC):
                        ins = e.matmul(P[:, 0:nb], hv[:, c, tt:tt + 128], Wv[:, c, 0:nb],
                                       start=(c == 0), stop=(c == KC - 1))
                    return ins
                s.op("pe", mm, reads=[W.key, hkey], writes=[P.key])
                evac(P, col0 + cb, nb, tt, 128)


def phase_A(b, stack):
    dense_bufs(b, stack)
    cfg, s = b.cfg, b.s
    T, TB, D, DC, GH, MH = cfg.T, cfg.TBLK, cfg.D, cfg.DC, cfg.GH, cfg.MH
    d = b.dram
    x, w = d["x"], d["b_ev_w_in"]
    for tb in range(T // TB):
        H = b.rot("H", b.H)
        hv = H[:, 0:DC * TB].rearrange("p (c t) -> p c t", c=DC)
        for tt in range(TB // 128):
            t0 = tb * TB + tt * 128
            rmsnorm_T(b, x[t0:t0 + 128, :], b.gmix[:, 0:DC], b.gmix.key, hv[:, :, tt * 128:(tt + 1) * 128], H.key)
        tok0 = tb * TB

        def ev_fm(dst, row0, scale=None):
            def f(P, c0, ncl, ts, nt):
                S = b.rot("SBF", b.SBF)
                b.copy(b.evac_eng(), S[0:ncl, 0:nt], P[0:ncl, 0:nt], [P.key], [S.key], scale=scale)
                b.store("sp", dst(c0 - row0, ncl, tok0 + ts, nt), S[0:ncl, 0:nt], S.key, [S.key], [dst.__name__])
            return f

        def ev_tm(dstname, col_base, dt):
            def f(P, c0, nb, tt, nt):
                S = b.rot("SBF", b.SBF) if dt == BF16 else b.rot("SF", b.SF)
                b.copy(b.evac_eng(), S[:, 0:nb], P[:, 0:nb], [P.key], [S.key])
                b.store("sp", d[dstname][tok0 + tt:tok0 + tt + 128, c0 - col_base:c0 - col_base + nb], S[:, 0:nb],
                        S.key, [S.key], [dstname])
            return f

        def QKVA_T(r, n, t, nt):
            return d["QKVA_T"][r:r + n, 1 + t:1 + t + nt]

        def QB_T(r, n, t, nt):
            return d["QB_T"][r:r + n, t:t + nt]

        def KB_T(r, n, t, nt):
            return d["KB_T"][r:r + n, t:t + nt]
        linear(b, "fm", hv, H.key, TB, w[0], 0, cfg.A_QKV, ev_fm(QKVA_T, 0))
        linear(b, "tm", hv, H.key, TB, w[0], cfg.c1, cfg.A_Z, ev_tm("Z", cfg.c1, BF16))
        linear(b, "tm", hv, H.key, TB, w[0], cfg.c2, cfg.A_G, ev_tm("GA", cfg.c2, F32))
        linear(b, "fm", hv, H.key, TB, w[0], cfg.c3, MH * 128, ev_fm(QB_T, cfg.c3, scale=128 ** -0.5))
        linear(b, "fm", hv, H.key, TB, w[0], cfg.c3 + MH * 128, MH * 128, ev_fm(KB_T, cfg.c3 + MH * 128))
        linear(b, "tm", hv, H.key, TB, w[0], cfg.c3 + MH * 128, MH * 384, ev_tm("KVB", cfg.c3 + MH * 128, BF16))
        linear(b, "tm", hv, H.key, TB, w[0], cfg.c4, cfg.B_O, ev_tm("OB", cfg.c4, BF16))
        linear(b, "tm", hv, H.key, TB, w[0], cfg.c5, cfg.B_G, ev_tm("GB", cfg.c5, F32))


def declare_io(b, stack):
    cfg = b.cfg
    T, D = cfg.T, cfg.D
    b.din("x", [T, D])
    b.din("norm_mix", [2, D])
    b.din("ev_w_in", [1, D, cfg.EVEN_IN])
    b.din("gdn_conv", [1, 3, cfg.A_QKV])
    b.din("gdn_a_log", [1, 2, cfg.GH])
    b.din("gdn_dt_bias", [1, 2, cfg.GH])
    b.din("gdn_norm", [1, 128])
    b.din("ml_gate_bias", [1, 2, 2, cfg.MH])
    b.din("ml_norm", [1, cfg.MH * 256])
    b.din("ev_w_out", [1, cfg.EVEN_MIX, D])
    b.din("od_w_in", [1, D, cfg.ODD_IN])
    b.din("od_w_out", [1, cfg.ODD_MIX, D])
    b.din("norm_ffn", [2, D])
    b.din("ffn_w_up", [2, D, 2 * cfg.DFF])
    b.din("ffn_conv", [2, 3, cfg.DFF])
    b.din("ffn_conv_b", [2, cfg.DFF])
    b.din("ffn_w_down", [2, cfg.DFF, D])
    b.din("norm_final", [D])
    for wn, shp in (("ev_w_in", [1, D, cfg.EVEN_IN]), ("ev_w_out", [1, cfg.EVEN_MIX, D]), ("od_w_in", [1, D, cfg.ODD_IN]),
                    ("od_w_out", [1, cfg.ODD_MIX, D]), ("ffn_w_up", [2, D, 2 * cfg.DFF]), ("ffn_w_down", [2, cfg.DFF, D])):
        b.dscr("b_" + wn, shp, BF16)
    b.din("kmask", [T + 2048], BF16)
    b.din("pos", [T])
    b.dout("y", [T, D])
    GH, MH = cfg.GH, cfg.MH
    b.dscr("QKVA_T", [cfg.A_QKV, T + 2], BF16)
    b.dscr("Z", [T, cfg.A_Z], BF16)
    b.dscr("GA", [T, cfg.A_G], F32)
    b.dscr("QB_T", [MH * 128, T], BF16)
    b.dscr("KB_T", [MH * 128, T], BF16)
    b.dscr("KVB", [T, MH * 384], BF16)
    b.dscr("OB", [T, cfg.B_O], BF16)
    b.dscr("GB", [T, cfg.B_G], F32)
    b.dscr("OA0", [T, GH * 128], F32)
    b.dscr("OA1", [T, GH * 128], F32)
    b.dscr("HB0", [T, MH * 256], F32)
    b.dscr("HB1", [T, MH * 256], F32)
    b.dscr("XL1", [T, D], F32)
    b.dscr("XL2", [T, D], F32)
    b.dscr("XL3", [T, D], F32)
    b.dscr("XL4", [T, D], F32)
    NQ = 3 * cfg.SLOTS * 128
    b.dscr("QT1", [NQ, T], BF16)
    b.dscr("KT1", [NQ, T + 2048], BF16)
    b.dscr("VT1", [T + 2048, NQ], BF16)
    b.dscr("MIXT", [D, T], BF16)
    b.din("ropec", [32, 2])
    b.din("qvalid", [T])
    DC = cfg.DC
    b.gmix = b.sb(stack, "gmix", [128, 2 * DC], F32)
    b.gffn = b.sb(stack, "gffn", [128, 2 * DC], F32)
    b.gfin = b.sb(stack, "gfin", [128, DC], F32)
    d = b.dram
    with b.nc.allow_non_contiguous_dma(reason="tiny param transposes"):
        pass
    for l in range(2):
        b.s.dma("sp", lambda e, l=l: e.dma_start(out=b.gmix[:, l * DC:(l + 1) * DC],
                                                 in_=d["norm_mix"][l].rearrange("(c p) -> p c", p=128),
                                                 allow_slow_non_contiguous=True), "gmix", writes=["gmix"])
        b.s.dma("sp", lambda e, l=l: e.dma_start(out=b.gffn[:, l * DC:(l + 1) * DC],
                                                 in_=d["norm_ffn"][l].rearrange("(c p) -> p c", p=128),
                                                 allow_slow_non_contiguous=True), "gffn", writes=["gffn"])
    b.s.dma("sp", lambda e: e.dma_start(out=b.gfin[:, 0:DC], in_=d["norm_final"].rearrange("(c p) -> p c", p=128),
                                        allow_slow_non_contiguous=True), "gfin", writes=["gfin"])


def phase_wcast(b, stack):
    d = b.dram
    for wn in ("ev_w_in", "ffn_w_up", "ffn_w_down", "ev_w_out", "od_w_in", "od_w_out"):
        src, dst = d[wn], d["b_" + wn]
        L, R, C = src.shape
        for l in range(L):
            for r0 in range(0, R, 512):
                r1 = min(R, r0 + 512)
                b.s.dma("pool", lambda e, l=l, r0=r0, r1=r1, src=src, dst=dst: e.dma_start(out=dst[l, r0:r1, :], in_=src[l, r0:r1, :]),
                        "wcast", dw=["b_" + wn])


def build(cfg, phases, debug_outs=()):
    from contextlib import ExitStack
    b = B(cfg, debug_outs)
    stack = ExitStack()
    with stack:
        declare_io(b, stack)
        build_common(b, stack)
        for ph in phases:
            with ExitStack() as pst:
                b.pid = getattr(b, "pid", 0) + 1
                ph(b, pst)
                b.s.barrier()
        b.s.finish()
        n, nd = b.s.emit(stack)
        print("ops", n, "dma sems", nd)
    return b


class PHalf:
    def __init__(self, buf, off):
        self.buf, self.off, self.key = buf, off, buf.key

    def __getitem__(self, k):
        rows, cols = k
        assert cols.step is None
        return self.buf[rows, self.off + cols.start:self.off + cols.stop]


def run_interleaved(b, groups, group_pre, unit_fn, width):
    pending = []
    gi = 0
    active = {}
    busy = {}
    free = list(range(width))
    while True:
        while free and (pending or gi < len(groups)):
            if not pending:
                pending = list(group_pre(*groups[gi]))
                gi += 1
            chain = (pending[0][0], pending[0][2])
            if chain in busy.values():
                break
            slot = free.pop(0)
            busy[slot] = chain
            active[slot] = unit_fn(slot, *pending.pop(0))
        if not active:
            break
        for slot in sorted(active):
            try:
                next(active[slot])
            except StopIteration:
                del active[slot]
                del busy[slot]
                free.append(slot)


def mm1(b, P, n, lhsT, rhs, reads, m=128):
    return b.s.op("pe", lambda e: e.matmul(P[0:m, 0:n], lhsT, rhs, start=True, stop=True), reads=reads, writes=[P.key])


class _Stop(Exception):
    pass


def chk(n):
    import os
    if os.environ.get("GDN_STOP", "") == str(n):
        raise _Stop()


def phase_gdn(b, stack):
    try:
        phase_gdn_(b, stack)
    except _Stop:
        print("GDN stopped early")


def phase_gdn_(b, stack):
    cfg, s, d = b.cfg, b.s, b.dram
    T, GH = cfg.T, cfg.GH
    NCH = T // 128
    NSL = 4
    tmps = [b.gtmp] + [[b.sb(stack, "g%d_%d" % (j, i), [128, 128], F32) for i in range(36)] for j in range(1, NSL)]
    wides = [[b.sb(stack, "gw%d_%d" % (j, i), [128, 384], F32) for i in range(3)] for j in range(NSL)]
    xins = [[b.sb(stack, "gx%d_%d" % (j, i), [128, 3, 130], BF16) for i in range(2)] for j in range(NSL)]
    gt = [b.sb(stack, "gt%d" % i, [128, 4 * GH], F32) for i in range(4)]
    gqv = [b.sb(stack, "gqv%d" % i, [128, 128], F32) for i in range(4)]
    gsm = [b.sb(stack, "gs%d" % i, [128, 6 * GH], F32) for i in range(4)]
    cols = [[b.sb(stack, "gc%d_%d" % (j, i), [128, 8], F32) for i in range(4)] for j in range(NSL)]
    oos = [[b.sb(stack, "go%d_%d" % (j, i), [128, 128], F32) for i in range(2)] for j in range(NSL)]
    S = {(h, dd, p): b.sb(stack, "S%d_%d_%d" % (h, dd, p), [128, 128], F32) for h in range(GH) for dd in range(2)
         for p in range(2)}
    cw = b.sb(stack, "gcw", [128, GH, 9], F32)
    dg = b.sb(stack, "gdg", [128, GH * 9, 128], BF16)
    ea = b.sb(stack, "gea", [128, 2, GH], F32)
    dtb = b.sb(stack, "gdtb", [128, 2, GH], F32)
    BMf, BMb, SMf, SMb = b.BMf, b.BMb, b.SMf, b.SMb
    for h in range(GH):
        for a in range(3):
            r0 = a * GH * 128 + h * 128
            s.dma("sp", lambda e, h=h, a=a, r0=r0: e.dma_start(
                out=cw[:, h, a * 3:(a + 1) * 3], in_=d["gdn_conv"][0][:, r0:r0 + 128].rearrange("j p -> p j"),
                allow_slow_non_contiguous=True), cw.key, writes=[cw.key])
    s.dma("sp", lambda e: e.dma_start(out=ea[:], in_=d["gdn_a_log"][0:1].partition_broadcast(128)), ea.key, writes=[ea.key])
    s.dma("sp", lambda e: e.dma_start(out=dtb[:], in_=d["gdn_dt_bias"][0:1].partition_broadcast(128)), dtb.key,
          writes=[dtb.key])
    s.op("act", lambda e: e.activation(out=ea[:], in_=ea[:], func=AF.Exp), reads=[ea.key], writes=[ea.key])
    for h in range(GH):
        for a in range(3):
            for j in range(3):
                s.op("dve", lambda e, h=h, a=a, j=j: e.tensor_scalar(
                    out=dg[:, h * 9 + a * 3 + j, :], in0=b.identb[:], scalar1=cw[:, h, a * 3 + j:a * 3 + j + 1],
                    scalar2=None, op0=ALU.mult), reads=[b.identb.key, cw.key], writes=[dg.key])
        for dd in range(2):
            s.op("pool", lambda e, h=h, dd=dd: e.memset(S[(h, dd, 0)][:], 0.0), writes=[S[(h, dd, 0)].key])
    for r in range(0, cfg.A_QKV, 128):
        b.store("sp", d["QKVA_T"][r:r + 128, 0:1], b.zerob[:, 0:1], "zpad", [b.zerob.key], ["QKVA_T"])
        b.store("sp", d["QKVA_T"][r:r + 128, T + 1:T + 2], b.zerob[:, 0:1], "zpad", [b.zerob.key], ["QKVA_T"])
    qkv3 = d["QKVA_T"].rearrange("(a h p) t -> p a h t", a=3, h=GH)
    units_q = [(step, dd) for step in range(NCH) for dd in range(2)]

    def group_pre(step, dd):
        if True:
            c = step if dd == 0 else NCH - 1 - step
            t0 = c * 128
            Tri = b.U if dd == 0 else b.L
            BM = BMf if dd == 0 else BMb
            SM = SMf if dd == 0 else SMb
            GT = b.rot("ggt", gt)
            GS = b.rot("ggs", gsm)
            b.load("sp", GT[:], d["GA"][t0:t0 + 128, :], GT.key, [], [GT.key], dr=["GA"])
            QV = b.rot("gqv", gqv)
            b.load("sp", QV[:], d["qvalid"][t0:t0 + 128].partition_broadcast(128), QV.key, [], [QV.key])
            s.op("dve", lambda e, GT=GT, GS=GS, dd=dd: e.tensor_tensor(
                out=GS[:, 0:GH], in0=GT[:, dd * GH:(dd + 1) * GH], in1=dtb[:, dd, :], op=ALU.add),
                reads=[GT.key, dtb.key], writes=[GS.key])
            s.op("act", lambda e, GS=GS: e.activation(out=GS[:, 0:GH], in_=GS[:, 0:GH], func=AF.Exp),
                 reads=[GS.key], writes=[GS.key])
            s.op("act", lambda e, GS=GS: e.activation(out=GS[:, 0:GH], in_=GS[:, 0:GH], func=AF.Ln, bias=1.0),
                 reads=[GS.key], writes=[GS.key])
            s.op("dve", lambda e, GS=GS, dd=dd: e.scalar_tensor_tensor(
                out=GS[:, GH:2 * GH], in0=GS[:, 0:GH], scalar=-1.0, in1=ea[:, dd, :], op0=ALU.mult, op1=ALU.mult),
                reads=[GS.key, ea.key], writes=[GS.key])
            s.op("act", lambda e, GS=GS, GT=GT, dd=dd: e.activation(
                out=GS[:, 2 * GH:3 * GH], in_=GT[:, (2 + dd) * GH:(3 + dd) * GH], func=AF.Sigmoid),
                reads=[GT.key], writes=[GS.key])
            s.op("dve", lambda e, GS=GS: e.tensor_scalar(out=GS[:, 3 * GH:4 * GH], in0=GS[:, 2 * GH:3 * GH], scalar1=-1.0,
                                                        scalar2=None, op0=ALU.mult), reads=[GS.key], writes=[GS.key])
            return [(h, step, dd, c, t0, Tri, BM, SM, GT, GS, QV, None) for h in range(GH)]
    def gdn_unit(slot, h, step, dd, c, t0, Tri, BM, SM, GT, GS, QV, KV):
        T_ = lambda: b.rot("gtmp%d" % slot, tmps[slot])
        C_ = lambda: b.rot("gcol%d" % slot, cols[slot])
        PS = lambda: b.rot("Ps%d" % slot, b.P[2 * slot:2 * slot + 2])
        wide = wides[slot]
        xin = xins[slot]
        gcolv = GS[:, GH + h:GH + h + 1]
        beta = GS[:, 2 * GH + h:2 * GH + h + 1]
        nbeta = GS[:, 3 * GH + h:3 * GH + h + 1]
        XI = b.rot("gxin%d" % slot, xin)
        b.load("sp", XI[:], qkv3[:, :, h, t0:t0 + 130], XI.key, [], [XI.key], dr=["QKVA_T"])
        yield
        Pc = PS()

        def conv(e, XI=XI, Pc=Pc, h=h):
            for a in range(3):
                for j in range(3):
                    ins = e.matmul(Pc[:, a * 128:(a + 1) * 128], dg[:, h * 9 + a * 3 + j, :], XI[:, a, j:j + 128],
                                   start=(j == 0), stop=(j == 2))
            return ins
        s.op("pe", conv, reads=[XI.key, dg.key], writes=[Pc.key])
        yield
        SL = b.rot("gwide%d" % slot, wide)
        s.op("act", lambda e, SL=SL, Pc=Pc: e.activation(out=SL[:], in_=Pc[:, 0:384], func=AF.Silu),
             reads=[Pc.key], writes=[SL.key])
        yield
        s.op("dve", lambda e, SL=SL, QV=QV: e.tensor_tensor(
            out=SL[:].rearrange("p (a t) -> p a t", a=3), in0=SL[:].rearrange("p (a t) -> p a t", a=3),
            in1=QV[:].unsqueeze(1).to_broadcast([128, 3, 128]), op=ALU.mult),
            reads=[SL.key, QV.key], writes=[SL.key])
        yield
        chk(2)
        SQ = b.rot("gwide%d" % slot, wide)
        s.op("act", lambda e, SL=SL, SQ=SQ: e.activation(out=SQ[:, 0:256], in_=SL[:, 0:256], func=AF.Square),
             reads=[SL.key], writes=[SQ.key])
        yield
        Pn = PS()
        mm1(b, Pn, 256, b.onesf[:], SQ[:, 0:256], [b.onesf.key, SQ.key])
        yield
        s.op("act", lambda e, SQ=SQ, Pn=Pn: e.activation(out=SQ[:, 0:256], in_=Pn[:, 0:256], func=AF.Sqrt, bias=EPS),
             reads=[Pn.key], writes=[SQ.key])
        yield
        s.op("dve", lambda e, SQ=SQ: e.reciprocal(out=SQ[:, 0:256], in_=SQ[:, 0:256]), reads=[SQ.key], writes=[SQ.key])
        yield
        QK = b.rot("gwide%d" % slot, wide)
        s.op("dve", lambda e, QK=QK, SL=SL, SQ=SQ: e.scalar_tensor_tensor(
            out=QK[:, 0:128], in0=SL[:, 0:128], scalar=128 ** -0.5, in1=SQ[:, 0:128], op0=ALU.mult, op1=ALU.mult),
            reads=[SL.key, SQ.key], writes=[QK.key])
        yield
        s.op("dve", lambda e, QK=QK, SL=SL, SQ=SQ: e.tensor_tensor(
            out=QK[:, 128:256], in0=SL[:, 128:256], in1=SQ[:, 128:256], op=ALU.mult),
            reads=[SL.key, SQ.key, QK.key], writes=[QK.key])
        yield
        qT, kT = QK[:, 0:128], QK[:, 128:256]
        chk(3)
        GB_ = T_()
        s.op("dve", lambda e, GB_=GB_, gcolv=gcolv: e.tensor_scalar(out=GB_[:], in0=b.onesf[:], scalar1=gcolv,
                                                                  scalar2=None, op0=ALU.mult),
             reads=[b.onesf.key, GS.key], writes=[GB_.key])
        yield
        Pg = PS()

        def cums(e, Pg=Pg, GB_=GB_, Tri=Tri, BM=BM, gcolv=gcolv):
            e.matmul(Pg[:, 0:128], GB_[:], Tri[:], start=True, stop=False)
            e.matmul(Pg[:, 0:128], b.identf[:], BM[:], start=False, stop=True)
            e.matmul(Pg[:, 128:129], Tri[:], gcolv, start=True, stop=True)
            return e.matmul(Pg[:, 160:161], GB_[:], b.onesf[:, 0:1], start=True, stop=True)
        s.op("pe", cums, reads=[GB_.key, Tri.key, BM.key, b.identf.key, GS.key, b.onesf.key], writes=[Pg.key])
        yield
        CL = C_()
        s.op("dve", lambda e, CL=CL, Pg=Pg: e.tensor_copy(out=CL[:, 0:1], in_=Pg[:, 128:129]),
             reads=[Pg.key], writes=[CL.key])
        yield
        s.op("dve", lambda e, CL=CL, Pg=Pg: e.tensor_copy(out=CL[:, 1:2], in_=Pg[:, 160:161]),
             reads=[Pg.key, CL.key], writes=[CL.key])
        yield
        E = T_()
        s.op("act", lambda e, E=E, Pg=Pg, CL=CL: e.activation(out=E[:], in_=Pg[:, 0:128], func=AF.Exp, scale=-1.0,
                                                             bias=CL[:, 0:1]), reads=[Pg.key, CL.key], writes=[E.key])
        yield
        s.op("act", lambda e, CL=CL: e.activation(out=CL[:, 2:4], in_=CL[:, 0:2], func=AF.Exp),
             reads=[CL.key], writes=[CL.key])
        yield
        s.op("act", lambda e, CL=CL: e.activation(out=CL[:, 4:5], in_=CL[:, 0:1], func=AF.Exp, scale=-1.0,
                                                 bias=CL[:, 1:2]), reads=[CL.key], writes=[CL.key])
        yield
        s.op("dve", lambda e, CL=CL, beta=beta: e.tensor_tensor(out=CL[:, 5:6], in0=CL[:, 2:3], in1=beta, op=ALU.mult),
             reads=[CL.key, GS.key], writes=[CL.key])
        yield
        chk(4)
        Pt = PS()

        def trkv(e, Pt=Pt, kT=kT, SL=SL):
            e.transpose(Pt[:, 0:128], kT, b.identf[:])
            return e.transpose(Pt[:, 128:256], SL[:, 256:384], b.identf[:])
        s.op("pe", trkv, reads=[QK.key, SL.key, b.identf.key], writes=[Pt.key])
        yield
        chk(41)
        KBG, KDEC, VB = T_(), T_(), T_()
        s.op("dve", lambda e, KBG=KBG, Pt=Pt, CL=CL: e.tensor_scalar(out=KBG[:], in0=Pt[:, 0:128], scalar1=CL[:, 5:6],
                                                                   scalar2=None, op0=ALU.mult),
             reads=[Pt.key, CL.key], writes=[KBG.key])
        yield
        chk(42)
        s.op("act", lambda e, KDEC=KDEC, Pt=Pt, CL=CL: e.activation(out=KDEC[:], in_=Pt[:, 0:128], func=AF.Identity,
                                                                  scale=CL[:, 4:5]),
             reads=[Pt.key, CL.key], writes=[KDEC.key])
        yield
        chk(43)
        s.op("dve", lambda e, VB=VB, Pt=Pt, beta=beta: e.tensor_scalar(out=VB[:], in0=Pt[:, 128:256], scalar1=beta,
                                                                     scalar2=None, op0=ALU.mult),
             reads=[Pt.key, GS.key], writes=[VB.key])
        yield
        chk(5)
        Pk = PS()

        def gqk(e, Pk=Pk, kT=kT, qT=qT):
            e.matmul(Pk[:, 0:128], kT, kT, start=True, stop=True)
            return e.matmul(Pk[:, 128:256], kT, qT, start=True, stop=True)
        s.op("pe", gqk, reads=[QK.key], writes=[Pk.key])
        yield
        ES = T_()
        s.op("dve", lambda e, ES=ES, E=E, SM=SM: e.tensor_tensor(out=ES[:], in0=E[:], in1=SM[:], op=ALU.mult),
             reads=[E.key, SM.key], writes=[ES.key])
        yield
        M = T_()
        GG = T_()
        s.op("act", lambda e, GG=GG, Pk=Pk, nbeta=nbeta: e.activation(out=GG[:], in_=Pk[:, 0:128], func=AF.Identity,
                                                                   scale=nbeta),
             reads=[Pk.key, GS.key], writes=[GG.key])
        yield
        s.op("dve", lambda e, M=M, GG=GG, ES=ES: e.tensor_tensor(out=M[:], in0=GG[:], in1=ES[:], op=ALU.mult),
             reads=[GG.key, ES.key], writes=[M.key])
        yield
        chk(51)
        Pe = PS()

        def trne(e, Pe=Pe, M=M, E=E):
            e.transpose(Pe[:, 0:128], M[:], b.identf[:])
            return e.transpose(Pe[:, 128:256], E[:], b.identf[:])
        s.op("pe", trne, reads=[M.key, E.key, b.identf.key], writes=[Pe.key])
        yield
        MT = T_()
        b.copy("act", MT[:], Pe[:, 0:128], [Pe.key], [MT.key])
        yield
        PP = T_()
        s.op("dve", lambda e, PP=PP, Pe=Pe: e.tensor_tensor(out=PP[:], in0=Pe[:, 0:128], in1=b.identf[:], op=ALU.add),
             reads=[Pe.key, b.identf.key], writes=[PP.key])
        yield
        AT = T_()
        s.op("dve", lambda e, AT=AT, Pe=Pe, Pk=Pk: e.tensor_copy(out=AT[:], in_=Pe[:, 128:256]),
             reads=[Pe.key], writes=[AT.key])
        yield
        s.op("dve", lambda e, AT=AT, Pk=Pk: e.tensor_tensor(out=AT[:], in0=Pk[:, 128:256], in1=AT[:], op=ALU.mult),
             reads=[Pk.key, AT.key], writes=[AT.key])
        yield
        chk(52)
        for k in range(1, 7):
            chk(52 + k)
            Pm = PS()

            def sq(e, Pm=Pm, M=M, MT=MT, k=k):
                ins = e.matmul(Pm[:, 0:128], MT[:], M[:], start=True, stop=True)
                if k < 6:
                    ins = e.matmul(Pm[:, 128:256], M[:], MT[:], start=True, stop=True)
                return ins
            s.op("pe", sq, reads=[M.key, MT.key], writes=[Pm.key])
            yield
            M2 = T_()
            b.copy("act", M2[:], Pm[:, 0:128], [Pm.key], [M2.key])
            yield
            if k < 6:
                MT2 = T_()
                b.copy("dve", MT2[:], Pm[:, 128:256], [Pm.key], [MT2.key])
                yield
            Pp = PS()
            mm1(b, Pp, 128, M2[:], PP[:], [M2.key, PP.key])
            yield
            PP2 = T_()
            s.op("dve", lambda e, PP2=PP2, Pp=Pp, PP=PP: e.tensor_tensor(out=PP2[:], in0=Pp[:, 0:128], in1=PP[:],
                                                                      op=ALU.add),
                 reads=[Pp.key, PP.key], writes=[PP2.key])
            yield
            PP = PP2
            M = M2
            if k < 6:
                MT = MT2
        chk(6)
        Pw = PS()

        def wu(e, Pw=Pw, KBG=KBG, PP=PP, VB=VB):
            e.matmul(Pw[:, 0:128], KBG[:], PP[:], start=True, stop=True)
            return e.matmul(Pw[:, 128:256], PP[:], VB[:], start=True, stop=True)
        s.op("pe", wu, reads=[KBG.key, PP.key, VB.key], writes=[Pw.key])
        yield
        WT, UU = T_(), T_()
        b.copy("act", WT[:], Pw[:, 0:128], [Pw.key], [WT.key])
        yield
        b.copy("dve", UU[:], Pw[:, 128:256], [Pw.key], [UU.key])
        yield
        chk(7)
        Sc = S[(h, dd, step % 2)]
        Sn = S[(h, dd, (step + 1) % 2)]
        Pr = PS()

        def r1(e, Pr=Pr, WT=WT, Sc=Sc, qT=qT):
            e.matmul(Pr[:, 0:128], WT[:], Sc[:], start=True, stop=True)
            return e.matmul(Pr[:, 128:256], qT, Sc[:], start=True, stop=True)
        s.op("pe", r1, reads=[WT.key, Sc.key, QK.key], writes=[Pr.key])
        yield
        VN = T_()
        s.op("dve", lambda e, VN=VN, UU=UU, Pr=Pr: e.tensor_tensor(out=VN[:], in0=UU[:], in1=Pr[:, 0:128],
                                                                op=ALU.subtract),
             reads=[UU.key, Pr.key], writes=[VN.key])
        yield
        OT = T_()
        s.op("act", lambda e, OT=OT, Pr=Pr, CL=CL: e.activation(out=OT[:], in_=Pr[:, 128:256], func=AF.Identity,
                                                               scale=CL[:, 2:3]),
             reads=[Pr.key, CL.key], writes=[OT.key])
        yield
        Po = PS()

        def r2(e, Po=Po, AT=AT, VN=VN, KDEC=KDEC):
            e.matmul(Po[:, 0:128], AT[:], VN[:], start=True, stop=True)
            return e.matmul(Po[:, 128:256], KDEC[:], VN[:], start=True, stop=True)
        s.op("pe", r2, reads=[AT.key, VN.key, KDEC.key], writes=[Po.key])
        yield
        OO = b.rot("goo%d" % slot, oos[slot])
        s.op("dve", lambda e, OO=OO, OT=OT, Po=Po: e.tensor_tensor(out=OO[:], in0=OT[:], in1=Po[:, 0:128], op=ALU.add),
             reads=[OT.key, Po.key], writes=[OO.key])
        yield
        SS = T_()
        s.op("act", lambda e, SS=SS, Sc=Sc, CL=CL: e.activation(out=SS[:], in_=Sc[:], func=AF.Identity, scale=CL[:, 3:4]),
             reads=[Sc.key, CL.key], writes=[SS.key])
        yield
        s.op("dve", lambda e, Sn=Sn, SS=SS, Po=Po: e.tensor_tensor(out=Sn[:], in0=SS[:], in1=Po[:, 128:256], op=ALU.add),
             reads=[SS.key, Po.key], writes=[Sn.key])
        yield
        b.store("sp", d["OA%d" % dd][t0:t0 + 128, h * 128:(h + 1) * 128], OO[:], OO.key, [OO.key], ["OA%d" % dd])
        yield
    run_interleaved(b, units_q, group_pre, gdn_unit, NSL)


def phase_mlstm(b, stack):
    cfg, s, d = b.cfg, b.s, b.dram
    T, MH = cfg.T, cfg.MH
    NCH = T // 128
    tmps = [b.gtmp] + [[b.sb(stack, "mt%d_%d" % (j, i), [128, 128], F32) for i in range(12)] for j in (1, 2, 3)]
    cols = [[b.sb(stack, "mc%d_%d" % (j, i), [128, 16], F32) for i in range(4)] for j in range(4)]
    gt = [b.sb(stack, "mgt%d" % i, [128, 4 * MH], F32) for i in range(4)]
    gs = [b.sb(stack, "mgs%d" % i, [128, 4 * MH], F32) for i in range(4)]
    kvt = [b.sb(stack, "mkv%d" % i, [128, MH * 384], BF16) for i in range(3)]
    qkts = [[b.sb(stack, "mqk%d_%d" % (j, i), [128, 2, 128], BF16) for i in range(2)] for j in range(4)]
    w257s = [[b.sb(stack, "mw%d_%d" % (j, i), [128, 264], F32) for i in range(8)] for j in range(4)]
    Cst = {(h, dd, p): b.sb(stack, "C%d_%d_%d" % (h, dd, p), [128, 264], F32) for h in range(MH) for dd in range(2)
           for p in range(2)}
    Mst = {(h, dd, p): b.sb(stack, "M%d_%d_%d" % (h, dd, p), [128, 2], F32) for h in range(MH) for dd in range(2)
           for p in range(2)}
    mlb = b.sb(stack, "mlb", [128, 4 * MH], F32)
    BMf, BMb = b.BMf, b.BMb
    s.dma("sp", lambda e: e.dma_start(out=mlb[:], in_=d["ml_gate_bias"].rearrange("a k d h -> a (k d h)").partition_broadcast(128)),
          mlb.key, writes=[mlb.key])
    for h in range(MH):
        for dd in range(2):
            s.op("pool", lambda e, h=h, dd=dd: e.memset(Cst[(h, dd, 0)][:], 0.0), writes=[Cst[(h, dd, 0)].key])
            s.op("pool", lambda e, h=h, dd=dd: e.memset(Mst[(h, dd, 0)][:], 0.0), writes=[Mst[(h, dd, 0)].key])
    qb3 = d["QB_T"].rearrange("(h p) t -> p h t", h=MH)
    kb3 = d["KB_T"].rearrange("(h p) t -> p h t", h=MH)
    units_q = [(step, dd) for step in range(NCH) for dd in range(2)]

    def group_pre(step, dd):
        if True:
            c = step if dd == 0 else NCH - 1 - step
            t0 = c * 128
            Tri = b.U if dd == 0 else b.L
            BM = BMf if dd == 0 else BMb
            GT = b.rot("mgt", gt)
            GS = b.rot("mgs", gs)
            b.load("sp", GT[:], d["GB"][t0:t0 + 128, :], GT.key, [], [GT.key], dr=["GB"])
            s.op("dve", lambda e, GT=GT: e.tensor_tensor(out=GT[:], in0=GT[:], in1=mlb[:], op=ALU.add),
                 reads=[GT.key, mlb.key], writes=[GT.key])
            s.op("dve", lambda e, GT=GT, GS=GS, dd=dd: e.tensor_copy(out=GS[:, 0:MH], in_=GT[:, dd * MH:(dd + 1) * MH]),
                 reads=[GT.key], writes=[GS.key])
            s.op("dve", lambda e, GT=GT, GS=GS, dd=dd: e.tensor_scalar(out=GS[:, MH:2 * MH], in0=GT[:, dd * MH:(dd + 1) * MH],
                                                                    scalar1=-1.0, scalar2=None, op0=ALU.mult),
                 reads=[GT.key, GS.key], writes=[GS.key])
            s.op("act", lambda e, GT=GT, GS=GS, dd=dd: e.activation(out=GS[:, 2 * MH:3 * MH],
                                                                 in_=GT[:, (2 + dd) * MH:(3 + dd) * MH], func=AF.Exp, scale=-1.0),
                 reads=[GT.key, GS.key], writes=[GS.key])
            s.op("act", lambda e, GS=GS: e.activation(out=GS[:, 2 * MH:3 * MH], in_=GS[:, 2 * MH:3 * MH], func=AF.Ln, bias=1.0),
                 reads=[GS.key], writes=[GS.key])
            s.op("dve", lambda e, GS=GS: e.tensor_scalar(out=GS[:, 2 * MH:3 * MH], in0=GS[:, 2 * MH:3 * MH], scalar1=-1.0,
                                                        scalar2=None, op0=ALU.mult), reads=[GS.key], writes=[GS.key])
            KV = b.rot("mkv", kvt)
            b.load("sp", KV[:], d["KVB"][t0:t0 + 128, :], KV.key, [], [KV.key], dr=["KVB"])
            return [(h, step, dd, c, t0, Tri, BM, None, GT, GS, None, KV) for h in range(MH)]
    def ml_unit(slot, h, step, dd, c, t0, Tri, BM, SM, GT, GS, QV, KV):
        T_ = lambda: b.rot("gtmp%d" % slot, tmps[slot])
        C_ = lambda: b.rot("mcol%d" % slot, cols[slot])
        PS = lambda: b.rot("Ps%d" % slot, b.P[2 * slot:2 * slot + 2])
        W_ = lambda: b.rot("mw%d" % slot, w257s[slot])
        qkt = qkts[slot]
        igc = GS[:, h:h + 1]
        nigc = GS[:, MH + h:MH + h + 1]
        lfc = GS[:, 2 * MH + h:2 * MH + h + 1]
        Mc, Mn = Mst[(h, dd, step % 2)], Mst[(h, dd, (step + 1) % 2)]
        Cc, Cn = Cst[(h, dd, step % 2)], Cst[(h, dd, (step + 1) % 2)]
        QKb = b.rot("mqk%d" % slot, qkt)
        b.load("sp", QKb[:, 0, :], qb3[:, h, t0:t0 + 128], QKb.key, [], [QKb.key], dr=["QB_T"])
        yield
        b.load("sp", QKb[:, 1, :], kb3[:, h, t0:t0 + 128], QKb.key, [], [QKb.key], dr=["KB_T"])
        yield
        QK = W_()
        s.op("act", lambda e, QK=QK, QKb=QKb: e.activation(out=QK[:, 0:256],
                                                          in_=QKb[:].rearrange("p a t -> p (a t)"), func=AF.Copy),
             reads=[QKb.key], writes=[QK.key])
        yield
        qT, kT = QK[:, 0:128], QK[:, 128:256]
        VP = W_()
        s.op("dve", lambda e, VP=VP, KV=KV, h=h: e.tensor_copy(out=VP[:, 0:256],
                                                            in_=KV[:, MH * 128 + h * 256:MH * 128 + (h + 1) * 256]),
             reads=[KV.key], writes=[VP.key])
        yield
        s.op("dve", lambda e, VP=VP: e.memset(VP[:, 256:257], 1.0), reads=[VP.key], writes=[VP.key])
        yield
        LFB, NIB = T_(), T_()
        s.op("dve", lambda e, LFB=LFB, lfc=lfc: e.tensor_scalar(out=LFB[:], in0=b.onesf[:], scalar1=lfc, scalar2=None,
                                                             op0=ALU.mult), reads=[b.onesf.key, GS.key], writes=[LFB.key])
        yield
        s.op("dve", lambda e, NIB=NIB, nigc=nigc: e.tensor_scalar(out=NIB[:], in0=b.onesf[:], scalar1=nigc, scalar2=None,
                                                               op0=ALU.mult), reads=[b.onesf.key, GS.key], writes=[NIB.key])
        yield
        Pg = PS()

        def cums(e, Pg=Pg, LFB=LFB, NIB=NIB, Tri=Tri, BM=BM, lfc=lfc):
            e.matmul(Pg[:, 0:128], LFB[:], Tri[:], start=True, stop=False)
            e.matmul(Pg[:, 0:128], NIB[:], b.identf[:], start=False, stop=False)
            e.matmul(Pg[:, 0:128], b.identf[:], BM[:], start=False, stop=True)
            e.matmul(Pg[:, 128:256], LFB[:], Tri[:], start=True, stop=False)
            e.matmul(Pg[:, 128:256], NIB[:], b.identf[:], start=False, stop=True)
            e.matmul(Pg[:, 256:257], Tri[:], lfc, start=True, stop=True)
            return e.matmul(Pg[:, 288:289], LFB[:], b.onesf[:, 0:1], start=True, stop=True)
        s.op("pe", cums, reads=[LFB.key, NIB.key, Tri.key, BM.key, b.identf.key, GS.key, b.onesf.key], writes=[Pg.key])
        yield
        CL = C_()
        s.op("dve", lambda e, CL=CL, Pg=Pg: e.tensor_copy(out=CL[:, 0:1], in_=Pg[:, 256:257]), reads=[Pg.key], writes=[CL.key])
        yield
        s.op("dve", lambda e, CL=CL, Pg=Pg: e.tensor_copy(out=CL[:, 1:2], in_=Pg[:, 288:289]),
             reads=[Pg.key, CL.key], writes=[CL.key])
        yield
        s.op("dve", lambda e, CL=CL, Pg=Pg: e.tensor_reduce(out=CL[:, 2:3], in_=Pg[:, 0:128], axis=AX.X, op=ALU.min),
             reads=[Pg.key, CL.key], writes=[CL.key])
        yield
        s.op("dve", lambda e, CL=CL, Pg=Pg: e.tensor_reduce(out=CL[:, 3:4], in_=Pg[:, 128:256], axis=AX.X, op=ALU.min),
             reads=[Pg.key, CL.key], writes=[CL.key])
        yield
        s.op("dve", lambda e, CL=CL, Mc=Mc: e.scalar_tensor_tensor(out=CL[:, 4:5], in0=CL[:, 2:3], scalar=-1.0, in1=Mc[:, 0:1],
                                                                op0=ALU.mult, op1=ALU.max),
             reads=[CL.key, Mc.key], writes=[CL.key])
        yield
        s.op("dve", lambda e, CL=CL, Mc=Mc: e.scalar_tensor_tensor(out=CL[:, 9:10], in0=CL[:, 3:4], scalar=-1.0, in1=Mc[:, 0:1],
                                                                op0=ALU.mult, op1=ALU.max),
             reads=[CL.key, Mc.key], writes=[CL.key])
        yield
        s.op("dve", lambda e, CL=CL: e.tensor_scalar(out=CL[:, 5:6], in0=CL[:, 4:5], scalar1=-1.0, scalar2=None, op0=ALU.mult),
             reads=[CL.key], writes=[CL.key])
        yield
        s.op("dve", lambda e, CL=CL: e.tensor_scalar(out=CL[:, 10:11], in0=CL[:, 9:10], scalar1=-1.0, scalar2=None, op0=ALU.mult),
             reads=[CL.key], writes=[CL.key])
        yield
        s.op("dve", lambda e, CL=CL: e.tensor_tensor(out=CL[:, 7:8], in0=CL[:, 0:1], in1=CL[:, 4:5], op=ALU.add),
             reads=[CL.key], writes=[CL.key])
        yield
        s.op("dve", lambda e, CL=CL, igc=igc: e.tensor_tensor(out=CL[:, 12:13], in0=CL[:, 0:1], in1=igc, op=ALU.subtract),
             reads=[CL.key, GS.key], writes=[CL.key])
        yield
        s.op("dve", lambda e, CL=CL, Mn=Mn: e.tensor_tensor(out=Mn[:, 0:1], in0=CL[:, 1:2], in1=CL[:, 9:10], op=ALU.add),
             reads=[CL.key], writes=[Mn.key])
        yield
        EW = T_()
        s.op("act", lambda e, EW=EW, Pg=Pg, CL=CL: e.activation(out=EW[:], in_=Pg[:, 0:128], func=AF.Exp, scale=-1.0,
                                                               bias=CL[:, 5:6]), reads=[Pg.key, CL.key], writes=[EW.key])
        yield
        s.op("act", lambda e, CL=CL, Mc=Mc: e.activation(out=CL[:, 6:7], in_=CL[:, 4:5], func=AF.Exp, scale=-1.0,
                                                        bias=Mc[:, 0:1]), reads=[CL.key, Mc.key], writes=[CL.key])
        yield
        s.op("act", lambda e, CL=CL: e.activation(out=CL[:, 8:9], in_=CL[:, 7:8], func=AF.Exp, scale=-1.0),
             reads=[CL.key], writes=[CL.key])
        yield
        s.op("act", lambda e, CL=CL, Mc=Mc: e.activation(out=CL[:, 11:12], in_=CL[:, 9:10], func=AF.Exp, scale=-1.0,
                                                        bias=Mc[:, 0:1]), reads=[CL.key, Mc.key], writes=[CL.key])
        yield
        s.op("act", lambda e, CL=CL: e.activation(out=CL[:, 13:14], in_=CL[:, 12:13], func=AF.Exp, scale=-1.0,
                                                 bias=CL[:, 10:11]), reads=[CL.key], writes=[CL.key])
        yield
        Pq = PS()
        mm1(b, Pq, 128, qT, kT, [QK.key])
        yield
        WI = T_()
        s.op("dve", lambda e, WI=WI, EW=EW, Pq=Pq: e.tensor_tensor(out=WI[:], in0=Pq[:, 0:128], in1=EW[:], op=ALU.mult),
             reads=[Pq.key, EW.key], writes=[WI.key])
        yield
        Pt = PS()
        s.op("pe", lambda e, Pt=Pt, WI=WI: e.transpose(Pt[:, 0:128], WI[:], b.identf[:]),
             reads=[WI.key, b.identf.key], writes=[Pt.key])
        yield
        WIT = T_()
        b.copy("act", WIT[:], Pt[:, 0:128], [Pt.key], [WIT.key])
        yield
        WK = T_()
        s.op("dve", lambda e, WK=WK, KV=KV, CL=CL, h=h: e.tensor_scalar(out=WK[:], in0=KV[:, h * 128:(h + 1) * 128],
                                                                     scalar1=CL[:, 13:14], scalar2=None, op0=ALU.mult),
             reads=[KV.key, CL.key], writes=[WK.key])
        yield
        Pa = PS()
        mm1(b, Pa, 257, qT, Cc[:, 0:257], [QK.key, Cc.key])
        yield
        T1 = W_()
        s.op("act", lambda e, T1=T1, Pa=Pa, CL=CL: e.activation(out=T1[:, 0:257], in_=Pa[:, 0:257], func=AF.Identity,
                                                               scale=CL[:, 6:7]), reads=[Pa.key, CL.key], writes=[T1.key])
        yield
        Pb = PS()
        mm1(b, Pb, 257, WIT[:], VP[:, 0:257], [WIT.key, VP.key])
        yield
        ND = W_()
        s.op("dve", lambda e, ND=ND, T1=T1, Pb=Pb: e.tensor_tensor(out=ND[:, 0:257], in0=Pb[:, 0:257], in1=T1[:, 0:257],
                                                                op=ALU.add), reads=[Pb.key, T1.key], writes=[ND.key])
        yield
        CD = C_()
        s.op("dve", lambda e, CD=CD, ND=ND: e.scalar_tensor_tensor(out=CD[:, 0:1], in0=ND[:, 256:257], scalar=-1.0,
                                                                  in1=ND[:, 256:257], op0=ALU.mult, op1=ALU.max),
             reads=[ND.key], writes=[CD.key])
        yield
        s.op("dve", lambda e, CD=CD, CL=CL: e.tensor_tensor(out=CD[:, 1:2], in0=CD[:, 0:1], in1=CL[:, 8:9], op=ALU.max),
             reads=[CD.key, CL.key], writes=[CD.key])
        yield
        s.op("dve", lambda e, CD=CD: e.reciprocal(out=CD[:, 2:3], in_=CD[:, 1:2]), reads=[CD.key], writes=[CD.key])
        yield
        HO = W_()
        s.op("dve", lambda e, HO=HO, ND=ND, CD=CD: e.tensor_scalar(out=HO[:, 0:256], in0=ND[:, 0:256], scalar1=CD[:, 2:3],
                                                                scalar2=None, op0=ALU.mult),
             reads=[ND.key, CD.key], writes=[HO.key])
        yield
        b.store("sp", d["HB%d" % dd][t0:t0 + 128, h * 256:(h + 1) * 256], HO[:, 0:256], HO.key, [HO.key], ["HB%d" % dd])
        yield
        Pc = PS()
        mm1(b, Pc, 257, WK[:], VP[:, 0:257], [WK.key, VP.key])
        yield
        T2 = W_()
        s.op("act", lambda e, T2=T2, Cc=Cc, CL=CL: e.activation(out=T2[:, 0:257], in_=Cc[:, 0:257], func=AF.Identity,
                                                               scale=CL[:, 11:12]), reads=[Cc.key, CL.key], writes=[T2.key])
        yield
        s.op("dve", lambda e, Cn=Cn, T2=T2, Pc=Pc: e.tensor_tensor(out=Cn[:, 0:257], in0=Pc[:, 0:257], in1=T2[:, 0:257],
                                                                op=ALU.add), reads=[Pc.key, T2.key], writes=[Cn.key])
        yield
    run_interleaved(b, units_q, group_pre, ml_unit, 4)


def transpose_to_fm(b, XBt, dst, dstkey, ncols):
    s = b.s
    NCk = ncols // 128
    for cg in range(0, NCk, 8):
        n = min(8, NCk - cg)
        PT = b.rot("PT", b.PT)

        def tr(e, cg=cg, n=n, PT=PT):
            for c in range(n):
                ins = e.transpose(PT[:, c, :], XBt[:, (cg + c) * 128:(cg + c + 1) * 128], b.identb[:])
            return ins
        s.op("pe", tr, reads=[XBt.key, b.identb.key], writes=[PT.key])
        b.copy(b.evac_eng(), dst[:, cg:cg + n, :], PT[:, 0:n, :], [PT.key], [dstkey])


def outproj_residual(b, hv, hkey, ntok, w, resid, dst, tok0, final_store=False):
    s, d = b.s, b.dram
    D = b.cfg.D

    def ev(P, c0, nb, tt, nt):
        R = b.rot("SF", b.SF)
        b.load("sp", R[:, 0:nb], d[resid][tok0 + tt:tok0 + tt + 128, c0:c0 + nb], R.key, [], [R.key], dr=[resid])
        O = b.rot("SF", b.SF)
        s.op("dve", lambda e: e.tensor_tensor(out=O[:, 0:nb], in0=P[:, 0:nb], in1=R[:, 0:nb], op=ALU.add),
             reads=[P.key, R.key], writes=[O.key])
        b.store("sp", d[dst][tok0 + tt:tok0 + tt + 128, c0:c0 + nb], O[:, 0:nb], O.key, [O.key], [dst], final=final_store)
    linear(b, "tm", hv, hkey, ntok, w, 0, D, ev)


def phase_mixout0(b, stack):
    dense_bufs(b, stack)
    cfg, s, d = b.cfg, b.s, b.dram
    T, D, DC, GH, MH = cfg.T, cfg.D, cfg.DC, cfg.GH, cfg.MH
    WA, WB = GH * 128, MH * 256
    TBM = 512
    gfull = b.sb(stack, "gfull", [128, D], F32)
    for h in range(GH):
        s.dma("sp", lambda e, h=h: e.dma_start(out=gfull[:, h * 128:(h + 1) * 128], in_=d["gdn_norm"][0:1, :].partition_broadcast(128)),
              gfull.key, writes=[gfull.key])
    s.dma("sp", lambda e: e.dma_start(out=gfull[:, WA:D], in_=d["ml_norm"][0:1, :].partition_broadcast(128)),
          gfull.key, writes=[gfull.key])
    rsb = [b.sb(stack, "rsb%d" % i, [128, 2 * (GH + MH)], F32) for i in range(2)]
    for tb in range(T // TBM):
        H = b.H[0]
        hv = H[:, 0:DC * TBM].rearrange("p (c t) -> p c t", c=DC)
        for tt in range(TBM // 128):
            t0 = tb * TBM + tt * 128
            XA, XC = b.X[0], b.X[1]
            ZB, MB = b.XB[0], b.XB[1]
            RS = b.rot("rsb", rsb)
            b.load("sp", XA[:, 0:WA], d["OA0"][t0:t0 + 128, :], XA.key, [], [XA.key], dr=["OA0"])
            b.load("sp", XA[:, WA:D], d["HB0"][t0:t0 + 128, :], XA.key, [], [XA.key], dr=["HB0"])
            b.load("sp", XC[:, 0:WA], d["OA1"][t0:t0 + 128, :], XC.key, [], [XC.key], dr=["OA1"])
            b.load("sp", XC[:, WA:D], d["HB1"][t0:t0 + 128, :], XC.key, [], [XC.key], dr=["HB1"])
            b.load("sp", ZB[:, 0:WA], d["Z"][t0:t0 + 128, :], ZB.key, [], [ZB.key], dr=["Z"])
            b.load("sp", ZB[:, WA:D], d["OB"][t0:t0 + 128, :], ZB.key, [], [ZB.key], dr=["OB"])
            s.op("dve", lambda e, XA=XA, XC=XC: e.tensor_tensor(out=XA[:, 0:D], in0=XA[:, 0:D], in1=XC[:, 0:D], op=ALU.add),
                 reads=[XA.key, XC.key], writes=[XA.key])
            s.op("act", lambda e, XA=XA, XC=XC: e.activation(out=XC[:, 0:D], in_=XA[:, 0:D], func=AF.Square),
                 reads=[XA.key], writes=[XC.key])
            s.op("dve", lambda e, XC=XC, RS=RS: e.tensor_reduce(out=RS[:, 0:GH], in_=XC[:, 0:WA].rearrange("p (h k) -> p h k", h=GH),
                                                              axis=AX.X, op=ALU.add), reads=[XC.key], writes=[RS.key])
            s.op("dve", lambda e, XC=XC, RS=RS: e.tensor_reduce(out=RS[:, GH:GH + MH],
                                                              in_=XC[:, WA:D].rearrange("p (h k) -> p h k", h=MH),
                                                              axis=AX.X, op=ALU.add), reads=[XC.key, RS.key], writes=[RS.key])
            s.op("act", lambda e, RS=RS: e.activation(out=RS[:, 0:GH], in_=RS[:, 0:GH], func=AF.Sqrt, scale=1.0 / 128, bias=EPS),
                 reads=[RS.key], writes=[RS.key])
            s.op("act", lambda e, RS=RS: e.activation(out=RS[:, GH:GH + MH], in_=RS[:, GH:GH + MH], func=AF.Sqrt, scale=1.0 / 256,
                                                     bias=EPS), reads=[RS.key], writes=[RS.key])
            s.op("dve", lambda e, RS=RS: e.reciprocal(out=RS[:, GH + MH:2 * (GH + MH)], in_=RS[:, 0:GH + MH]),
                 reads=[RS.key], writes=[RS.key])
            s.op("dve", lambda e, XA=XA, RS=RS: e.tensor_tensor(
                out=XA[:, 0:WA].rearrange("p (h k) -> p h k", h=GH), in0=XA[:, 0:WA].rearrange("p (h k) -> p h k", h=GH),
                in1=RS[:, GH + MH:2 * GH + MH].unsqueeze(2).to_broadcast([128, GH, 128]), op=ALU.mult),
                reads=[XA.key, RS.key], writes=[XA.key])
            s.op("dve", lambda e, XA=XA, RS=RS: e.tensor_tensor(
                out=XA[:, WA:D].rearrange("p (h k) -> p h k", h=MH), in0=XA[:, WA:D].rearrange("p (h k) -> p h k", h=MH),
                in1=RS[:, 2 * GH + MH:2 * (GH + MH)].unsqueeze(2).to_broadcast([128, MH, 256]), op=ALU.mult),
                reads=[XA.key, RS.key], writes=[XA.key])
            s.op("dve", lambda e, XA=XA: e.tensor_tensor(out=XA[:, 0:D], in0=XA[:, 0:D], in1=gfull[:], op=ALU.mult),
                 reads=[XA.key, gfull.key], writes=[XA.key])
            s.op("act", lambda e, XC=XC, ZB=ZB: e.activation(out=XC[:, 0:WA], in_=ZB[:, 0:WA], func=AF.Silu),
                 reads=[ZB.key], writes=[XC.key])
            s.op("act", lambda e, XC=XC, ZB=ZB: e.activation(out=XC[:, WA:D], in_=ZB[:, WA:D], func=AF.Sigmoid),
                 reads=[ZB.key, XC.key], writes=[XC.key])
            s.op("dve", lambda e, XA=XA, XC=XC, MB=MB: e.tensor_tensor(out=MB[:, 0:D], in0=XA[:, 0:D], in1=XC[:, 0:D], op=ALU.mult),
                 reads=[XA.key, XC.key], writes=[MB.key])
            transpose_to_fm(b, MB, hv[:, :, tt * 128:(tt + 1) * 128], H.key, D)
        outproj_residual(b, hv, H.key, TBM, d["b_ev_w_out"][0], "x", "XL1", tb * TBM)


def phase_ffn(layer, src, dst):
    def ph(b, stack):
        dense_bufs(b, stack)
        cfg, s, d = b.cfg, b.s, b.dram
        T, D, DC, DFF = cfg.T, cfg.D, cfg.DC, cfg.DFF
        FC = DFF // 128
        TBF = 512
        pre = "f%d_" % b.pid
        cwt = b.sb(stack, pre + "cw", [128, FC, 4], F32)
        for j in range(3):
            s.dma("sp", lambda e, j=j: e.dma_start(out=cwt[:, :, j:j + 1],
                                                   in_=d["ffn_conv"][layer, j].rearrange("(c p o) -> p c o", p=128, o=1),
                                                   allow_slow_non_contiguous=True), cwt.key, writes=[cwt.key])
        s.dma("sp", lambda e: e.dma_start(out=cwt[:, :, 3:4], in_=d["ffn_conv_b"][layer].rearrange("(c p o) -> p c o", p=128, o=1),
                                          allow_slow_non_contiguous=True), cwt.key, writes=[cwt.key])
        hn2 = b.sb(stack, pre + "hn", [128, DC, TBF + 2], BF16)
        actT = b.H[0]
        av = actT[:, 0:FC * TBF].rearrange("p (c t) -> p c t", c=FC)
        gbuf = [b.sb(stack, pre + "g%d" % i, [128, TBF + 2], F32) for i in range(2)]
        cbuf = [b.sb(stack, pre + "c%d" % i, [128, TBF], F32) for i in range(2)]
        halo = b.sb(stack, pre + "halo", [128, D], F32)
        w_up, w_dn = d["b_ffn_w_up"][layer], d["b_ffn_w_down"][layer]
        wupv = w_up.rearrange("(c p) n -> p c n", p=128)
        gsrc = b.gffn[:, layer * DC:(layer + 1) * DC]
        for tb in range(T // TBF):
            tok0 = tb * TBF
            for tt in range(TBF // 128):
                t0 = tok0 + tt * 128
                rmsnorm_T(b, d[src][t0:t0 + 128, :], gsrc, b.gffn.key, hn2[:, :, 1 + tt * 128:1 + (tt + 1) * 128], hn2.key,
                          src_reads=[], dr=[src])
            s.op("pool", lambda e: e.memset(halo[:], 0.0), writes=[halo.key])
            if tok0 > 0:
                b.load("sp", halo[0:1, :], d[src][tok0 - 1:tok0, :], halo.key, [], [halo.key], dr=[src])
            if tok0 + TBF < T:
                b.load("sp", halo[1:2, :], d[src][tok0 + TBF:tok0 + TBF + 1, :], halo.key, [], [halo.key], dr=[src])
            rmsnorm_T(b, None, gsrc, b.gffn.key, None, hn2.key, preloaded=halo,
                      halo_dst=(hn2[:, :, 0:1], hn2[:, :, TBF + 1:TBF + 2]))
            for fb in range(0, DFF, 512):
                nb = min(512, DFF - fb)
                Wg = b.rot("W", b.W)
                Wgv = Wg[:, 0:DC * 512].rearrange("p (c n) -> p c n", c=DC)
                b.load("sp", Wgv[:, :, 0:nb], wupv[:, :, fb:fb + nb], Wg.key, [], [Wg.key], dr=["b_ffn_w_up"])
                Wv = b.rot("W", b.W)
                Wvv = Wv[:, 0:DC * 512].rearrange("p (c n) -> p c n", c=DC)
                b.load("sp", Wvv[:, :, 0:nb], wupv[:, :, DFF + fb:DFF + fb + nb], Wv.key, [], [Wv.key], dr=["b_ffn_w_up"])
                for ft in range(0, nb, 128):
                    f = (fb + ft) // 128
                    G = b.rot(pre + "g", gbuf)
                    half = (TBF + 2) // 2
                    for (c0, c1) in ((0, half), (half, TBF + 2)):
                        P = b.rot("P", b.PD)

                        def mm(e, P=P, c0=c0, c1=c1, ft=ft, Wgv=Wgv):
                            for c in range(DC):
                                ins = e.matmul(P[:, 0:c1 - c0], Wgv[:, c, ft:ft + 128], hn2[:, c, c0:c1], start=(c == 0),
                                               stop=(c == DC - 1))
                            return ins
                        s.op("pe", mm, reads=[Wg.key, hn2.key], writes=[P.key])
                        b.copy(b.evac_eng(), G[:, c0:c1], P[:, 0:c1 - c0], [P.key], [G.key])
                    Cv = b.rot(pre + "c", cbuf)
                    s.op("dve", lambda e, Cv=Cv, G=G, f=f: e.tensor_scalar(out=Cv[:], in0=G[:, 0:TBF], scalar1=cwt[:, f, 0:1],
                                                                        scalar2=None, op0=ALU.mult),
                         reads=[G.key, cwt.key], writes=[Cv.key])
                    s.op("dve", lambda e, Cv=Cv, G=G, f=f: e.scalar_tensor_tensor(out=Cv[:], in0=G[:, 1:TBF + 1], scalar=cwt[:, f, 1:2],
                                                                               in1=Cv[:], op0=ALU.mult, op1=ALU.add),
                         reads=[G.key, cwt.key, Cv.key], writes=[Cv.key])
                    s.op("dve", lambda e, Cv=Cv, G=G, f=f: e.scalar_tensor_tensor(out=Cv[:], in0=G[:, 2:TBF + 2], scalar=cwt[:, f, 2:3],
                                                                               in1=Cv[:], op0=ALU.mult, op1=ALU.add),
                         reads=[G.key, cwt.key, Cv.key], writes=[Cv.key])
                    s.op("act", lambda e, Cv=Cv, f=f: e.activation(out=Cv[:], in_=Cv[:], func=AF.Silu, bias=cwt[:, f, 3:4]),
                         reads=[Cv.key, cwt.key], writes=[Cv.key])
                    P = b.rot("P", b.PD)

                    def mv(e, P=P, ft=ft, Wvv=Wvv):
                        for c in range(DC):
                            ins = e.matmul(P[:, 0:TBF], Wvv[:, c, ft:ft + 128], hn2[:, c, 1:TBF + 1], start=(c == 0), stop=(c == DC - 1))
                        return ins
                    s.op("pe", mv, reads=[Wv.key, hn2.key], writes=[P.key])
                    s.op("dve", lambda e, P=P, Cv=Cv, f=f: e.tensor_tensor(out=av[:, f, :], in0=P[:, 0:TBF], in1=Cv[:], op=ALU.mult),
                         reads=[P.key, Cv.key], writes=[actT.key])
            outproj_residual(b, av, actT.key, TBF, w_dn, src, dst, tok0)
    return ph


PADK = 1024
TWO_PI = 6.283185307179586


def rope_consts():
    inv = (500000.0 ** (-np.arange(0, 32, 2, dtype=np.float32) / 32)).astype(np.float32)
    c = np.zeros((32, 2), np.float32)
    c[:, 0] = np.concatenate([inv, inv])
    c[0:16, 1] = -TWO_PI
    c[16:32, 1] = TWO_PI
    return c


def phase_qkv1(b, stack):
    dense_bufs(b, stack)
    cfg, s, d = b.cfg, b.s, b.dram
    T, TB, D, DC, S_ = cfg.T, cfg.TBLK, cfg.D, cfg.DC, cfg.SLOTS
    NQ = 3 * S_ * 128
    w = d["b_od_w_in"][0]
    ropec = b.sb(stack, "ropec_sb", [32, 4], F32)
    s.dma("sp", lambda e: e.dma_start(out=ropec[:, 0:2], in_=d["ropec"][:, :]), ropec.key, writes=[ropec.key])
    permT = b.sb(stack, "permT", [128, 32], BF16)
    s.op("pool", lambda e: e.memset(permT[:], 0.0), writes=[permT.key])
    s.op("pool", lambda e: e.affine_select(out=permT[:, 0:16], in_=permT[:, 0:16], pattern=[[-1, 16]], compare_op=ALU.not_equal,
                                          fill=1.0, base=-16, channel_multiplier=1), reads=[permT.key], writes=[permT.key])
    s.op("pool", lambda e: e.affine_select(out=permT[:, 16:32], in_=permT[:, 16:32], pattern=[[-1, 16]], compare_op=ALU.not_equal,
                                          fill=1.0, base=0, channel_multiplier=1), reads=[permT.key], writes=[permT.key])
    ang = [b.sb(stack, "ang%d" % i, [32, 512], F32) for i in range(2)]
    cst = [b.sb(stack, "cst%d" % i, [32, 2, 512], F32) for i in range(2)]
    cki = [b.sb(stack, "cki%d" % i, [32, 2, 512], mybir.dt.int32) for i in range(2)]
    ckf = [b.sb(stack, "ckf%d" % i, [32, 2, 512], F32) for i in range(2)]
    qa = [b.sb(stack, "qa%d" % i, [128, 512], BF16) for i in range(3)]
    qr = [b.sb(stack, "qr%d" % i, [128, 512], BF16) for i in range(3)]
    rt = [b.sb(stack, "rt%d" % i, [32, 512], F32) for i in range(3)]
    for r in range(0, NQ, 128):
        for (c0) in (0, PADK + T):
            for cc in range(0, PADK, 512):
                b.store("sp", d["KT1"][r:r + 128, c0 + cc:c0 + cc + 512], b.zerob[:, 0:512], "zpad1", [b.zerob.key], ["KT1"])
    for (r0) in (0, PADK + T):
        for rr in range(0, PADK, 128):
            for cc in range(0, NQ, 512):
                b.store("sp", d["VT1"][r0 + rr:r0 + rr + 128, cc:cc + 512], b.zerob[:, 0:512], "zpad1", [b.zerob.key], ["VT1"])
    for tb in range(T // TB):
        H = b.H[0]
        hv = H[:, 0:DC * TB].rearrange("p (c t) -> p c t", c=DC)
        tok0 = tb * TB
        for tt in range(TB // 128):
            t0 = tok0 + tt * 128
            rmsnorm_T(b, d["XL2"][t0:t0 + 128, :], b.gmix[:, DC:2 * DC], b.gmix.key, hv[:, :, tt * 128:(tt + 1) * 128], H.key,
                      dr=["XL2"])
        tabs = {}
        for ts in range(0, TB, 512):
            A = b.rot("ang", ang)
            CS = b.rot("cst", cst)
            b.load("sp", A[:], d["pos"][tok0 + ts:tok0 + ts + 512].partition_broadcast(32), A.key, [], [A.key])
            s.op("dve", lambda e, A=A: e.tensor_scalar(out=A[:], in0=A[:], scalar1=ropec[:, 0:1], scalar2=None, op0=ALU.mult),
                 reads=[A.key, ropec.key], writes=[A.key])
            KI = b.rot("cki", cki)
            KF = b.rot("ckf", ckf)
            s.op("dve", lambda e, A=A, CS=CS: e.tensor_scalar(out=CS[:, 0, :], in0=A[:], scalar1=1.0 / TWO_PI, scalar2=0.25,
                                                            op0=ALU.mult, op1=ALU.add), reads=[A.key], writes=[CS.key])
            s.op("dve", lambda e, A=A, CS=CS: e.tensor_scalar(out=CS[:, 1, :], in0=A[:], scalar1=1.0 / TWO_PI, scalar2=None,
                                                            op0=ALU.mult), reads=[A.key, CS.key], writes=[CS.key])
            s.op("dve", lambda e, CS=CS, KI=KI: e.tensor_copy(out=KI[:], in_=CS[:]), reads=[CS.key], writes=[KI.key])
            s.op("dve", lambda e, KF=KF, KI=KI: e.tensor_copy(out=KF[:], in_=KI[:]), reads=[KI.key], writes=[KF.key])
            s.op("dve", lambda e, CS=CS, KF=KF: e.tensor_tensor(out=CS[:], in0=CS[:], in1=KF[:], op=ALU.subtract),
                 reads=[CS.key, KF.key], writes=[CS.key])
            s.op("act", lambda e, CS=CS: e.activation(out=CS[:, 0, :], in_=CS[:, 0, :], func=AF.Sin, scale=TWO_PI),
                 reads=[CS.key], writes=[CS.key])
            s.op("act", lambda e, CS=CS: e.activation(out=CS[:, 1, :], in_=CS[:, 1, :], func=AF.Sin, scale=ropec[:, 1:2]),
                 reads=[CS.key, ropec.key], writes=[CS.key])
            tabs[ts] = CS

        def ev_rope(dstname, row_base, col_off):
            def f(P, c0, ncl, ts, nt):
                CS = tabs[ts]
                A_ = b.rot("qa", qa)
                R_ = b.rot("qr", qr)
                TT = b.rot("rt", rt)
                b.copy("act", A_[:, 0:nt], P[:, 0:nt], [P.key], [A_.key])
                b.copy("dve", R_[:, 0:nt], P[:, 0:nt], [P.key], [R_.key])
                P2 = b.rot("P", b.PD)
                s.op("pe", lambda e: e.matmul(P2[0:32, 0:nt], permT[:, 0:32], A_[:, 0:nt], start=True, stop=True),
                     reads=[permT.key, A_.key], writes=[P2.key])
                s.op("dve", lambda e: e.tensor_tensor(out=TT[:, 0:nt], in0=P2[0:32, 0:nt], in1=CS[:, 1, 0:nt], op=ALU.mult),
                     reads=[P2.key, CS.key], writes=[TT.key])
                s.op("dve", lambda e: e.tensor_tensor(out=R_[0:32, 0:nt], in0=A_[0:32, 0:nt], in1=CS[:, 0, 0:nt], op=ALU.mult),
                     reads=[A_.key, CS.key, R_.key], writes=[R_.key])
                s.op("dve", lambda e: e.tensor_tensor(out=R_[0:32, 0:nt], in0=R_[0:32, 0:nt], in1=TT[:, 0:nt], op=ALU.add),
                     reads=[R_.key, TT.key], writes=[R_.key])
                b.store("sp", d[dstname][c0 - row_base:c0 - row_base + ncl, col_off + tok0 + ts:col_off + tok0 + ts + nt],
                        R_[0:ncl, 0:nt], R_.key, [R_.key], [dstname])
            return f

        def ev_v(P, c0, nb, tt, nt):
            S = b.rot("SBF", b.SBF)
            b.copy(b.evac_eng(), S[:, 0:nb], P[:, 0:nb], [P.key], [S.key])
            b.store("sp", d["VT1"][PADK + tok0 + tt:PADK + tok0 + tt + 128, c0 - 2 * NQ:c0 - 2 * NQ + nb], S[:, 0:nb], S.key,
                    [S.key], ["VT1"])
        linear(b, "fm", hv, H.key, TB, w, 0, NQ, ev_rope("QT1", 0, 0))
        linear(b, "fm", hv, H.key, TB, w, NQ, NQ, ev_rope("KT1", NQ, PADK))
        linear(b, "tm", hv, H.key, TB, w, 2 * NQ, NQ, ev_v)


def phase_attn(b, stack):
    cfg, s, d = b.cfg, b.s, b.dram
    T, S_ = cfg.T, cfg.SLOTS
    DILS = (1, 4, 16)
    SBLK = 2048
    NSB = T // SBLK
    sc = 128 ** -0.5
    qb_ = [b.sb(stack, "aq%d" % i, [128, SBLK], BF16) for i in range(2)]
    kb_ = [b.sb(stack, "ak%d" % i, [128, SBLK + 2048], BF16) for i in range(1)]
    kmb = [b.sb(stack, "akmb%d" % i, [1, SBLK + 2048], BF16) for i in range(1)]
    vt = [b.sb(stack, "av%d" % i, [128, 2, 128], BF16) for i in range(8)]
    pt = [b.sb(stack, "ap%d" % i, [128, 256], BF16) for i in range(6)]
    osum = [b.sb(stack, "aos%d" % i, [128, SBLK], F32) for i in range(1)]
    dsum = [b.sb(stack, "ads%d" % i, [128, SBLK], F32) for i in range(1)]
    mo = [b.sb(stack, "amo%d" % i, [128, SBLK], BF16) for i in range(2)]
    qv = b.sb(stack, "aqv", [128, SBLK], F32)
    mab = b.sb(stack, "amab", [128, 256], BF16)
    s.op("dve", lambda e: e.tensor_copy(out=mab[:, 0:128], in_=b.L[:]), reads=[b.L.key], writes=[mab.key])
    s.op("dve", lambda e: e.tensor_copy(out=mab[:, 128:256], in_=b.U[:]), reads=[b.U.key, mab.key], writes=[mab.key])
    for slot in range(S_):
        for sb_ in range(NSB):
            tokS = sb_ * SBLK
            OS = b.rot("aos", osum)
            DS = b.rot("ads", dsum)
            units = []
            for g in (2, 1, 0):
                dil = DILS[g]
                blk = 128 * dil
                for bi in range(SBLK // blk):
                    for r in range(dil):
                        units.append((g, bi, r))
            state = {}

            def stage1(u):
                g, bi, r = u
                dil = DILS[g]
                blk = 128 * dil
                row0 = (g * S_ + slot) * 128
                halo = 64 * dil
                nk = SBLK + 2 * halo
                k0 = PADK + tokS - halo
                if (bi, r) == (0, 0):
                    Q = b.rot("aq", qb_)
                    Kt = b.rot("ak", kb_)
                    KMB = b.rot("akmb", kmb)
                    b.load("sp", Q[:], d["QT1"][row0:row0 + 128, tokS:tokS + SBLK], Q.key, [], [Q.key], dr=["QT1"])
                    b.load("sp", Kt[:, 0:nk], d["KT1"][row0:row0 + 128, k0:k0 + nk], Kt.key, [], [Kt.key], dr=["KT1"])
                    b.load("sp", KMB[0:1, 0:nk], d["kmask"][k0:k0 + nk].rearrange("(o n) -> o n", o=1), KMB.key, [], [KMB.key])
                    state["qk"] = (Q, Kt, KMB)
                Q, Kt, KMB = state["qk"]
                qsl = slice(bi * blk + r, bi * blk + r + 127 * dil + 1, dil)
                kA = slice(bi * blk + r, bi * blk + r + 127 * dil + 1, dil)
                kB = slice(bi * blk + blk + r, bi * blk + blk + r + 127 * dil + 1, dil)
                V = b.rot("av", vt)
                tA = k0 + bi * blk + r
                vrowsA = d["VT1"][tA:tA + 127 * dil + 1:dil, row0:row0 + 128]
                vrowsB = d["VT1"][tA + blk:tA + blk + 127 * dil + 1:dil, row0:row0 + 128]
                b.load("sp", V[:, 0, :], vrowsA, V.key, [], [V.key], dr=["VT1"])
                b.load("sp", V[:, 1, :], vrowsB, V.key, [], [V.key], dr=["VT1"])
                Ps = b.rot("Pa", b.P[0:4])

                def sc_mm(e):
                    e.matmul(Ps[:, 0:128], Kt[:, kA], Q[:, qsl], start=True, stop=False)
                    e.matmul(Ps[:, 0:128], KMB[0:1, kA], b.onesb[0:1, 0:128], start=False, stop=True)
                    e.matmul(Ps[:, 128:256], Kt[:, kB], Q[:, qsl], start=True, stop=False)
                    return e.matmul(Ps[:, 128:256], KMB[0:1, kB], b.onesb[0:1, 0:128], start=False, stop=True)
                s.op("pe", sc_mm, reads=[Kt.key, Q.key, KMB.key, b.onesb.key], writes=[Ps.key])
                PT_ = b.rot("ap", pt)
                s.op("act", lambda e: e.activation(out=PT_[:], in_=Ps[:, 0:256], func=AF.Exp, scale=sc),
                     reads=[Ps.key], writes=[PT_.key])
                s.op("dve", lambda e: e.tensor_tensor(out=PT_[:], in0=PT_[:], in1=mab[:], op=ALU.mult),
                     reads=[PT_.key, mab.key], writes=[PT_.key])
                return (V, PT_, qsl, g)

            def stage2(st):
                V, PT_, qsl, g = st
                Po = b.rot("Pb", b.P[4:8])

                def pv_mm(e):
                    e.matmul(Po[:, 0:128], V[:, 0, :], PT_[:, 0:128], start=True, stop=False)
                    e.matmul(Po[:, 0:128], V[:, 1, :], PT_[:, 128:256], start=False, stop=True)
                    e.matmul(Po[:, 128:256], b.onesb[:], PT_[:, 0:128], start=True, stop=False)
                    return e.matmul(Po[:, 128:256], b.onesb[:], PT_[:, 128:256], start=False, stop=True)
                s.op("pe", pv_mm, reads=[V.key, PT_.key, b.onesb.key], writes=[Po.key])
                if g == 2:
                    s.op("act", lambda e: e.activation(out=OS[:, qsl], in_=Po[:, 0:128], func=AF.Copy),
                         reads=[Po.key], writes=[OS.key])
                    s.op("dve", lambda e: e.tensor_copy(out=DS[:, qsl], in_=Po[:, 128:256]), reads=[Po.key], writes=[DS.key])
                else:
                    s.op("dve", lambda e: e.tensor_tensor(out=OS[:, qsl], in0=Po[:, 0:128], in1=OS[:, qsl], op=ALU.add),
                         reads=[Po.key, OS.key], writes=[OS.key])
                    s.op("dve", lambda e: e.tensor_tensor(out=DS[:, qsl], in0=Po[:, 128:256], in1=DS[:, qsl], op=ALU.add),
                         reads=[Po.key, DS.key], writes=[DS.key])
            pend = []
            for u in units:
                pend.append(stage1(u))
                if len(pend) > 2:
                    stage2(pend.pop(0))
            while pend:
                stage2(pend.pop(0))
            MO = b.rot("amo", mo)
            b.load("sp", qv[:], d["qvalid"][tokS:tokS + SBLK].partition_broadcast(128), qv.key, [], [qv.key])
            s.op("dve", lambda e, DS=DS: e.tensor_scalar(out=DS[:], in0=DS[:], scalar1=1e-30, scalar2=None, op0=ALU.max),
                 reads=[DS.key], writes=[DS.key])
            s.op("dve", lambda e, DS=DS: e.reciprocal(out=DS[:], in_=DS[:]), reads=[DS.key], writes=[DS.key])
            s.op("dve", lambda e, DS=DS: e.tensor_tensor(out=DS[:], in0=DS[:], in1=qv[:], op=ALU.mult),
                 reads=[DS.key, qv.key], writes=[DS.key])
            s.op("dve", lambda e, MO=MO, OS=OS, DS=DS: e.tensor_tensor(out=MO[:], in0=OS[:], in1=DS[:], op=ALU.mult),
                 reads=[OS.key, DS.key], writes=[MO.key])
            b.store("sp", d["MIXT"][slot * 128:(slot + 1) * 128, tokS:tokS + SBLK], MO[:], MO.key, [MO.key], ["MIXT"])


def phase_mixout1(b, stack):
    dense_bufs(b, stack)
    cfg, s, d = b.cfg, b.s, b.dram
    T, D, DC = cfg.T, cfg.D, cfg.DC
    TBM = 512
    mv = d["MIXT"].rearrange("(c p) t -> p c t", p=128)
    for tb in range(T // TBM):
        H = b.H[0]
        hv = H[:, 0:DC * TBM].rearrange("p (c t) -> p c t", c=DC)
        b.load("sp", hv, mv[:, :, tb * TBM:(tb + 1) * TBM], H.key, [], [H.key], dr=["MIXT"])
        outproj_residual(b, hv, H.key, TBM, d["b_od_w_out"][0], "XL2", "XL3", tb * TBM)


def phase_final(b, stack):
    dense_bufs(b, stack, need_h=False)
    cfg, s, d = b.cfg, b.s, b.dram
    T, D = cfg.T, cfg.D
    gf = b.sb(stack, "gfinb", [128, D], F32)
    s.dma("sp", lambda e: e.dma_start(out=gf[:], in_=d["norm_final"].rearrange("(o n) -> o n", o=1).partition_broadcast(128)),
          gf.key, writes=[gf.key])
    yo = [b.sb(stack, "yo%d" % i, [128, D], F32) for i in range(2)]
    for t0 in range(0, T, 128):
        X = b.rot("X", b.X)
        XB = b.rot("XB", b.XB)
        SC = b.rot("SC", b.SC)
        Y = b.rot("yo", yo)
        b.load("sp", X[:, 0:D], d["XL4"][t0:t0 + 128, :], X.key, [], [X.key], dr=["XL4"])
        s.op("act", lambda e, X=X, XB=XB, SC=SC: e.activation(out=XB[:, 0:D], in_=X[:, 0:D], func=AF.Square, accum_out=SC[:, 0:1]),
             reads=[X.key], writes=[XB.key, SC.key])
        s.op("act", lambda e, SC=SC: e.activation(out=SC[:, 1:2], in_=SC[:, 0:1], func=AF.Sqrt, scale=1.0 / D, bias=EPS),
             reads=[SC.key], writes=[SC.key])
        s.op("dve", lambda e, SC=SC: e.reciprocal(out=SC[:, 2:3], in_=SC[:, 1:2]), reads=[SC.key], writes=[SC.key])
        s.op("dve", lambda e, X=X, SC=SC, Y=Y: e.scalar_tensor_tensor(out=Y[:], in0=X[:, 0:D], scalar=SC[:, 2:3], in1=gf[:],
                                                                   op0=ALU.mult, op1=ALU.mult),
             reads=[X.key, SC.key, gf.key], writes=[Y.key])
        b.store("sp", d["y"][t0:t0 + 128, :], Y[:], Y.key, [Y.key], ["y"], final=True)


def all_phases():
    return [phase_wcast, phase_A, phase_gdn, phase_mlstm, phase_mixout0, phase_ffn(0, "XL1", "XL2"), phase_qkv1, phase_attn, phase_mixout1,
            phase_ffn(1, "XL3", "XL4"), phase_final]


N_CORES = 4


def kernel(**inputs):
    import ml_dtypes
    cfg = Cfg()
    T = cfg.T
    xp = np.asarray(inputs["x_prompt"], np.float32)
    xs = np.asarray(inputs["x_sample"], np.float32)
    shared = {k: np.ascontiguousarray(np.asarray(v, np.float32)) for k, v in inputs.items()
              if k not in ("x_prompt", "x_sample")}
    in_maps = []
    lens = []
    for c in range(N_CORES):
        seq = xp[c] if c < 2 else xs[c - 2]
        n = seq.shape[0]
        x = np.zeros((T, cfg.D), np.float32)
        x[:n] = seq
        km = np.full((T + 2048,), -BIG, np.float32)
        km[PADK:PADK + n] = 0.0
        m = dict(shared)
        m["x"] = x
        m["kmask"] = km.astype(ml_dtypes.bfloat16)
        m["pos"] = np.arange(T, dtype=np.float32)
        m["ropec"] = rope_consts()
        qv = np.zeros((T,), np.float32)
        qv[:n] = 1.0
        m["qvalid"] = qv
        in_maps.append(m)
        lens.append(n)
    b = build(cfg, all_phases())
    res = run_bass_kernel_spmd(b.nc, in_maps, core_ids=list(range(N_CORES)))
    ys = [np.asarray(res.results[c]["y"], np.float32)[:lens[c]] for c in range(N_CORES)]
    return (np.stack(ys[0:2], 0), np.stack(ys[2:4], 0))
```
